# Optimizing a Trainium2 kernel written in Bass

```python
import math
import jax, jax.numpy as jnp
from jax import lax
import numpy as np

D_MODEL = 1024
BATCH = 1
SEQ = 16384
DEPTH = 1
DEC_BATCH = 4
DEC_SEQ = 4096
PAST_LEN = 128

MIX_WIDTH = D_MODEL
ATTN_WIDTH = D_MODEL // 2
SSM_WIDTH = MIX_WIDTH - ATTN_WIDTH
HEAD_DIM = 64
N_HEADS = ATTN_WIDTH // HEAD_DIM
DILATION_BRANCHES = ((128, 1), (512, 4), (2048, 16))
Q_BLOCK = 128
ROPE_THETA = 10000.0
SSM_GROUP = 16
N_SSM_GROUPS = SSM_WIDTH // SSM_GROUP
SSM_STATE = 64
FFN_HIDDEN = -(-8 * D_MODEL // (3 * 256)) * 256
PROJ_WIDTH = 3 * ATTN_WIDTH + SSM_WIDTH
N_MOD = 6
EPS = 1e-6

kernel_name = "hymba_dilated_s5_adaln_encoder"


def _rmsnorm(x, g):
    x32 = x.astype(jnp.float32)
    y = x32 * lax.rsqrt(jnp.mean(x32 * x32, axis=-1, keepdims=True) + EPS)
    return (y * g.astype(jnp.float32)).astype(x.dtype)


def _rope_tables(seq_len):
    inv = 1.0 / (ROPE_THETA ** (jnp.arange(0, HEAD_DIM, 2, dtype=jnp.float32) / HEAD_DIM))
    ang = jnp.arange(seq_len, dtype=jnp.float32)[:, None] * inv[None, :]
    return jnp.cos(ang), jnp.sin(ang)


def _rope(x, cos, sin):
    x32 = x.astype(jnp.float32)
    x1, x2 = jnp.split(x32, 2, axis=-1)
    c = cos[None, :, None, :]
    s = sin[None, :, None, :]
    return jnp.concatenate([x1 * c - x2 * s, x2 * c + x1 * s], axis=-1).astype(x.dtype)


def _dilated_attention(q, k, v):
    b, s_len, h, dh = q.shape
    n_blk = s_len // Q_BLOCK
    scale = HEAD_DIM ** -0.5
    branches = []
    for window, dil in DILATION_BRANCHES:
        half = (window // (2 * dil)) * dil
        pads = ((0, 0), (half, half), (0, 0), (0, 0))
        branches.append((jnp.pad(k, pads), jnp.pad(v, pads), half, dil))

    def block(i):
        start = i * Q_BLOCK
        qb = lax.dynamic_slice_in_dim(q, start, Q_BLOCK, axis=1)
        qpos = start + jnp.arange(Q_BLOCK)
        outs, lses = [], []
        for kp, vp, half, dil in branches:
            offs = jnp.arange(-half, half + 1, dil)
            pos = qpos[:, None] + offs[None, :]
            valid = (pos >= 0) & (pos < s_len)
            kg = jnp.take(kp, pos + half, axis=1)
            vg = jnp.take(vp, pos + half, axis=1)
            sc = jnp.einsum('bqhd,bqnhd->bhqn', qb, kg,
                            preferred_element_type=jnp.float32) * scale
            sc = jnp.where(valid[None, None], sc, -jnp.inf)
            m = jnp.max(sc, axis=-1, keepdims=True)
            p = jnp.exp(sc - m)
            den = jnp.sum(p, axis=-1, keepdims=True)
            o = jnp.einsum('bhqn,bqnhd->bqhd', p / den, vg.astype(jnp.float32))
            outs.append(o)
            lses.append(jnp.transpose((m + jnp.log(den))[..., 0], (0, 2, 1)))
        wts = jax.nn.softmax(jnp.stack(lses, axis=0), axis=0)
        out = jnp.sum(wts[..., None] * jnp.stack(outs, axis=0), axis=0)
        return out.astype(q.dtype)

    out = lax.map(block, jnp.arange(n_blk))
    return jnp.moveaxis(out, 0, 1).reshape(b, s_len, h * dh)


def _complex_linear_combine(e1, e2):
    a1r, a1i, b1r, b1i = e1
    a2r, a2i, b2r, b2i = e2
    return (a2r * a1r - a2i * a1i,
            a2r * a1i + a2i * a1r,
            a2r * b1r - a2i * b1i + b2r,
            a2r * b1i + a2i * b1r + b2i)


def _s5_direction(ug, lam_re, lam_im, log_dt, b_re, b_im, c_re, c_im, reverse):
    dt = jnp.exp(log_dt)[:, None]
    mag = jnp.exp(lam_re * dt)
    ang = lam_im * dt
    a_re = mag * jnp.cos(ang)
    a_im = mag * jnp.sin(ang)
    nr = a_re - 1.0
    den = lam_re * lam_re + lam_im * lam_im
    z_re = ((nr * lam_re + a_im * lam_im) / den)[..., None]
    z_im = ((a_im * lam_re - nr * lam_im) / den)[..., None]
    bb_re = z_re * b_re - z_im * b_im
    bb_im = z_re * b_im + z_im * b_re
    bu_re = jnp.einsum('bsgc,gpc->bsgp', ug, bb_re)
    bu_im = jnp.einsum('bsgc,gpc->bsgp', ug, bb_im)
    shape = bu_re.shape
    elems = (jnp.broadcast_to(a_re, shape), jnp.broadcast_to(a_im, shape), bu_re, bu_im)
    _, _, h_re, h_im = lax.associative_scan(_complex_linear_combine, elems,
                                            reverse=reverse, axis=1)
    return (jnp.einsum('bsgp,gcp->bsgc', h_re, c_re)
            - jnp.einsum('bsgp,gcp->bsgc', h_im, c_im))


def _s5_bidirectional(u, lam_re, lam_im, log_dt, b_re, b_im, c_re, c_im, d_skip, w_glu, b_glu):
    b, s_len, _ = u.shape
    ug = u.reshape(b, s_len, N_SSM_GROUPS, SSM_GROUP)
    y_f = _s5_direction(ug, lam_re[0], lam_im[0], log_dt[0], b_re[0], b_im[0],
                        c_re[0], c_im[0], reverse=False)
    y_b = _s5_direction(ug, lam_re[1], lam_im[1], log_dt[1], b_re[1], b_im[1],
                        c_re[1], c_im[1], reverse=True)
    y = (y_f + y_b).reshape(b, s_len, SSM_WIDTH) + d_skip * u
    g = jax.nn.gelu(y)
    return g * jax.nn.sigmoid(g @ w_glu + b_glu)


def _encode(x, c, w_ada, b_ada, norm1_g, w_in, lam_re, lam_im, log_dt, b_re, b_im,
            c_re, c_im, d_skip, w_glu, b_glu, attn_norm_g, ssm_norm_g, w_o,
            norm2_g, w1, w3, w2, final_g):
    b, s_len, _ = x.shape
    cos, sin = _rope_tables(s_len)
    for l in range(DEPTH):
        mod = (jax.nn.silu(c) @ w_ada[l] + b_ada[l])[:, None, :]
        sh1, sc1, g1, sh2, sc2, g2 = jnp.split(mod, N_MOD, axis=-1)
        h = _rmsnorm(x, norm1_g[l]) * (1.0 + sc1) + sh1
        proj = h @ w_in[l]
        q, k, v, u = jnp.split(proj, [ATTN_WIDTH, 2 * ATTN_WIDTH, 3 * ATTN_WIDTH], axis=-1)
        q = _rope(q.reshape(b, s_len, N_HEADS, HEAD_DIM), cos, sin)
        k = _rope(k.reshape(b, s_len, N_HEADS, HEAD_DIM), cos, sin)
        v = v.reshape(b, s_len, N_HEADS, HEAD_DIM)
        attn = _dilated_attention(q, k, v)
        ssm = _s5_bidirectional(u, lam_re[l], lam_im[l], log_dt[l], b_re[l], b_im[l],
                                c_re[l], c_im[l], d_skip[l], w_glu[l], b_glu[l])
        mixed = jnp.concatenate([_rmsnorm(attn, attn_norm_g[l]),
                                 _rmsnorm(ssm, ssm_norm_g[l])], axis=-1) @ w_o[l]
        x = x + g1 * mixed
        h = _rmsnorm(x, norm2_g[l]) * (1.0 + sc2) + sh2
        ffn = (jax.nn.silu(h @ w1[l]) * (h @ w3[l])) @ w2[l]
        x = x + g2 * ffn
    return _rmsnorm(x, final_g)


def setup_inputs(seed: int = 0) -> dict:
    key = jax.random.key(seed)
    ks = jax.random.split(key, 32)
    f32 = jnp.float32
    L, D, G, P, C = DEPTH, D_MODEL, N_SSM_GROUPS, SSM_STATE, SSM_GROUP

    def nrm(k, shape, scale):
        return jax.random.normal(k, shape, f32) * scale

    lam_im0 = jnp.broadcast_to(jnp.pi * jnp.arange(P, dtype=f32), (L, 2, G, P))
    return {
        "x_prompt": nrm(ks[0], (BATCH, SEQ, D), 1.0),
        "x_sample": nrm(ks[1], (DEC_BATCH, DEC_SEQ, D), 1.0),
        "c_prompt": nrm(ks[2], (BATCH, D), 1.0),
        "c_sample": nrm(ks[3], (DEC_BATCH, D), 1.0),
        "w_ada": nrm(ks[4], (L, D, N_MOD * D), 0.5 * D ** -0.5),
        "b_ada": nrm(ks[5], (L, N_MOD * D), 0.01),
        "norm1_g": 1.0 + nrm(ks[6], (L, D), 0.02),
        "w_in": nrm(ks[7], (L, D, PROJ_WIDTH), D ** -0.5),
        "lam_re": -0.5 + nrm(ks[8], (L, 2, G, P), 0.01),
        "lam_im": lam_im0 + nrm(ks[9], (L, 2, G, P), 0.01),
        "log_dt": jax.random.uniform(ks[10], (L, 2, G), f32, math.log(0.001), math.log(0.1)),
        "b_re": nrm(ks[11], (L, 2, G, P, C), (2 * C) ** -0.5),
        "b_im": nrm(ks[12], (L, 2, G, P, C), (2 * C) ** -0.5),
        "c_re": nrm(ks[13], (L, 2, G, C, P), (2 * P) ** -0.5),
        "c_im": nrm(ks[14], (L, 2, G, C, P), (2 * P) ** -0.5),
        "d_skip": nrm(ks[15], (L, SSM_WIDTH), 1.0),
        "w_glu": nrm(ks[16], (L, SSM_WIDTH, SSM_WIDTH), SSM_WIDTH ** -0.5),
        "b_glu": nrm(ks[17], (L, SSM_WIDTH), 0.01),
        "attn_norm_g": 1.0 + nrm(ks[18], (L, ATTN_WIDTH), 0.02),
        "ssm_norm_g": 1.0 + nrm(ks[19], (L, SSM_WIDTH), 0.02),
        "w_o": nrm(ks[20], (L, MIX_WIDTH, D), MIX_WIDTH ** -0.5),
        "norm2_g": 1.0 + nrm(ks[21], (L, D), 0.02),
        "w1": nrm(ks[22], (L, D, FFN_HIDDEN), D ** -0.5),
        "w3": nrm(ks[23], (L, D, FFN_HIDDEN), D ** -0.5),
        "w2": nrm(ks[24], (L, FFN_HIDDEN, D), FFN_HIDDEN ** -0.5),
        "final_g": 1.0 + nrm(ks[25], (D,), 0.02),
    }


def reference(x_prompt, x_sample, c_prompt, c_sample, w_ada, b_ada, norm1_g, w_in,
              lam_re, lam_im, log_dt, b_re, b_im, c_re, c_im, d_skip, w_glu, b_glu,
              attn_norm_g, ssm_norm_g, w_o, norm2_g, w1, w3, w2, final_g):
    y_prompt = _encode(x_prompt, c_prompt, w_ada, b_ada, norm1_g, w_in, lam_re, lam_im,
                       log_dt, b_re, b_im, c_re, c_im, d_skip, w_glu, b_glu,
                       attn_norm_g, ssm_norm_g, w_o, norm2_g, w1, w3, w2, final_g)
    y_sample = _encode(x_sample, c_sample, w_ada, b_ada, norm1_g, w_in, lam_re, lam_im,
                       log_dt, b_re, b_im, c_re, c_im, d_skip, w_glu, b_glu,
                       attn_norm_g, ssm_norm_g, w_o, norm2_g, w1, w3, w2, final_g)
    return (y_prompt, y_sample)
```

```python
import os
import numpy as np
from contextlib import ExitStack
import concourse.bass as bass
import concourse.mybir as mybir
from concourse.bass_utils import run_bass_kernel_spmd

F32 = mybir.dt.float32
BF16 = mybir.dt.bfloat16
I32 = mybir.dt.int32
ALU = mybir.AluOpType
AF = mybir.ActivationFunctionType
AX = mybir.AxisListType

D = 1024
SEG = 4096
HALO = 1024
EXT = SEG + 2 * HALO
TA = 512
NTA = EXT // TA
EPS = 1e-6
FFN = 2816
NFC = FFN // 128


class Sched:
    COMPUTE = ("tensor", "vector", "scalar", "gpsimd")
    QUEUES = ("sync", "gpsimd")

    def __init__(self, nc, es, ndma=12):
        self.nc = nc
        self.es = es
        self.ops = []
        self.last_w = {}
        self.readers = {}
        self.ndma = ndma
        self.dma_pool = {q: [None] * ndma for q in self.QUEUES}
        self.dma_rr = {q: 0 for q in self.QUEUES}

    def _deps(self, reads, writes):
        d = set()
        for k in reads:
            if k in self.last_w:
                d.add(self.last_w[k])
        for k in writes:
            if k in self.last_w:
                d.add(self.last_w[k])
            for r in self.readers.get(k, ()):
                d.add(r)
        return d

    def _commit(self, oid, reads, writes):
        for k in reads:
            self.readers.setdefault(k, []).append(oid)
        for k in writes:
            self.last_w[k] = oid
            self.readers[k] = []

    @staticmethod
    def _excl(reads, writes):
        px = [k for k in reads if k.startswith(("ps", "pj", "tp", "sw"))]
        return (reads, list(writes) + px) if px else (reads, writes)

    def op(self, eng, fn, reads=(), writes=()):
        reads, writes = self._excl(reads, writes)
        oid = len(self.ops)
        deps = self._deps(reads, writes)
        self.ops.append(dict(eng=eng, fn=fn, deps=deps, kind="c", id=oid))
        self._commit(oid, reads, writes)
        return oid

    def dma(self, q, fn, reads=(), writes=()):
        oid = len(self.ops)
        deps = self._deps(reads, writes)
        slot = self.dma_rr[q]
        self.dma_rr[q] = (slot + 1) % self.ndma
        prev = self.dma_pool[q][slot]
        if prev is not None:
            deps.add(prev)
        self.dma_pool[q][slot] = oid
        self.ops.append(dict(eng=q, fn=fn, deps=deps, kind="d", id=oid, slot=slot))
        self._commit(oid, reads, writes)
        return oid

    def barrier(self):
        last = {}
        for o in self.ops:
            if o["kind"] == "c":
                if o["eng"] != "sync":
                    last[("c", o["eng"])] = o["id"]
            else:
                last[("d", o["eng"], o["slot"])] = o["id"]
        deps = set(last.values())
        for eng in ("tensor", "vector", "scalar", "gpsimd", "sync"):
            oid = len(self.ops)
            self.ops.append(dict(eng=eng, fn=(lambda e: e.nop()), deps=set(deps), kind="c", id=oid))

    def emit(self, final_wait_eng="sync"):
        nc, es = self.nc, self.es
        ops = self.ops

        def needs_wait(o, t):
            if t["kind"] == "d":
                return True
            if t["eng"] == o["eng"] and o["kind"] == "c" and o["eng"] == "tensor":
                return False
            return True

        sig = [False] * len(ops)
        for o in ops:
            for d in o["deps"]:
                t = ops[d]
                if t["kind"] == "c" and needs_wait(o, t):
                    sig[d] = True
        csem = {e: es.enter_context(nc.semaphore("cs_" + e)) for e in self.COMPUTE}
        dsem = {q: [es.enter_context(nc.semaphore(f"ds_{q}{i}")) for i in range(self.ndma)]
                for q in self.QUEUES}
        ccount = {e: 0 for e in self.COMPUTE}
        dcount = {q: [0] * self.ndma for q in self.QUEUES}
        val = [None] * len(ops)
        for o in ops:
            if o["kind"] == "d":
                q, s = o["eng"], o["slot"]
                dcount[q][s] += 16
                val[o["id"]] = (dsem[q][s], dcount[q][s])
            elif sig[o["id"]]:
                e = o["eng"]
                assert e in csem, e
                ccount[e] += 1
                val[o["id"]] = (csem[e], ccount[e])
        per = {}
        for o in ops:
            per.setdefault(o["eng"], []).append(o)
        block = es.enter_context(nc.Block())
        self.nwaits = 0

        def run(engname, eng):
            waited = {}
            for o in per.get(engname, []):
                need = {}
                for d in o["deps"]:
                    t = ops[d]
                    if not needs_wait(o, t):
                        continue
                    sem, v = val[d]
                    key = id(sem)
                    if key not in need or need[key][1] < v:
                        need[key] = (sem, v)
                for key, (sem, v) in need.items():
                    if waited.get(key, 0) >= v:
                        continue
                    waited[key] = v
                    eng.wait_ge(sem, v)
                    self.nwaits += 1
                ins = o["fn"](eng)
                if o["kind"] == "d":
                    ins.then_inc(val[o["id"]][0], 16)
                elif sig[o["id"]]:
                    ins.then_inc(val[o["id"]][0], 1)
            if engname == final_wait_eng:
                for q in self.QUEUES:
                    for s in range(self.ndma):
                        if dcount[q][s] and waited.get(id(dsem[q][s]), 0) < dcount[q][s]:
                            eng.wait_ge(dsem[q][s], dcount[q][s])

        @block.sync
        def _(e):
            run("sync", e)

        @block.scalar
        def _(e):
            run("scalar", e)

        @block.gpsimd
        def _(e):
            run("gpsimd", e)

        @block.vector
        def _(e):
            run("vector", e)

        @block.tensor
        def _(e):
            run("tensor", e)


DILS = (1, 4, 16)


def sl(start, n, step=1):
    return slice(start, start + (n - 1) * step + 1, step)


def kb_index(di, r, end):
    base = (0, 2, 10)[di]
    return base + r * 2 + end


NKB = 42

C_BADA, C_N1, C_N2, C_C, C_AG, C_SG, C_DS, C_BG, C_FG, NCOLS = 0, 48, 56, 64, 72, 76, 80, 84, 88, 96


def build(debug=False, stage=99, sub=99, tiles=None):
    nc = bass.Bass("TRN2", target_bir_lowering=False)
    dk = "ExternalOutput" if debug else "Internal"

    def din(name, shape, dt=F32):
        return nc.dram_tensor(name, list(shape), dt, kind="ExternalInput")

    xext = din("xext", [EXT, D])
    cvec = din("cvec", [8, 128])
    cosT = din("cosT", [128, EXT])
    sinT = din("sinT", [128, EXT])
    consts = din("consts", [128, 1280])
    w_ada = din("w_ada", [D, 6 * D])
    b_ada = din("b_ada", [48, 128])
    norm1_g = din("norm1_g", [8, 128])
    norm2_g = din("norm2_g", [8, 128])
    final_g = din("final_g", [8, 128])
    attn_g = din("attn_norm_g", [4, 128])
    ssm_g = din("ssm_norm_g", [4, 128])
    d_skip = din("d_skip", [4, 128])
    b_glu = din("b_glu", [4, 128])
    w_in = din("w_in", [D, 2048])
    tvalid = din("tvalid", [128, EXT // 128])
    consts2 = din("consts2", [128, 72 + 256])
    lam_re = din("lam_re", [32, 128])
    lam_im = din("lam_im", [32, 128])
    log_dt = din("log_dt", [32, 2])
    b_re = din("b_re", [32, 128, 16])
    b_im = din("b_im", [32, 128, 16])
    c_re = din("c_re", [64, 16, 64])
    c_im = din("c_im", [64, 16, 64])
    w_glu = din("w_glu", [512, 512])
    w_o = din("w_o", [D, D])
    w1 = din("w1", [D, FFN])
    w3 = din("w3", [D, FFN])
    w2 = din("w2", [FFN, D])
    attn_gT = din("attn_gT", [64, 8])
    xoth = din("xoth", [3 * SEG, D])
    flcol = din("flcol", [128, 6])
    y_out = nc.dram_tensor("y_out", [SEG, D], F32, kind="ExternalOutput")

    qT_s = nc.dram_tensor("qT_s", [512, SEG], BF16, kind=dk)
    kT_s = nc.dram_tensor("kT_s", [512, EXT], BF16, kind=dk)
    v_s = nc.dram_tensor("v_s", [EXT, 520], BF16, kind=dk)
    Us = nc.dram_tensor("Us", [4, 32, 128, 512], BF16, kind=dk)
    attn_s = nc.dram_tensor("attn_s", [8, 64, SEG], F32, kind=dk)
    y_s = nc.dram_tensor("y_s", [512, 8, 512], F32, kind=dk)
    ssmn_s = nc.dram_tensor("ssmn_s", [512, SEG], BF16, kind=dk)
    w1b = nc.dram_tensor("w1b", [NFC, 128, 8, 128], BF16, kind="Internal")
    w3b = nc.dram_tensor("w3b", [NFC, 128, 8, 128], BF16, kind="Internal")
    if debug:
        dbg_s5 = nc.dram_tensor("dbg_s5", [128, 32768], F32, kind="ExternalOutput")
        dbg_cols = nc.dram_tensor("dbg_cols", [128, 160], F32, kind="ExternalOutput")
        dbg_rows = nc.dram_tensor("dbg_rows", [128, 3 * D], F32, kind="ExternalOutput")

    with ExitStack() as es:
        S = Sched(nc, es)

        def sb(name, shape, dt=F32):
            return es.enter_context(nc.sbuf_tensor(name, list(shape), dt))

        def ps(name, shape, dt=F32):
            return es.enter_context(nc.psum_tensor(name, list(shape), dt))

        ident_f = sb("ident_f", [128, 128])
        ones_f = sb("ones_f", [128, 128])
        ident_b = sb("ident_b", [128, 128], BF16)
        perm_b = sb("perm_b", [128, 128], BF16)
        band_b = sb("band_b", [128, 2, 128], BF16)
        cols = sb("cols", [128, NCOLS])
        modc = sb("modc", [128, 48])
        sc12 = sb("sc12", [128, 16])
        silu_c = sb("silu_c", [128, 8])
        rows = sb("rows", [128, 3, D])

        S.dma("sync", lambda e: e.dma_start(out=ident_f[:], in_=consts.ap()[:, 0:128]), writes=["ident_f"])
        S.dma("sync", lambda e: e.dma_start(out=ones_f[:], in_=consts.ap()[:, 512:640]), writes=["ones_f"])
        S.dma("gpsimd", lambda e: e.dma_start(out=ident_b[:], in_=consts.ap()[:, 0:128]), writes=["ident_b"])
        S.dma("gpsimd", lambda e: e.dma_start(out=perm_b[:], in_=consts.ap()[:, 128:256]), writes=["perm_b"])
        S.dma("gpsimd", lambda e: e.dma_start(
            out=band_b[:], in_=consts.ap()[:, 256:512].rearrange("p (c q) -> p c q", c=2)), writes=["band_b"])

        with ExitStack() as p0:
            stg = p0.enter_context(nc.sbuf_tensor("stg", [NCOLS, 128], F32))
            wada_t = [p0.enter_context(nc.sbuf_tensor(f"wada{i}", [128, 6 * D], BF16)) for i in range(2)]
            wada_f = [p0.enter_context(nc.sbuf_tensor(f"wadaf{i}", [128, 6 * D], F32)) for i in range(3)]
            silu_cb = p0.enter_context(nc.sbuf_tensor("silu_cb", [128, 8], BF16))
            diag = [p0.enter_context(nc.sbuf_tensor(f"diag{i}", [128, 128], F32)) for i in range(2)]
            ps_cols = p0.enter_context(nc.psum_tensor("ps_cols", [128, NCOLS], F32))
            ps_mod = p0.enter_context(nc.psum_tensor("ps_mod", [128, 48], F32))
            ps_row = [p0.enter_context(nc.psum_tensor(f"ps_row{i}", [128, 512], F32)) for i in range(2)]
            for (src, off, n) in ((b_ada, C_BADA, 48), (norm1_g, C_N1, 8), (norm2_g, C_N2, 8), (cvec, C_C, 8),
                                  (attn_g, C_AG, 4), (ssm_g, C_SG, 4), (d_skip, C_DS, 4), (b_glu, C_BG, 4),
                                  (final_g, C_FG, 8)):
                S.dma("sync", lambda e, src=src, off=off, n=n: e.dma_start(out=stg[off:off + n, :], in_=src.ap()),
                      writes=["stg"])
            S.op("tensor", lambda e: e.transpose(out=ps_cols[:], in_=stg[:], identity=ident_f[0:NCOLS, 0:NCOLS]),
                 reads=["stg", "ident_f"], writes=["ps_cols"])
            S.op("vector", lambda e: e.tensor_copy(out=cols[:], in_=ps_cols[:]), reads=["ps_cols"], writes=["cols"])
            S.op("scalar", lambda e: e.activation(out=silu_c[:], in_=cols[:, C_C:C_C + 8], func=AF.Silu),
                 reads=["cols"], writes=["silu_c"])
            S.op("vector", lambda e: e.tensor_copy(out=silu_cb[:], in_=silu_c[:]), reads=["silu_c"], writes=["silu_cb"])
            for kc in range(8 if stage >= -1 else 0):
                b = kc % 2
                bf = kc % 3
                S.dma("sync", lambda e, kc=kc, bf=bf: e.dma_start(out=wada_f[bf][:], in_=w_ada.ap()[kc * 128:(kc + 1) * 128, :]),
                      writes=[f"wadaf{bf}"])
                for hq in range(4):
                    cs_ = slice(hq * 1536, (hq + 1) * 1536)
                    if hq % 2 == 0:
                        S.op("scalar", lambda e, b=b, bf=bf, cs_=cs_: e.activation(out=wada_t[b][:, cs_], in_=wada_f[bf][:, cs_], func=AF.Copy),
                             reads=[f"wadaf{bf}"], writes=[f"wada{b}"])
                    else:
                        S.op("vector", lambda e, b=b, bf=bf, cs_=cs_: e.tensor_copy(out=wada_t[b][:, cs_], in_=wada_f[bf][:, cs_]),
                             reads=[f"wadaf{bf}"], writes=[f"wada{b}"])
                for j in range(48):
                    S.op("tensor", lambda e, kc=kc, b=b, j=j: e.matmul(
                        out=ps_mod[:, j:j + 1], lhsT=wada_t[b][:, j * 128:(j + 1) * 128], rhs=silu_cb[:, kc:kc + 1],
                        start=(kc == 0 and j == 0), stop=(kc == 7 and j == 47), skip_group_check=True),
                        reads=[f"wada{b}", "silu_cb"], writes=["ps_mod"])
            S.op("vector", lambda e: e.tensor_tensor(out=modc[:], in0=ps_mod[:], in1=cols[:, C_BADA:C_BADA + 48], op=ALU.add),
                 reads=["ps_mod", "cols"], writes=["modc"])
            S.op("vector", lambda e: e.scalar_tensor_tensor(out=sc12[:, 0:8], in0=modc[:, 8:16], scalar=1.0,
                                                            in1=cols[:, C_N1:C_N1 + 8], op0=ALU.add, op1=ALU.mult),
                 reads=["modc", "cols"], writes=["sc12a"])
            S.op("vector", lambda e: e.scalar_tensor_tensor(out=sc12[:, 8:16], in0=modc[:, 32:40], scalar=1.0,
                                                            in1=cols[:, C_N2:C_N2 + 8], op0=ALU.add, op1=ALU.mult),
                 reads=["modc", "cols"], writes=["sc12b"])
            k = 0
            for ri, (srct, soff, skey) in enumerate(((modc, 16, "modc"), (modc, 40, "modc"), (cols, C_FG, "cols")) if stage >= 0 else ()):
                for half in range(2):
                    pr = ps_row[(ri * 2 + half) % 2]
                    prk = f"ps_row{(ri * 2 + half) % 2}"
                    for c4 in range(4):
                        c = half * 4 + c4
                        dg = diag[k % 2]
                        dgk = f"diag{k % 2}"
                        k += 1
                        S.op("vector", lambda e, dg=dg, srct=srct, soff=soff, c=c: e.tensor_scalar(
                            out=dg[:], in0=ident_f[:], scalar1=srct[:, soff + c:soff + c + 1], scalar2=None, op0=ALU.mult),
                            reads=["ident_f", skey], writes=[dgk])
                        S.op("tensor", lambda e, dg=dg, pr=pr, c4=c4: e.matmul(
                            out=pr[:, c4 * 128:(c4 + 1) * 128], lhsT=ones_f[:], rhs=dg[:], start=True, stop=True),
                            reads=["ones_f", dgk], writes=[prk])
                    S.op("vector", lambda e, pr=pr, ri=ri, half=half: e.tensor_copy(
                        out=rows[:, ri, half * 512:(half + 1) * 512], in_=pr[:]), reads=[prk], writes=["rows"])
            if debug:
                dcol = p0.enter_context(nc.sbuf_tensor("dcol", [128, 160], F32))
                S.op("vector", lambda e: e.tensor_copy(out=dcol[:, 0:NCOLS], in_=cols[:]), reads=["cols"], writes=["dcol"])
                S.op("vector", lambda e: e.tensor_copy(out=dcol[:, 96:144], in_=modc[:]), reads=["modc"], writes=["dcol"])
                S.op("vector", lambda e: e.tensor_copy(out=dcol[:, 144:160], in_=sc12[:]), reads=["sc12a", "sc12b"], writes=["dcol"])
                S.dma("sync", lambda e: e.dma_start(out=dbg_cols.ap(), in_=dcol[:]), reads=["dcol"])
                S.dma("sync", lambda e: e.dma_start(out=dbg_rows.ap(), in_=rows[:].rearrange("p a d -> p (a d)")), reads=["rows"])


        if stage >= 5:
            for wsrc, wdst in ((w1, w1b), (w3, w3b)):
                for j in range(NFC):
                    S.dma("gpsimd", lambda e, wsrc=wsrc, wdst=wdst, j=j: e.dma_start(
                        out=wdst.ap()[j], in_=wsrc.ap()[:, j * 128:(j + 1) * 128].rearrange("(kc p) f -> p kc f", p=128)),
                        writes=["wffn_b"])

        if stage >= 1:
            S.barrier()
            with ExitStack() as pa:
                def sba(name, shape, dt=F32):
                    return pa.enter_context(nc.sbuf_tensor(name, list(shape), dt))

                def psa(name, shape, dt=F32):
                    return pa.enter_context(nc.psum_tensor(name, list(shape), dt))
                win_b = sba("win_b", [128, 8, 2048], BF16)
                S.dma("gpsimd", lambda e: e.dma_start(out=win_b[:], in_=w_in.ap().rearrange("(kc p) n -> p kc n", p=128)),
                      writes=["win_b"])
                xt = [sba(f"xt{i}", [128, 4, D]) for i in range(3)]
                cs_t = [sba(f"cs{i}", [128, 2, TA]) for i in range(3)]
                sqj = sba("sqj", [128, D], BF16)
                ss = [sba(f"ss{i}", [128, 4]) for i in range(2)]
                rstd = [sba(f"rstd{i}", [128, 4]) for i in range(2)]
                xn = [sba(f"xn{i}", [128, 4, D], BF16) for i in range(2)]
                hT = [sba(f"hT{i}", [128, 8, TA], BF16) for i in range(2)]
                qraw = [sba(f"qraw{i}", [128, TA], BF16) for i in range(2)]
                t1 = [sba(f"t1{i}", [128, TA]) for i in range(2)]
                t2 = [sba(f"t2{i}", [128, TA]) for i in range(2)]
                qk_o = [sba(f"qko{i}", [128, 8, TA], BF16) for i in range(2)]
                v_o = [sba(f"vo{i}", [128, 4, 520], BF16) for i in range(2)]
                tval = sba("tval", [128, EXT // 128])
                S.dma("sync", lambda e: e.dma_start(out=tval[:], in_=tvalid.ap()), writes=["tval"])
                ud = sba("ud", [128, 4, 8, 512], BF16)
                tp_ps = [psa(f"tp_ps{i}", [128, TA], BF16) for i in range(2)]
                pj_ps = [psa(f"pj_ps{i}", [128, 512]) for i in range(4)]
                sw_ps = [psa(f"sw_ps{i}", [128, 512]) for i in range(2)]
                pjn = 0
                swn = 0
                rn = 0
                tile_list = [("main", t) for t in (range(NTA) if tiles is None else tiles)]
                if stage >= 3 and tiles is None:
                    tile_list += [("oth", t) for t in range(3 * SEG // TA)]
                def tile_part(ti, part):
                    nonlocal pjn, swn, rn
                    tkind, t = tile_list[ti]
                    b = ti % 2
                    b3 = ti % 3
                    other = tkind == "oth"
                    own = (not other) and HALO // TA <= t < (HALO + SEG) // TA
                    to = (t % 8) if other else t - HALO // TA
                    segi = (1 + t // 8) if other else 0
                    xsrc = xoth if other else xext
                    if part == "1a":
                        S.dma("sync", lambda e, b3=b3, t=t, b=b, xsrc=xsrc: e.dma_start(
                            out=xt[b3][:], in_=xsrc.ap()[t * TA:(t + 1) * TA, :].rearrange("(st p) d -> p st d", p=128)),
                            writes=[f"xt{b3}"])
                        if not other:
                            S.dma("sync", lambda e, b3=b3, t=t, b=b: e.dma_start(out=cs_t[b3][:, 0, :], in_=cosT.ap()[:, t * TA:(t + 1) * TA]),
                                  writes=[f"cs{b3}"])
                            S.dma("sync", lambda e, b3=b3, t=t, b=b: e.dma_start(out=cs_t[b3][:, 1, :], in_=sinT.ap()[:, t * TA:(t + 1) * TA]),
                                  writes=[f"cs{b3}"])
                        for st in range(4):
                            S.op("scalar", lambda e, b3=b3, b=b, st=st: e.activation(
                                out=sqj[:], in_=xt[b3][:, st, :], func=AF.Square, accum_out=ss[b][:, st:st + 1]),
                                reads=[f"xt{b3}"], writes=["sqj", f"ss{b}"])
                        S.op("vector", lambda e, b=b: e.tensor_scalar(out=rstd[b][:], in0=ss[b][:], scalar1=1.0 / D, scalar2=EPS,
                                                                      op0=ALU.mult, op1=ALU.add),
                             reads=[f"ss{b}"], writes=[f"rstd{b}"])
                        S.op("scalar", lambda e, b=b: e.activation(out=rstd[b][:], in_=rstd[b][:], func=AF.Sqrt),
                             reads=[f"rstd{b}"], writes=[f"rstd{b}"])
                        S.op("vector", lambda e, b=b: e.reciprocal(out=rstd[b][:], in_=rstd[b][:]),
                             reads=[f"rstd{b}"], writes=[f"rstd{b}"])
                        for st in range(4):
                            S.op("vector", lambda e, b3=b3, b=b, st=st: e.tensor_scalar(
                                out=xn[b][:, st, :], in0=xt[b3][:, st, :], scalar1=rstd[b][:, st:st + 1], scalar2=None, op0=ALU.mult),
                                reads=[f"xt{b3}", f"rstd{b}"], writes=[f"xn{b}_{st}"])
                        return
                    if part == "1b":
                        for fc in range(8):
                            tb = fc % 2
                            for st in range(4):
                                S.op("tensor", lambda e, b=b, st=st, fc=fc, tb=tb: e.transpose(
                                    out=tp_ps[tb][:, st * 128:(st + 1) * 128], in_=xn[b][:, st, fc * 128:(fc + 1) * 128],
                                    identity=ident_b[:]), reads=[f"xn{b}_{st}", "ident_b"], writes=[f"tp_ps{tb}"])
                            if other and fc % 2 == 1:
                                S.op("vector", lambda e, b=b, fc=fc, tb=tb: e.tensor_scalar(
                                    out=hT[b][:, fc, :], in0=tp_ps[tb][:], scalar1=sc12[:, fc:fc + 1], scalar2=modc[:, fc:fc + 1],
                                    op0=ALU.mult, op1=ALU.add),
                                    reads=[f"tp_ps{tb}", "sc12a", "modc"], writes=[f"hT{b}_{fc}"])
                            else:
                                S.op("scalar", lambda e, b=b, fc=fc, tb=tb: e.activation(
                                    out=hT[b][:, fc, :], in_=tp_ps[tb][:], func=AF.Identity,
                                    scale=sc12[:, fc:fc + 1], bias=modc[:, fc:fc + 1]),
                                    reads=[f"tp_ps{tb}", "sc12a", "modc"], writes=[f"hT{b}_{fc}"])
                        return
                    hkeys = [f"hT{b}_{fc}" for fc in range(8)]
                    if sub < 3:
                        return
                    for which in (() if other else ((0, 1) if own else (1,))):
                        for cc in range(4):
                            pp = pj_ps[pjn % 4]; ppk = f"pj_ps{pjn % 4}"; pjn += 1
                            col0 = which * 512 + cc * 128
                            for kc in range(8):
                                S.op("tensor", lambda e, pp=pp, b=b, kc=kc, col0=col0: e.matmul(
                                    out=pp[:], lhsT=win_b[:, kc, col0:col0 + 128], rhs=hT[b][:, kc, :],
                                    start=(kc == 0), stop=(kc == 7)), reads=["win_b", hkeys[kc]], writes=[ppk])
                            r = rn % 2; rn += 1
                            S.op("scalar", lambda e, pp=pp, r=r: e.activation(out=qraw[r][:], in_=pp[:], func=AF.Copy),
                                 reads=[ppk], writes=[f"qraw{r}", ppk + "x"])
                            sp = sw_ps[swn % 2]; spk = f"sw_ps{swn % 2}"; swn += 1
                            S.op("tensor", lambda e, sp=sp, r=r: e.matmul(out=sp[:], lhsT=perm_b[:], rhs=qraw[r][:],
                                                                         start=True, stop=True),
                                 reads=["perm_b", f"qraw{r}"], writes=[spk])
                            S.op("vector", lambda e, b3=b3, pp=pp, r=r, b=b: e.tensor_tensor(
                                out=t1[r][:], in0=pp[:], in1=cs_t[b3][:, 0, :], op=ALU.mult),
                                reads=[ppk, f"cs{b3}"], writes=[f"t1{r}", ppk + "x"])
                            S.op("vector", lambda e, b3=b3, sp=sp, r=r, b=b: e.tensor_tensor(
                                out=t2[r][:], in0=sp[:], in1=cs_t[b3][:, 1, :], op=ALU.mult),
                                reads=[spk, f"cs{b3}"], writes=[f"t2{r}"])
                            S.op("vector", lambda e, r=r, b=b, which=which, cc=cc: e.tensor_tensor(
                                out=qk_o[b][:, which * 4 + cc, :], in0=t1[r][:], in1=t2[r][:], op=ALU.add),
                                reads=[f"t1{r}", f"t2{r}"], writes=[f"qko{b}"])
                    if own:
                        S.dma("gpsimd", lambda e, b=b, to=to: e.dma_start(
                            out=qT_s.ap()[:, to * TA:(to + 1) * TA].rearrange("(cc p) n -> p cc n", p=128),
                            in_=qk_o[b][:, 0:4, :]), reads=[f"qko{b}"], writes=["qT_s"])
                    if not other:
                        S.dma("gpsimd", lambda e, b=b, t=t: e.dma_start(
                            out=kT_s.ap()[:, t * TA:(t + 1) * TA].rearrange("(cc p) n -> p cc n", p=128),
                            in_=qk_o[b][:, 4:8, :]), reads=[f"qko{b}"], writes=["kT_s"])
                    if sub < 4:
                        return
                    for st in range(0 if other else 4):
                        pp = pj_ps[pjn % 4]; ppk = f"pj_ps{pjn % 4}"; pjn += 1
                        for kc in range(8):
                            S.op("tensor", lambda e, pp=pp, b=b, kc=kc, st=st: e.matmul(
                                out=pp[:], lhsT=hT[b][:, kc, st * 128:(st + 1) * 128], rhs=win_b[:, kc, 1024:1536],
                                start=(kc == 0), stop=(kc == 7)), reads=["win_b", hkeys[kc]], writes=[ppk])
                        S.op("vector", lambda e, b=b, st=st, t=t: e.tensor_copy(
                            out=v_o[b][:, st, :].rearrange("p (h e) -> p h e", e=65)[:, :, 64],
                            in_=bass.AP(tval, t * 4 + st, [[EXT // 128, 128], [0, 8]])), reads=["tval"], writes=[f"vo{b}"])
                        S.op("vector", lambda e, pp=pp, b=b, st=st, t=t: e.tensor_scalar(
                            out=v_o[b][:, st, :].rearrange("p (h e) -> p h e", e=65)[:, :, 0:64],
                            in0=pp[:].rearrange("p (h e) -> p h e", e=64), scalar1=tval[:, t * 4 + st:t * 4 + st + 1], scalar2=None,
                            op0=ALU.mult),
                             reads=[ppk, "tval"], writes=[f"vo{b}"])
                    if not other:
                        S.dma("gpsimd", lambda e, b=b, t=t: e.dma_start(
                            out=v_s.ap()[t * TA:(t + 1) * TA, :].rearrange("(st p) n -> p st n", p=128), in_=v_o[b][:]),
                            reads=[f"vo{b}"], writes=["v_s"])
                    if (own or other) and sub >= 5:
                        for cc in range(4):
                            pp = pj_ps[pjn % 4]; ppk = f"pj_ps{pjn % 4}"; pjn += 1
                            col0 = 1536 + cc * 128
                            for kc in range(8):
                                S.op("tensor", lambda e, pp=pp, b=b, kc=kc, col0=col0: e.matmul(
                                    out=pp[:], lhsT=win_b[:, kc, col0:col0 + 128], rhs=hT[b][:, kc, :],
                                    start=(kc == 0), stop=(kc == 7)), reads=["win_b", hkeys[kc]], writes=[ppk])
                            if other:
                                S.op("vector", lambda e, pp=pp, cc=cc, to=to: e.tensor_copy(
                                    out=ud[:, cc, :, to * 64:(to + 1) * 64], in_=pp[:].rearrange("p (n s) -> p s n", s=8)),
                                    reads=[ppk], writes=["ud"])
                            else:
                                S.op("scalar", lambda e, pp=pp, cc=cc, to=to: e.activation(
                                    out=ud[:, cc, :, to * 64:(to + 1) * 64], in_=pp[:].rearrange("p (n s) -> p s n", s=8),
                                    func=AF.Copy), reads=[ppk], writes=["ud"])
                    if (own or other) and to == 7 and sub >= 6:
                      for cc in range(4):
                        for gl in range(8):
                          S.dma("gpsimd", lambda e, cc=cc, gl=gl, segi=segi: e.dma_start(
                            out=Us.ap()[segi, cc * 8 + gl].rearrange("(s c) n -> c s n", c=16),
                            in_=ud[gl * 16:(gl + 1) * 16, cc, :, :]), reads=["ud"], writes=["Us"])
                ntl = len(tile_list)
                tile_part(0, "1a")
                tile_part(0, "1b")
                for ti in range(ntl):
                    if ti + 1 < ntl:
                        tile_part(ti + 1, "1a")
                    if ti >= 1:
                        tile_part(ti - 1, 2)
                    if ti + 1 < ntl:
                        tile_part(ti + 1, "1b")
                tile_part(ntl - 1, 2)

        if stage >= 2:
            S.barrier()
            with ExitStack() as pb:
                def sbb(name, shape, dt=F32):
                    return pb.enter_context(nc.sbuf_tensor(name, list(shape), dt))

                def psb(name, shape, dt=F32):
                    return pb.enter_context(nc.psum_tensor(name, list(shape), dt))
                maskT = sbb("maskT", [128, 640], BF16)
                S.dma("gpsimd", lambda e: e.dma_start(out=maskT[:], in_=consts.ap()[:, 640:1280]), writes=["maskT"])
                kT_all = sbb("kT_all", [128, 4, 4096], BF16)
                qT_all = sbb("qT_all", [128, 4, 2048], BF16)
                nchs = {1: 17, 4: 5, 16: 2}
                varr = {}
                for d in DILS:
                    for r in range(d):
                        varr[(d, r)] = sbb(f"va{d}_{r}", [128, nchs[d], 520], BF16)
                NSB = 4
                LOOK = 3
                pT = [sbb(f"pT{i}", [128, 512], BF16) for i in range(NSB)]
                pTm = [sbb(f"pTm{i}", [128, 512], BF16) for i in range(NSB)]
                accsb = [sbb(f"accsb{i}", [65, 1024]) for i in range(2)]
                acc16 = sbb("acc16", [65, 16, 128])
                rden = [sbb(f"rden{i}", [64, 512]) for i in range(2)]
                acc_ps = psb("ps_acc", [128, 1024])
                a16_ps = psb("ps_a16", [128, 512])
                s_ps = [psb(f"ps_s{i}", [128, 512]) for i in range(NSB)]
                bc_ps = psb("ps_bc", [64, 512])
                un = 0
                ev = 0
                for sbi in range(2):
                    S.dma("sync", lambda e, sbi=sbi: e.dma_start(
                        out=kT_all[:], in_=kT_s.ap()[:, 2048 * sbi:2048 * sbi + 4096].rearrange("(cc p) n -> p cc n", p=128)),
                        reads=["kT_s"], writes=["kT_all"])
                    S.dma("sync", lambda e, sbi=sbi: e.dma_start(
                        out=qT_all[:], in_=qT_s.ap()[:, 2048 * sbi:2048 * sbi + 2048].rearrange("(cc p) n -> p cc n", p=128)),
                        reads=["qT_s"], writes=["qT_all"])
                    for d in DILS:
                        nq = SEG // d // 128
                        ca0 = (nq // 2) * sbi
                        for r in range(d):
                            row0 = r + HALO - 64 * d + 128 * d * ca0
                            src = bass.AP(v_s, row0 * 520, [[d * 520, 128], [128 * d * 520, nchs[d]], [1, 520]])
                            S.dma("sync", lambda e, d=d, r=r, src=src: e.dma_start(out=varr[(d, r)][:], in_=src),
                                  reads=["v_s"], writes=[f"va{d}_{r}"])
                    for h in range(8):
                        hp, hc = (h % 2) * 64, h // 2

                        def mk_chunks(dsel, half):
                            out = []
                            for di, d in enumerate(DILS):
                                if d not in dsel:
                                    continue
                                nql = SEG // d // 128 // 2
                                if half is None:
                                    b_lo, b_hi = 0, nql
                                else:
                                    b_lo, b_hi = half * (nql // 2), (half + 1) * (nql // 2)
                                for r in range(d):
                                    for cal in range(b_lo, b_hi + 1):
                                        blocks = [bl for bl in (cal - 1, cal) if b_lo <= bl < b_hi]
                                        out.append((di, d, r, cal, blocks))
                            return out

                        def run_pipeline(chunks, pv_emit):
                            nonlocal un
                            groups, cur, used = [], [], 0
                            for ch in chunks:
                                w = 128 * len(ch[4])
                                if used + w > 512:
                                    groups.append(cur); cur, used = [], 0
                                cur.append((ch, used)); used += w
                            if cur:
                                groups.append(cur)
                            ginfo = []
                            for gi in range(len(groups) + LOOK):
                                if gi < len(groups):
                                    grp = groups[gi]
                                    sp = s_ps[un % NSB]; spk = f"ps_s{un % NSB}"
                                    pt = pT[un % NSB]; ptk = f"pT{un % NSB}"
                                    pm = pTm[un % NSB]; pmk = f"pTm{un % NSB}"
                                    un += 1
                                    ginfo.append((grp, pm, pmk))
                                    width = grp[-1][1] + 128 * len(grp[-1][0][4])
                                    for (di, d, r, cal, blocks), off in grp:
                                        nq = SEG // d // 128
                                        ca = (nq // 2) * sbi + cal
                                        kbase = r + HALO - 64 * d + 128 * d * ca - 2048 * sbi
                                        bq0 = (nq // 2) * sbi + blocks[0]
                                        qbase = r + 128 * d * bq0 - 2048 * sbi
                                        nqc = 128 * len(blocks)
                                        S.op("tensor", lambda e, sp=sp, off=off, nqc=nqc, kbase=kbase, qbase=qbase, d=d, hp=hp, hc=hc: e.matmul(
                                            out=sp[:, off:off + nqc], lhsT=kT_all[hp:hp + 64, hc, sl(kbase, 128, d)],
                                            rhs=qT_all[hp:hp + 64, hc, sl(qbase, nqc, d)], start=True, stop=True),
                                            reads=["kT_all", "qT_all"], writes=[spk])
                                    S.op("scalar", lambda e, sp=sp, pt=pt, width=width: e.activation(
                                        out=pt[:, 0:width], in_=sp[:, 0:width], func=AF.Exp, scale=0.125), reads=[spk], writes=[ptk])
                                    (_di, _d, _r, cal0, blocks0), _ = grp[0]
                                    m0 = 128 if blocks0[0] == cal0 - 1 else 0
                                    S.op("vector", lambda e, pt=pt, pm=pm, width=width, m0=m0: e.tensor_tensor(
                                        out=pm[:, 0:width], in0=pt[:, 0:width], in1=maskT[:, m0:m0 + width], op=ALU.mult),
                                        reads=[ptk, "maskT"], writes=[pmk])
                                if gi >= LOOK:
                                    grp, pm, pmk = ginfo[gi - LOOK]
                                    for ch, off in grp:
                                        pv_emit(ch, off, pm, pmk)

                        def pv16(ch, off, pm, pmk):
                            di, d, r, cal, blocks = ch
                            rr = r % 4
                            S.op("tensor", lambda e, pm=pm, off=off, rr=rr, r=r, cal=cal, h=h: e.matmul(
                                out=a16_ps[0:65, rr * 128:(rr + 1) * 128], lhsT=varr[(16, r)][:, cal, h * 65:(h + 1) * 65],
                                rhs=pm[:, off:off + 128], start=(rr == 0 and cal == 0), stop=False, skip_group_check=True),
                                reads=[pmk, f"va16_{r}"], writes=["ps_a16"])
                            if rr == 3 and cal == 1:
                                S.op("scalar", lambda e, r=r: e.activation(
                                    out=acc16[:, r - 3:r + 1, :].rearrange("p a b -> p (a b)"), in_=a16_ps[0:65, :], func=AF.Copy),
                                    reads=["ps_a16"], writes=["acc16"])
                        run_pipeline(mk_chunks((16,), None), pv16)

                        for half in range(2):
                            started = set()

                            def pv14(ch, off, pm, pmk, half=half, started=started):
                                di, d, r, cal, blocks = ch
                                lhsT = varr[(d, r)][:, cal, h * 65:(h + 1) * 65]
                                pieces = []
                                if d == 1:
                                    lb = [bl - 8 * half for bl in blocks]
                                    if len(lb) == 2 and lb[1] % 4 != 0:
                                        pieces.append((pm[:, off:off + 256], lb[0] // 4, acc_ps[0:65, lb[0] * 128:(lb[0] + 2) * 128]))
                                    else:
                                        for bi, bl in enumerate(lb):
                                            pieces.append((pm[:, off + 128 * bi:off + 128 * (bi + 1)], bl // 4,
                                                           acc_ps[0:65, bl * 128:(bl + 1) * 128]))
                                else:
                                    for bi, bl in enumerate(blocks):
                                        lbl = bl - 2 * half
                                        pieces.append((pm[:, off + 128 * bi:off + 128 * (bi + 1)], lbl,
                                                       acc_ps[0:65, sl(lbl * 512 + r, 128, 4)]))
                                for rhs, bank, o_ap in pieces:
                                    st = bank not in started
                                    started.add(bank)
                                    S.op("tensor", lambda e, lhsT=lhsT, rhs=rhs, o_ap=o_ap, st=st: e.matmul(
                                        out=o_ap, lhsT=lhsT, rhs=rhs, start=st, stop=False, skip_group_check=True),
                                        reads=[pmk, f"va{d}_{r}"], writes=["ps_acc"])
                            run_pipeline(mk_chunks((1, 4), half), pv14)
                            eb = ev % 2; ev += 1
                            S.op("scalar", lambda e, eb=eb: e.activation(out=accsb[eb][:], in_=acc_ps[0:65, :], func=AF.Copy),
                                 reads=["ps_acc"], writes=[f"accsb{eb}"])
                            S.op("vector", lambda e, eb=eb, half=half: e.tensor_tensor(
                                out=accsb[eb][:].rearrange("p (q r) -> p q r", r=16), in0=accsb[eb][:].rearrange("p (q r) -> p q r", r=16),
                                in1=acc16[:, :, 64 * half:64 * half + 64].rearrange("p r q -> p q r"), op=ALU.add),
                                reads=[f"accsb{eb}", "acc16"], writes=[f"accsb{eb}"])
                            for j in range(2):
                                rb = j % 2
                                S.op("tensor", lambda e, eb=eb, j=j: e.matmul(
                                    out=bc_ps[:], lhsT=ones_f[64:65, 0:64], rhs=accsb[eb][64:65, j * 512:(j + 1) * 512],
                                    start=True, stop=True), reads=["ones_f", f"accsb{eb}"], writes=["ps_bc"])
                                S.op("vector", lambda e, rb=rb: e.reciprocal(out=rden[rb][:], in_=bc_ps[:]),
                                     reads=["ps_bc"], writes=[f"rden{rb}"])
                                S.op("vector", lambda e, eb=eb, rb=rb, j=j: e.tensor_tensor(
                                    out=accsb[eb][0:64, j * 512:(j + 1) * 512], in0=accsb[eb][0:64, j * 512:(j + 1) * 512],
                                    in1=rden[rb][:], op=ALU.mult), reads=[f"accsb{eb}", f"rden{rb}"], writes=[f"accsb{eb}"])
                            S.dma("sync", lambda e, eb=eb, h=h, sbi=sbi, half=half: e.dma_start(
                                out=attn_s.ap()[h, :, sbi * 2048 + half * 1024:sbi * 2048 + (half + 1) * 1024], in_=accsb[eb][0:64, :]),
                                reads=[f"accsb{eb}"], writes=["attn_s"])

        if stage >= 3:
            S.barrier()
            TWO_PI = 6.283185307179586
            with ExitStack() as pc:
                def sbc(name, shape, dt=F32):
                    return pc.enter_context(nc.sbuf_tensor(name, list(shape), dt))
                kv = sbc("kv", [128, 72])
                maskfb = sbc("maskfb", [128, 2, 128])
                P8 = sbc("P8", [128, 4, 32, 9])
                Y0 = sbc("Y0", [128, 2, 32, 128])
                Xpb = sbc("Xpb", [128, 2, 32, 128], BF16)
                A64 = sbc("A64", [128, 2, 2, 32])
                CT = sbc("CT", [128, 2, 32, 16])
                Pw18 = sbc("Pw18", [128, 2, 32, 8])
                G2o = Y0[:].rearrange("p a q x -> p (a q x)").bitcast(BF16)[:, 0:3 * 64 * 2 * 32].rearrange(
                    "p (s n c q) -> p s n c q", s=3, n=64, c=2)
                S.dma("sync", lambda e: e.dma_start(out=kv[:], in_=consts2.ap()[:, 0:72]), writes=["kv"])
                S.dma("sync", lambda e: e.dma_start(out=maskfb[:], in_=consts2.ap()[:, 72:328].rearrange("p (a b) -> p a b", a=2)),
                      writes=["maskfb"])
                with ExitStack() as pc0:
                    lrli = pc0.enter_context(nc.sbuf_tensor("lrli", [128, 3, 32], F32))
                    th = pc0.enter_context(nc.sbuf_tensor("th", [128, 3, 32], F32))
                    PwR = pc0.enter_context(nc.sbuf_tensor("PwR", [128, 32, 72], F32))
                    PwI = pc0.enter_context(nc.sbuf_tensor("PwI", [128, 32, 72], F32))
                    Bb = pc0.enter_context(nc.sbuf_tensor("Bb", [128, 2, 32, 16], F32))
                    Xp = pc0.enter_context(nc.sbuf_tensor("Xp", [128, 2, 32, 128], F32))
                    lst = pc0.enter_context(nc.sbuf_tensor("lst", [32, 3, 128], F32))
                    ldt2 = pc0.enter_context(nc.sbuf_tensor("ldt2", [32, 2], F32))
                    cst = [pc0.enter_context(nc.sbuf_tensor(f"cst{i}", [128, 2, 64], F32)) for i in range(2)]
                    Braw = pc0.enter_context(nc.sbuf_tensor("Braw", [128, 2, 32, 16], F32))
                    zz = pc0.enter_context(nc.sbuf_tensor("zz", [128, 8, 32], F32))
                    tb1 = pc0.enter_context(nc.sbuf_tensor("tb1", [128, 32, 16], F32))
                    tb2 = pc0.enter_context(nc.sbuf_tensor("tb2", [128, 32, 16], F32))
                    ps_p = pc0.enter_context(nc.psum_tensor("ps_par", [128, 96], F32))
                    ps_c = [pc0.enter_context(nc.psum_tensor(f"ps_ct{i}", [128, 128], F32)) for i in range(2)]
                    S.dma("sync", lambda e: e.dma_start(out=lst[:, 0, :], in_=lam_re.ap()), writes=["lst"])
                    S.dma("sync", lambda e: e.dma_start(out=lst[:, 1, :], in_=lam_im.ap()), writes=["lst"])
                    S.dma("sync", lambda e: e.dma_start(out=ldt2[:], in_=log_dt.ap()), writes=["ldt2"])
                    S.op("vector", lambda e: e.tensor_copy(
                        out=lst[:, 2, :].rearrange("q (j p) -> q j p", j=2),
                        in_=bass.AP(ldt2, 0, [[2, 32], [1, 2], [0, 64]])), reads=["ldt2"], writes=["lst"])
                    for a in range(3):
                        S.op("tensor", lambda e, a=a: e.transpose(out=ps_p[:, a * 32:(a + 1) * 32], in_=lst[:, a, :],
                                                                 identity=ident_f[0:32, 0:32]),
                             reads=["lst", "ident_f"], writes=["ps_par"])
                    S.op("vector", lambda e: e.tensor_copy(out=lrli[:].rearrange("p a q -> p (a q)"), in_=ps_p[:]),
                         reads=["ps_par"], writes=["lrli"])
                    S.dma("sync", lambda e: e.dma_start(out=Braw[:, 0, :, :], in_=b_re.ap().rearrange("q p c -> p q c")),
                          writes=["Braw"])
                    S.dma("sync", lambda e: e.dma_start(out=Braw[:, 1, :, :], in_=b_im.ap().rearrange("q p c -> p q c")),
                          writes=["Braw"])
                    n = 0
                    for comp, csrc in enumerate((c_re, c_im)):
                        for qb in range(4):
                            cb = n % 2; n += 1
                            for ql in range(8):
                                q = qb * 8 + ql
                                S.dma("sync", lambda e, cb=cb, ql=ql, q=q, csrc=csrc: e.dma_start(
                                    out=cst[cb][ql * 16:(ql + 1) * 16, :, :],
                                    in_=csrc.ap()[2 * q:2 * q + 2].rearrange("j c p -> c j p")), writes=[f"cst{cb}"])
                            S.op("tensor", lambda e, cb=cb: e.transpose(
                                out=ps_c[cb][:], in_=cst[cb][:].rearrange("p j x -> p (j x)"), identity=ident_f[:]),
                                reads=[f"cst{cb}", "ident_f"], writes=[f"ps_ct{cb}"])
                            S.op("vector", lambda e, cb=cb, comp=comp, qb=qb: e.tensor_copy(
                                out=CT[:, comp, qb * 8:(qb + 1) * 8, :], in_=ps_c[cb][:].rearrange("p (q c) -> p q c", c=16)),
                                reads=[f"ps_ct{cb}"], writes=["CT"])
                    S.op("scalar", lambda e: e.activation(out=th[:, 2, :], in_=lrli[:, 2, :], func=AF.Exp),
                         reads=["lrli"], writes=["th"])
                    S.op("vector", lambda e: e.tensor_tensor(out=th[:, 0, :], in0=lrli[:, 0, :], in1=th[:, 2, :], op=ALU.mult),
                         reads=["lrli", "th"], writes=["th"])
                    S.op("vector", lambda e: e.tensor_tensor(out=th[:, 1, :], in0=lrli[:, 1, :], in1=th[:, 2, :], op=ALU.mult),
                         reads=["lrli", "th"], writes=["th"])
                    bt = pc0.enter_context(nc.sbuf_tensor("bt", [128, 12, 32], F32))
                    halfpi = pc0.enter_context(nc.sbuf_tensor("halfpi", [128, 1], F32))
                    cm = [pc0.enter_context(nc.sbuf_tensor(f"cm{i}", [128, 32, 32], F32)) for i in range(4)]
                    B_ = lambda i: bt[:, i, :]
                    def dvp(fn, reads, writes):
                        S.op("vector", fn, reads=reads, writes=writes)
                    dvp(lambda e: e.memset(halfpi[:], 1.5707963267948966), [], ["halfpi"])
                    S.op("scalar", lambda e: e.activation(out=B_(0), in_=th[:, 0, :], func=AF.Exp, scale=1.0 / 16), reads=["th"], writes=["bt0"])
                    S.op("scalar", lambda e: e.activation(out=B_(1), in_=th[:, 1, :], func=AF.Sin, scale=1.0 / 16), reads=["th"], writes=["bt1"])
                    S.op("scalar", lambda e: e.activation(out=B_(2), in_=th[:, 1, :], func=AF.Sin, scale=-1.0 / 16, bias=halfpi[:]),
                         reads=["th", "halfpi"], writes=["bt2"])
                    dvp(lambda e: e.tensor_tensor(out=B_(3), in0=B_(0), in1=B_(2), op=ALU.mult), ["bt0", "bt2"], ["bt3"])
                    dvp(lambda e: e.tensor_tensor(out=B_(4), in0=B_(0), in1=B_(1), op=ALU.mult), ["bt0", "bt1"], ["bt4"])
                    cur_r, cur_i = 3, 4
                    for k in range(4):
                        nr, ni = (5, 6) if cur_r == 3 else (3, 4)
                        dvp(lambda e, cr_=cur_r: e.tensor_tensor(out=B_(7), in0=B_(cr_), in1=B_(cr_), op=ALU.mult), [f"bt{cur_r}"], ["bt7"])
                        dvp(lambda e, ci_=cur_i: e.tensor_tensor(out=B_(8), in0=B_(ci_), in1=B_(ci_), op=ALU.mult), [f"bt{cur_i}"], ["bt8"])
                        dvp(lambda e, nr=nr: e.tensor_tensor(out=B_(nr), in0=B_(7), in1=B_(8), op=ALU.subtract), ["bt7", "bt8"], [f"bt{nr}"])
                        dvp(lambda e, cr_=cur_r, ci_=cur_i: e.tensor_tensor(out=B_(9), in0=B_(cr_), in1=B_(ci_), op=ALU.mult),
                            [f"bt{cur_r}", f"bt{cur_i}"], ["bt9"])
                        dvp(lambda e, ni=ni: e.tensor_scalar(out=B_(ni), in0=B_(9), scalar1=2.0, scalar2=None, op0=ALU.mult), ["bt9"], [f"bt{ni}"])
                        cur_r, cur_i = nr, ni
                    assert (cur_r, cur_i) == (3, 4)
                    dvp(lambda e: e.tensor_copy(out=PwR[:, :, 0], in_=B_(3)), ["bt3"], ["PwR"])
                    dvp(lambda e: e.tensor_copy(out=PwI[:, :, 0], in_=B_(4)), ["bt4"], ["PwI"])
                    dvp(lambda e: e.tensor_tensor(out=B_(7), in0=B_(3), in1=B_(3), op=ALU.mult), ["bt3"], ["bt7"])
                    dvp(lambda e: e.tensor_tensor(out=B_(8), in0=B_(4), in1=B_(4), op=ALU.mult), ["bt4"], ["bt8"])
                    dvp(lambda e: e.tensor_tensor(out=B_(7), in0=B_(7), in1=B_(8), op=ALU.add), ["bt7", "bt8"], ["bt7"])
                    dvp(lambda e: e.reciprocal(out=B_(7), in_=B_(7)), ["bt7"], ["bt7"])
                    dvp(lambda e: e.tensor_tensor(out=PwR[:, :, 71], in0=B_(3), in1=B_(7), op=ALU.mult), ["bt3", "bt7"], ["PwR"])
                    dvp(lambda e: e.scalar_tensor_tensor(out=PwI[:, :, 71], in0=B_(4), scalar=-1.0, in1=B_(7), op0=ALU.mult, op1=ALU.mult),
                        ["bt4", "bt7"], ["PwI"])

                    def cmul(o_sl, x_sl, y_sl, n):
                        t = [c_[:, :, 0:n] for c_ in cm]
                        xr, xi = PwR[:, :, x_sl], PwI[:, :, x_sl]
                        yr, yi = y_sl
                        dvp(lambda e: e.tensor_tensor(out=t[0], in0=xr, in1=yr, op=ALU.mult), ["PwR"], ["cm0"])
                        dvp(lambda e: e.tensor_tensor(out=t[1], in0=xi, in1=yi, op=ALU.mult), ["PwI"], ["cm1"])
                        dvp(lambda e: e.tensor_tensor(out=t[2], in0=xr, in1=yi, op=ALU.mult), ["PwR", "PwI"], ["cm2"])
                        dvp(lambda e: e.tensor_tensor(out=t[3], in0=xi, in1=yr, op=ALU.mult), ["PwR", "PwI"], ["cm3"])
                        dvp(lambda e: e.tensor_tensor(out=PwR[:, :, o_sl], in0=t[0], in1=t[1], op=ALU.subtract), ["cm0", "cm1"], ["PwR"])
                        dvp(lambda e: e.tensor_tensor(out=PwI[:, :, o_sl], in0=t[2], in1=t[3], op=ALU.add), ["cm2", "cm3"], ["PwI"])

                    def bc(kidx_, n):
                        return (bass.AP(PwR, kidx_, [[32 * 72, 128], [72, 32], [0, n]]), bass.AP(PwI, kidx_, [[32 * 72, 128], [72, 32], [0, n]]))
                    for k in range(2, 9):
                        cmul(slice(k - 1, k), slice(k - 2, k - 1), bc(0, 1), 1)
                    cmul(slice(8, 16), slice(0, 8), bc(7, 8), 8)
                    cmul(slice(16, 32), slice(0, 16), bc(15, 16), 16)
                    cmul(slice(32, 64), slice(0, 32), bc(31, 32), 32)
                    for k in range(2, 9):
                        cmul(slice(72 - k, 73 - k), slice(73 - k, 74 - k), bc(71, 1), 1)
                    S.op("vector", lambda e: e.tensor_copy(out=P8[:, 0, :, 1:9], in_=PwR[:, :, sl(7, 8, 8)]), reads=["PwR"], writes=["P8"])
                    S.op("vector", lambda e: e.tensor_copy(out=P8[:, 1, :, 1:9], in_=PwI[:, :, sl(7, 8, 8)]), reads=["PwI"], writes=["P8"])
                    S.op("vector", lambda e: e.tensor_scalar(out=P8[:, 2, :, 1:9], in0=PwR[:, :, sl(7, 8, 8)], scalar1=-1.0, scalar2=None,
                                                             op0=ALU.mult), reads=["PwR"], writes=["P8"])
                    S.op("vector", lambda e: e.tensor_scalar(out=P8[:, 3, :, 1:9], in0=PwI[:, :, sl(7, 8, 8)], scalar1=-1.0, scalar2=None,
                                                             op0=ALU.mult), reads=["PwI"], writes=["P8"])
                    lr, li = lrli[:, 0, :], lrli[:, 1, :]
                    ar, ai = PwR[:, :, 0], PwI[:, :, 0]
                    V = lambda i: zz[:, i, :]
                    def dv(fn, reads, writes):
                        S.op("vector", fn, reads=reads, writes=writes)
                    dv(lambda e: e.tensor_scalar(out=V(0), in0=ar, scalar1=-1.0, scalar2=None, op0=ALU.add), ["PwR"], ["zz"])
                    dv(lambda e: e.tensor_tensor(out=V(1), in0=lr, in1=lr, op=ALU.mult), ["lrli"], ["zz"])
                    dv(lambda e: e.tensor_tensor(out=V(2), in0=li, in1=li, op=ALU.mult), ["lrli"], ["zz"])
                    dv(lambda e: e.tensor_tensor(out=V(1), in0=V(1), in1=V(2), op=ALU.add), ["zz"], ["zz"])
                    dv(lambda e: e.reciprocal(out=V(6), in_=V(1)), ["zz"], ["zz"])
                    dv(lambda e: e.tensor_tensor(out=V(2), in0=V(0), in1=lr, op=ALU.mult), ["zz", "lrli"], ["zz"])
                    dv(lambda e: e.tensor_tensor(out=V(3), in0=ai, in1=li, op=ALU.mult), ["PwI", "lrli"], ["zz"])
                    dv(lambda e: e.tensor_tensor(out=V(2), in0=V(2), in1=V(3), op=ALU.add), ["zz"], ["zz"])
                    dv(lambda e: e.tensor_tensor(out=V(4), in0=V(2), in1=V(6), op=ALU.mult), ["zz"], ["zz"])
                    dv(lambda e: e.tensor_tensor(out=V(2), in0=ai, in1=lr, op=ALU.mult), ["PwI", "lrli"], ["zz"])
                    dv(lambda e: e.tensor_tensor(out=V(3), in0=V(0), in1=li, op=ALU.mult), ["zz", "lrli"], ["zz"])
                    dv(lambda e: e.tensor_tensor(out=V(2), in0=V(2), in1=V(3), op=ALU.subtract), ["zz"], ["zz"])
                    dv(lambda e: e.tensor_tensor(out=V(5), in0=V(2), in1=V(6), op=ALU.mult), ["zz"], ["zz"])
                    bc = lambda i: bass.AP(zz, i * 32, [[8 * 32, 128], [1, 32], [0, 16]])
                    dv(lambda e: e.tensor_tensor(out=tb1[:], in0=Braw[:, 0, :, :], in1=bc(4), op=ALU.mult), ["Braw", "zz"], ["tb1"])
                    dv(lambda e: e.tensor_tensor(out=tb2[:], in0=Braw[:, 1, :, :], in1=bc(5), op=ALU.mult), ["Braw", "zz"], ["tb2"])
                    dv(lambda e: e.tensor_tensor(out=Bb[:, 0, :, :], in0=tb1[:], in1=tb2[:], op=ALU.subtract), ["tb1", "tb2"], ["Bb"])
                    dv(lambda e: e.tensor_tensor(out=tb1[:], in0=Braw[:, 1, :, :], in1=bc(4), op=ALU.mult), ["Braw", "zz"], ["tb1"])
                    dv(lambda e: e.tensor_tensor(out=tb2[:], in0=Braw[:, 0, :, :], in1=bc(5), op=ALU.mult), ["Braw", "zz"], ["tb2"])
                    dv(lambda e: e.tensor_tensor(out=Bb[:, 1, :, :], in0=tb1[:], in1=tb2[:], op=ALU.add), ["tb1", "tb2"], ["Bb"])
                    def kidx(k):
                        return k - 1 if k >= 1 else 72 + k
                    t3 = pc0.enter_context(nc.sbuf_tensor("t3", [128, 16, 16], F32))
                    t4 = pc0.enter_context(nc.sbuf_tensor("t4", [128, 16, 16], F32))
                    for dr in range(2):
                        qs = slice(dr * 16, dr * 16 + 16)
                        for sidx in range(8):
                            tau = sidx if dr == 0 else 7 - sidx
                            for (dst, srcT, kk, sgn) in ((Xp, Bb, kidx(-tau - 1), None),):
                                pw_r = bass.AP(PwR, dr * 16 * 72 + kk, [[32 * 72, 128], [72, 16], [0, 16]])
                                pw_i = bass.AP(PwI, dr * 16 * 72 + kk, [[32 * 72, 128], [72, 16], [0, 16]])
                                o_re = dst[:, 0, qs, sidx * 16:(sidx + 1) * 16]
                                o_im = dst[:, 1, qs, sidx * 16:(sidx + 1) * 16]
                                sk = "Xp" if dst is Xp else "Y0"
                                tk = "Bb" if srcT is Bb else "CT"
                                dv(lambda e, srcT=srcT, pw_r=pw_r, qs=qs: e.tensor_tensor(out=t3[:], in0=srcT[:, 0, qs, :], in1=pw_r, op=ALU.mult), [tk, "PwR"], ["t3"])
                                dv(lambda e, srcT=srcT, pw_i=pw_i, qs=qs: e.tensor_tensor(out=t4[:], in0=srcT[:, 1, qs, :], in1=pw_i, op=ALU.mult), [tk, "PwI"], ["t4"])
                                dv(lambda e, o_re=o_re: e.tensor_tensor(out=o_re, in0=t3[:], in1=t4[:], op=ALU.subtract), ["t3", "t4"], [sk])
                                dv(lambda e, srcT=srcT, pw_i=pw_i, qs=qs: e.tensor_tensor(out=t3[:], in0=srcT[:, 0, qs, :], in1=pw_i, op=ALU.mult), [tk, "PwI"], ["t3"])
                                dv(lambda e, srcT=srcT, pw_r=pw_r, qs=qs: e.tensor_tensor(out=t4[:], in0=srcT[:, 1, qs, :], in1=pw_r, op=ALU.mult), [tk, "PwR"], ["t4"])
                                dv(lambda e, o_im=o_im: e.tensor_tensor(out=o_im, in0=t3[:], in1=t4[:], op=ALU.add), ["t3", "t4"], [sk])
                    S.op("gpsimd", lambda e: e.tensor_copy(out=Xpb[:].rearrange("p a q x -> p (a q x)"),
                                                           in_=Xp[:].rearrange("p a q x -> p (a q x)")), reads=["Xp"], writes=["Xpb"])
                    dv(lambda e: e.tensor_copy(out=Pw18[:, 0, :, :], in_=PwR[:, :, 0:8]), ["PwR"], ["Pw18"])
                    dv(lambda e: e.tensor_copy(out=Pw18[:, 1, :, :], in_=PwI[:, :, 0:8]), ["PwI"], ["Pw18"])
                    dv(lambda e: e.tensor_copy(out=A64[:, 0, 0, :], in_=P8[:, 0, :, 8]), ["P8"], ["A64"])
                    dv(lambda e: e.tensor_copy(out=A64[:, 0, 1, :], in_=P8[:, 0, :, 8]), ["P8"], ["A64"])
                    dv(lambda e: e.tensor_copy(out=A64[:, 1, 0, :], in_=P8[:, 3, :, 8]), ["P8"], ["A64"])
                    dv(lambda e: e.tensor_copy(out=A64[:, 1, 1, :], in_=P8[:, 1, :, 8]), ["P8"], ["A64"])
                    if debug:
                        S.dma("sync", lambda e: e.dma_start(out=dbg_s5.ap()[:, 0:2304], in_=PwR[:].rearrange("p q k -> p (q k)")), reads=["PwR"])
                        S.dma("sync", lambda e: e.dma_start(out=dbg_s5.ap()[:, 2304:4608], in_=PwI[:].rearrange("p q k -> p (q k)")), reads=["PwI"])
                        S.dma("sync", lambda e: e.dma_start(out=dbg_s5.ap()[:, 4608:4608 + 8192], in_=Xp[:].rearrange("p a q x -> p (a q x)")), reads=["Xp"])
                    S.barrier()
                G2 = sbc("G2", [128, 64, 2, 32])
                Hall = sbc("Hall", [128, 65, 2, 32])
                Hb = sbc("Hb", [128, 64, 2, 32], BF16)
                U_all = sbc("U_all", [128, 32, 512], BF16)
                Wtab = sbc("Wtab", [128, 64, 2, 32])
                flc = sbc("flc", [128, 6])
                S.dma("sync", lambda e: e.dma_start(out=flc[:], in_=flcol.ap()), writes=["flc"])
                wp_ = [sbc(f"wP{i}", [128, 2, 32]) for i in range(2)]
                wq_ = [sbc(f"wQ{i}", [128, 2, 32]) for i in range(2)]
                S.op("gpsimd", lambda e: e.memset(Wtab[:], 0.0), writes=["Wt_init"])
                S.op("gpsimd", lambda e: e.memset(Wtab[:, 63, 0, 0:16], 1.0), reads=["Wt_init"], writes=["Wt0"])
                S.op("gpsimd", lambda e: e.memset(Wtab[:, 0, 0, 16:32], 1.0), reads=["Wt_init"], writes=["Wt0"])
                WS = 64 * 64
                for st in range(63):
                    b2 = st % 2
                    cur = bass.AP(Wtab, (63 - st) * 64, [[WS, 128], [32, 2], [(2 * st - 63) * 64 + 16, 2], [1, 16]])
                    swp = bass.AP(Wtab, (63 - st) * 64 + 32, [[WS, 128], [-32, 2], [(2 * st - 63) * 64 + 16, 2], [1, 16]])
                    nxt = bass.AP(Wtab, (62 - st) * 64, [[WS, 128], [32, 2], [(2 * st - 61) * 64 + 16, 2], [1, 16]])
                    v4 = lambda t_: t_[:].rearrange("p c (d q) -> p c d q", d=2)
                    S.op("gpsimd", lambda e, cur=cur, b2=b2: e.tensor_tensor(out=v4(wp_[b2]), in0=cur, in1=v4(A64[:, 0, :, :]), op=ALU.mult)
                         if False else e.tensor_tensor(out=wp_[b2][:].rearrange("p c (d q) -> p c d q", d=2), in0=cur,
                                                       in1=A64[:, 0, :, :].rearrange("p c (d q) -> p c d q", d=2), op=ALU.mult),
                         reads=[f"Wt{st}", "A64"], writes=[f"wP{b2}"])
                    S.op("gpsimd", lambda e, swp=swp, b2=b2: e.tensor_tensor(
                        out=wq_[b2][:].rearrange("p c (d q) -> p c d q", d=2), in0=swp,
                        in1=A64[:, 1, :, :].rearrange("p c (d q) -> p c d q", d=2), op=ALU.mult),
                        reads=[f"Wt{st}", "A64"], writes=[f"wQ{b2}"])
                    S.op("gpsimd", lambda e, nxt=nxt, b2=b2: e.tensor_tensor(
                        out=nxt, in0=wp_[b2][:].rearrange("p c (d q) -> p c d q", d=2),
                        in1=wq_[b2][:].rearrange("p c (d q) -> p c d q", d=2), op=ALU.add),
                        reads=[f"wP{b2}", f"wQ{b2}"], writes=[f"Wt{st + 1}"])
                pcu = ExitStack()
                Uraw = [pcu.enter_context(nc.sbuf_tensor("Uraw0", [128, 8, 512], BF16))] * 2
                for gq in range(4):
                    ur = Uraw[0]; urk = "Uraw0"
                    S.dma("sync", lambda e, ur=ur, gq=gq: e.dma_start(out=ur[:], in_=Us.ap()[0, gq * 8:(gq + 1) * 8].rearrange("g p n -> p g n")),
                          reads=["Us"], writes=[urk])
                    S.op("vector", lambda e, ur=ur, gq=gq: e.tensor_copy(
                        out=U_all[:, gq * 8:(gq + 1) * 8, :].rearrange("p g (m n) -> p g m n", m=8),
                        in_=ur[:].rearrange("p g (n m) -> p g m n", m=8)), reads=[urk], writes=["U_all"])

                def rot_batch(src, sk, q, items, neg_im):
                    for ki, key_out, dstfn, (tr, ti), tkey in items:
                        cr = P8[:, 0, q, ki:ki + 1]; ci = P8[:, 1, q, ki:ki + 1]; nci = P8[:, 3, q, ki:ki + 1]
                        S.op("scalar", lambda e, tr=tr, cr=cr: e.activation(out=tr[:], in_=src[:, 0, q, :], func=AF.Copy, scale=cr),
                             reads=[sk, "P8"], writes=[tkey + "r"])
                        S.op("scalar", lambda e, ti=ti, ci=ci, nci=nci: e.activation(out=ti[:], in_=src[:, 0, q, :], func=AF.Copy,
                                                                                 scale=(nci if neg_im else ci)),
                             reads=[sk, "P8"], writes=[tkey + "i"])
                    for ki, key_out, dstfn, (tr, ti), tkey in items:
                        cr = P8[:, 0, q, ki:ki + 1]; ncr = P8[:, 2, q, ki:ki + 1]; nci = P8[:, 3, q, ki:ki + 1]
                        S.op("vector", lambda e, tr=tr, nci=nci, dstfn=dstfn: e.scalar_tensor_tensor(
                            out=dstfn(0), in0=src[:, 1, q, :], scalar=nci, in1=tr[:], op0=ALU.mult, op1=ALU.add),
                            reads=[sk, "P8", tkey + "r"], writes=[key_out])
                        S.op("vector", lambda e, ti=ti, cr=cr, ncr=ncr, dstfn=dstfn: e.scalar_tensor_tensor(
                            out=dstfn(1), in0=src[:, 1, q, :], scalar=(ncr if neg_im else cr), in1=ti[:], op0=ALU.mult, op1=ALU.add),
                            reads=[sk, "P8", tkey + "i"], writes=[key_out])

                def rot_tables(src, sk, q, ki, outs, neg_im, key_out, dstfn, tkey):
                    cr = P8[:, 0, q, ki:ki + 1]; ci = P8[:, 1, q, ki:ki + 1]
                    ncr = P8[:, 2, q, ki:ki + 1]; nci = P8[:, 3, q, ki:ki + 1]
                    tr, ti = outs
                    S.op("scalar", lambda e: e.activation(out=tr[:], in_=src[:, 0, q, :], func=AF.Copy, scale=cr),
                         reads=[sk, "P8"], writes=[tkey + "r"])
                    S.op("vector", lambda e: e.scalar_tensor_tensor(out=dstfn(0), in0=src[:, 1, q, :], scalar=nci, in1=tr[:],
                                                                    op0=ALU.mult, op1=ALU.add),
                         reads=[sk, "P8", tkey + "r"], writes=[key_out])
                    S.op("scalar", lambda e: e.activation(out=ti[:], in_=src[:, 0, q, :], func=AF.Copy,
                                                          scale=(nci if neg_im else ci)),
                         reads=[sk, "P8"], writes=[tkey + "i"])
                    S.op("vector", lambda e: e.scalar_tensor_tensor(out=dstfn(1), in0=src[:, 1, q, :],
                                                                    scalar=(ncr if neg_im else cr), in1=ti[:],
                                                                    op0=ALU.mult, op1=ALU.add),
                         reads=[sk, "P8", tkey + "i"], writes=[key_out])

                with ExitStack() as pc1:
                    XE = [pc1.enter_context(nc.sbuf_tensor(f"XE{i}", [128, 8, 2, 128], BF16)) for i in range(2)]
                    ET = [pc1.enter_context(nc.sbuf_tensor(f"ET{i}", [128, 8, 2, 128], BF16)) for i in range(2)]
                    trt = [[pc1.enter_context(nc.sbuf_tensor(f"trt{i}{j}", [128, 128], BF16)) for j in range(2)] for i in range(6)]
                    tr_ps = [pc1.enter_context(nc.psum_tensor(f"ps_tr{i}", [128, 2, 2, 128], BF16)) for i in range(2)]
                    g_ps = [pc1.enter_context(nc.psum_tensor(f"ps_g{i}", [128, 2, 4, 64], F32)) for i in range(2)]
                    Ucat = [pc1.enter_context(nc.sbuf_tensor(f"Ucat{i}", [128, 4, 2, 512], BF16)) for i in range(2)]
                    Ucr = Uraw[0][:].rearrange("p g n -> p (g n)").rearrange("p (s g n) -> p s g n", s=4, g=2)
                    tn = 0
                    trn = 0
                    for qq in range(16):
                        ub = qq % 2
                        for sg_ in range(4):
                            S.dma("sync", lambda e, sg_=sg_, qq=qq: e.dma_start(
                                out=Ucr[:, sg_, :, :], in_=Us.ap()[sg_, 2 * qq:2 * qq + 2].rearrange("g p n -> p g n")),
                                reads=["Us"], writes=["Uraw0"])
                        for sg_ in range(4):
                            if sg_ % 2 == 0:
                                S.op("vector", lambda e, ub=ub, sg_=sg_: e.tensor_copy(
                                    out=Ucat[ub][:, sg_, :, :].rearrange("p g (m n) -> p g m n", m=8),
                                    in_=Ucr[:, sg_, :, :].rearrange("p g (n m) -> p g m n", m=8)), reads=["Uraw0"], writes=[f"Ucat{ub}"])
                            else:
                                for g2_ in range(2):
                                    S.op("scalar", lambda e, ub=ub, sg_=sg_, g2_=g2_: e.activation(
                                        out=Ucat[ub][:, sg_, g2_, :].rearrange("p (m n) -> p m n", m=8),
                                        in_=Ucr[:, sg_, g2_, :].rearrange("p (n m) -> p m n", m=8), func=AF.Copy),
                                        reads=["Uraw0"], writes=[f"Ucat{ub}"])
                        for dr in range(2):
                            q = dr * 16 + qq
                            xb = q % 2 if False else dr
                            for m4 in range(0, 8, 4):
                                items = []
                                for m in range(m4, m4 + 4):
                                    mu = m if dr == 0 else 7 - m
                                    outs = trt[tn % 6]; tk_ = f"trt{tn % 6}"; tn += 1
                                    items.append((8 - mu, f"XE{xb}_{m}", (lambda comp, xb=xb, m=m: XE[xb][:, m, comp, :]), outs, tk_))
                                rot_batch(Xpb, "Xpb", q, items, False)
                            for m2 in range(0, 8, 2):
                                tp = tr_ps[trn % 2]; tpk = f"ps_tr{trn % 2}"; trn += 1
                                for mm in range(2):
                                    for comp in range(2):
                                        S.op("tensor", lambda e, tp=tp, xb=xb, m=m2 + mm, mm=mm, comp=comp: e.transpose(
                                            out=tp[:, mm, comp, :], in_=XE[xb][:, m, comp, :], identity=ident_b[:]),
                                            reads=[f"XE{xb}_{m2 + mm}", "ident_b"], writes=[tpk])
                                S.op("vector", lambda e, tp=tp, xb=xb, m2=m2: e.tensor_copy(
                                    out=ET[xb][:, m2:m2 + 2, :, :], in_=tp[:]), reads=[tpk], writes=[f"ET{xb}"])
                            gp = g_ps[dr]; gpk = f"ps_g{dr}"
                            for j2 in range(2):
                                for comp in range(2):
                                    for m in range(8):
                                        S.op("tensor", lambda e, gp=gp, xb=xb, j2=j2, comp=comp, m=m, ub=ub: e.matmul(
                                            out=gp[64 * j2:64 * j2 + 64, comp, :, :], lhsT=ET[xb][:, m, comp, 64 * j2:64 * j2 + 64],
                                            rhs=Ucat[ub][:, :, j2, m * 64:(m + 1) * 64], start=(m == 0), stop=(m == 7)),
                                            reads=[f"ET{xb}", f"Ucat{ub}"], writes=[gpk])
                            S.op("scalar", lambda e, gp=gp, q=q: e.activation(
                                out=G2[:, :, :, q].rearrange("p n c -> p c n"), in_=gp[:, :, 0, :], func=AF.Copy),
                                reads=[gpk], writes=["G2"])
                            S.op("scalar", lambda e, gp=gp, q=q: e.activation(
                                out=G2o[:, :, :, :, q].rearrange("p s n c -> p c s n"), in_=gp[:, :, 1:4, :], func=AF.Copy),
                                reads=[gpk], writes=["G2o"])
                    S.barrier()
                pcu.close()
                cacc = sbc("cacc", [128, 2, 32])
                with ExitStack() as pcc:
                    def sbx(name, shape, dt=F32):
                        return pcc.enter_context(nc.sbuf_tensor(name, list(shape), dt))
                    T1 = sbx("cT1", [128, 64, 32], BF16); T2 = sbx("cT2", [128, 64, 32], BF16)
                    Wtb = sbx("Wtb", [128, 64, 2, 32], BF16)
                    S.op("vector", lambda e: e.tensor_copy(out=Wtb[:].rearrange("p n c q -> p (n c q)"),
                                                           in_=Wtab[:].rearrange("p n c q -> p (n c q)")),
                         reads=[f"Wt{i}" for i in range(64)], writes=["Wtb"])
                    Sall = sbx("Sall", [128, 3, 2, 32])
                    Asq = [sbx(f"Asq{i}", [128, 2, 32]) for i in range(2)]
                    ctm = [sbx(f"ctm{i}", [128, 32]) for i in range(4)]
                    ctt = sbx("ctt", [128, 2, 32])
                    wt_keys = [f"Wt{i}" for i in range(64)]
                    Wc = lambda c: Wtb[:, :, c, :]
                    for i in range(3):
                        Gc = lambda c, i=i: G2o[:, i, :, c, :]
                        for comp, (wa, ga, wb_, gb_, op) in enumerate(((0, 0, 1, 1, ALU.subtract), (0, 1, 1, 0, ALU.add))):
                            S.op("vector", lambda e, wa=wa, ga=ga, Gc=Gc: e.tensor_tensor(out=T1[:], in0=Wc(wa), in1=Gc(ga), op=ALU.mult),
                                 reads=["Wtb", "G2o"], writes=["cT1"])
                            S.op("vector", lambda e, wb_=wb_, gb_=gb_, Gc=Gc: e.tensor_tensor(out=T2[:], in0=Wc(wb_), in1=Gc(gb_), op=ALU.mult),
                                 reads=["Wtb", "G2o"], writes=["cT2"])
                            S.op("vector", lambda e, op=op: e.tensor_tensor(out=T1[:], in0=T1[:], in1=T2[:], op=op),
                                 reads=["cT1", "cT2"], writes=["cT1"])
                            S.op("vector", lambda e, i=i, comp=comp: e.tensor_reduce(
                                out=Sall[:, i, comp, :], in_=T1[:].rearrange("p n q -> p q n"), axis=AX.X, op=ALU.add),
                                reads=["cT1"], writes=["Sall"])
                    S.op("vector", lambda e: e.tensor_copy(out=Asq[0][:, 0, :], in_=A64[:, 0, 0, :]), reads=["A64"], writes=["Asq0"])
                    S.op("vector", lambda e: e.tensor_copy(out=Asq[0][:, 1, :], in_=A64[:, 1, 1, :]), reads=["A64"], writes=["Asq0"])
                    for k in range(6):
                        a_, b_ = Asq[k % 2], Asq[(k + 1) % 2]
                        ak, bk = f"Asq{k % 2}", f"Asq{(k + 1) % 2}"
                        S.op("vector", lambda e, a_=a_: e.tensor_tensor(out=ctm[0][:], in0=a_[:, 0, :], in1=a_[:, 0, :], op=ALU.mult), reads=[ak], writes=["ctm0"])
                        S.op("vector", lambda e, a_=a_: e.tensor_tensor(out=ctm[1][:], in0=a_[:, 1, :], in1=a_[:, 1, :], op=ALU.mult), reads=[ak], writes=["ctm1"])
                        S.op("vector", lambda e, b_=b_: e.tensor_tensor(out=b_[:, 0, :], in0=ctm[0][:], in1=ctm[1][:], op=ALU.subtract), reads=["ctm0", "ctm1"], writes=[bk])
                        S.op("vector", lambda e, a_=a_: e.tensor_tensor(out=ctm[2][:], in0=a_[:, 0, :], in1=a_[:, 1, :], op=ALU.mult), reads=[ak], writes=["ctm2"])
                        S.op("vector", lambda e, b_=b_: e.tensor_scalar(out=b_[:, 1, :], in0=ctm[2][:], scalar1=2.0, scalar2=None, op0=ALU.mult), reads=["ctm2"], writes=[bk])
                    A4k = Asq[0]; A4kk = "Asq0"
                    S.op("vector", lambda e: e.memset(cacc[:], 0.0), writes=["cacc"])
                    for half, order, fbase in ((slice(0, 16), (0, 1, 2), 0), (slice(16, 32), (2, 1, 0), 3)):
                        for i in order:
                            S.op("vector", lambda e, half=half: e.tensor_tensor(out=ctm[0][:, half], in0=A4k[:, 0, half], in1=cacc[:, 0, half], op=ALU.mult), reads=[A4kk, "cacc"], writes=["ctm0"])
                            S.op("vector", lambda e, half=half: e.tensor_tensor(out=ctm[1][:, half], in0=A4k[:, 1, half], in1=cacc[:, 1, half], op=ALU.mult), reads=[A4kk, "cacc"], writes=["ctm1"])
                            S.op("vector", lambda e, half=half: e.tensor_tensor(out=ctm[2][:, half], in0=A4k[:, 0, half], in1=cacc[:, 1, half], op=ALU.mult), reads=[A4kk, "cacc"], writes=["ctm2"])
                            S.op("vector", lambda e, half=half: e.tensor_tensor(out=ctm[3][:, half], in0=A4k[:, 1, half], in1=cacc[:, 0, half], op=ALU.mult), reads=[A4kk, "cacc"], writes=["ctm3"])
                            S.op("vector", lambda e, half=half: e.tensor_tensor(out=ctt[:, 0, half], in0=ctm[0][:, half], in1=ctm[1][:, half], op=ALU.subtract), reads=["ctm0", "ctm1"], writes=["ctt"])
                            S.op("vector", lambda e, half=half: e.tensor_tensor(out=ctt[:, 1, half], in0=ctm[2][:, half], in1=ctm[3][:, half], op=ALU.add), reads=["ctm2", "ctm3"], writes=["ctt"])
                            S.op("vector", lambda e, half=half, i=i: e.tensor_tensor(out=ctt[:, :, half], in0=ctt[:, :, half], in1=Sall[:, i, :, half], op=ALU.add), reads=["ctt", "Sall"], writes=["ctt"])
                            S.op("vector", lambda e, half=half: e.tensor_tensor(out=ctt[:, :, half], in0=ctt[:, :, half], in1=cacc[:, :, half], op=ALU.subtract), reads=["ctt", "cacc"], writes=["ctt"])
                            S.op("vector", lambda e, half=half, i=i, fbase=fbase: e.scalar_tensor_tensor(
                                out=cacc[:, :, half], in0=ctt[:, :, half], scalar=flc[:, fbase + i:fbase + i + 1], in1=cacc[:, :, half],
                                op0=ALU.mult, op1=ALU.add), reads=["ctt", "cacc", "flc"], writes=["cacc"])
                    S.barrier()
                with ExitStack() as pcy:
                    t3y_ = pcy.enter_context(nc.sbuf_tensor("t3y", [128, 16, 16], F32))
                    t4y_ = pcy.enter_context(nc.sbuf_tensor("t4y", [128, 16, 16], F32))
                    def dvy(fn, reads, writes):
                        S.op("vector", fn, reads=reads, writes=writes)
                    for dr in range(2):
                        qs = slice(dr * 16, dr * 16 + 16)
                        for sidx in range(8):
                            tau = sidx if dr == 0 else 7 - sidx
                            pw_r = bass.AP(Pw18, (0 * 32 + dr * 16) * 8 + tau, [[2 * 32 * 8, 128], [8, 16], [0, 16]])
                            pw_i = bass.AP(Pw18, (1 * 32 + dr * 16) * 8 + tau, [[2 * 32 * 8, 128], [8, 16], [0, 16]])
                            o_re = Y0[:, 0, qs, sidx * 16:(sidx + 1) * 16]
                            o_im = Y0[:, 1, qs, sidx * 16:(sidx + 1) * 16]
                            dvy(lambda e, pw_r=pw_r, qs=qs: e.tensor_tensor(out=t3y_[:], in0=CT[:, 0, qs, :], in1=pw_r, op=ALU.mult), ["CT", "Pw18"], ["t3y"])
                            dvy(lambda e, pw_i=pw_i, qs=qs: e.tensor_tensor(out=t4y_[:], in0=CT[:, 1, qs, :], in1=pw_i, op=ALU.mult), ["CT", "Pw18"], ["t4y"])
                            dvy(lambda e, o_re=o_re: e.tensor_tensor(out=o_re, in0=t3y_[:], in1=t4y_[:], op=ALU.subtract), ["t3y", "t4y"], ["Y0"])
                            dvy(lambda e, pw_i=pw_i, qs=qs: e.tensor_tensor(out=t3y_[:], in0=CT[:, 0, qs, :], in1=pw_i, op=ALU.mult), ["CT", "Pw18"], ["t3y"])
                            dvy(lambda e, pw_r=pw_r, qs=qs: e.tensor_tensor(out=t4y_[:], in0=CT[:, 1, qs, :], in1=pw_r, op=ALU.mult), ["CT", "Pw18"], ["t4y"])
                            dvy(lambda e, o_im=o_im: e.tensor_tensor(out=o_im, in0=t3y_[:], in1=t4y_[:], op=ALU.add), ["t3y", "t4y"], ["Y0"])
                with ExitStack() as pc2:
                    hp_ = [pc2.enter_context(nc.sbuf_tensor(f"hP{i}", [128, 2, 32], F32)) for i in range(2)]
                    hq_ = [pc2.enter_context(nc.sbuf_tensor(f"hQ{i}", [128, 2, 32], F32)) for i in range(2)]
                    S.op("vector", lambda e: e.tensor_copy(out=Hall[:, 0, :, :], in_=cacc[:]), reads=["cacc"], writes=["H0"])
                    for st in range(64):
                        b2 = st % 2
                        hcur = Hall[:, st, :, :]
                        hswap = bass.AP(Hall, st * 64 + 32, [[65 * 64, 128], [-32, 2], [1, 32]])
                        gcat = bass.AP(G2, st * 64, [[64 * 64, 128], [32, 2], [(63 - 2 * st) * 64 + 16, 2], [1, 16]])
                        S.op("vector", lambda e, hcur=hcur, b2=b2: e.tensor_tensor(out=hp_[b2][:], in0=hcur, in1=A64[:, 0, :, :], op=ALU.mult),
                             reads=[f"H{st}", "A64"], writes=[f"hP{b2}"])
                        S.op("vector", lambda e, hswap=hswap, b2=b2: e.tensor_tensor(out=hq_[b2][:], in0=hswap, in1=A64[:, 1, :, :], op=ALU.mult),
                             reads=[f"H{st}", "A64"], writes=[f"hQ{b2}"])
                        S.op("vector", lambda e, b2=b2: e.tensor_tensor(out=hp_[b2][:], in0=hp_[b2][:], in1=hq_[b2][:], op=ALU.add),
                             reads=[f"hP{b2}", f"hQ{b2}"], writes=[f"hP{b2}"])
                        S.op("vector", lambda e, b2=b2, gcat=gcat, st=st: e.tensor_tensor(
                            out=Hall[:, st + 1, :, :].rearrange("p c (d q) -> p c d q", d=2), in0=hp_[b2][:].rearrange("p c (d q) -> p c d q", d=2),
                            in1=gcat, op=ALU.add), reads=[f"hP{b2}", "G2"], writes=[f"H{st + 1}"])
                    S.op("vector", lambda e: e.tensor_copy(out=Hb[:].rearrange("p n c q -> p (n c q)"),
                                                           in_=Hall[:, 0:64, :, :].rearrange("p n c q -> p (n c q)")),
                         reads=[f"H{i}" for i in range(65)], writes=["Hb"])
                    S.barrier()
                if debug:
                    S.dma("sync", lambda e: e.dma_start(out=dbg_s5.ap()[:, 20992:20992 + 4096], in_=G2[:].rearrange("p n c q -> p (n c q)")), reads=["G2"])
                    S.dma("sync", lambda e: e.dma_start(out=dbg_s5.ap()[:, 25088:25088 + 4160], in_=Hall[:].rearrange("p n c q -> p (n c q)")), reads=["Hb"])
                with ExitStack() as pc3:
                    YT = [[pc3.enter_context(nc.sbuf_tensor(f"YT{d_}{i}", [128, 8, 2, 128], BF16)) for i in range(2)] for d_ in range(2)]
                    trt = [[pc3.enter_context(nc.sbuf_tensor(f"trs{i}{j}", [128, 128], F32)) for j in range(2)] for i in range(7)]
                    Tsb = [[[pc3.enter_context(nc.sbuf_tensor(f"T{d_}{j2}{i}", [128, 8, 128], BF16)) for i in range(2)]
                            for j2 in range(2)] for d_ in range(2)]
                    ysb = [pc3.enter_context(nc.sbuf_tensor(f"ysb{i}", [128, 512], F32)) for i in range(2)]
                    t_ps = [pc3.enter_context(nc.psum_tensor(f"ps_T{i}", [128, 1024], F32)) for i in range(2)]
                    y_ps = [pc3.enter_context(nc.psum_tensor(f"ps_y{i}", [128, 512], F32)) for i in range(2)]
                    tn = 0; tpn = 0; yn = 0
                    for qq in range(16):
                        pb_ = qq % 2
                        for dr in range(2):
                            q = dr * 16 + qq
                            items = []
                            for m in range(8):
                                mu = m if dr == 0 else 7 - m
                                outs = trt[tn % 7]; tk_ = f"trs{tn % 7}"; tn += 1
                                yb = YT[dr][pb_]
                                if mu == 0:
                                    S.op("scalar", lambda e, yb=yb, m=m, q=q: e.activation(out=yb[:, m, 0, :], in_=Y0[:, 0, q, :], func=AF.Copy),
                                         reads=["Y0"], writes=[f"YT{dr}{pb_}_{m}"])
                                    S.op("vector", lambda e, yb=yb, m=m, q=q: e.tensor_scalar(out=yb[:, m, 1, :], in0=Y0[:, 1, q, :], scalar1=-1.0,
                                                                                            scalar2=None, op0=ALU.mult),
                                         reads=["Y0"], writes=[f"YT{dr}{pb_}_{m}"])
                                else:
                                    items.append((mu, f"YT{dr}{pb_}_{m}", (lambda comp, yb=yb, m=m: yb[:, m, comp, :]), outs, tk_))
                            rot_batch(Y0, "Y0", q, items, True)
                            for j2 in range(2):
                                tp = t_ps[tpn % 2]; tpk = f"ps_T{tpn % 2}"; tpn += 1
                                hs = slice(64 * j2, 64 * j2 + 64)
                                for half in range(2):
                                    for comp in range(2):
                                        S.op("tensor", lambda e, tp=tp, hs=hs, half=half, comp=comp, q=q, yb=yb: e.matmul(
                                            out=tp[:, half * 512:(half + 1) * 512].rearrange("p (m x) -> p m x", m=4),
                                            lhsT=Xpb[hs, comp, q, :], rhs=yb[hs, half * 4:(half + 1) * 4, comp, :],
                                            start=(comp == 0), stop=(comp == 1)),
                                            reads=["Xpb"] + [f"YT{dr}{pb_}_{m}" for m in range(8)], writes=[tpk])
                                tsb = Tsb[dr][j2][pb_]; tsk = f"T{dr}{j2}{pb_}"
                                m0 = 0 if dr == 0 else 7
                                S.op("scalar", lambda e, tp=tp, tsb=tsb: e.activation(
                                    out=tsb[:].rearrange("p m x -> p (m x)"), in_=tp[:], func=AF.Copy), reads=[tpk], writes=[tsk])
                                S.op("vector", lambda e, tp=tp, tsb=tsb, m0=m0, dr=dr: e.tensor_tensor(
                                    out=tsb[:, m0, :], in0=tp[:, m0 * 128:(m0 + 1) * 128], in1=maskfb[:, dr, :], op=ALU.mult),
                                    reads=[tpk, "maskfb"], writes=[tsk])
                        for j2 in range(2):
                            g = qq * 2 + j2
                            hs = slice(64 * j2, 64 * j2 + 64)
                            yp = y_ps[yn % 2]; ypk = f"ps_y{yn % 2}"; yb_ = ysb[yn % 2]; ybk = f"ysb{yn % 2}"; yn += 1
                            u1 = U_all[:, g, :]
                            first = True
                            for dr in range(2):
                                tsb = Tsb[dr][j2][pb_]; tsk = f"T{dr}{j2}{pb_}"
                                for dl in range(8):
                                    blk = dl if dr == 0 else 7 - dl
                                    if dr == 0:
                                        o_ap, r_ap = yp[:, dl * 64:512], u1[:, 0:(8 - dl) * 64]
                                    else:
                                        o_ap, r_ap = yp[:, 0:(8 - dl) * 64], u1[:, dl * 64:512]
                                    S.op("tensor", lambda e, o_ap=o_ap, r_ap=r_ap, tsb=tsb, blk=blk, first=first: e.matmul(
                                        out=o_ap, lhsT=tsb[:, blk, :], rhs=r_ap, start=first, stop=False, skip_group_check=True),
                                        reads=[tsk, "U_all"], writes=[ypk])
                                    first = False
                            for dr in range(2):
                                q = dr * 16 + qq
                                yb = YT[dr][pb_]
                                for m in range(8):
                                    for comp in range(2):
                                        if dr == 0:
                                            rhs = Hb[hs, :, comp, q]
                                        else:
                                            rhs = bass.AP(Hb, 64 * j2 * (64 * 64) + 63 * 64 + comp * 32 + q, [[64 * 64, 64], [-64, 64]])
                                        last = (dr == 1 and m == 7 and comp == 1)
                                        S.op("tensor", lambda e, yp=yp, m=m, hs=hs, yb=yb, comp=comp, rhs=rhs, last=last: e.matmul(
                                            out=yp[:, m * 64:(m + 1) * 64], lhsT=yb[hs, m, comp, :], rhs=rhs, start=False, stop=last,
                                            skip_group_check=True),
                                            reads=[f"YT{dr}{pb_}_{m}", "Hb"], writes=[ypk])
                            S.op("scalar", lambda e, yp=yp, yb_=yb_: e.activation(
                                out=yb_[:].rearrange("p (n m) -> p m n", m=8), in_=yp[:].rearrange("p (m n) -> p m n", m=8), func=AF.Copy),
                                reads=[ypk], writes=[ybk])
                            for tq in range(8):
                                S.dma("sync", lambda e, g=g, yb_=yb_, tq=tq: e.dma_start(
                                    out=y_s.ap()[16 * g:16 * g + 16, tq, :], in_=yb_[tq * 16:(tq + 1) * 16, :]),
                                    reads=[ybk], writes=["y_s"])

        if stage >= 4:
            S.barrier()
            with ExitStack() as pd:
                def sbd(name, shape, dt=F32):
                    return pd.enter_context(nc.sbuf_tensor(name, list(shape), dt))

                def psd(name, shape, dt=F32):
                    return pd.enter_context(nc.psum_tensor(name, list(shape), dt))
                udd = sbd("udD", [128, 4, 8, 512], BF16)
                yall = sbd("yall", [128, 4, 8, 512])
                wglu_b = sbd("wglu_b", [128, 4, 512], BF16)
                y2 = [sbd(f"y2{i}", [128, 8, 64]) for i in range(4)]
                g32 = [sbd(f"g32{i}", [128, 4, 512]) for i in range(2)]
                gb = [sbd(f"gb{i}", [128, 4, 512], BF16) for i in range(2)]
                sg = [sbd(f"sg{i}", [128, 512]) for i in range(4)]
                o32 = [sbd(f"o32{i}", [128, 4, 512]) for i in range(2)]
                sq = [sbd(f"sq{i}", [128, 512]) for i in range(4)]
                rs = [sbd(f"rs{i}", [128, 512]) for i in range(2)]
                sn = [sbd(f"sn{i}", [128, 4, 512], BF16) for i in range(2)]
                z_ps = [psd(f"ps_z{i}", [128, 512]) for i in range(4)]
                ss_ps = [psd(f"ps_ss{i}", [128, 512]) for i in range(2)]
                S.dma("gpsimd", lambda e: e.dma_start(out=wglu_b[:], in_=w_glu.ap().rearrange("(kc p) n -> p kc n", p=128)),
                      writes=["wglu_b"])
                for cc in range(4):
                    for gl in range(8):
                        S.dma("sync", lambda e, cc=cc, gl=gl: e.dma_start(
                            out=udd[gl * 16:(gl + 1) * 16, cc, :, :],
                            in_=Us.ap()[0, cc * 8 + gl].rearrange("(s c) n -> c s n", c=16)), reads=["Us"], writes=["udD"])
                    S.dma("sync", lambda e, cc=cc: e.dma_start(out=yall[:, cc, :, :], in_=y_s.ap()[cc * 128:(cc + 1) * 128]),
                          reads=["y_s"], writes=[f"yall{cc}"])
                yn = 0
                def d_part(tt, part):
                    nonlocal yn
                    b = tt % 2
                    nsl = slice(tt * 64, tt * 64 + 64)
                    if part == 1:
                        for cc in range(4):
                            S.op("vector", lambda e, cc=cc, nsl=nsl: e.scalar_tensor_tensor(
                                out=y2[cc][:], in0=udd[:, cc, :, nsl], scalar=cols[:, C_DS + cc:C_DS + cc + 1], in1=yall[:, cc, :, nsl],
                                op0=ALU.mult, op1=ALU.add), reads=["udD", f"yall{cc}", "cols"], writes=[f"y2{cc}"])
                        for cc in range(4):
                            S.op("scalar", lambda e, cc=cc, b=b: e.activation(
                                out=g32[b][:, cc, :].rearrange("p (n s) -> p s n", s=8), in_=y2[cc][:], func=AF.Gelu_apprx_tanh),
                                reads=[f"y2{cc}"], writes=[f"g32{b}_{cc}"])
                        for cc in range(4):
                            S.op("vector", lambda e, cc=cc, b=b: e.tensor_copy(out=gb[b][:, cc, :], in_=g32[b][:, cc, :]),
                                 reads=[f"g32{b}_{cc}"], writes=[f"gb{b}_{cc}"])
                        return
                    sp = ss_ps[b]; spk = f"ps_ss{b}"
                    for jc in range(4):
                        zp = z_ps[jc]; zpk = f"ps_z{jc}"
                        for kc in range(4):
                            S.op("tensor", lambda e, zp=zp, kc=kc, jc=jc, b=b: e.matmul(
                                out=zp[:], lhsT=wglu_b[:, kc, jc * 128:(jc + 1) * 128], rhs=gb[b][:, kc, :],
                                start=(kc == 0), stop=(kc == 3)), reads=["wglu_b", f"gb{b}_{kc}"], writes=[zpk])
                    for jc in range(4):
                        S.op("scalar", lambda e, jc=jc: e.activation(
                            out=sg[jc][:], in_=z_ps[jc][:], func=AF.Sigmoid, bias=cols[:, C_BG + jc:C_BG + jc + 1]),
                            reads=[f"ps_z{jc}", "cols"], writes=[f"sg{jc}"])
                    for jc in range(4):
                        S.op("vector", lambda e, jc=jc, b=b: e.tensor_tensor(
                            out=o32[b][:, jc, :], in0=g32[b][:, jc, :], in1=sg[jc][:], op=ALU.mult),
                            reads=[f"g32{b}_{jc}", f"sg{jc}"], writes=[f"o32{b}_{jc}"])
                    for jc in range(4):
                        S.op("scalar", lambda e, jc=jc, b=b: e.activation(out=sq[jc][:], in_=o32[b][:, jc, :], func=AF.Square),
                             reads=[f"o32{b}_{jc}"], writes=[f"sq{jc}"])
                    for jc in range(4):
                        S.op("tensor", lambda e, sp=sp, jc=jc: e.matmul(out=sp[:], lhsT=ones_f[:], rhs=sq[jc][:],
                                                                       start=(jc == 0), stop=(jc == 3)),
                             reads=["ones_f", f"sq{jc}"], writes=[spk])
                    S.op("vector", lambda e, sp=sp, b=b: e.tensor_scalar(out=rs[b][:], in0=sp[:], scalar1=1.0 / 512, scalar2=EPS,
                                                                         op0=ALU.mult, op1=ALU.add), reads=[spk], writes=[f"rs{b}"])
                    S.op("scalar", lambda e, b=b: e.activation(out=rs[b][:], in_=rs[b][:], func=AF.Sqrt),
                         reads=[f"rs{b}"], writes=[f"rs{b}"])
                    S.op("vector", lambda e, b=b: e.reciprocal(out=rs[b][:], in_=rs[b][:]), reads=[f"rs{b}"], writes=[f"rs{b}"])
                    for jc in range(4):
                        S.op("vector", lambda e, jc=jc, b=b: e.scalar_tensor_tensor(
                            out=sn[b][:, jc, :], in0=o32[b][:, jc, :], scalar=cols[:, C_SG + jc:C_SG + jc + 1], in1=rs[b][:],
                            op0=ALU.mult, op1=ALU.mult), reads=[f"o32{b}_{jc}", f"rs{b}", "cols"], writes=[f"sn{b}"])
                    S.dma("sync", lambda e, b=b, tt=tt: e.dma_start(
                        out=ssmn_s.ap()[:, tt * 512:(tt + 1) * 512].rearrange("(jc p) n -> p jc n", p=128), in_=sn[b][:]),
                        reads=[f"sn{b}"], writes=["ssmn_s"])
                for tt in range(9):
                    if tt < 8:
                        d_part(tt, 1)
                    if tt >= 1:
                        d_part(tt - 1, 2)

        if stage >= 5:
            S.barrier()
            with ExitStack() as pp_:
                def sbp(name, shape, dt=F32):
                    return pp_.enter_context(nc.sbuf_tensor(name, list(shape), dt))

                def psp(name, shape, dt=F32):
                    return pp_.enter_context(nc.psum_tensor(name, list(shape), dt))
                wo_a = sbp("wo_a", [128, 4, D], BF16)
                wo_s = sbp("wo_s", [128, 4, D], BF16)
                w2_b = sbp("w2_b", [128, NFC, D], BF16)
                agT = sbp("agT", [64, 8])
                xq = [sbp(f"xp{i}", [128, 4, D]) for i in range(2)]
                at = [sbp("at0", [128, 4, 512], BF16)] * 2
                st_ = [sbp(f"st{i}", [128, 4, 512], BF16) for i in range(2)]
                asq = [sbp(f"asq{i}", [128, 512]) for i in range(2)]
                ars = sbp("ars", [128, 512])
                junq = sbp("junkp", [128, D], BF16)
                ssq = sbp("ssp", [128, 16])
                rsq = sbp("rstdp", [128, 16])
                xnq = [sbp(f"xnp{i}", [128, 4, D], BF16) for i in range(2)]
                an_ = [x_[:, 0:2, :].rearrange("p a (b n) -> p (a b) n", n=512) for x_ in xnq]
                h2T = sbp("h2T", [128, 8, 512], BF16)
                w13 = [sbp(f"w13{i}", [128, 2, 8, 128], BF16) for i in range(3)]
                silu = [sbp("silu0", [128, 512])] * 2
                actT = sbp("actT", [128, NFC, 512], BF16)
                yo = [sbp("yo0", [128, D])] * 2
                mm_ps = [psp(f"ps_mm{i}", [128, 512]) for i in range(2)]
                tq_ps = [psp(f"tp_pp{i}", [128, 512], BF16) for i in range(2)]
                h_ps = [psp(f"ps_h{i}", [128, 512]) for i in range(4)]
                wn_ = 0
                for c in range(4):
                    wst = xq[wn_ % 2]; wsk = f"xp{wn_ % 2}"; wn_ += 1
                    S.dma("sync", lambda e, wst=wst, c=c: e.dma_start(out=wst[:, 0, :], in_=w_o.ap()[c * 128:(c + 1) * 128, :]), writes=[wsk])
                    S.op("vector", lambda e, wst=wst, c=c: e.tensor_tensor(out=wo_a[:, c, :], in0=wst[:, 0, :], in1=rows[:, 0, :], op=ALU.mult),
                         reads=[wsk, "rows"], writes=["wo_a"])
                for c in range(4):
                    wst = xq[wn_ % 2]; wsk = f"xp{wn_ % 2}"; wn_ += 1
                    S.dma("sync", lambda e, wst=wst, c=c: e.dma_start(out=wst[:, 0, :], in_=w_o.ap()[512 + c * 128:512 + (c + 1) * 128, :]), writes=[wsk])
                    S.op("vector", lambda e, wst=wst, c=c: e.tensor_tensor(out=wo_s[:, c, :], in0=wst[:, 0, :], in1=rows[:, 0, :], op=ALU.mult),
                         reads=[wsk, "rows"], writes=["wo_s"])
                for j in range(NFC):
                    wst = xq[wn_ % 2]; wsk = f"xp{wn_ % 2}"; wn_ += 1
                    S.dma("sync", lambda e, wst=wst, j=j: e.dma_start(out=wst[:, 0, :], in_=w2.ap()[j * 128:(j + 1) * 128, :]), writes=[wsk])
                    S.op("vector", lambda e, wst=wst, j=j: e.tensor_tensor(out=w2_b[:, j, :], in0=wst[:, 0, :], in1=rows[:, 1, :], op=ALU.mult),
                         reads=[wsk, "rows"], writes=["w2_b"])
                S.dma("sync", lambda e: e.dma_start(out=agT[:], in_=attn_gT.ap()), writes=["agT"])
                mmn = 0; hn = 0; wn = 0; yon = 0
                NT = 8
                def p_front(tt):
                    nonlocal mmn
                    b = tt % 2
                    tsl = slice(tt * 512, (tt + 1) * 512)
                    S.dma("sync", lambda e, tt=tt, b=b: e.dma_start(
                        out=xq[b][:], in_=xext.ap()[HALO + tt * 512:HALO + (tt + 1) * 512, :].rearrange("(st p) d -> p st d", p=128)),
                        writes=[f"xp{b}"])
                    S.dma("gpsimd", lambda e, tsl=tsl, b=b: e.dma_start(out=at[b][:], in_=attn_s.ap()[:, :, tsl].rearrange("(c t) d n -> (t d) c n", t=2)),
                          reads=["attn_s"], writes=["at0"])
                    S.dma("sync", lambda e, tsl=tsl, b=b: e.dma_start(
                        out=st_[b][:], in_=ssmn_s.ap()[:, tsl].rearrange("(c p) n -> p c n", p=128)), reads=["ssmn_s"], writes=[f"st{b}"])
                    ap_ = mm_ps[mmn % 2]; apk = f"ps_mm{mmn % 2}"; mmn += 1
                    for h in range(4):
                        S.op("scalar", lambda e, h=h, b=b: e.activation(out=asq[h % 2][:], in_=at[b][:, h, :], func=AF.Square),
                             reads=["at0"], writes=[f"asq{h % 2}"])
                        S.op("tensor", lambda e, ap_=ap_, h=h: e.matmul(out=ap_[:], lhsT=ones_f[:], rhs=asq[h % 2][:],
                                                                       start=(h == 0), stop=(h == 3)),
                             reads=["ones_f", f"asq{h % 2}"], writes=[apk])
                    S.op("vector", lambda e, ap_=ap_: e.tensor_scalar(out=ars[:], in0=ap_[:], scalar1=1.0 / 512, scalar2=EPS,
                                                                      op0=ALU.mult, op1=ALU.add), reads=[apk], writes=["ars"])
                    S.op("scalar", lambda e: e.activation(out=ars[:], in_=ars[:], func=AF.Sqrt), reads=["ars"], writes=["ars"])
                    S.op("vector", lambda e: e.reciprocal(out=ars[:], in_=ars[:]), reads=["ars"], writes=["ars"])
                    for h in range(4):
                        S.op("vector", lambda e, h=h, b=b: e.scalar_tensor_tensor(
                            out=an_[b][:, h, :], in0=at[b][:, h, :], scalar=cols[:, C_AG + h:C_AG + h + 1], in1=ars[:], op0=ALU.mult, op1=ALU.mult),
                            reads=["at0", "cols", "ars"], writes=[f"an{b}_{h}"] + [f"xnp{b}_{i}" for i in range(4)])
                    for st in range(4):
                        tk = slice(st * 128, (st + 1) * 128)
                        for half in range(2):
                            hs_ = slice(half * 512, (half + 1) * 512)
                            mp = mm_ps[mmn % 2]; mpk = f"ps_mm{mmn % 2}"; mmn += 1
                            for h in range(4):
                                S.op("tensor", lambda e, mp=mp, h=h, tk=tk, hs_=hs_: e.matmul(
                                    out=mp[:], lhsT=an_[b][:, h, tk], rhs=wo_a[:, h, hs_], start=(h == 0), stop=False),
                                    reads=[f"an{b}_{h}", "wo_a"], writes=[mpk])
                            for c in range(4):
                                S.op("tensor", lambda e, mp=mp, c=c, tk=tk, hs_=hs_, b=b: e.matmul(
                                    out=mp[:], lhsT=st_[b][:, c, tk], rhs=wo_s[:, c, hs_], start=False, stop=(c == 3)),
                                    reads=[f"st{b}", "wo_s"], writes=[mpk])
                            S.op("vector", lambda e, mp=mp, b=b, st=st, hs_=hs_: e.tensor_tensor(
                                out=xq[b][:, st, hs_], in0=mp[:], in1=xq[b][:, st, hs_], op=ALU.add),
                                reads=[mpk, f"xp{b}"], writes=[f"xp{b}"])
                    for st in range(4):
                        S.op("scalar", lambda e, b=b, st=st: e.activation(
                            out=junq[:], in_=xq[b][:, st, :], func=AF.Square, accum_out=ssq[:, 8 * b + st:8 * b + st + 1]),
                            reads=[f"xp{b}"], writes=["junkp", f"ssp{b}"])
                    S.op("vector", lambda e, b=b: e.tensor_scalar(out=rsq[:, 8 * b:8 * b + 4], in0=ssq[:, 8 * b:8 * b + 4], scalar1=1.0 / D, scalar2=EPS,
                                                             op0=ALU.mult, op1=ALU.add), reads=[f"ssp{b}"], writes=[f"rstdp{b}"])
                    S.op("scalar", lambda e, b=b: e.activation(out=rsq[:, 8 * b:8 * b + 4], in_=rsq[:, 8 * b:8 * b + 4], func=AF.Sqrt), reads=[f"rstdp{b}"], writes=[f"rstdp{b}"])
                    S.op("vector", lambda e, b=b: e.reciprocal(out=rsq[:, 8 * b:8 * b + 4], in_=rsq[:, 8 * b:8 * b + 4]), reads=[f"rstdp{b}"], writes=[f"rstdp{b}"])
                    for st in range(4):
                        S.op("scalar", lambda e, b=b, st=st: e.activation(
                            out=xnq[b][:, st, :], in_=xq[b][:, st, :], func=AF.Copy, scale=rsq[:, 8 * b + st:8 * b + st + 1]),
                            reads=[f"xp{b}", f"rstdp{b}"], writes=[f"xnp{b}_{st}"])
                def p_main(tt):
                    nonlocal mmn, hn, wn, yon
                    b = tt % 2
                    for fc in range(8):
                        tb_ = fc % 2
                        for st in range(4):
                            S.op("tensor", lambda e, st=st, fc=fc, tb_=tb_: e.transpose(
                                out=tq_ps[tb_][:, st * 128:(st + 1) * 128], in_=xnq[b][:, st, fc * 128:(fc + 1) * 128], identity=ident_b[:]),
                                reads=[f"xnp{b}_{st}", "ident_b"], writes=[f"tp_pp{tb_}"])
                        S.op("scalar", lambda e, fc=fc, tb_=tb_: e.activation(
                            out=h2T[:, fc, :], in_=tq_ps[tb_][:], func=AF.Identity, scale=sc12[:, 8 + fc:9 + fc],
                            bias=modc[:, 24 + fc:25 + fc]), reads=[f"tp_pp{tb_}", "sc12b", "modc"], writes=[f"h2T{fc}"])
                    for j in range(NFC):
                        wb = w13[wn % 3]; wbk = f"w13{wn % 3}"; wn += 1
                        S.dma("sync", lambda e, wb=wb, j=j: e.dma_start(out=wb[:, 0, :, :], in_=w1b.ap()[j]), reads=["wffn_b"], writes=[wbk])
                        S.dma("sync", lambda e, wb=wb, j=j: e.dma_start(out=wb[:, 1, :, :], in_=w3b.ap()[j]), reads=["wffn_b"], writes=[wbk])
                        h1 = h_ps[hn % 4]; h1k = f"ps_h{hn % 4}"; hn += 1
                        h3 = h_ps[hn % 4]; h3k = f"ps_h{hn % 4}"; hn += 1
                        for (hpp, hkk, wi) in ((h1, h1k, 0), (h3, h3k, 1)):
                            for kc in range(8):
                                S.op("tensor", lambda e, hpp=hpp, wb=wb, wi=wi, kc=kc: e.matmul(
                                    out=hpp[:], lhsT=wb[:, wi, kc, :], rhs=h2T[:, kc, :], start=(kc == 0), stop=(kc == 7)),
                                    reads=[wbk, f"h2T{kc}"], writes=[hkk])
                        sb_ = silu[0]; sbk = "silu0"
                        S.op("scalar", lambda e, h1=h1, sb_=sb_: e.activation(out=sb_[:], in_=h1[:], func=AF.Silu),
                             reads=[h1k], writes=[sbk])
                        S.op("vector", lambda e, h3=h3, sb_=sb_, j=j: e.tensor_tensor(out=actT[:, j, :], in0=h3[:], in1=sb_[:], op=ALU.mult),
                             reads=[h3k, sbk], writes=[f"actT{j}"])
                        if j == 6 and tt + 1 < NT:
                            p_front(tt + 1)
                    for st in range(4):
                        tk = slice(st * 128, (st + 1) * 128)
                        for half in range(2):
                            hs_ = slice(half * 512, (half + 1) * 512)
                            mp = mm_ps[mmn % 2]; mpk = f"ps_mm{mmn % 2}"; mmn += 1
                            for j in range(NFC):
                                S.op("tensor", lambda e, mp=mp, j=j, tk=tk, hs_=hs_: e.matmul(
                                    out=mp[:], lhsT=actT[:, j, tk], rhs=w2_b[:, j, hs_], start=(j == 0), stop=(j == NFC - 1)),
                                    reads=[f"actT{j}", "w2_b"], writes=[mpk])
                            S.op("vector", lambda e, mp=mp, b=b, st=st, hs_=hs_: e.tensor_tensor(
                                out=xq[b][:, st, hs_], in0=mp[:], in1=xq[b][:, st, hs_], op=ALU.add),
                                reads=[mpk, f"xp{b}"], writes=[f"xp{b}"])
                        S.op("scalar", lambda e, b=b, st=st: e.activation(
                            out=junq[:], in_=xq[b][:, st, :], func=AF.Square, accum_out=ssq[:, 4 + st:5 + st]),
                            reads=[f"xp{b}"], writes=["junkp", f"ss3_{st}"])
                        S.op("vector", lambda e, st=st: e.tensor_scalar(out=rsq[:, 4 + st:5 + st], in0=ssq[:, 4 + st:5 + st], scalar1=1.0 / D,
                                                                        scalar2=EPS, op0=ALU.mult, op1=ALU.add),
                             reads=[f"ss3_{st}"], writes=[f"rs3_{st}"])
                        S.op("scalar", lambda e, st=st: e.activation(out=rsq[:, 4 + st:5 + st], in_=rsq[:, 4 + st:5 + st], func=AF.Sqrt),
                             reads=[f"rs3_{st}"], writes=[f"rs3_{st}"])
                        S.op("vector", lambda e, st=st: e.reciprocal(out=rsq[:, 4 + st:5 + st], in_=rsq[:, 4 + st:5 + st]),
                             reads=[f"rs3_{st}"], writes=[f"rs3_{st}"])
                        yb = yo[0]; ybk = "yo0"
                        S.op("vector", lambda e, b=b, st=st, yb=yb: e.scalar_tensor_tensor(
                            out=yb[:], in0=xq[b][:, st, :], scalar=rsq[:, 4 + st:5 + st], in1=rows[:, 2, :], op0=ALU.mult, op1=ALU.mult),
                            reads=[f"xp{b}", f"rs3_{st}", "rows"], writes=[ybk])
                        S.dma("gpsimd", lambda e, yb=yb, tt=tt, st=st: e.dma_start(
                            out=y_out.ap()[tt * 512 + st * 128: tt * 512 + (st + 1) * 128, :], in_=yb[:]), reads=[ybk], writes=["y_out"])

                p_front(0)
                for tt in range(NT):
                    p_main(tt)
        S.emit()
        nops = len(S.ops)
    return nc, nops


def _rope_tables(pos):
    inv = (1.0 / (10000.0 ** (np.arange(0, 64, 2, dtype=np.float32) / np.float32(64)))).astype(np.float32)
    ang = pos.astype(np.float32)[:, None] * inv[None, :]
    c, s = np.cos(ang).astype(np.float32), np.sin(ang).astype(np.float32)
    cos2 = np.concatenate([c, c], 1)
    sin2 = np.concatenate([-s, s], 1)
    cosT = np.ascontiguousarray(np.concatenate([cos2, cos2], 1).T)
    sinT = np.ascontiguousarray(np.concatenate([sin2, sin2], 1).T)
    return cosT, sinT


def _consts():
    c = np.zeros((128, 1280), np.float32)
    c[:, 0:128] = np.eye(128, dtype=np.float32)
    perm = np.zeros((128, 128), np.float32)
    for m in range(128):
        j = m % 64
        k = m - j + (j + 32) % 64
        perm[k, m] = 1.0
    c[:, 128:256] = perm
    kk = np.arange(128)[:, None]
    p = np.arange(128)[None, :]
    c[:, 256:384] = np.where(kk >= p, 0.0, -30000.0)
    c[:, 384:512] = np.where(kk <= p, 0.0, -30000.0)
    c[:, 512:640] = 1.0
    for i in range(5):
        c[:, 640 + 128 * i:768 + 128 * i] = (kk >= p) if i % 2 == 0 else (kk <= p)
    return c


def _consts2():
    c = np.zeros((128, 72 + 256), np.float32)
    c[:, 0:64] = np.arange(1, 65, dtype=np.float32)[None, :]
    c[:, 64:72] = np.arange(-8, 0, dtype=np.float32)[None, :]
    sidx = (np.arange(128) // 16)[:, None]
    tidx = (np.arange(128) // 16)[None, :]
    c[:, 72:200] = (tidx >= sidx)
    c[:, 200:328] = (tidx <= sidx)
    return c


def make_in_maps(inp):
    f = lambda a: np.ascontiguousarray(np.asarray(a, dtype=np.float32))
    xp = f(inp["x_prompt"])[0]
    xs = f(inp["x_sample"])
    cp = f(inp["c_prompt"])
    csm = f(inp["c_sample"])
    shared = {
        "consts": _consts(),
        "w_ada": f(inp["w_ada"])[0],
        "b_ada": f(inp["b_ada"]).reshape(48, 128),
        "norm1_g": f(inp["norm1_g"]).reshape(8, 128),
        "norm2_g": f(inp["norm2_g"]).reshape(8, 128),
        "final_g": f(inp["final_g"]).reshape(8, 128),
        "attn_norm_g": f(inp["attn_norm_g"]).reshape(4, 128),
        "ssm_norm_g": f(inp["ssm_norm_g"]).reshape(4, 128),
        "d_skip": f(inp["d_skip"]).reshape(4, 128),
        "b_glu": f(inp["b_glu"]).reshape(4, 128),
        "w_in": f(inp["w_in"])[0],
        "consts2": _consts2(),
        "lam_re": f(inp["lam_re"]).reshape(32, 128),
        "lam_im": f(inp["lam_im"]).reshape(32, 128),
        "log_dt": f(inp["log_dt"]).reshape(32, 2),
        "b_re": f(inp["b_re"]).reshape(32, 128, 16),
        "b_im": f(inp["b_im"]).reshape(32, 128, 16),
        "c_re": f(inp["c_re"]).reshape(64, 16, 64),
        "c_im": f(inp["c_im"]).reshape(64, 16, 64),
        "w_glu": f(inp["w_glu"])[0],
        "w_o": f(inp["w_o"])[0],
        "w1": f(inp["w1"])[0],
        "w3": f(inp["w3"])[0],
        "w2": f(inp["w2"])[0],
        "attn_gT": np.ascontiguousarray(f(inp["attn_norm_g"]).reshape(8, 64).T),
    }
    maps = []
    for core in range(8):
        if core < 4:
            seq, start, L, c = xs[core], 0, SEG, csm[core]
        else:
            seq, start, L, c = xp, (core - 4) * SEG, 4 * SEG, cp[0]
        xe = np.zeros((EXT, D), np.float32)
        lo, hi = start - HALO, start + SEG + HALO
        slo, shi = max(lo, 0), min(hi, L)
        xe[slo - lo:shi - lo] = seq[slo:shi]
        pos = np.arange(lo, hi)
        cT, sT = _rope_tables(pos)
        valid = ((pos >= 0) & (pos < L))
        fl = np.zeros((128, 6), np.float32)
        if core < 4:
            xo = np.zeros((3 * SEG, D), np.float32)
        else:
            r = core - 4
            others = [j for j in range(4) if j != r]
            xo = np.concatenate([seq[j * SEG:(j + 1) * SEG] for j in others], 0)
            for i, j in enumerate(others):
                fl[:, i] = 1.0 if j < r else 0.0
                fl[:, 3 + i] = 1.0 if j > r else 0.0
        m = dict(shared)
        m["xoth"] = np.ascontiguousarray(xo)
        m["flcol"] = fl
        m["tvalid"] = np.ascontiguousarray(valid.astype(np.float32).reshape(EXT // 128, 128).T)
        m.update({"xext": xe, "cvec": np.ascontiguousarray(c.reshape(8, 128)), "cosT": cT, "sinT": sT})
        maps.append(m)
    return maps


_CACHE = {}


def kernel(**inputs):
    maps = make_in_maps(inputs)
    if "nc" not in _CACHE:
        _CACHE["nc"] = build(debug=False)[0]
    nc = _CACHE["nc"]
    res = run_bass_kernel_spmd(nc, maps, core_ids=list(range(8)))
    outs = [np.asarray(r["y_out"], dtype=np.float32) for r in res.results]
    y_sample = np.stack(outs[0:4], 0)
    y_prompt = np.concatenate(outs[4:8], 0)[None]
    return (y_prompt, y_sample)
```

```python
import os
import numpy as np
from contextlib import ExitStack
import concourse.bass as bass
import concourse.mybir as mybir
from concourse.bass_utils import run_bass_kernel_spmd

F32 = mybir.dt.float32
BF16 = mybir.dt.bfloat16
I32 = mybir.dt.int32
ALU = mybir.AluOpType
AF = mybir.ActivationFunctionType
AX = mybir.AxisListType

D = 1024
SEG = 4096
HALO = 1024
EXT = SEG + 2 * HALO
TA = 512
NTA = EXT // TA
EPS = 1e-6
FFN = 2816
NFC = FFN // 128


class Sched:
    COMPUTE = ("tensor", "vector", "scalar", "gpsimd")
    QUEUES = ("sync", "gpsimd")

    def __init__(self, nc, es, ndma=12):
        self.nc = nc
        self.es = es
        self.ops = []
        self.last_w = {}
        self.readers = {}
        self.ndma = ndma
        self.dma_pool = {q: [None] * ndma for q in self.QUEUES}
        self.dma_rr = {q: 0 for q in self.QUEUES}

    def _deps(self, reads, writes):
        d = set()
        for k in reads:
            if k in self.last_w:
                d.add(self.last_w[k])
        for k in writes:
            if k in self.last_w:
                d.add(self.last_w[k])
            for r in self.readers.get(k, ()):
                d.add(r)
        return d

    def _commit(self, oid, reads, writes):
        for k in reads:
            self.readers.setdefault(k, []).append(oid)
        for k in writes:
            self.last_w[k] = oid
            self.readers[k] = []

    @staticmethod
    def _excl(reads, writes):
        px = [k for k in reads if k.startswith(("ps", "pj", "tp", "sw"))]
        return (reads, list(writes) + px) if px else (reads, writes)

    def op(self, eng, fn, reads=(), writes=()):
        reads, writes = self._excl(reads, writes)
        oid = len(self.ops)
        deps = self._deps(reads, writes)
        self.ops.append(dict(eng=eng, fn=fn, deps=deps, kind="c", id=oid))
        self._commit(oid, reads, writes)
        return oid

    def dma(self, q, fn, reads=(), writes=()):
        oid = len(self.ops)
        deps = self._deps(reads, writes)
        slot = self.dma_rr[q]
        self.dma_rr[q] = (slot + 1) % self.ndma
        prev = self.dma_pool[q][slot]
        if prev is not None:
            deps.add(prev)
        self.dma_pool[q][slot] = oid
        self.ops.append(dict(eng=q, fn=fn, deps=deps, kind="d", id=oid, slot=slot))
        self._commit(oid, reads, writes)
        return oid

    def barrier(self):
        last = {}
        for o in self.ops:
            if o["kind"] == "c":
                if o["eng"] != "sync":
                    last[("c", o["eng"])] = o["id"]
            else:
                last[("d", o["eng"], o["slot"])] = o["id"]
        deps = set(last.values())
        for eng in ("tensor", "vector", "scalar", "gpsimd", "sync"):
            oid = len(self.ops)
            self.ops.append(dict(eng=eng, fn=(lambda e: e.nop()), deps=set(deps), kind="c", id=oid))

    def emit(self, final_wait_eng="sync"):
        nc, es = self.nc, self.es
        ops = self.ops

        def needs_wait(o, t):
            if t["kind"] == "d":
                return True
            if t["eng"] == o["eng"] and o["kind"] == "c" and o["eng"] == "tensor":
                return False
            return True

        sig = [False] * len(ops)
        for o in ops:
            for d in o["deps"]:
                t = ops[d]
                if t["kind"] == "c" and needs_wait(o, t):
                    sig[d] = True
        csem = {e: es.enter_context(nc.semaphore("cs_" + e)) for e in self.COMPUTE}
        dsem = {q: [es.enter_context(nc.semaphore(f"ds_{q}{i}")) for i in range(self.ndma)]
                for q in self.QUEUES}
        ccount = {e: 0 for e in self.COMPUTE}
        dcount = {q: [0] * self.ndma for q in self.QUEUES}
        val = [None] * len(ops)
        for o in ops:
            if o["kind"] == "d":
                q, s = o["eng"], o["slot"]
                dcount[q][s] += 16
                val[o["id"]] = (dsem[q][s], dcount[q][s])
            elif sig[o["id"]]:
                e = o["eng"]
                assert e in csem, e
                ccount[e] += 1
                val[o["id"]] = (csem[e], ccount[e])
        per = {}
        for o in ops:
            per.setdefault(o["eng"], []).append(o)
        block = es.enter_context(nc.Block())
        self.nwaits = 0

        def run(engname, eng):
            waited = {}
            for o in per.get(engname, []):
                need = {}
                for d in o["deps"]:
                    t = ops[d]
                    if not needs_wait(o, t):
                        continue
                    sem, v = val[d]
                    key = id(sem)
                    if key not in need or need[key][1] < v:
                        need[key] = (sem, v)
                for key, (sem, v) in need.items():
                    if waited.get(key, 0) >= v:
                        continue
                    waited[key] = v
                    eng.wait_ge(sem, v)
                    self.nwaits += 1
                ins = o["fn"](eng)
                if o["kind"] == "d":
                    ins.then_inc(val[o["id"]][0], 16)
                elif sig[o["id"]]:
                    ins.then_inc(val[o["id"]][0], 1)
            if engname == final_wait_eng:
                for q in self.QUEUES:
                    for s in range(self.ndma):
                        if dcount[q][s] and waited.get(id(dsem[q][s]), 0) < dcount[q][s]:
                            eng.wait_ge(dsem[q][s], dcount[q][s])

        @block.sync
        def _(e):
            run("sync", e)

        @block.scalar
        def _(e):
            run("scalar", e)

        @block.gpsimd
        def _(e):
            run("gpsimd", e)

        @block.vector
        def _(e):
            run("vector", e)

        @block.tensor
        def _(e):
            run("tensor", e)


DILS = (1, 4, 16)


def sl(start, n, step=1):
    return slice(start, start + (n - 1) * step + 1, step)


def kb_index(di, r, end):
    base = (0, 2, 10)[di]
    return base + r * 2 + end


NKB = 42

C_BADA, C_N1, C_N2, C_C, C_AG, C_SG, C_DS, C_BG, C_FG, NCOLS = 0, 48, 56, 64, 72, 76, 80, 84, 88, 96


def build(debug=False, stage=99, sub=99, tiles=None):
    nc = bass.Bass("TRN2", target_bir_lowering=False)
    dk = "ExternalOutput" if debug else "Internal"

    def din(name, shape, dt=F32):
        return nc.dram_tensor(name, list(shape), dt, kind="ExternalInput")

    xext = din("xext", [EXT, D])
    cvec = din("cvec", [8, 128])
    cosT = din("cosT", [128, EXT])
    sinT = din("sinT", [128, EXT])
    consts = din("consts", [128, 1280])
    w_ada = din("w_ada", [D, 6 * D])
    b_ada = din("b_ada", [48, 128])
    norm1_g = din("norm1_g", [8, 128])
    norm2_g = din("norm2_g", [8, 128])
    final_g = din("final_g", [8, 128])
    attn_g = din("attn_norm_g", [4, 128])
    ssm_g = din("ssm_norm_g", [4, 128])
    d_skip = din("d_skip", [4, 128])
    b_glu = din("b_glu", [4, 128])
    w_in = din("w_in", [D, 2048])
    tvalid = din("tvalid", [128, EXT // 128])
    consts2 = din("consts2", [128, 72 + 256])
    lam_re = din("lam_re", [32, 128])
    lam_im = din("lam_im", [32, 128])
    log_dt = din("log_dt", [32, 2])
    b_re = din("b_re", [32, 128, 16])
    b_im = din("b_im", [32, 128, 16])
    c_re = din("c_re", [64, 16, 64])
    c_im = din("c_im", [64, 16, 64])
    w_glu = din("w_glu", [512, 512])
    w_o = din("w_o", [D, D])
    w1 = din("w1", [D, FFN])
    w3 = din("w3", [D, FFN])
    w2 = din("w2", [FFN, D])
    attn_gT = din("attn_gT", [64, 8])
    xoth = din("xoth", [3 * SEG, D])
    flcol = din("flcol", [128, 6])
    y_out = nc.dram_tensor("y_out", [SEG, D], F32, kind="ExternalOutput")

    qT_s = nc.dram_tensor("qT_s", [512, SEG], BF16, kind=dk)
    kT_s = nc.dram_tensor("kT_s", [512, EXT], BF16, kind=dk)
    v_s = nc.dram_tensor("v_s", [EXT, 520], BF16, kind=dk)
    Us = nc.dram_tensor("Us", [4, 32, 128, 512], BF16, kind=dk)
    attn_s = nc.dram_tensor("attn_s", [8, 64, SEG], F32, kind=dk)
    y_s = nc.dram_tensor("y_s", [512, 8, 512], F32, kind=dk)
    ssmn_s = nc.dram_tensor("ssmn_s", [512, SEG], BF16, kind=dk)
    w1b = nc.dram_tensor("w1b", [NFC, 128, 8, 128], BF16, kind="Internal")
    w3b = nc.dram_tensor("w3b", [NFC, 128, 8, 128], BF16, kind="Internal")
    if debug:
        dbg_s5 = nc.dram_tensor("dbg_s5", [128, 32768], F32, kind="ExternalOutput")
        dbg_cols = nc.dram_tensor("dbg_cols", [128, 160], F32, kind="ExternalOutput")
        dbg_rows = nc.dram_tensor("dbg_rows", [128, 3 * D], F32, kind="ExternalOutput")

    with ExitStack() as es:
        S = Sched(nc, es)

        def sb(name, shape, dt=F32):
            return es.enter_context(nc.sbuf_tensor(name, list(shape), dt))

        def ps(name, shape, dt=F32):
            return es.enter_context(nc.psum_tensor(name, list(shape), dt))

        ident_f = sb("ident_f", [128, 128])
        ones_f = sb("ones_f", [128, 128])
        ident_b = sb("ident_b", [128, 128], BF16)
        perm_b = sb("perm_b", [128, 128], BF16)
        band_b = sb("band_b", [128, 2, 128], BF16)
        cols = sb("cols", [128, NCOLS])
        modc = sb("modc", [128, 48])
        sc12 = sb("sc12", [128, 16])
        silu_c = sb("silu_c", [128, 8])
        rows = sb("rows", [128, 3, D])

        S.dma("sync", lambda e: e.dma_start(out=ident_f[:], in_=consts.ap()[:, 0:128]), writes=["ident_f"])
        S.dma("sync", lambda e: e.dma_start(out=ones_f[:], in_=consts.ap()[:, 512:640]), writes=["ones_f"])
        S.dma("gpsimd", lambda e: e.dma_start(out=ident_b[:], in_=consts.ap()[:, 0:128]), writes=["ident_b"])
        S.dma("gpsimd", lambda e: e.dma_start(out=perm_b[:], in_=consts.ap()[:, 128:256]), writes=["perm_b"])
        S.dma("gpsimd", lambda e: e.dma_start(
            out=band_b[:], in_=consts.ap()[:, 256:512].rearrange("p (c q) -> p c q", c=2)), writes=["band_b"])

        with ExitStack() as p0:
            stg = p0.enter_context(nc.sbuf_tensor("stg", [NCOLS, 128], F32))
            wada_t = [p0.enter_context(nc.sbuf_tensor(f"wada{i}", [128, 6 * D], BF16)) for i in range(2)]
            wada_f = [p0.enter_context(nc.sbuf_tensor(f"wadaf{i}", [128, 6 * D], F32)) for i in range(3)]
            silu_cb = p0.enter_context(nc.sbuf_tensor("silu_cb", [128, 8], BF16))
            diag = [p0.enter_context(nc.sbuf_tensor(f"diag{i}", [128, 128], F32)) for i in range(2)]
            ps_cols = p0.enter_context(nc.psum_tensor("ps_cols", [128, NCOLS], F32))
            ps_mod = p0.enter_context(nc.psum_tensor("ps_mod", [128, 48], F32))
            ps_row = [p0.enter_context(nc.psum_tensor(f"ps_row{i}", [128, 512], F32)) for i in range(2)]
            for (src, off, n) in ((b_ada, C_BADA, 48), (norm1_g, C_N1, 8), (norm2_g, C_N2, 8), (cvec, C_C, 8),
                                  (attn_g, C_AG, 4), (ssm_g, C_SG, 4), (d_skip, C_DS, 4), (b_glu, C_BG, 4),
                                  (final_g, C_FG, 8)):
                S.dma("sync", lambda e, src=src, off=off, n=n: e.dma_start(out=stg[off:off + n, :], in_=src.ap()),
                      writes=["stg"])
            S.op("tensor", lambda e: e.transpose(out=ps_cols[:], in_=stg[:], identity=ident_f[0:NCOLS, 0:NCOLS]),
                 reads=["stg", "ident_f"], writes=["ps_cols"])
            S.op("vector", lambda e: e.tensor_copy(out=cols[:], in_=ps_cols[:]), reads=["ps_cols"], writes=["cols"])
            S.op("scalar", lambda e: e.activation(out=silu_c[:], in_=cols[:, C_C:C_C + 8], func=AF.Silu),
                 reads=["cols"], writes=["silu_c"])
            S.op("vector", lambda e: e.tensor_copy(out=silu_cb[:], in_=silu_c[:]), reads=["silu_c"], writes=["silu_cb"])
            for kc in range(8 if stage >= -1 else 0):
                b = kc % 2
                bf = kc % 3
                S.dma("sync", lambda e, kc=kc, bf=bf: e.dma_start(out=wada_f[bf][:], in_=w_ada.ap()[kc * 128:(kc + 1) * 128, :]),
                      writes=[f"wadaf{bf}"])
                for hq in range(4):
                    cs_ = slice(hq * 1536, (hq + 1) * 1536)
                    if hq % 2 == 0:
                        S.op("scalar", lambda e, b=b, bf=bf, cs_=cs_: e.activation(out=wada_t[b][:, cs_], in_=wada_f[bf][:, cs_], func=AF.Copy),
                             reads=[f"wadaf{bf}"], writes=[f"wada{b}"])
                    else:
                        S.op("vector", lambda e, b=b, bf=bf, cs_=cs_: e.tensor_copy(out=wada_t[b][:, cs_], in_=wada_f[bf][:, cs_]),
                             reads=[f"wadaf{bf}"], writes=[f"wada{b}"])
                for j in range(48):
                    S.op("tensor", lambda e, kc=kc, b=b, j=j: e.matmul(
                        out=ps_mod[:, j:j + 1], lhsT=wada_t[b][:, j * 128:(j + 1) * 128], rhs=silu_cb[:, kc:kc + 1],
                        start=(kc == 0 and j == 0), stop=(kc == 7 and j == 47), skip_group_check=True),
                        reads=[f"wada{b}", "silu_cb"], writes=["ps_mod"])
            S.op("vector", lambda e: e.tensor_tensor(out=modc[:], in0=ps_mod[:], in1=cols[:, C_BADA:C_BADA + 48], op=ALU.add),
                 reads=["ps_mod", "cols"], writes=["modc"])
            S.op("vector", lambda e: e.scalar_tensor_tensor(out=sc12[:, 0:8], in0=modc[:, 8:16], scalar=1.0,
                                                            in1=cols[:, C_N1:C_N1 + 8], op0=ALU.add, op1=ALU.mult),
                 reads=["modc", "cols"], writes=["sc12a"])
            S.op("vector", lambda e: e.scalar_tensor_tensor(out=sc12[:, 8:16], in0=modc[:, 32:40], scalar=1.0,
                                                            in1=cols[:, C_N2:C_N2 + 8], op0=ALU.add, op1=ALU.mult),
                 reads=["modc", "cols"], writes=["sc12b"])
            k = 0
            for ri, (srct, soff, skey) in enumerate(((modc, 16, "modc"), (modc, 40, "modc"), (cols, C_FG, "cols")) if stage >= 0 else ()):
                for half in range(2):
                    pr = ps_row[(ri * 2 + half) % 2]
                    prk = f"ps_row{(ri * 2 + half) % 2}"
                    for c4 in range(4):
                        c = half * 4 + c4
                        dg = diag[k % 2]
                        dgk = f"diag{k % 2}"
                        k += 1
                        S.op("vector", lambda e, dg=dg, srct=srct, soff=soff, c=c: e.tensor_scalar(
                            out=dg[:], in0=ident_f[:], scalar1=srct[:, soff + c:soff + c + 1], scalar2=None, op0=ALU.mult),
                            reads=["ident_f", skey], writes=[dgk])
                        S.op("tensor", lambda e, dg=dg, pr=pr, c4=c4: e.matmul(
                            out=pr[:, c4 * 128:(c4 + 1) * 128], lhsT=ones_f[:], rhs=dg[:], start=True, stop=True),
                            reads=["ones_f", dgk], writes=[prk])
                    S.op("vector", lambda e, pr=pr, ri=ri, half=half: e.tensor_copy(
                        out=rows[:, ri, half * 512:(half + 1) * 512], in_=pr[:]), reads=[prk], writes=["rows"])
            if debug:
                dcol = p0.enter_context(nc.sbuf_tensor("dcol", [128, 160], F32))
                S.op("vector", lambda e: e.tensor_copy(out=dcol[:, 0:NCOLS], in_=cols[:]), reads=["cols"], writes=["dcol"])
                S.op("vector", lambda e: e.tensor_copy(out=dcol[:, 96:144], in_=modc[:]), reads=["modc"], writes=["dcol"])
                S.op("vector", lambda e: e.tensor_copy(out=dcol[:, 144:160], in_=sc12[:]), reads=["sc12a", "sc12b"], writes=["dcol"])
                S.dma("sync", lambda e: e.dma_start(out=dbg_cols.ap(), in_=dcol[:]), reads=["dcol"])
                S.dma("sync", lambda e: e.dma_start(out=dbg_rows.ap(), in_=rows[:].rearrange("p a d -> p (a d)")), reads=["rows"])


        if stage >= 5:
            for wsrc, wdst in ((w1, w1b), (w3, w3b)):
                for j in range(NFC):
                    S.dma("gpsimd", lambda e, wsrc=wsrc, wdst=wdst, j=j: e.dma_start(
                        out=wdst.ap()[j], in_=wsrc.ap()[:, j * 128:(j + 1) * 128].rearrange("(kc p) f -> p kc f", p=128)),
                        writes=["wffn_b"])

        if stage >= 1:
            S.barrier()
            with ExitStack() as pa:
                def sba(name, shape, dt=F32):
                    return pa.enter_context(nc.sbuf_tensor(name, list(shape), dt))

                def psa(name, shape, dt=F32):
                    return pa.enter_context(nc.psum_tensor(name, list(shape), dt))
                win_b = sba("win_b", [128, 8, 2048], BF16)
                S.dma("gpsimd", lambda e: e.dma_start(out=win_b[:], in_=w_in.ap().rearrange("(kc p) n -> p kc n", p=128)),
                      writes=["win_b"])
                xt = [sba(f"xt{i}", [128, 4, D]) for i in range(3)]
                cs_t = [sba(f"cs{i}", [128, 2, TA]) for i in range(3)]
                sqj = sba("sqj", [128, D], BF16)
                ss = [sba(f"ss{i}", [128, 4]) for i in range(2)]
                rstd = [sba(f"rstd{i}", [128, 4]) for i in range(2)]
                xn = [sba(f"xn{i}", [128, 4, D], BF16) for i in range(2)]
                hT = [sba(f"hT{i}", [128, 8, TA], BF16) for i in range(2)]
                qraw = [sba(f"qraw{i}", [128, TA], BF16) for i in range(2)]
                t1 = [sba(f"t1{i}", [128, TA]) for i in range(2)]
                t2 = [sba(f"t2{i}", [128, TA]) for i in range(2)]
                qk_o = [sba(f"qko{i}", [128, 8, TA], BF16) for i in range(2)]
                v_o = [sba(f"vo{i}", [128, 4, 520], BF16) for i in range(2)]
                tval = sba("tval", [128, EXT // 128])
                S.dma("sync", lambda e: e.dma_start(out=tval[:], in_=tvalid.ap()), writes=["tval"])
                ud = sba("ud", [128, 4, 8, 512], BF16)
                tp_ps = [psa(f"tp_ps{i}", [128, TA], BF16) for i in range(2)]
                pj_ps = [psa(f"pj_ps{i}", [128, 512]) for i in range(4)]
                sw_ps = [psa(f"sw_ps{i}", [128, 512]) for i in range(2)]
                pjn = 0
                swn = 0
                rn = 0
                tile_list = [("main", t) for t in (range(NTA) if tiles is None else tiles)]
                if stage >= 3 and tiles is None:
                    tile_list += [("oth", t) for t in range(3 * SEG // TA)]
                def tile_part(ti, part):
                    nonlocal pjn, swn, rn
                    tkind, t = tile_list[ti]
                    b = ti % 2
                    b3 = ti % 3
                    other = tkind == "oth"
                    own = (not other) and HALO // TA <= t < (HALO + SEG) // TA
                    to = (t % 8) if other else t - HALO // TA
                    segi = (1 + t // 8) if other else 0
                    xsrc = xoth if other else xext
                    if part == "1a":
                        S.dma("sync", lambda e, b3=b3, t=t, b=b, xsrc=xsrc: e.dma_start(
                            out=xt[b3][:], in_=xsrc.ap()[t * TA:(t + 1) * TA, :].rearrange("(st p) d -> p st d", p=128)),
                            writes=[f"xt{b3}"])
                        if not other:
                            S.dma("sync", lambda e, b3=b3, t=t, b=b: e.dma_start(out=cs_t[b3][:, 0, :], in_=cosT.ap()[:, t * TA:(t + 1) * TA]),
                                  writes=[f"cs{b3}"])
                            S.dma("sync", lambda e, b3=b3, t=t, b=b: e.dma_start(out=cs_t[b3][:, 1, :], in_=sinT.ap()[:, t * TA:(t + 1) * TA]),
                                  writes=[f"cs{b3}"])
                        for st in range(4):
                            S.op("scalar", lambda e, b3=b3, b=b, st=st: e.activation(
                                out=sqj[:], in_=xt[b3][:, st, :], func=AF.Square, accum_out=ss[b][:, st:st + 1]),
                                reads=[f"xt{b3}"], writes=["sqj", f"ss{b}"])
                        S.op("vector", lambda e, b=b: e.tensor_scalar(out=rstd[b][:], in0=ss[b][:], scalar1=1.0 / D, scalar2=EPS,
                                                                      op0=ALU.mult, op1=ALU.add),
                             reads=[f"ss{b}"], writes=[f"rstd{b}"])
                        S.op("scalar", lambda e, b=b: e.activation(out=rstd[b][:], in_=rstd[b][:], func=AF.Sqrt),
                             reads=[f"rstd{b}"], writes=[f"rstd{b}"])
                        S.op("vector", lambda e, b=b: e.reciprocal(out=rstd[b][:], in_=rstd[b][:]),
                             reads=[f"rstd{b}"], writes=[f"rstd{b}"])
                        for st in range(4):
                            S.op("vector", lambda e, b3=b3, b=b, st=st: e.tensor_scalar(
                                out=xn[b][:, st, :], in0=xt[b3][:, st, :], scalar1=rstd[b][:, st:st + 1], scalar2=None, op0=ALU.mult),
                                reads=[f"xt{b3}", f"rstd{b}"], writes=[f"xn{b}_{st}"])
                        return
                    if part == "1b":
                        for fc in range(8):
                            tb = fc % 2
                            for st in range(4):
                                S.op("tensor", lambda e, b=b, st=st, fc=fc, tb=tb: e.transpose(
                                    out=tp_ps[tb][:, st * 128:(st + 1) * 128], in_=xn[b][:, st, fc * 128:(fc + 1) * 128],
                                    identity=ident_b[:]), reads=[f"xn{b}_{st}", "ident_b"], writes=[f"tp_ps{tb}"])
                            if other and fc % 2 == 1:
                                S.op("vector", lambda e, b=b, fc=fc, tb=tb: e.tensor_scalar(
                                    out=hT[b][:, fc, :], in0=tp_ps[tb][:], scalar1=sc12[:, fc:fc + 1], scalar2=modc[:, fc:fc + 1],
                                    op0=ALU.mult, op1=ALU.add),
                                    reads=[f"tp_ps{tb}", "sc12a", "modc"], writes=[f"hT{b}_{fc}"])
                            else:
                                S.op("scalar", lambda e, b=b, fc=fc, tb=tb: e.activation(
                                    out=hT[b][:, fc, :], in_=tp_ps[tb][:], func=AF.Identity,
                                    scale=sc12[:, fc:fc + 1], bias=modc[:, fc:fc + 1]),
                                    reads=[f"tp_ps{tb}", "sc12a", "modc"], writes=[f"hT{b}_{fc}"])
                        return
                    hkeys = [f"hT{b}_{fc}" for fc in range(8)]
                    if sub < 3:
                        return
                    for which in (() if other else ((0, 1) if own else (1,))):
                        for cc in range(4):
                            pp = pj_ps[pjn % 4]; ppk = f"pj_ps{pjn % 4}"; pjn += 1
                            col0 = which * 512 + cc * 128
                            for kc in range(8):
                                S.op("tensor", lambda e, pp=pp, b=b, kc=kc, col0=col0: e.matmul(
                                    out=pp[:], lhsT=win_b[:, kc, col0:col0 + 128], rhs=hT[b][:, kc, :],
                                    start=(kc == 0), stop=(kc == 7)), reads=["win_b", hkeys[kc]], writes=[ppk])
                            r = rn % 2; rn += 1
                            S.op("scalar", lambda e, pp=pp, r=r: e.activation(out=qraw[r][:], in_=pp[:], func=AF.Copy),
                                 reads=[ppk], writes=[f"qraw{r}", ppk + "x"])
                            sp = sw_ps[swn % 2]; spk = f"sw_ps{swn % 2}"; swn += 1
                            S.op("tensor", lambda e, sp=sp, r=r: e.matmul(out=sp[:], lhsT=perm_b[:], rhs=qraw[r][:],
                                                                         start=True, stop=True),
                                 reads=["perm_b", f"qraw{r}"], writes=[spk])
                            S.op("vector", lambda e, b3=b3, pp=pp, r=r, b=b: e.tensor_tensor(
                                out=t1[r][:], in0=pp[:], in1=cs_t[b3][:, 0, :], op=ALU.mult),
                                reads=[ppk, f"cs{b3}"], writes=[f"t1{r}", ppk + "x"])
                            S.op("vector", lambda e, b3=b3, sp=sp, r=r, b=b: e.tensor_tensor(
                                out=t2[r][:], in0=sp[:], in1=cs_t[b3][:, 1, :], op=ALU.mult),
                                reads=[spk, f"cs{b3}"], writes=[f"t2{r}"])
                            S.op("vector", lambda e, r=r, b=b, which=which, cc=cc: e.tensor_tensor(
                                out=qk_o[b][:, which * 4 + cc, :], in0=t1[r][:], in1=t2[r][:], op=ALU.add),
                                reads=[f"t1{r}", f"t2{r}"], writes=[f"qko{b}"])
                    if own:
                        S.dma("gpsimd", lambda e, b=b, to=to: e.dma_start(
                            out=qT_s.ap()[:, to * TA:(to + 1) * TA].rearrange("(cc p) n -> p cc n", p=128),
                            in_=qk_o[b][:, 0:4, :]), reads=[f"qko{b}"], writes=["qT_s"])
                    if not other:
                        S.dma("gpsimd", lambda e, b=b, t=t: e.dma_start(
                            out=kT_s.ap()[:, t * TA:(t + 1) * TA].rearrange("(cc p) n -> p cc n", p=128),
                            in_=qk_o[b][:, 4:8, :]), reads=[f"qko{b}"], writes=["kT_s"])
                    if sub < 4:
                        return
                    for st in range(0 if other else 4):
                        pp = pj_ps[pjn % 4]; ppk = f"pj_ps{pjn % 4}"; pjn += 1
                        for kc in range(8):
                            S.op("tensor", lambda e, pp=pp, b=b, kc=kc, st=st: e.matmul(
                                out=pp[:], lhsT=hT[b][:, kc, st * 128:(st + 1) * 128], rhs=win_b[:, kc, 1024:1536],
                                start=(kc == 0), stop=(kc == 7)), reads=["win_b", hkeys[kc]], writes=[ppk])
                        S.op("vector", lambda e, b=b, st=st, t=t: e.tensor_copy(
                            out=v_o[b][:, st, :].rearrange("p (h e) -> p h e", e=65)[:, :, 64],
                            in_=bass.AP(tval, t * 4 + st, [[EXT // 128, 128], [0, 8]])), reads=["tval"], writes=[f"vo{b}"])
                        S.op("vector", lambda e, pp=pp, b=b, st=st, t=t: e.tensor_scalar(
                            out=v_o[b][:, st, :].rearrange("p (h e) -> p h e", e=65)[:, :, 0:64],
                            in0=pp[:].rearrange("p (h e) -> p h e", e=64), scalar1=tval[:, t * 4 + st:t * 4 + st + 1], scalar2=None,
                            op0=ALU.mult),
                             reads=[ppk, "tval"], writes=[f"vo{b}"])
                    if not other:
                        S.dma("gpsimd", lambda e, b=b, t=t: e.dma_start(
                            out=v_s.ap()[t * TA:(t + 1) * TA, :].rearrange("(st p) n -> p st n", p=128), in_=v_o[b][:]),
                            reads=[f"vo{b}"], writes=["v_s"])
                    if (own or other) and sub >= 5:
                        for cc in range(4):
                            pp = pj_ps[pjn % 4]; ppk = f"pj_ps{pjn % 4}"; pjn += 1
                            col0 = 1536 + cc * 128
                            for kc in range(8):
                                S.op("tensor", lambda e, pp=pp, b=b, kc=kc, col0=col0: e.matmul(
                                    out=pp[:], lhsT=win_b[:, kc, col0:col0 + 128], rhs=hT[b][:, kc, :],
                                    start=(kc == 0), stop=(kc == 7)), reads=["win_b", hkeys[kc]], writes=[ppk])
                            if other:
                                S.op("vector", lambda e, pp=pp, cc=cc, to=to: e.tensor_copy(
                                    out=ud[:, cc, :, to * 64:(to + 1) * 64], in_=pp[:].rearrange("p (n s) -> p s n", s=8)),
                                    reads=[ppk], writes=["ud"])
                            else:
                                S.op("scalar", lambda e, pp=pp, cc=cc, to=to: e.activation(
                                    out=ud[:, cc, :, to * 64:(to + 1) * 64], in_=pp[:].rearrange("p (n s) -> p s n", s=8),
                                    func=AF.Copy), reads=[ppk], writes=["ud"])
                    if (own or other) and to == 7 and sub >= 6:
                      for cc in range(4):
                        for gl in range(8):
                          S.dma("gpsimd", lambda e, cc=cc, gl=gl, segi=segi: e.dma_start(
                            out=Us.ap()[segi, cc * 8 + gl].rearrange("(s c) n -> c s n", c=16),
                            in_=ud[gl * 16:(gl + 1) * 16, cc, :, :]), reads=["ud"], writes=["Us"])
                ntl = len(tile_list)
                tile_part(0, "1a")
                tile_part(0, "1b")
                for ti in range(ntl):
                    if ti + 1 < ntl:
                        tile_part(ti + 1, "1a")
                    if ti >= 1:
                        tile_part(ti - 1, 2)
                    if ti + 1 < ntl:
                        tile_part(ti + 1, "1b")
                tile_part(ntl - 1, 2)

        if stage >= 2:
            S.barrier()
            with ExitStack() as pb:
                def sbb(name, shape, dt=F32):
                    return pb.enter_context(nc.sbuf_tensor(name, list(shape), dt))

                def psb(name, shape, dt=F32):
                    return pb.enter_context(nc.psum_tensor(name, list(shape), dt))
                maskT = sbb("maskT", [128, 640], BF16)
                S.dma("gpsimd", lambda e: e.dma_start(out=maskT[:], in_=consts.ap()[:, 640:1280]), writes=["maskT"])
                kT_all = sbb("kT_all", [128, 4, 4096], BF16)
                qT_all = sbb("qT_all", [128, 4, 2048], BF16)
                nchs = {1: 17, 4: 5, 16: 2}
                varr = {}
                for d in DILS:
                    for r in range(d):
                        varr[(d, r)] = sbb(f"va{d}_{r}", [128, nchs[d], 520], BF16)
                NSB = 4
                LOOK = 3
                pT = [sbb(f"pT{i}", [128, 512], BF16) for i in range(NSB)]
                pTm = [sbb(f"pTm{i}", [128, 512], BF16) for i in range(NSB)]
                accsb = [sbb(f"accsb{i}", [65, 1024]) for i in range(2)]
                acc16 = sbb("acc16", [65, 16, 128])
                rden = [sbb(f"rden{i}", [64, 512]) for i in range(2)]
                acc_ps = psb("ps_acc", [128, 1024])
                a16_ps = psb("ps_a16", [128, 512])
                s_ps = [psb(f"ps_s{i}", [128, 512]) for i in range(NSB)]
                bc_ps = psb("ps_bc", [64, 512])
                un = 0
                ev = 0
                for sbi in range(2):
                    S.dma("sync", lambda e, sbi=sbi: e.dma_start(
                        out=kT_all[:], in_=kT_s.ap()[:, 2048 * sbi:2048 * sbi + 4096].rearrange("(cc p) n -> p cc n", p=128)),
                        reads=["kT_s"], writes=["kT_all"])
                    S.dma("sync", lambda e, sbi=sbi: e.dma_start(
                        out=qT_all[:], in_=qT_s.ap()[:, 2048 * sbi:2048 * sbi + 2048].rearrange("(cc p) n -> p cc n", p=128)),
                        reads=["qT_s"], writes=["qT_all"])
                    for d in DILS:
                        nq = SEG // d // 128
                        ca0 = (nq // 2) * sbi
                        for r in range(d):
                            row0 = r + HALO - 64 * d + 128 * d * ca0
                            src = bass.AP(v_s, row0 * 520, [[d * 520, 128], [128 * d * 520, nchs[d]], [1, 520]])
                            S.dma("sync", lambda e, d=d, r=r, src=src: e.dma_start(out=varr[(d, r)][:], in_=src),
                                  reads=["v_s"], writes=[f"va{d}_{r}"])
                    for h in range(8):
                        hp, hc = (h % 2) * 64, h // 2

                        def mk_chunks(dsel, half):
                            out = []
                            for di, d in enumerate(DILS):
                                if d not in dsel:
                                    continue
                                nql = SEG // d // 128 // 2
                                if half is None:
                                    b_lo, b_hi = 0, nql
                                else:
                                    b_lo, b_hi = half * (nql // 2), (half + 1) * (nql // 2)
                                for r in range(d):
                                    for cal in range(b_lo, b_hi + 1):
                                        blocks = [bl for bl in (cal - 1, cal) if b_lo <= bl < b_hi]
                                        out.append((di, d, r, cal, blocks))
                            return out

                        def run_pipeline(chunks, pv_emit):
                            nonlocal un
                            groups, cur, used = [], [], 0
                            for ch in chunks:
                                w = 128 * len(ch[4])
                                if used + w > 512:
                                    groups.append(cur); cur, used = [], 0
                                cur.append((ch, used)); used += w
                            if cur:
                                groups.append(cur)
                            ginfo = []
                            for gi in range(len(groups) + LOOK):
                                if gi < len(groups):
                                    grp = groups[gi]
                                    sp = s_ps[un % NSB]; spk = f"ps_s{un % NSB}"
                                    pt = pT[un % NSB]; ptk = f"pT{un % NSB}"
                                    pm = pTm[un % NSB]; pmk = f"pTm{un % NSB}"
                                    un += 1
                                    ginfo.append((grp, pm, pmk))
                                    width = grp[-1][1] + 128 * len(grp[-1][0][4])
                                    for (di, d, r, cal, blocks), off in grp:
                                        nq = SEG // d // 128
                                        ca = (nq // 2) * sbi + cal
                                        kbase = r + HALO - 64 * d + 128 * d * ca - 2048 * sbi
                                        bq0 = (nq // 2) * sbi + blocks[0]
                                        qbase = r + 128 * d * bq0 - 2048 * sbi
                                        nqc = 128 * len(blocks)
                                        S.op("tensor", lambda e, sp=sp, off=off, nqc=nqc, kbase=kbase, qbase=qbase, d=d, hp=hp, hc=hc: e.matmul(
                                            out=sp[:, off:off + nqc], lhsT=kT_all[hp:hp + 64, hc, sl(kbase, 128, d)],
                                            rhs=qT_all[hp:hp + 64, hc, sl(qbase, nqc, d)], start=True, stop=True),
                                            reads=["kT_all", "qT_all"], writes=[spk])
                                    S.op("scalar", lambda e, sp=sp, pt=pt, width=width: e.activation(
                                        out=pt[:, 0:width], in_=sp[:, 0:width], func=AF.Exp, scale=0.125), reads=[spk], writes=[ptk])
                                    (_di, _d, _r, cal0, blocks0), _ = grp[0]
                                    m0 = 128 if blocks0[0] == cal0 - 1 else 0
                                    S.op("vector", lambda e, pt=pt, pm=pm, width=width, m0=m0: e.tensor_tensor(
                                        out=pm[:, 0:width], in0=pt[:, 0:width], in1=maskT[:, m0:m0 + width], op=ALU.mult),
                                        reads=[ptk, "maskT"], writes=[pmk])
                                if gi >= LOOK:
                                    grp, pm, pmk = ginfo[gi - LOOK]
                                    for ch, off in grp:
                                        pv_emit(ch, off, pm, pmk)

                        def pv16(ch, off, pm, pmk):
                            di, d, r, cal, blocks = ch
                            rr = r % 4
                            S.op("tensor", lambda e, pm=pm, off=off, rr=rr, r=r, cal=cal, h=h: e.matmul(
                                out=a16_ps[0:65, rr * 128:(rr + 1) * 128], lhsT=varr[(16, r)][:, cal, h * 65:(h + 1) * 65],
                                rhs=pm[:, off:off + 128], start=(rr == 0 and cal == 0), stop=False, skip_group_check=True),
                                reads=[pmk, f"va16_{r}"], writes=["ps_a16"])
                            if rr == 3 and cal == 1:
                                S.op("scalar", lambda e, r=r: e.activation(
                                    out=acc16[:, r - 3:r + 1, :].rearrange("p a b -> p (a b)"), in_=a16_ps[0:65, :], func=AF.Copy),
                                    reads=["ps_a16"], writes=["acc16"])
                        run_pipeline(mk_chunks((16,), None), pv16)

                        for half in range(2):
                            started = set()

                            def pv14(ch, off, pm, pmk, half=half, started=started):
                                di, d, r, cal, blocks = ch
                                lhsT = varr[(d, r)][:, cal, h * 65:(h + 1) * 65]
                                pieces = []
                                if d == 1:
                                    lb = [bl - 8 * half for bl in blocks]
                                    if len(lb) == 2 and lb[1] % 4 != 0:
                                        pieces.append((pm[:, off:off + 256], lb[0] // 4, acc_ps[0:65, lb[0] * 128:(lb[0] + 2) * 128]))
                                    else:
                                        for bi, bl in enumerate(lb):
                                            pieces.append((pm[:, off + 128 * bi:off + 128 * (bi + 1)], bl // 4,
                                                           acc_ps[0:65, bl * 128:(bl + 1) * 128]))
                                else:
                                    for bi, bl in enumerate(blocks):
                                        lbl = bl - 2 * half
                                        pieces.append((pm[:, off + 128 * bi:off + 128 * (bi + 1)], lbl,
                                                       acc_ps[0:65, sl(lbl * 512 + r, 128, 4)]))
                                for rhs, bank, o_ap in pieces:
                                    st = bank not in started
                                    started.add(bank)
                                    S.op("tensor", lambda e, lhsT=lhsT, rhs=rhs, o_ap=o_ap, st=st: e.matmul(
                                        out=o_ap, lhsT=lhsT, rhs=rhs, start=st, stop=False, skip_group_check=True),
                                        reads=[pmk, f"va{d}_{r}"], writes=["ps_acc"])
                            run_pipeline(mk_chunks((1, 4), half), pv14)
                            eb = ev % 2; ev += 1
                            S.op("scalar", lambda e, eb=eb: e.activation(out=accsb[eb][:], in_=acc_ps[0:65, :], func=AF.Copy),
                                 reads=["ps_acc"], writes=[f"accsb{eb}"])
                            S.op("vector", lambda e, eb=eb, half=half: e.tensor_tensor(
                                out=accsb[eb][:].rearrange("p (q r) -> p q r", r=16), in0=accsb[eb][:].rearrange("p (q r) -> p q r", r=16),
                                in1=acc16[:, :, 64 * half:64 * half + 64].rearrange("p r q -> p q r"), op=ALU.add),
                                reads=[f"accsb{eb}", "acc16"], writes=[f"accsb{eb}"])
                            for j in range(2):
                                rb = j % 2
                                S.op("tensor", lambda e, eb=eb, j=j: e.matmul(
                                    out=bc_ps[:], lhsT=ones_f[64:65, 0:64], rhs=accsb[eb][64:65, j * 512:(j + 1) * 512],
                                    start=True, stop=True), reads=["ones_f", f"accsb{eb}"], writes=["ps_bc"])
                                S.op("vector", lambda e, rb=rb: e.reciprocal(out=rden[rb][:], in_=bc_ps[:]),
                                     reads=["ps_bc"], writes=[f"rden{rb}"])
                                S.op("vector", lambda e, eb=eb, rb=rb, j=j: e.tensor_tensor(
                                    out=accsb[eb][0:64, j * 512:(j + 1) * 512], in0=accsb[eb][0:64, j * 512:(j + 1) * 512],
                                    in1=rden[rb][:], op=ALU.mult), reads=[f"accsb{eb}", f"rden{rb}"], writes=[f"accsb{eb}"])
                            S.dma("sync", lambda e, eb=eb, h=h, sbi=sbi, half=half: e.dma_start(
                                out=attn_s.ap()[h, :, sbi * 2048 + half * 1024:sbi * 2048 + (half + 1) * 1024], in_=accsb[eb][0:64, :]),
                                reads=[f"accsb{eb}"], writes=["attn_s"])

        if stage >= 3:
            S.barrier()
            TWO_PI = 6.283185307179586
            with ExitStack() as pc:
                def sbc(name, shape, dt=F32):
                    return pc.enter_context(nc.sbuf_tensor(name, list(shape), dt))
                kv = sbc("kv", [128, 72])
                maskfb = sbc("maskfb", [128, 2, 128])
                P8 = sbc("P8", [128, 4, 32, 9])
                Y0 = sbc("Y0", [128, 2, 32, 128])
                Xpb = sbc("Xpb", [128, 2, 32, 128], BF16)
                A64 = sbc("A64", [128, 2, 2, 32])
                CT = sbc("CT", [128, 2, 32, 16])
                Pw18 = sbc("Pw18", [128, 2, 32, 8])
                G2o = Y0[:].rearrange("p a q x -> p (a q x)").bitcast(BF16)[:, 0:3 * 64 * 2 * 32].rearrange(
                    "p (s n c q) -> p s n c q", s=3, n=64, c=2)
                S.dma("sync", lambda e: e.dma_start(out=kv[:], in_=consts2.ap()[:, 0:72]), writes=["kv"])
                S.dma("sync", lambda e: e.dma_start(out=maskfb[:], in_=consts2.ap()[:, 72:328].rearrange("p (a b) -> p a b", a=2)),
                      writes=["maskfb"])
                with ExitStack() as pc0:
                    lrli = pc0.enter_context(nc.sbuf_tensor("lrli", [128, 3, 32], F32))
                    th = pc0.enter_context(nc.sbuf_tensor("th", [128, 3, 32], F32))
                    PwR = pc0.enter_context(nc.sbuf_tensor("PwR", [128, 32, 72], F32))
                    PwI = pc0.enter_context(nc.sbuf_tensor("PwI", [128, 32, 72], F32))
                    Bb = pc0.enter_context(nc.sbuf_tensor("Bb", [128, 2, 32, 16], F32))
                    Xp = pc0.enter_context(nc.sbuf_tensor("Xp", [128, 2, 32, 128], F32))
                    lst = pc0.enter_context(nc.sbuf_tensor("lst", [32, 3, 128], F32))
                    ldt2 = pc0.enter_context(nc.sbuf_tensor("ldt2", [32, 2], F32))
                    cst = [pc0.enter_context(nc.sbuf_tensor(f"cst{i}", [128, 2, 64], F32)) for i in range(2)]
                    Braw = pc0.enter_context(nc.sbuf_tensor("Braw", [128, 2, 32, 16], F32))
                    zz = pc0.enter_context(nc.sbuf_tensor("zz", [128, 8, 32], F32))
                    tb1 = pc0.enter_context(nc.sbuf_tensor("tb1", [128, 32, 16], F32))
                    tb2 = pc0.enter_context(nc.sbuf_tensor("tb2", [128, 32, 16], F32))
                    ps_p = pc0.enter_context(nc.psum_tensor("ps_par", [128, 96], F32))
                    ps_c = [pc0.enter_context(nc.psum_tensor(f"ps_ct{i}", [128, 128], F32)) for i in range(2)]
                    S.dma("sync", lambda e: e.dma_start(out=lst[:, 0, :], in_=lam_re.ap()), writes=["lst"])
                    S.dma("sync", lambda e: e.dma_start(out=lst[:, 1, :], in_=lam_im.ap()), writes=["lst"])
                    S.dma("sync", lambda e: e.dma_start(out=ldt2[:], in_=log_dt.ap()), writes=["ldt2"])
                    S.op("vector", lambda e: e.tensor_copy(
                        out=lst[:, 2, :].rearrange("q (j p) -> q j p", j=2),
                        in_=bass.AP(ldt2, 0, [[2, 32], [1, 2], [0, 64]])), reads=["ldt2"], writes=["lst"])
                    for a in range(3):
                        S.op("tensor", lambda e, a=a: e.transpose(out=ps_p[:, a * 32:(a + 1) * 32], in_=lst[:, a, :],
                                                                 identity=ident_f[0:32, 0:32]),
                             reads=["lst", "ident_f"], writes=["ps_par"])
                    S.op("vector", lambda e: e.tensor_copy(out=lrli[:].rearrange("p a q -> p (a q)"), in_=ps_p[:]),
                         reads=["ps_par"], writes=["lrli"])
                    S.dma("sync", lambda e: e.dma_start(out=Braw[:, 0, :, :], in_=b_re.ap().rearrange("q p c -> p q c")),
                          writes=["Braw"])
                    S.dma("sync", lambda e: e.dma_start(out=Braw[:, 1, :, :], in_=b_im.ap().rearrange("q p c -> p q c")),
                          writes=["Braw"])
                    n = 0
                    for comp, csrc in enumerate((c_re, c_im)):
                        for qb in range(4):
                            cb = n % 2; n += 1
                            for ql in range(8):
                                q = qb * 8 + ql
                                S.dma("sync", lambda e, cb=cb, ql=ql, q=q, csrc=csrc: e.dma_start(
                                    out=cst[cb][ql * 16:(ql + 1) * 16, :, :],
                                    in_=csrc.ap()[2 * q:2 * q + 2].rearrange("j c p -> c j p")), writes=[f"cst{cb}"])
                            S.op("tensor", lambda e, cb=cb: e.transpose(
                                out=ps_c[cb][:], in_=cst[cb][:].rearrange("p j x -> p (j x)"), identity=ident_f[:]),
                                reads=[f"cst{cb}", "ident_f"], writes=[f"ps_ct{cb}"])
                            S.op("vector", lambda e, cb=cb, comp=comp, qb=qb: e.tensor_copy(
                                out=CT[:, comp, qb * 8:(qb + 1) * 8, :], in_=ps_c[cb][:].rearrange("p (q c) -> p q c", c=16)),
                                reads=[f"ps_ct{cb}"], writes=["CT"])
                    S.op("scalar", lambda e: e.activation(out=th[:, 2, :], in_=lrli[:, 2, :], func=AF.Exp),
                         reads=["lrli"], writes=["th"])
                    S.op("vector", lambda e: e.tensor_tensor(out=th[:, 0, :], in0=lrli[:, 0, :], in1=th[:, 2, :], op=ALU.mult),
                         reads=["lrli", "th"], writes=["th"])
                    S.op("vector", lambda e: e.tensor_tensor(out=th[:, 1, :], in0=lrli[:, 1, :], in1=th[:, 2, :], op=ALU.mult),
                         reads=["lrli", "th"], writes=["th"])
                    bt = pc0.enter_context(nc.sbuf_tensor("bt", [128, 12, 32], F32))
                    halfpi = pc0.enter_context(nc.sbuf_tensor("halfpi", [128, 1], F32))
                    cm = [pc0.enter_context(nc.sbuf_tensor(f"cm{i}", [128, 32, 32], F32)) for i in range(4)]
                    B_ = lambda i: bt[:, i, :]
                    def dvp(fn, reads, writes):
                        S.op("vector", fn, reads=reads, writes=writes)
                    dvp(lambda e: e.memset(halfpi[:], 1.5707963267948966), [], ["halfpi"])
                    S.op("scalar", lambda e: e.activation(out=B_(0), in_=th[:, 0, :], func=AF.Exp, scale=1.0 / 16), reads=["th"], writes=["bt0"])
                    S.op("scalar", lambda e: e.activation(out=B_(1), in_=th[:, 1, :], func=AF.Sin, scale=1.0 / 16), reads=["th"], writes=["bt1"])
                    S.op("scalar", lambda e: e.activation(out=B_(2), in_=th[:, 1, :], func=AF.Sin, scale=-1.0 / 16, bias=halfpi[:]),
                         reads=["th", "halfpi"], writes=["bt2"])
                    dvp(lambda e: e.tensor_tensor(out=B_(3), in0=B_(0), in1=B_(2), op=ALU.mult), ["bt0", "bt2"], ["bt3"])
                    dvp(lambda e: e.tensor_tensor(out=B_(4), in0=B_(0), in1=B_(1), op=ALU.mult), ["bt0", "bt1"], ["bt4"])
                    cur_r, cur_i = 3, 4
                    for k in range(4):
                        nr, ni = (5, 6) if cur_r == 3 else (3, 4)
                        dvp(lambda e, cr_=cur_r: e.tensor_tensor(out=B_(7), in0=B_(cr_), in1=B_(cr_), op=ALU.mult), [f"bt{cur_r}"], ["bt7"])
                        dvp(lambda e, ci_=cur_i: e.tensor_tensor(out=B_(8), in0=B_(ci_), in1=B_(ci_), op=ALU.mult), [f"bt{cur_i}"], ["bt8"])
                        dvp(lambda e, nr=nr: e.tensor_tensor(out=B_(nr), in0=B_(7), in1=B_(8), op=ALU.subtract), ["bt7", "bt8"], [f"bt{nr}"])
                        dvp(lambda e, cr_=cur_r, ci_=cur_i: e.tensor_tensor(out=B_(9), in0=B_(cr_), in1=B_(ci_), op=ALU.mult),
                            [f"bt{cur_r}", f"bt{cur_i}"], ["bt9"])
                        dvp(lambda e, ni=ni: e.tensor_scalar(out=B_(ni), in0=B_(9), scalar1=2.0, scalar2=None, op0=ALU.mult), ["bt9"], [f"bt{ni}"])
                        cur_r, cur_i = nr, ni
                    assert (cur_r, cur_i) == (3, 4)
                    dvp(lambda e: e.tensor_copy(out=PwR[:, :, 0], in_=B_(3)), ["bt3"], ["PwR"])
                    dvp(lambda e: e.tensor_copy(out=PwI[:, :, 0], in_=B_(4)), ["bt4"], ["PwI"])
                    dvp(lambda e: e.tensor_tensor(out=B_(7), in0=B_(3), in1=B_(3), op=ALU.mult), ["bt3"], ["bt7"])
                    dvp(lambda e: e.tensor_tensor(out=B_(8), in0=B_(4), in1=B_(4), op=ALU.mult), ["bt4"], ["bt8"])
                    dvp(lambda e: e.tensor_tensor(out=B_(7), in0=B_(7), in1=B_(8), op=ALU.add), ["bt7", "bt8"], ["bt7"])
                    dvp(lambda e: e.reciprocal(out=B_(7), in_=B_(7)), ["bt7"], ["bt7"])
                    dvp(lambda e: e.tensor_tensor(out=PwR[:, :, 71], in0=B_(3), in1=B_(7), op=ALU.mult), ["bt3", "bt7"], ["PwR"])
                    dvp(lambda e: e.scalar_tensor_tensor(out=PwI[:, :, 71], in0=B_(4), scalar=-1.0, in1=B_(7), op0=ALU.mult, op1=ALU.mult),
                        ["bt4", "bt7"], ["PwI"])

                    def cmul(o_sl, x_sl, y_sl, n):
                        t = [c_[:, :, 0:n] for c_ in cm]
                        xr, xi = PwR[:, :, x_sl], PwI[:, :, x_sl]
                        yr, yi = y_sl
                        dvp(lambda e: e.tensor_tensor(out=t[0], in0=xr, in1=yr, op=ALU.mult), ["PwR"], ["cm0"])
                        dvp(lambda e: e.tensor_tensor(out=t[1], in0=xi, in1=yi, op=ALU.mult), ["PwI"], ["cm1"])
                        dvp(lambda e: e.tensor_tensor(out=t[2], in0=xr, in1=yi, op=ALU.mult), ["PwR", "PwI"], ["cm2"])
                        dvp(lambda e: e.tensor_tensor(out=t[3], in0=xi, in1=yr, op=ALU.mult), ["PwR", "PwI"], ["cm3"])
                        dvp(lambda e: e.tensor_tensor(out=PwR[:, :, o_sl], in0=t[0], in1=t[1], op=ALU.subtract), ["cm0", "cm1"], ["PwR"])
                        dvp(lambda e: e.tensor_tensor(out=PwI[:, :, o_sl], in0=t[2], in1=t[3], op=ALU.add), ["cm2", "cm3"], ["PwI"])

                    def bc(kidx_, n):
                        return (bass.AP(PwR, kidx_, [[32 * 72, 128], [72, 32], [0, n]]), bass.AP(PwI, kidx_, [[32 * 72, 128], [72, 32], [0, n]]))
                    for k in range(2, 9):
                        cmul(slice(k - 1, k), slice(k - 2, k - 1), bc(0, 1), 1)
                    cmul(slice(8, 16), slice(0, 8), bc(7, 8), 8)
                    cmul(slice(16, 32), slice(0, 16), bc(15, 16), 16)
                    cmul(slice(32, 64), slice(0, 32), bc(31, 32), 32)
                    for k in range(2, 9):
                        cmul(slice(72 - k, 73 - k), slice(73 - k, 74 - k), bc(71, 1), 1)
                    S.op("vector", lambda e: e.tensor_copy(out=P8[:, 0, :, 1:9], in_=PwR[:, :, sl(7, 8, 8)]), reads=["PwR"], writes=["P8"])
                    S.op("vector", lambda e: e.tensor_copy(out=P8[:, 1, :, 1:9], in_=PwI[:, :, sl(7, 8, 8)]), reads=["PwI"], writes=["P8"])
                    S.op("vector", lambda e: e.tensor_scalar(out=P8[:, 2, :, 1:9], in0=PwR[:, :, sl(7, 8, 8)], scalar1=-1.0, scalar2=None,
                                                             op0=ALU.mult), reads=["PwR"], writes=["P8"])
                    S.op("vector", lambda e: e.tensor_scalar(out=P8[:, 3, :, 1:9], in0=PwI[:, :, sl(7, 8, 8)], scalar1=-1.0, scalar2=None,
                                                             op0=ALU.mult), reads=["PwI"], writes=["P8"])
                    lr, li = lrli[:, 0, :], lrli[:, 1, :]
                    ar, ai = PwR[:, :, 0], PwI[:, :, 0]
                    V = lambda i: zz[:, i, :]
                    def dv(fn, reads, writes):
                        S.op("vector", fn, reads=reads, writes=writes)
                    dv(lambda e: e.tensor_scalar(out=V(0), in0=ar, scalar1=-1.0, scalar2=None, op0=ALU.add), ["PwR"], ["zz"])
                    dv(lambda e: e.tensor_tensor(out=V(1), in0=lr, in1=lr, op=ALU.mult), ["lrli"], ["zz"])
                    dv(lambda e: e.tensor_tensor(out=V(2), in0=li, in1=li, op=ALU.mult), ["lrli"], ["zz"])
                    dv(lambda e: e.tensor_tensor(out=V(1), in0=V(1), in1=V(2), op=ALU.add), ["zz"], ["zz"])
                    dv(lambda e: e.reciprocal(out=V(6), in_=V(1)), ["zz"], ["zz"])
                    dv(lambda e: e.tensor_tensor(out=V(2), in0=V(0), in1=lr, op=ALU.mult), ["zz", "lrli"], ["zz"])
                    dv(lambda e: e.tensor_tensor(out=V(3), in0=ai, in1=li, op=ALU.mult), ["PwI", "lrli"], ["zz"])
                    dv(lambda e: e.tensor_tensor(out=V(2), in0=V(2), in1=V(3), op=ALU.add), ["zz"], ["zz"])
                    dv(lambda e: e.tensor_tensor(out=V(4), in0=V(2), in1=V(6), op=ALU.mult), ["zz"], ["zz"])
                    dv(lambda e: e.tensor_tensor(out=V(2), in0=ai, in1=lr, op=ALU.mult), ["PwI", "lrli"], ["zz"])
                    dv(lambda e: e.tensor_tensor(out=V(3), in0=V(0), in1=li, op=ALU.mult), ["zz", "lrli"], ["zz"])
                    dv(lambda e: e.tensor_tensor(out=V(2), in0=V(2), in1=V(3), op=ALU.subtract), ["zz"], ["zz"])
                    dv(lambda e: e.tensor_tensor(out=V(5), in0=V(2), in1=V(6), op=ALU.mult), ["zz"], ["zz"])
                    bc = lambda i: bass.AP(zz, i * 32, [[8 * 32, 128], [1, 32], [0, 16]])
                    dv(lambda e: e.tensor_tensor(out=tb1[:], in0=Braw[:, 0, :, :], in1=bc(4), op=ALU.mult), ["Braw", "zz"], ["tb1"])
                    dv(lambda e: e.tensor_tensor(out=tb2[:], in0=Braw[:, 1, :, :], in1=bc(5), op=ALU.mult), ["Braw", "zz"], ["tb2"])
                    dv(lambda e: e.tensor_tensor(out=Bb[:, 0, :, :], in0=tb1[:], in1=tb2[:], op=ALU.subtract), ["tb1", "tb2"], ["Bb"])
                    dv(lambda e: e.tensor_tensor(out=tb1[:], in0=Braw[:, 1, :, :], in1=bc(4), op=ALU.mult), ["Braw", "zz"], ["tb1"])
                    dv(lambda e: e.tensor_tensor(out=tb2[:], in0=Braw[:, 0, :, :], in1=bc(5), op=ALU.mult), ["Braw", "zz"], ["tb2"])
                    dv(lambda e: e.tensor_tensor(out=Bb[:, 1, :, :], in0=tb1[:], in1=tb2[:], op=ALU.add), ["tb1", "tb2"], ["Bb"])
                    def kidx(k):
                        return k - 1 if k >= 1 else 72 + k
                    t3 = pc0.enter_context(nc.sbuf_tensor("t3", [128, 16, 16], F32))
                    t4 = pc0.enter_context(nc.sbuf_tensor("t4", [128, 16, 16], F32))
                    for dr in range(2):
                        qs = slice(dr * 16, dr * 16 + 16)
                        for sidx in range(8):
                            tau = sidx if dr == 0 else 7 - sidx
                            for (dst, srcT, kk, sgn) in ((Xp, Bb, kidx(-tau - 1), None),):
                                pw_r = bass.AP(PwR, dr * 16 * 72 + kk, [[32 * 72, 128], [72, 16], [0, 16]])
                                pw_i = bass.AP(PwI, dr * 16 * 72 + kk, [[32 * 72, 128], [72, 16], [0, 16]])
                                o_re = dst[:, 0, qs, sidx * 16:(sidx + 1) * 16]
                                o_im = dst[:, 1, qs, sidx * 16:(sidx + 1) * 16]
                                sk = "Xp" if dst is Xp else "Y0"
                                tk = "Bb" if srcT is Bb else "CT"
                                dv(lambda e, srcT=srcT, pw_r=pw_r, qs=qs: e.tensor_tensor(out=t3[:], in0=srcT[:, 0, qs, :], in1=pw_r, op=ALU.mult), [tk, "PwR"], ["t3"])
                                dv(lambda e, srcT=srcT, pw_i=pw_i, qs=qs: e.tensor_tensor(out=t4[:], in0=srcT[:, 1, qs, :], in1=pw_i, op=ALU.mult), [tk, "PwI"], ["t4"])
                                dv(lambda e, o_re=o_re: e.tensor_tensor(out=o_re, in0=t3[:], in1=t4[:], op=ALU.subtract), ["t3", "t4"], [sk])
                                dv(lambda e, srcT=srcT, pw_i=pw_i, qs=qs: e.tensor_tensor(out=t3[:], in0=srcT[:, 0, qs, :], in1=pw_i, op=ALU.mult), [tk, "PwI"], ["t3"])
                                dv(lambda e, srcT=srcT, pw_r=pw_r, qs=qs: e.tensor_tensor(out=t4[:], in0=srcT[:, 1, qs, :], in1=pw_r, op=ALU.mult), [tk, "PwR"], ["t4"])
                                dv(lambda e, o_im=o_im: e.tensor_tensor(out=o_im, in0=t3[:], in1=t4[:], op=ALU.add), ["t3", "t4"], [sk])
                    S.op("gpsimd", lambda e: e.tensor_copy(out=Xpb[:].rearrange("p a q x -> p (a q x)"),
                                                           in_=Xp[:].rearrange("p a q x -> p (a q x)")), reads=["Xp"], writes=["Xpb"])
                    dv(lambda e: e.tensor_copy(out=Pw18[:, 0, :, :], in_=PwR[:, :, 0:8]), ["PwR"], ["Pw18"])
                    dv(lambda e: e.tensor_copy(out=Pw18[:, 1, :, :], in_=PwI[:, :, 0:8]), ["PwI"], ["Pw18"])
                    dv(lambda e: e.tensor_copy(out=A64[:, 0, 0, :], in_=P8[:, 0, :, 8]), ["P8"], ["A64"])
                    dv(lambda e: e.tensor_copy(out=A64[:, 0, 1, :], in_=P8[:, 0, :, 8]), ["P8"], ["A64"])
                    dv(lambda e: e.tensor_copy(out=A64[:, 1, 0, :], in_=P8[:, 3, :, 8]), ["P8"], ["A64"])
                    dv(lambda e: e.tensor_copy(out=A64[:, 1, 1, :], in_=P8[:, 1, :, 8]), ["P8"], ["A64"])
                    if debug:
                        S.dma("sync", lambda e: e.dma_start(out=dbg_s5.ap()[:, 0:2304], in_=PwR[:].rearrange("p q k -> p (q k)")), reads=["PwR"])
                        S.dma("sync", lambda e: e.dma_start(out=dbg_s5.ap()[:, 2304:4608], in_=PwI[:].rearrange("p q k -> p (q k)")), reads=["PwI"])
                        S.dma("sync", lambda e: e.dma_start(out=dbg_s5.ap()[:, 4608:4608 + 8192], in_=Xp[:].rearrange("p a q x -> p (a q x)")), reads=["Xp"])
                    S.barrier()
                G2 = sbc("G2", [128, 64, 2, 32])
                Hall = sbc("Hall", [128, 65, 2, 32])
                Hb = sbc("Hb", [128, 64, 2, 32], BF16)
                U_all = sbc("U_all", [128, 32, 512], BF16)
                Wtab = sbc("Wtab", [128, 64, 2, 32])
                flc = sbc("flc", [128, 6])
                S.dma("sync", lambda e: e.dma_start(out=flc[:], in_=flcol.ap()), writes=["flc"])
                wp_ = [sbc(f"wP{i}", [128, 2, 32]) for i in range(2)]
                wq_ = [sbc(f"wQ{i}", [128, 2, 32]) for i in range(2)]
                S.op("gpsimd", lambda e: e.memset(Wtab[:], 0.0), writes=["Wt_init"])
                S.op("gpsimd", lambda e: e.memset(Wtab[:, 63, 0, 0:16], 1.0), reads=["Wt_init"], writes=["Wt0"])
                S.op("gpsimd", lambda e: e.memset(Wtab[:, 0, 0, 16:32], 1.0), reads=["Wt_init"], writes=["Wt0"])
                WS = 64 * 64
                for st in range(63):
                    b2 = st % 2
                    cur = bass.AP(Wtab, (63 - st) * 64, [[WS, 128], [32, 2], [(2 * st - 63) * 64 + 16, 2], [1, 16]])
                    swp = bass.AP(Wtab, (63 - st) * 64 + 32, [[WS, 128], [-32, 2], [(2 * st - 63) * 64 + 16, 2], [1, 16]])
                    nxt = bass.AP(Wtab, (62 - st) * 64, [[WS, 128], [32, 2], [(2 * st - 61) * 64 + 16, 2], [1, 16]])
                    v4 = lambda t_: t_[:].rearrange("p c (d q) -> p c d q", d=2)
                    S.op("gpsimd", lambda e, cur=cur, b2=b2: e.tensor_tensor(out=v4(wp_[b2]), in0=cur, in1=v4(A64[:, 0, :, :]), op=ALU.mult)
                         if False else e.tensor_tensor(out=wp_[b2][:].rearrange("p c (d q) -> p c d q", d=2), in0=cur,
                                                       in1=A64[:, 0, :, :].rearrange("p c (d q) -> p c d q", d=2), op=ALU.mult),
                         reads=[f"Wt{st}", "A64"], writes=[f"wP{b2}"])
                    S.op("gpsimd", lambda e, swp=swp, b2=b2: e.tensor_tensor(
                        out=wq_[b2][:].rearrange("p c (d q) -> p c d q", d=2), in0=swp,
                        in1=A64[:, 1, :, :].rearrange("p c (d q) -> p c d q", d=2), op=ALU.mult),
                        reads=[f"Wt{st}", "A64"], writes=[f"wQ{b2}"])
                    S.op("gpsimd", lambda e, nxt=nxt, b2=b2: e.tensor_tensor(
                        out=nxt, in0=wp_[b2][:].rearrange("p c (d q) -> p c d q", d=2),
                        in1=wq_[b2][:].rearrange("p c (d q) -> p c d q", d=2), op=ALU.add),
                        reads=[f"wP{b2}", f"wQ{b2}"], writes=[f"Wt{st + 1}"])
                pcu = ExitStack()
                Uraw = [pcu.enter_context(nc.sbuf_tensor("Uraw0", [128, 8, 512], BF16))] * 2
                for gq in range(4):
                    ur = Uraw[0]; urk = "Uraw0"
                    S.dma("sync", lambda e, ur=ur, gq=gq: e.dma_start(out=ur[:], in_=Us.ap()[0, gq * 8:(gq + 1) * 8].rearrange("g p n -> p g n")),
                          reads=["Us"], writes=[urk])
                    S.op("vector", lambda e, ur=ur, gq=gq: e.tensor_copy(
                        out=U_all[:, gq * 8:(gq + 1) * 8, :].rearrange("p g (m n) -> p g m n", m=8),
                        in_=ur[:].rearrange("p g (n m) -> p g m n", m=8)), reads=[urk], writes=["U_all"])

                def rot_batch(src, sk, q, items, neg_im):
                    for ki, key_out, dstfn, (tr, ti), tkey in items:
                        cr = P8[:, 0, q, ki:ki + 1]; ci = P8[:, 1, q, ki:ki + 1]; nci = P8[:, 3, q, ki:ki + 1]
                        S.op("scalar", lambda e, tr=tr, cr=cr: e.activation(out=tr[:], in_=src[:, 0, q, :], func=AF.Copy, scale=cr),
                             reads=[sk, "P8"], writes=[tkey + "r"])
                        S.op("scalar", lambda e, ti=ti, ci=ci, nci=nci: e.activation(out=ti[:], in_=src[:, 0, q, :], func=AF.Copy,
                                                                                 scale=(nci if neg_im else ci)),
                             reads=[sk, "P8"], writes=[tkey + "i"])
                    for ki, key_out, dstfn, (tr, ti), tkey in items:
                        cr = P8[:, 0, q, ki:ki + 1]; ncr = P8[:, 2, q, ki:ki + 1]; nci = P8[:, 3, q, ki:ki + 1]
                        S.op("vector", lambda e, tr=tr, nci=nci, dstfn=dstfn: e.scalar_tensor_tensor(
                            out=dstfn(0), in0=src[:, 1, q, :], scalar=nci, in1=tr[:], op0=ALU.mult, op1=ALU.add),
                            reads=[sk, "P8", tkey + "r"], writes=[key_out])
                        S.op("vector", lambda e, ti=ti, cr=cr, ncr=ncr, dstfn=dstfn: e.scalar_tensor_tensor(
                            out=dstfn(1), in0=src[:, 1, q, :], scalar=(ncr if neg_im else cr), in1=ti[:], op0=ALU.mult, op1=ALU.add),
                            reads=[sk, "P8", tkey + "i"], writes=[key_out])

                def rot_tables(src, sk, q, ki, outs, neg_im, key_out, dstfn, tkey):
                    cr = P8[:, 0, q, ki:ki + 1]; ci = P8[:, 1, q, ki:ki + 1]
                    ncr = P8[:, 2, q, ki:ki + 1]; nci = P8[:, 3, q, ki:ki + 1]
                    tr, ti = outs
                    S.op("scalar", lambda e: e.activation(out=tr[:], in_=src[:, 0, q, :], func=AF.Copy, scale=cr),
                         reads=[sk, "P8"], writes=[tkey + "r"])
                    S.op("vector", lambda e: e.scalar_tensor_tensor(out=dstfn(0), in0=src[:, 1, q, :], scalar=nci, in1=tr[:],
                                                                    op0=ALU.mult, op1=ALU.add),
                         reads=[sk, "P8", tkey + "r"], writes=[key_out])
                    S.op("scalar", lambda e: e.activation(out=ti[:], in_=src[:, 0, q, :], func=AF.Copy,
                                                          scale=(nci if neg_im else ci)),
                         reads=[sk, "P8"], writes=[tkey + "i"])
                    S.op("vector", lambda e: e.scalar_tensor_tensor(out=dstfn(1), in0=src[:, 1, q, :],
                                                                    scalar=(ncr if neg_im else cr), in1=ti[:],
                                                                    op0=ALU.mult, op1=ALU.add),
                         reads=[sk, "P8", tkey + "i"], writes=[key_out])

                with ExitStack() as pc1:
                    XE = [pc1.enter_context(nc.sbuf_tensor(f"XE{i}", [128, 8, 2, 128], BF16)) for i in range(2)]
                    ET = [pc1.enter_context(nc.sbuf_tensor(f"ET{i}", [128, 8, 2, 128], BF16)) for i in range(2)]
                    trt = [[pc1.enter_context(nc.sbuf_tensor(f"trt{i}{j}", [128, 128], BF16)) for j in range(2)] for i in range(6)]
                    tr_ps = [pc1.enter_context(nc.psum_tensor(f"ps_tr{i}", [128, 2, 2, 128], BF16)) for i in range(2)]
                    g_ps = [pc1.enter_context(nc.psum_tensor(f"ps_g{i}", [128, 2, 4, 64], F32)) for i in range(2)]
                    Ucat = [pc1.enter_context(nc.sbuf_tensor(f"Ucat{i}", [128, 4, 2, 512], BF16)) for i in range(2)]
                    Ucr = Uraw[0][:].rearrange("p g n -> p (g n)").rearrange("p (s g n) -> p s g n", s=4, g=2)
                    tn = 0
                    trn = 0
                    for qq in range(16):
                        ub = qq % 2
                        for sg_ in range(4):
                            S.dma("sync", lambda e, sg_=sg_, qq=qq: e.dma_start(
                                out=Ucr[:, sg_, :, :], in_=Us.ap()[sg_, 2 * qq:2 * qq + 2].rearrange("g p n -> p g n")),
                                reads=["Us"], writes=["Uraw0"])
                        for sg_ in range(4):
                            if sg_ % 2 == 0:
                                S.op("vector", lambda e, ub=ub, sg_=sg_: e.tensor_copy(
                                    out=Ucat[ub][:, sg_, :, :].rearrange("p g (m n) -> p g m n", m=8),
                                    in_=Ucr[:, sg_, :, :].rearrange("p g (n m) -> p g m n", m=8)), reads=["Uraw0"], writes=[f"Ucat{ub}"])
                            else:
                                for g2_ in range(2):
                                    S.op("scalar", lambda e, ub=ub, sg_=sg_, g2_=g2_: e.activation(
                                        out=Ucat[ub][:, sg_, g2_, :].rearrange("p (m n) -> p m n", m=8),
                                        in_=Ucr[:, sg_, g2_, :].rearrange("p (n m) -> p m n", m=8), func=AF.Copy),
                                        reads=["Uraw0"], writes=[f"Ucat{ub}"])
                        for dr in range(2):
                            q = dr * 16 + qq
                            xb = q % 2 if False else dr
                            for m4 in range(0, 8, 4):
                                items = []
                                for m in range(m4, m4 + 4):
                                    mu = m if dr == 0 else 7 - m
                                    outs = trt[tn % 6]; tk_ = f"trt{tn % 6}"; tn += 1
                                    items.append((8 - mu, f"XE{xb}_{m}", (lambda comp, xb=xb, m=m: XE[xb][:, m, comp, :]), outs, tk_))
                                rot_batch(Xpb, "Xpb", q, items, False)
                            for m2 in range(0, 8, 2):
                                tp = tr_ps[trn % 2]; tpk = f"ps_tr{trn % 2}"; trn += 1
                                for mm in range(2):
                                    for comp in range(2):
                                        S.op("tensor", lambda e, tp=tp, xb=xb, m=m2 + mm, mm=mm, comp=comp: e.transpose(
                                            out=tp[:, mm, comp, :], in_=XE[xb][:, m, comp, :], identity=ident_b[:]),
                                            reads=[f"XE{xb}_{m2 + mm}", "ident_b"], writes=[tpk])
                                S.op("vector", lambda e, tp=tp, xb=xb, m2=m2: e.tensor_copy(
                                    out=ET[xb][:, m2:m2 + 2, :, :], in_=tp[:]), reads=[tpk], writes=[f"ET{xb}"])
                            gp = g_ps[dr]; gpk = f"ps_g{dr}"
                            for j2 in range(2):
                                for comp in range(2):
                                    for m in range(8):
                                        S.op("tensor", lambda e, gp=gp, xb=xb, j2=j2, comp=comp, m=m, ub=ub: e.matmul(
                                            out=gp[64 * j2:64 * j2 + 64, comp, :, :], lhsT=ET[xb][:, m, comp, 64 * j2:64 * j2 + 64],
                                            rhs=Ucat[ub][:, :, j2, m * 64:(m + 1) * 64], start=(m == 0), stop=(m == 7)),
                                            reads=[f"ET{xb}", f"Ucat{ub}"], writes=[gpk])
                            S.op("scalar", lambda e, gp=gp, q=q: e.activation(
                                out=G2[:, :, :, q].rearrange("p n c -> p c n"), in_=gp[:, :, 0, :], func=AF.Copy),
                                reads=[gpk], writes=["G2"])
                            S.op("scalar", lambda e, gp=gp, q=q: e.activation(
                                out=G2o[:, :, :, :, q].rearrange("p s n c -> p c s n"), in_=gp[:, :, 1:4, :], func=AF.Copy),
                                reads=[gpk], writes=["G2o"])
                    S.barrier()
                pcu.close()
                cacc = sbc("cacc", [128, 2, 32])
                with ExitStack() as pcc:
                    def sbx(name, shape, dt=F32):
                        return pcc.enter_context(nc.sbuf_tensor(name, list(shape), dt))
                    T1 = sbx("cT1", [128, 64, 32], BF16); T2 = sbx("cT2", [128, 64, 32], BF16)
                    Wtb = sbx("Wtb", [128, 64, 2, 32], BF16)
                    S.op("vector", lambda e: e.tensor_copy(out=Wtb[:].rearrange("p n c q -> p (n c q)"),
                                                           in_=Wtab[:].rearrange("p n c q -> p (n c q)")),
                         reads=[f"Wt{i}" for i in range(64)], writes=["Wtb"])
                    Sall = sbx("Sall", [128, 3, 2, 32])
                    Asq = [sbx(f"Asq{i}", [128, 2, 32]) for i in range(2)]
                    ctm = [sbx(f"ctm{i}", [128, 32]) for i in range(4)]
                    ctt = sbx("ctt", [128, 2, 32])
                    wt_keys = [f"Wt{i}" for i in range(64)]
                    Wc = lambda c: Wtb[:, :, c, :]
                    for i in range(3):
                        Gc = lambda c, i=i: G2o[:, i, :, c, :]
                        for comp, (wa, ga, wb_, gb_, op) in enumerate(((0, 0, 1, 1, ALU.subtract), (0, 1, 1, 0, ALU.add))):
                            S.op("vector", lambda e, wa=wa, ga=ga, Gc=Gc: e.tensor_tensor(out=T1[:], in0=Wc(wa), in1=Gc(ga), op=ALU.mult),
                                 reads=["Wtb", "G2o"], writes=["cT1"])
                            S.op("vector", lambda e, wb_=wb_, gb_=gb_, Gc=Gc: e.tensor_tensor(out=T2[:], in0=Wc(wb_), in1=Gc(gb_), op=ALU.mult),
                                 reads=["Wtb", "G2o"], writes=["cT2"])
                            S.op("vector", lambda e, op=op: e.tensor_tensor(out=T1[:], in0=T1[:], in1=T2[:], op=op),
                                 reads=["cT1", "cT2"], writes=["cT1"])
                            S.op("vector", lambda e, i=i, comp=comp: e.tensor_reduce(
                                out=Sall[:, i, comp, :], in_=T1[:].rearrange("p n q -> p q n"), axis=AX.X, op=ALU.add),
                                reads=["cT1"], writes=["Sall"])
                    S.op("vector", lambda e: e.tensor_copy(out=Asq[0][:, 0, :], in_=A64[:, 0, 0, :]), reads=["A64"], writes=["Asq0"])
                    S.op("vector", lambda e: e.tensor_copy(out=Asq[0][:, 1, :], in_=A64[:, 1, 1, :]), reads=["A64"], writes=["Asq0"])
                    for k in range(6):
                        a_, b_ = Asq[k % 2], Asq[(k + 1) % 2]
                        ak, bk = f"Asq{k % 2}", f"Asq{(k + 1) % 2}"
                        S.op("vector", lambda e, a_=a_: e.tensor_tensor(out=ctm[0][:], in0=a_[:, 0, :], in1=a_[:, 0, :], op=ALU.mult), reads=[ak], writes=["ctm0"])
                        S.op("vector", lambda e, a_=a_: e.tensor_tensor(out=ctm[1][:], in0=a_[:, 1, :], in1=a_[:, 1, :], op=ALU.mult), reads=[ak], writes=["ctm1"])
                        S.op("vector", lambda e, b_=b_: e.tensor_tensor(out=b_[:, 0, :], in0=ctm[0][:], in1=ctm[1][:], op=ALU.subtract), reads=["ctm0", "ctm1"], writes=[bk])
                        S.op("vector", lambda e, a_=a_: e.tensor_tensor(out=ctm[2][:], in0=a_[:, 0, :], in1=a_[:, 1, :], op=ALU.mult), reads=[ak], writes=["ctm2"])
                        S.op("vector", lambda e, b_=b_: e.tensor_scalar(out=b_[:, 1, :], in0=ctm[2][:], scalar1=2.0, scalar2=None, op0=ALU.mult), reads=["ctm2"], writes=[bk])
                    A4k = Asq[0]; A4kk = "Asq0"
                    S.op("vector", lambda e: e.memset(cacc[:], 0.0), writes=["cacc"])
                    for half, order, fbase in ((slice(0, 16), (0, 1, 2), 0), (slice(16, 32), (2, 1, 0), 3)):
                        for i in order:
                            S.op("vector", lambda e, half=half: e.tensor_tensor(out=ctm[0][:, half], in0=A4k[:, 0, half], in1=cacc[:, 0, half], op=ALU.mult), reads=[A4kk, "cacc"], writes=["ctm0"])
                            S.op("vector", lambda e, half=half: e.tensor_tensor(out=ctm[1][:, half], in0=A4k[:, 1, half], in1=cacc[:, 1, half], op=ALU.mult), reads=[A4kk, "cacc"], writes=["ctm1"])
                            S.op("vector", lambda e, half=half: e.tensor_tensor(out=ctm[2][:, half], in0=A4k[:, 0, half], in1=cacc[:, 1, half], op=ALU.mult), reads=[A4kk, "cacc"], writes=["ctm2"])
                            S.op("vector", lambda e, half=half: e.tensor_tensor(out=ctm[3][:, half], in0=A4k[:, 1, half], in1=cacc[:, 0, half], op=ALU.mult), reads=[A4kk, "cacc"], writes=["ctm3"])
                            S.op("vector", lambda e, half=half: e.tensor_tensor(out=ctt[:, 0, half], in0=ctm[0][:, half], in1=ctm[1][:, half], op=ALU.subtract), reads=["ctm0", "ctm1"], writes=["ctt"])
                            S.op("vector", lambda e, half=half: e.tensor_tensor(out=ctt[:, 1, half], in0=ctm[2][:, half], in1=ctm[3][:, half], op=ALU.add), reads=["ctm2", "ctm3"], writes=["ctt"])
                            S.op("vector", lambda e, half=half, i=i: e.tensor_tensor(out=ctt[:, :, half], in0=ctt[:, :, half], in1=Sall[:, i, :, half], op=ALU.add), reads=["ctt", "Sall"], writes=["ctt"])
                            S.op("vector", lambda e, half=half: e.tensor_tensor(out=ctt[:, :, half], in0=ctt[:, :, half], in1=cacc[:, :, half], op=ALU.subtract), reads=["ctt", "cacc"], writes=["ctt"])
                            S.op("vector", lambda e, half=half, i=i, fbase=fbase: e.scalar_tensor_tensor(
                                out=cacc[:, :, half], in0=ctt[:, :, half], scalar=flc[:, fbase + i:fbase + i + 1], in1=cacc[:, :, half],
                                op0=ALU.mult, op1=ALU.add), reads=["ctt", "cacc", "flc"], writes=["cacc"])
                    S.barrier()
                with ExitStack() as pcy:
                    t3y_ = pcy.enter_context(nc.sbuf_tensor("t3y", [128, 16, 16], F32))
                    t4y_ = pcy.enter_context(nc.sbuf_tensor("t4y", [128, 16, 16], F32))
                    def dvy(fn, reads, writes):
                        S.op("vector", fn, reads=reads, writes=writes)
                    for dr in range(2):
                        qs = slice(dr * 16, dr * 16 + 16)
                        for sidx in range(8):
                            tau = sidx if dr == 0 else 7 - sidx
                            pw_r = bass.AP(Pw18, (0 * 32 + dr * 16) * 8 + tau, [[2 * 32 * 8, 128], [8, 16], [0, 16]])
                            pw_i = bass.AP(Pw18, (1 * 32 + dr * 16) * 8 + tau, [[2 * 32 * 8, 128], [8, 16], [0, 16]])
                            o_re = Y0[:, 0, qs, sidx * 16:(sidx + 1) * 16]
                            o_im = Y0[:, 1, qs, sidx * 16:(sidx + 1) * 16]
                            dvy(lambda e, pw_r=pw_r, qs=qs: e.tensor_tensor(out=t3y_[:], in0=CT[:, 0, qs, :], in1=pw_r, op=ALU.mult), ["CT", "Pw18"], ["t3y"])
                            dvy(lambda e, pw_i=pw_i, qs=qs: e.tensor_tensor(out=t4y_[:], in0=CT[:, 1, qs, :], in1=pw_i, op=ALU.mult), ["CT", "Pw18"], ["t4y"])
                            dvy(lambda e, o_re=o_re: e.tensor_tensor(out=o_re, in0=t3y_[:], in1=t4y_[:], op=ALU.subtract), ["t3y", "t4y"], ["Y0"])
                            dvy(lambda e, pw_i=pw_i, qs=qs: e.tensor_tensor(out=t3y_[:], in0=CT[:, 0, qs, :], in1=pw_i, op=ALU.mult), ["CT", "Pw18"], ["t3y"])
                            dvy(lambda e, pw_r=pw_r, qs=qs: e.tensor_tensor(out=t4y_[:], in0=CT[:, 1, qs, :], in1=pw_r, op=ALU.mult), ["CT", "Pw18"], ["t4y"])
                            dvy(lambda e, o_im=o_im: e.tensor_tensor(out=o_im, in0=t3y_[:], in1=t4y_[:], op=ALU.add), ["t3y", "t4y"], ["Y0"])
                with ExitStack() as pc2:
                    hp_ = [pc2.enter_context(nc.sbuf_tensor(f"hP{i}", [128, 2, 32], F32)) for i in range(2)]
                    hq_ = [pc2.enter_context(nc.sbuf_tensor(f"hQ{i}", [128, 2, 32], F32)) for i in range(2)]
                    S.op("vector", lambda e: e.tensor_copy(out=Hall[:, 0, :, :], in_=cacc[:]), reads=["cacc"], writes=["H0"])
                    for st in range(64):
                        b2 = st % 2
                        hcur = Hall[:, st, :, :]
                        hswap = bass.AP(Hall, st * 64 + 32, [[65 * 64, 128], [-32, 2], [1, 32]])
                        gcat = bass.AP(G2, st * 64, [[64 * 64, 128], [32, 2], [(63 - 2 * st) * 64 + 16, 2], [1, 16]])
                        S.op("vector", lambda e, hcur=hcur, b2=b2: e.tensor_tensor(out=hp_[b2][:], in0=hcur, in1=A64[:, 0, :, :], op=ALU.mult),
                             reads=[f"H{st}", "A64"], writes=[f"hP{b2}"])
                        S.op("vector", lambda e, hswap=hswap, b2=b2: e.tensor_tensor(out=hq_[b2][:], in0=hswap, in1=A64[:, 1, :, :], op=ALU.mult),
                             reads=[f"H{st}", "A64"], writes=[f"hQ{b2}"])
                        S.op("vector", lambda e, b2=b2: e.tensor_tensor(out=hp_[b2][:], in0=hp_[b2][:], in1=hq_[b2][:], op=ALU.add),
                             reads=[f"hP{b2}", f"hQ{b2}"], writes=[f"hP{b2}"])
                        S.op("vector", lambda e, b2=b2, gcat=gcat, st=st: e.tensor_tensor(
                            out=Hall[:, st + 1, :, :].rearrange("p c (d q) -> p c d q", d=2), in0=hp_[b2][:].rearrange("p c (d q) -> p c d q", d=2),
                            in1=gcat, op=ALU.add), reads=[f"hP{b2}", "G2"], writes=[f"H{st + 1}"])
                    S.op("vector", lambda e: e.tensor_copy(out=Hb[:].rearrange("p n c q -> p (n c q)"),
                                                           in_=Hall[:, 0:64, :, :].rearrange("p n c q -> p (n c q)")),
                         reads=[f"H{i}" for i in range(65)], writes=["Hb"])
                    S.barrier()
                if debug:
                    S.dma("sync", lambda e: e.dma_start(out=dbg_s5.ap()[:, 20992:20992 + 4096], in_=G2[:].rearrange("p n c q -> p (n c q)")), reads=["G2"])
                    S.dma("sync", lambda e: e.dma_start(out=dbg_s5.ap()[:, 25088:25088 + 4160], in_=Hall[:].rearrange("p n c q -> p (n c q)")), reads=["Hb"])
                with ExitStack() as pc3:
                    YT = [[pc3.enter_context(nc.sbuf_tensor(f"YT{d_}{i}", [128, 8, 2, 128], BF16)) for i in range(2)] for d_ in range(2)]
                    trt = [[pc3.enter_context(nc.sbuf_tensor(f"trs{i}{j}", [128, 128], F32)) for j in range(2)] for i in range(7)]
                    Tsb = [[[pc3.enter_context(nc.sbuf_tensor(f"T{d_}{j2}{i}", [128, 8, 128], BF16)) for i in range(2)]
                            for j2 in range(2)] for d_ in range(2)]
                    ysb = [pc3.enter_context(nc.sbuf_tensor(f"ysb{i}", [128, 512], F32)) for i in range(2)]
                    t_ps = [pc3.enter_context(nc.psum_tensor(f"ps_T{i}", [128, 1024], F32)) for i in range(2)]
                    y_ps = [pc3.enter_context(nc.psum_tensor(f"ps_y{i}", [128, 512], F32)) for i in range(2)]
                    tn = 0; tpn = 0; yn = 0
                    for qq in range(16):
                        pb_ = qq % 2
                        for dr in range(2):
                            q = dr * 16 + qq
                            items = []
                            for m in range(8):
                                mu = m if dr == 0 else 7 - m
                                outs = trt[tn % 7]; tk_ = f"trs{tn % 7}"; tn += 1
                                yb = YT[dr][pb_]
                                if mu == 0:
                                    S.op("scalar", lambda e, yb=yb, m=m, q=q: e.activation(out=yb[:, m, 0, :], in_=Y0[:, 0, q, :], func=AF.Copy),
                                         reads=["Y0"], writes=[f"YT{dr}{pb_}_{m}"])
                                    S.op("vector", lambda e, yb=yb, m=m, q=q: e.tensor_scalar(out=yb[:, m, 1, :], in0=Y0[:, 1, q, :], scalar1=-1.0,
                                                                                            scalar2=None, op0=ALU.mult),
                                         reads=["Y0"], writes=[f"YT{dr}{pb_}_{m}"])
                                else:
                                    items.append((mu, f"YT{dr}{pb_}_{m}", (lambda comp, yb=yb, m=m: yb[:, m, comp, :]), outs, tk_))
                            rot_batch(Y0, "Y0", q, items, True)
                            for j2 in range(2):
                                tp = t_ps[tpn % 2]; tpk = f"ps_T{tpn % 2}"; tpn += 1
                                hs = slice(64 * j2, 64 * j2 + 64)
                                for half in range(2):
                                    for comp in range(2):
                                        S.op("tensor", lambda e, tp=tp, hs=hs, half=half, comp=comp, q=q, yb=yb: e.matmul(
                                            out=tp[:, half * 512:(half + 1) * 512].rearrange("p (m x) -> p m x", m=4),
                                            lhsT=Xpb[hs, comp, q, :], rhs=yb[hs, half * 4:(half + 1) * 4, comp, :],
                                            start=(comp == 0), stop=(comp == 1)),
                                            reads=["Xpb"] + [f"YT{dr}{pb_}_{m}" for m in range(8)], writes=[tpk])
                                tsb = Tsb[dr][j2][pb_]; tsk = f"T{dr}{j2}{pb_}"
                                m0 = 0 if dr == 0 else 7
                                S.op("scalar", lambda e, tp=tp, tsb=tsb: e.activation(
                                    out=tsb[:].rearrange("p m x -> p (m x)"), in_=tp[:], func=AF.Copy), reads=[tpk], writes=[tsk])
                                S.op("vector", lambda e, tp=tp, tsb=tsb, m0=m0, dr=dr: e.tensor_tensor(
                                    out=tsb[:, m0, :], in0=tp[:, m0 * 128:(m0 + 1) * 128], in1=maskfb[:, dr, :], op=ALU.mult),
                                    reads=[tpk, "maskfb"], writes=[tsk])
                        for j2 in range(2):
                            g = qq * 2 + j2
                            hs = slice(64 * j2, 64 * j2 + 64)
                            yp = y_ps[yn % 2]; ypk = f"ps_y{yn % 2}"; yb_ = ysb[yn % 2]; ybk = f"ysb{yn % 2}"; yn += 1
                            u1 = U_all[:, g, :]
                            first = True
                            for dr in range(2):
                                tsb = Tsb[dr][j2][pb_]; tsk = f"T{dr}{j2}{pb_}"
                                for dl in range(8):
                                    blk = dl if dr == 0 else 7 - dl
                                    if dr == 0:
                                        o_ap, r_ap = yp[:, dl * 64:512], u1[:, 0:(8 - dl) * 64]
                                    else:
                                        o_ap, r_ap = yp[:, 0:(8 - dl) * 64], u1[:, dl * 64:512]
                                    S.op("tensor", lambda e, o_ap=o_ap, r_ap=r_ap, tsb=tsb, blk=blk, first=first: e.matmul(
                                        out=o_ap, lhsT=tsb[:, blk, :], rhs=r_ap, start=first, stop=False, skip_group_check=True),
                                        reads=[tsk, "U_all"], writes=[ypk])
                                    first = False
                            for dr in range(2):
                                q = dr * 16 + qq
                                yb = YT[dr][pb_]
                                for m in range(8):
                                    for comp in range(2):
                                        if dr == 0:
                                            rhs = Hb[hs, :, comp, q]
                                        else:
                                            rhs = bass.AP(Hb, 64 * j2 * (64 * 64) + 63 * 64 + comp * 32 + q, [[64 * 64, 64], [-64, 64]])
                                        last = (dr == 1 and m == 7 and comp == 1)
                                        S.op("tensor", lambda e, yp=yp, m=m, hs=hs, yb=yb, comp=comp, rhs=rhs, last=last: e.matmul(
                                            out=yp[:, m * 64:(m + 1) * 64], lhsT=yb[hs, m, comp, :], rhs=rhs, start=False, stop=last,
                                            skip_group_check=True),
                                            reads=[f"YT{dr}{pb_}_{m}", "Hb"], writes=[ypk])
                            S.op("scalar", lambda e, yp=yp, yb_=yb_: e.activation(
                                out=yb_[:].rearrange("p (n m) -> p m n", m=8), in_=yp[:].rearrange("p (m n) -> p m n", m=8), func=AF.Copy),
                                reads=[ypk], writes=[ybk])
                            for tq in range(8):
                                S.dma("sync", lambda e, g=g, yb_=yb_, tq=tq: e.dma_start(
                                    out=y_s.ap()[16 * g:16 * g + 16, tq, :], in_=yb_[tq * 16:(tq + 1) * 16, :]),
                                    reads=[ybk], writes=["y_s"])

        if stage >= 4:
            S.barrier()
            with ExitStack() as pd:
                def sbd(name, shape, dt=F32):
                    return pd.enter_context(nc.sbuf_tensor(name, list(shape), dt))

                def psd(name, shape, dt=F32):
                    return pd.enter_context(nc.psum_tensor(name, list(shape), dt))
                udd = sbd("udD", [128, 4, 8, 512], BF16)
                yall = sbd("yall", [128, 4, 8, 512])
                wglu_b = sbd("wglu_b", [128, 4, 512], BF16)
                y2 = [sbd(f"y2{i}", [128, 8, 64]) for i in range(4)]
                g32 = [sbd(f"g32{i}", [128, 4, 512]) for i in range(2)]
                gb = [sbd(f"gb{i}", [128, 4, 512], BF16) for i in range(2)]
                sg = [sbd(f"sg{i}", [128, 512]) for i in range(4)]
                o32 = [sbd(f"o32{i}", [128, 4, 512]) for i in range(2)]
                sq = [sbd(f"sq{i}", [128, 512]) for i in range(4)]
                rs = [sbd(f"rs{i}", [128, 512]) for i in range(2)]
                sn = [sbd(f"sn{i}", [128, 4, 512], BF16) for i in range(2)]
                z_ps = [psd(f"ps_z{i}", [128, 512]) for i in range(4)]
                ss_ps = [psd(f"ps_ss{i}", [128, 512]) for i in range(2)]
                S.dma("gpsimd", lambda e: e.dma_start(out=wglu_b[:], in_=w_glu.ap().rearrange("(kc p) n -> p kc n", p=128)),
                      writes=["wglu_b"])
                for cc in range(4):
                    for gl in range(8):
                        S.dma("sync", lambda e, cc=cc, gl=gl: e.dma_start(
                            out=udd[gl * 16:(gl + 1) * 16, cc, :, :],
                            in_=Us.ap()[0, cc * 8 + gl].rearrange("(s c) n -> c s n", c=16)), reads=["Us"], writes=["udD"])
                    S.dma("sync", lambda e, cc=cc: e.dma_start(out=yall[:, cc, :, :], in_=y_s.ap()[cc * 128:(cc + 1) * 128]),
                          reads=["y_s"], writes=[f"yall{cc}"])
                yn = 0
                def d_part(tt, part):
                    nonlocal yn
                    b = tt % 2
                    nsl = slice(tt * 64, tt * 64 + 64)
                    if part == 1:
                        for cc in range(4):
                            S.op("vector", lambda e, cc=cc, nsl=nsl: e.scalar_tensor_tensor(
                                out=y2[cc][:], in0=udd[:, cc, :, nsl], scalar=cols[:, C_DS + cc:C_DS + cc + 1], in1=yall[:, cc, :, nsl],
                                op0=ALU.mult, op1=ALU.add), reads=["udD", f"yall{cc}", "cols"], writes=[f"y2{cc}"])
                        for cc in range(4):
                            S.op("scalar", lambda e, cc=cc, b=b: e.activation(
                                out=g32[b][:, cc, :].rearrange("p (n s) -> p s n", s=8), in_=y2[cc][:], func=AF.Gelu_apprx_tanh),
                                reads=[f"y2{cc}"], writes=[f"g32{b}_{cc}"])
                        for cc in range(4):
                            S.op("vector", lambda e, cc=cc, b=b: e.tensor_copy(out=gb[b][:, cc, :], in_=g32[b][:, cc, :]),
                                 reads=[f"g32{b}_{cc}"], writes=[f"gb{b}_{cc}"])
                        return
                    sp = ss_ps[b]; spk = f"ps_ss{b}"
                    for jc in range(4):
                        zp = z_ps[jc]; zpk = f"ps_z{jc}"
                        for kc in range(4):
                            S.op("tensor", lambda e, zp=zp, kc=kc, jc=jc, b=b: e.matmul(
                                out=zp[:], lhsT=wglu_b[:, kc, jc * 128:(jc + 1) * 128], rhs=gb[b][:, kc, :],
                                start=(kc == 0), stop=(kc == 3)), reads=["wglu_b", f"gb{b}_{kc}"], writes=[zpk])
                    for jc in range(4):
                        S.op("scalar", lambda e, jc=jc: e.activation(
                            out=sg[jc][:], in_=z_ps[jc][:], func=AF.Sigmoid, bias=cols[:, C_BG + jc:C_BG + jc + 1]),
                            reads=[f"ps_z{jc}", "cols"], writes=[f"sg{jc}"])
                    for jc in range(4):
                        S.op("vector", lambda e, jc=jc, b=b: e.tensor_tensor(
                            out=o32[b][:, jc, :], in0=g32[b][:, jc, :], in1=sg[jc][:], op=ALU.mult),
                            reads=[f"g32{b}_{jc}", f"sg{jc}"], writes=[f"o32{b}_{jc}"])
                    for jc in range(4):
                        S.op("scalar", lambda e, jc=jc, b=b: e.activation(out=sq[jc][:], in_=o32[b][:, jc, :], func=AF.Square),
                             reads=[f"o32{b}_{jc}"], writes=[f"sq{jc}"])
                    for jc in range(4):
                        S.op("tensor", lambda e, sp=sp, jc=jc: e.matmul(out=sp[:], lhsT=ones_f[:], rhs=sq[jc][:],
                                                                       start=(jc == 0), stop=(jc == 3)),
                             reads=["ones_f", f"sq{jc}"], writes=[spk])
                    S.op("vector", lambda e, sp=sp, b=b: e.tensor_scalar(out=rs[b][:], in0=sp[:], scalar1=1.0 / 512, scalar2=EPS,
                                                                         op0=ALU.mult, op1=ALU.add), reads=[spk], writes=[f"rs{b}"])
                    S.op("scalar", lambda e, b=b: e.activation(out=rs[b][:], in_=rs[b][:], func=AF.Sqrt),
                         reads=[f"rs{b}"], writes=[f"rs{b}"])
                    S.op("vector", lambda e, b=b: e.reciprocal(out=rs[b][:], in_=rs[b][:]), reads=[f"rs{b}"], writes=[f"rs{b}"])
                    for jc in range(4):
                        S.op("vector", lambda e, jc=jc, b=b: e.scalar_tensor_tensor(
                            out=sn[b][:, jc, :], in0=o32[b][:, jc, :], scalar=cols[:, C_SG + jc:C_SG + jc + 1], in1=rs[b][:],
                            op0=ALU.mult, op1=ALU.mult), reads=[f"o32{b}_{jc}", f"rs{b}", "cols"], writes=[f"sn{b}"])
                    S.dma("sync", lambda e, b=b, tt=tt: e.dma_start(
                        out=ssmn_s.ap()[:, tt * 512:(tt + 1) * 512].rearrange("(jc p) n -> p jc n", p=128), in_=sn[b][:]),
                        reads=[f"sn{b}"], writes=["ssmn_s"])
                for tt in range(9):
                    if tt < 8:
                        d_part(tt, 1)
                    if tt >= 1:
                        d_part(tt - 1, 2)

        if stage >= 5:
            S.barrier()
            with ExitStack() as pp_:
                def sbp(name, shape, dt=F32):
                    return pp_.enter_context(nc.sbuf_tensor(name, list(shape), dt))

                def psp(name, shape, dt=F32):
                    return pp_.enter_context(nc.psum_tensor(name, list(shape), dt))
                wo_a = sbp("wo_a", [128, 4, D], BF16)
                wo_s = sbp("wo_s", [128, 4, D], BF16)
                w2_b = sbp("w2_b", [128, NFC, D], BF16)
                agT = sbp("agT", [64, 8])
                xq = [sbp(f"xp{i}", [128, 4, D]) for i in range(2)]
                at = [sbp("at0", [128, 4, 512], BF16)] * 2
                st_ = [sbp(f"st{i}", [128, 4, 512], BF16) for i in range(2)]
                asq = [sbp(f"asq{i}", [128, 512]) for i in range(2)]
                ars = sbp("ars", [128, 512])
                junq = sbp("junkp", [128, D], BF16)
                ssq = sbp("ssp", [128, 16])
                rsq = sbp("rstdp", [128, 16])
                xnq = [sbp(f"xnp{i}", [128, 4, D], BF16) for i in range(2)]
                an_ = [x_[:, 0:2, :].rearrange("p a (b n) -> p (a b) n", n=512) for x_ in xnq]
                h2T = sbp("h2T", [128, 8, 512], BF16)
                w13 = [sbp(f"w13{i}", [128, 2, 8, 128], BF16) for i in range(3)]
                silu = [sbp("silu0", [128, 512])] * 2
                actT = sbp("actT", [128, NFC, 512], BF16)
                yo = [sbp("yo0", [128, D])] * 2
                mm_ps = [psp(f"ps_mm{i}", [128, 512]) for i in range(2)]
                tq_ps = [psp(f"tp_pp{i}", [128, 512], BF16) for i in range(2)]
                h_ps = [psp(f"ps_h{i}", [128, 512]) for i in range(4)]
                wn_ = 0
                for c in range(4):
                    wst = xq[wn_ % 2]; wsk = f"xp{wn_ % 2}"; wn_ += 1
                    S.dma("sync", lambda e, wst=wst, c=c: e.dma_start(out=wst[:, 0, :], in_=w_o.ap()[c * 128:(c + 1) * 128, :]), writes=[wsk])
                    S.op("vector", lambda e, wst=wst, c=c: e.tensor_tensor(out=wo_a[:, c, :], in0=wst[:, 0, :], in1=rows[:, 0, :], op=ALU.mult),
                         reads=[wsk, "rows"], writes=["wo_a"])
                for c in range(4):
                    wst = xq[wn_ % 2]; wsk = f"xp{wn_ % 2}"; wn_ += 1
                    S.dma("sync", lambda e, wst=wst, c=c: e.dma_start(out=wst[:, 0, :], in_=w_o.ap()[512 + c * 128:512 + (c + 1) * 128, :]), writes=[wsk])
                    S.op("vector", lambda e, wst=wst, c=c: e.tensor_tensor(out=wo_s[:, c, :], in0=wst[:, 0, :], in1=rows[:, 0, :], op=ALU.mult),
                         reads=[wsk, "rows"], writes=["wo_s"])
                for j in range(NFC):
                    wst = xq[wn_ % 2]; wsk = f"xp{wn_ % 2}"; wn_ += 1
                    S.dma("sync", lambda e, wst=wst, j=j: e.dma_start(out=wst[:, 0, :], in_=w2.ap()[j * 128:(j + 1) * 128, :]), writes=[wsk])
                    S.op("vector", lambda e, wst=wst, j=j: e.tensor_tensor(out=w2_b[:, j, :], in0=wst[:, 0, :], in1=rows[:, 1, :], op=ALU.mult),
                         reads=[wsk, "rows"], writes=["w2_b"])
                S.dma("sync", lambda e: e.dma_start(out=agT[:], in_=attn_gT.ap()), writes=["agT"])
                mmn = 0; hn = 0; wn = 0; yon = 0
                NT = 8
                def p_front(tt):
                    nonlocal mmn
                    b = tt % 2
                    tsl = slice(tt * 512, (tt + 1) * 512)
                    S.dma("sync", lambda e, tt=tt, b=b: e.dma_start(
                        out=xq[b][:], in_=xext.ap()[HALO + tt * 512:HALO + (tt + 1) * 512, :].rearrange("(st p) d -> p st d", p=128)),
                        writes=[f"xp{b}"])
                    S.dma("gpsimd", lambda e, tsl=tsl, b=b: e.dma_start(out=at[b][:], in_=attn_s.ap()[:, :, tsl].rearrange("(c t) d n -> (t d) c n", t=2)),
                          reads=["attn_s"], writes=["at0"])
                    S.dma("sync", lambda e, tsl=tsl, b=b: e.dma_start(
                        out=st_[b][:], in_=ssmn_s.ap()[:, tsl].rearrange("(c p) n -> p c n", p=128)), reads=["ssmn_s"], writes=[f"st{b}"])
                    ap_ = mm_ps[mmn % 2]; apk = f"ps_mm{mmn % 2}"; mmn += 1
                    for h in range(4):
                        S.op("scalar", lambda e, h=h, b=b: e.activation(out=asq[h % 2][:], in_=at[b][:, h, :], func=AF.Square),
                             reads=["at0"], writes=[f"asq{h % 2}"])
                        S.op("tensor", lambda e, ap_=ap_, h=h: e.matmul(out=ap_[:], lhsT=ones_f[:], rhs=asq[h % 2][:],
                                                                       start=(h == 0), stop=(h == 3)),
                             reads=["ones_f", f"asq{h % 2}"], writes=[apk])
                    S.op("vector", lambda e, ap_=ap_: e.tensor_scalar(out=ars[:], in0=ap_[:], scalar1=1.0 / 512, scalar2=EPS,
                                                                      op0=ALU.mult, op1=ALU.add), reads=[apk], writes=["ars"])
                    S.op("scalar", lambda e: e.activation(out=ars[:], in_=ars[:], func=AF.Sqrt), reads=["ars"], writes=["ars"])
                    S.op("vector", lambda e: e.reciprocal(out=ars[:], in_=ars[:]), reads=["ars"], writes=["ars"])
                    for h in range(4):
                        S.op("vector", lambda e, h=h, b=b: e.scalar_tensor_tensor(
                            out=an_[b][:, h, :], in0=at[b][:, h, :], scalar=cols[:, C_AG + h:C_AG + h + 1], in1=ars[:], op0=ALU.mult, op1=ALU.mult),
                            reads=["at0", "cols", "ars"], writes=[f"an{b}_{h}"] + [f"xnp{b}_{i}" for i in range(4)])
                    for st in range(4):
                        tk = slice(st * 128, (st + 1) * 128)
                        for half in range(2):
                            hs_ = slice(half * 512, (half + 1) * 512)
                            mp = mm_ps[mmn % 2]; mpk = f"ps_mm{mmn % 2}"; mmn += 1
                            for h in range(4):
                                S.op("tensor", lambda e, mp=mp, h=h, tk=tk, hs_=hs_: e.matmul(
                                    out=mp[:], lhsT=an_[b][:, h, tk], rhs=wo_a[:, h, hs_], start=(h == 0), stop=False),
                                    reads=[f"an{b}_{h}", "wo_a"], writes=[mpk])
                            for c in range(4):
                                S.op("tensor", lambda e, mp=mp, c=c, tk=tk, hs_=hs_, b=b: e.matmul(
                                    out=mp[:], lhsT=st_[b][:, c, tk], rhs=wo_s[:, c, hs_], start=False, stop=(c == 3)),
                                    reads=[f"st{b}", "wo_s"], writes=[mpk])
                            S.op("vector", lambda e, mp=mp, b=b, st=st, hs_=hs_: e.tensor_tensor(
                                out=xq[b][:, st, hs_], in0=mp[:], in1=xq[b][:, st, hs_], op=ALU.add),
                                reads=[mpk, f"xp{b}"], writes=[f"xp{b}"])
                    for st in range(4):
                        S.op("scalar", lambda e, b=b, st=st: e.activation(
                            out=junq[:], in_=xq[b][:, st, :], func=AF.Square, accum_out=ssq[:, 8 * b + st:8 * b + st + 1]),
                            reads=[f"xp{b}"], writes=["junkp", f"ssp{b}"])
                    S.op("vector", lambda e, b=b: e.tensor_scalar(out=rsq[:, 8 * b:8 * b + 4], in0=ssq[:, 8 * b:8 * b + 4], scalar1=1.0 / D, scalar2=EPS,
                                                             op0=ALU.mult, op1=ALU.add), reads=[f"ssp{b}"], writes=[f"rstdp{b}"])
                    S.op("scalar", lambda e, b=b: e.activation(out=rsq[:, 8 * b:8 * b + 4], in_=rsq[:, 8 * b:8 * b + 4], func=AF.Sqrt), reads=[f"rstdp{b}"], writes=[f"rstdp{b}"])
                    S.op("vector", lambda e, b=b: e.reciprocal(out=rsq[:, 8 * b:8 * b + 4], in_=rsq[:, 8 * b:8 * b + 4]), reads=[f"rstdp{b}"], writes=[f"rstdp{b}"])
                    for st in range(4):
                        S.op("scalar", lambda e, b=b, st=st: e.activation(
                            out=xnq[b][:, st, :], in_=xq[b][:, st, :], func=AF.Copy, scale=rsq[:, 8 * b + st:8 * b + st + 1]),
                            reads=[f"xp{b}", f"rstdp{b}"], writes=[f"xnp{b}_{st}"])
                def p_main(tt):
                    nonlocal mmn, hn, wn, yon
                    b = tt % 2
                    for fc in range(8):
                        tb_ = fc % 2
                        for st in range(4):
                            S.op("tensor", lambda e, st=st, fc=fc, tb_=tb_: e.transpose(
                                out=tq_ps[tb_][:, st * 128:(st + 1) * 128], in_=xnq[b][:, st, fc * 128:(fc + 1) * 128], identity=ident_b[:]),
                                reads=[f"xnp{b}_{st}", "ident_b"], writes=[f"tp_pp{tb_}"])
                        S.op("scalar", lambda e, fc=fc, tb_=tb_: e.activation(
                            out=h2T[:, fc, :], in_=tq_ps[tb_][:], func=AF.Identity, scale=sc12[:, 8 + fc:9 + fc],
                            bias=modc[:, 24 + fc:25 + fc]), reads=[f"tp_pp{tb_}", "sc12b", "modc"], writes=[f"h2T{fc}"])
                    for j in range(NFC):
                        wb = w13[wn % 3]; wbk = f"w13{wn % 3}"; wn += 1
                        S.dma("sync", lambda e, wb=wb, j=j: e.dma_start(out=wb[:, 0, :, :], in_=w1b.ap()[j]), reads=["wffn_b"], writes=[wbk])
                        S.dma("sync", lambda e, wb=wb, j=j: e.dma_start(out=wb[:, 1, :, :], in_=w3b.ap()[j]), reads=["wffn_b"], writes=[wbk])
                        h1 = h_ps[hn % 4]; h1k = f"ps_h{hn % 4}"; hn += 1
                        h3 = h_ps[hn % 4]; h3k = f"ps_h{hn % 4}"; hn += 1
                        for (hpp, hkk, wi) in ((h1, h1k, 0), (h3, h3k, 1)):
                            for kc in range(8):
                                S.op("tensor", lambda e, hpp=hpp, wb=wb, wi=wi, kc=kc: e.matmul(
                                    out=hpp[:], lhsT=wb[:, wi, kc, :], rhs=h2T[:, kc, :], start=(kc == 0), stop=(kc == 7)),
                                    reads=[wbk, f"h2T{kc}"], writes=[hkk])
                        sb_ = silu[0]; sbk = "silu0"
                        S.op("scalar", lambda e, h1=h1, sb_=sb_: e.activation(out=sb_[:], in_=h1[:], func=AF.Silu),
                             reads=[h1k], writes=[sbk])
                        S.op("vector", lambda e, h3=h3, sb_=sb_, j=j: e.tensor_tensor(out=actT[:, j, :], in0=h3[:], in1=sb_[:], op=ALU.mult),
                             reads=[h3k, sbk], writes=[f"actT{j}"])
                        if j == 6 and tt + 1 < NT:
                            p_front(tt + 1)
                    for st in range(4):
                        tk = slice(st * 128, (st + 1) * 128)
                        for half in range(2):
                            hs_ = slice(half * 512, (half + 1) * 512)
                            mp = mm_ps[mmn % 2]; mpk = f"ps_mm{mmn % 2}"; mmn += 1
                            for j in range(NFC):
                                S.op("tensor", lambda e, mp=mp, j=j, tk=tk, hs_=hs_: e.matmul(
                                    out=mp[:], lhsT=actT[:, j, tk], rhs=w2_b[:, j, hs_], start=(j == 0), stop=(j == NFC - 1)),
                                    reads=[f"actT{j}", "w2_b"], writes=[mpk])
                            S.op("vector", lambda e, mp=mp, b=b, st=st, hs_=hs_: e.tensor_tensor(
                                out=xq[b][:, st, hs_], in0=mp[:], in1=xq[b][:, st, hs_], op=ALU.add),
                                reads=[mpk, f"xp{b}"], writes=[f"xp{b}"])
                        S.op("scalar", lambda e, b=b, st=st: e.activation(
                            out=junq[:], in_=xq[b][:, st, :], func=AF.Square, accum_out=ssq[:, 4 + st:5 + st]),
                            reads=[f"xp{b}"], writes=["junkp", f"ss3_{st}"])
                        S.op("vector", lambda e, st=st: e.tensor_scalar(out=rsq[:, 4 + st:5 + st], in0=ssq[:, 4 + st:5 + st], scalar1=1.0 / D,
                                                                        scalar2=EPS, op0=ALU.mult, op1=ALU.add),
                             reads=[f"ss3_{st}"], writes=[f"rs3_{st}"])
                        S.op("scalar", lambda e, st=st: e.activation(out=rsq[:, 4 + st:5 + st], in_=rsq[:, 4 + st:5 + st], func=AF.Sqrt),
                             reads=[f"rs3_{st}"], writes=[f"rs3_{st}"])
                        S.op("vector", lambda e, st=st: e.reciprocal(out=rsq[:, 4 + st:5 + st], in_=rsq[:, 4 + st:5 + st]),
                             reads=[f"rs3_{st}"], writes=[f"rs3_{st}"])
                        yb = yo[0]; ybk = "yo0"
                        S.op("vector", lambda e, b=b, st=st, yb=yb: e.scalar_tensor_tensor(
                            out=yb[:], in0=xq[b][:, st, :], scalar=rsq[:, 4 + st:5 + st], in1=rows[:, 2, :], op0=ALU.mult, op1=ALU.mult),
                            reads=[f"xp{b}", f"rs3_{st}", "rows"], writes=[ybk])
                        S.dma("gpsimd", lambda e, yb=yb, tt=tt, st=st: e.dma_start(
                            out=y_out.ap()[tt * 512 + st * 128: tt * 512 + (st + 1) * 128, :], in_=yb[:]), reads=[ybk], writes=["y_out"])

                p_front(0)
                for tt in range(NT):
                    p_main(tt)
        S.emit()
        nops = len(S.ops)
    return nc, nops


def _rope_tables(pos):
    inv = 1.0 / (10000.0 ** (np.arange(0, 64, 2, dtype=np.float64) / 64.0))
    ang = pos.astype(np.float64)[:, None] * inv[None, :]
    c, s = np.cos(ang).astype(np.float32), np.sin(ang).astype(np.float32)
    cos2 = np.concatenate([c, c], 1)
    sin2 = np.concatenate([-s, s], 1)
    cosT = np.ascontiguousarray(np.concatenate([cos2, cos2], 1).T)
    sinT = np.ascontiguousarray(np.concatenate([sin2, sin2], 1).T)
    return cosT, sinT


def _consts():
    c = np.zeros((128, 1280), np.float32)
    c[:, 0:128] = np.eye(128, dtype=np.float32)
    perm = np.zeros((128, 128), np.float32)
    for m in range(128):
        j = m % 64
        k = m - j + (j + 32) % 64
        perm[k, m] = 1.0
    c[:, 128:256] = perm
    kk = np.arange(128)[:, None]
    p = np.arange(128)[None, :]
    c[:, 256:384] = np.where(kk >= p, 0.0, -30000.0)
    c[:, 384:512] = np.where(kk <= p, 0.0, -30000.0)
    c[:, 512:640] = 1.0
    for i in range(5):
        c[:, 640 + 128 * i:768 + 128 * i] = (kk >= p) if i % 2 == 0 else (kk <= p)
    return c


def _consts2():
    c = np.zeros((128, 72 + 256), np.float32)
    c[:, 0:64] = np.arange(1, 65, dtype=np.float32)[None, :]
    c[:, 64:72] = np.arange(-8, 0, dtype=np.float32)[None, :]
    sidx = (np.arange(128) // 16)[:, None]
    tidx = (np.arange(128) // 16)[None, :]
    c[:, 72:200] = (tidx >= sidx)
    c[:, 200:328] = (tidx <= sidx)
    return c


def make_in_maps(inp):
    f = lambda a: np.ascontiguousarray(np.asarray(a, dtype=np.float32))
    xp = f(inp["x_prompt"])[0]
    xs = f(inp["x_sample"])
    cp = f(inp["c_prompt"])
    csm = f(inp["c_sample"])
    shared = {
        "consts": _consts(),
        "w_ada": f(inp["w_ada"])[0],
        "b_ada": f(inp["b_ada"]).reshape(48, 128),
        "norm1_g": f(inp["norm1_g"]).reshape(8, 128),
        "norm2_g": f(inp["norm2_g"]).reshape(8, 128),
        "final_g": f(inp["final_g"]).reshape(8, 128),
        "attn_norm_g": f(inp["attn_norm_g"]).reshape(4, 128),
        "ssm_norm_g": f(inp["ssm_norm_g"]).reshape(4, 128),
        "d_skip": f(inp["d_skip"]).reshape(4, 128),
        "b_glu": f(inp["b_glu"]).reshape(4, 128),
        "w_in": f(inp["w_in"])[0],
        "consts2": _consts2(),
        "lam_re": f(inp["lam_re"]).reshape(32, 128),
        "lam_im": f(inp["lam_im"]).reshape(32, 128),
        "log_dt": f(inp["log_dt"]).reshape(32, 2),
        "b_re": f(inp["b_re"]).reshape(32, 128, 16),
        "b_im": f(inp["b_im"]).reshape(32, 128, 16),
        "c_re": f(inp["c_re"]).reshape(64, 16, 64),
        "c_im": f(inp["c_im"]).reshape(64, 16, 64),
        "w_glu": f(inp["w_glu"])[0],
        "w_o": f(inp["w_o"])[0],
        "w1": f(inp["w1"])[0],
        "w3": f(inp["w3"])[0],
        "w2": f(inp["w2"])[0],
        "attn_gT": np.ascontiguousarray(f(inp["attn_norm_g"]).reshape(8, 64).T),
    }
    maps = []
    for core in range(8):
        if core < 4:
            seq, start, L, c = xs[core], 0, SEG, csm[core]
        else:
            seq, start, L, c = xp, (core - 4) * SEG, 4 * SEG, cp[0]
        xe = np.zeros((EXT, D), np.float32)
        lo, hi = start - HALO, start + SEG + HALO
        slo, shi = max(lo, 0), min(hi, L)
        xe[slo - lo:shi - lo] = seq[slo:shi]
        pos = np.arange(lo, hi)
        cT, sT = _rope_tables(pos)
        valid = ((pos >= 0) & (pos < L))
        fl = np.zeros((128, 6), np.float32)
        if core < 4:
            xo = np.zeros((3 * SEG, D), np.float32)
        else:
            r = core - 4
            others = [j for j in range(4) if j != r]
            xo = np.concatenate([seq[j * SEG:(j + 1) * SEG] for j in others], 0)
            for i, j in enumerate(others):
                fl[:, i] = 1.0 if j < r else 0.0
                fl[:, 3 + i] = 1.0 if j > r else 0.0
        m = dict(shared)
        m["xoth"] = np.ascontiguousarray(xo)
        m["flcol"] = fl
        m["tvalid"] = np.ascontiguousarray(valid.astype(np.float32).reshape(EXT // 128, 128).T)
        m.update({"xext": xe, "cvec": np.ascontiguousarray(c.reshape(8, 128)), "cosT": cT, "sinT": sT})
        maps.append(m)
    return maps


_CACHE = {}


def kernel(**inputs):
    maps = make_in_maps(inputs)
    if "nc" not in _CACHE:
        _CACHE["nc"] = build(debug=False)[0]
    nc = _CACHE["nc"]
    res = run_bass_kernel_spmd(nc, maps, core_ids=list(range(8)))
    outs = [np.asarray(r["y_out"], dtype=np.float32) for r in res.results]
    y_sample = np.stack(outs[0:4], 0)
    y_prompt = np.concatenate(outs[4:8], 0)[None]
    return (y_prompt, y_sample)
```

```python
import os
import numpy as np
from contextlib import ExitStack
import concourse.bass as bass
import concourse.mybir as mybir
from concourse.bass_utils import run_bass_kernel_spmd

F32 = mybir.dt.float32
BF16 = mybir.dt.bfloat16
I32 = mybir.dt.int32
ALU = mybir.AluOpType
AF = mybir.ActivationFunctionType
AX = mybir.AxisListType

D = 1024
SEG = 4096
HALO = 1024
EXT = SEG + 2 * HALO
TA = 512
NTA = EXT // TA
EPS = 1e-6
FFN = 2816
NFC = FFN // 128


class Sched:
    COMPUTE = ("tensor", "vector", "scalar", "gpsimd")
    QUEUES = ("sync", "gpsimd")

    def __init__(self, nc, es, ndma=12):
        self.nc = nc
        self.es = es
        self.ops = []
        self.last_w = {}
        self.readers = {}
        self.ndma = ndma
        self.dma_pool = {q: [None] * ndma for q in self.QUEUES}
        self.dma_rr = {q: 0 for q in self.QUEUES}

    def _deps(self, reads, writes):
        d = set()
        for k in reads:
            if k in self.last_w:
                d.add(self.last_w[k])
        for k in writes:
            if k in self.last_w:
                d.add(self.last_w[k])
            for r in self.readers.get(k, ()):
                d.add(r)
        return d

    def _commit(self, oid, reads, writes):
        for k in reads:
            self.readers.setdefault(k, []).append(oid)
        for k in writes:
            self.last_w[k] = oid
            self.readers[k] = []

    @staticmethod
    def _excl(reads, writes):
        px = [k for k in reads if k.startswith(("ps", "pj", "tp", "sw"))]
        return (reads, list(writes) + px) if px else (reads, writes)

    def op(self, eng, fn, reads=(), writes=()):
        reads, writes = self._excl(reads, writes)
        oid = len(self.ops)
        deps = self._deps(reads, writes)
        self.ops.append(dict(eng=eng, fn=fn, deps=deps, kind="c", id=oid))
        self._commit(oid, reads, writes)
        return oid

    def dma(self, q, fn, reads=(), writes=()):
        oid = len(self.ops)
        deps = self._deps(reads, writes)
        slot = self.dma_rr[q]
        self.dma_rr[q] = (slot + 1) % self.ndma
        prev = self.dma_pool[q][slot]
        if prev is not None:
            deps.add(prev)
        self.dma_pool[q][slot] = oid
        self.ops.append(dict(eng=q, fn=fn, deps=deps, kind="d", id=oid, slot=slot))
        self._commit(oid, reads, writes)
        return oid

    def barrier(self):
        last = {}
        for o in self.ops:
            if o["kind"] == "c":
                if o["eng"] != "sync":
                    last[("c", o["eng"])] = o["id"]
            else:
                last[("d", o["eng"], o["slot"])] = o["id"]
        deps = set(last.values())
        for eng in ("tensor", "vector", "scalar", "gpsimd", "sync"):
            oid = len(self.ops)
            self.ops.append(dict(eng=eng, fn=(lambda e: e.nop()), deps=set(deps), kind="c", id=oid))

    def emit(self, final_wait_eng="sync"):
        nc, es = self.nc, self.es
        ops = self.ops

        def needs_wait(o, t):
            if t["kind"] == "d":
                return True
            if t["eng"] == o["eng"] and o["kind"] == "c" and o["eng"] == "tensor":
                return False
            return True

        sig = [False] * len(ops)
        for o in ops:
            for d in o["deps"]:
                t = ops[d]
                if t["kind"] == "c" and needs_wait(o, t):
                    sig[d] = True
        csem = {e: es.enter_context(nc.semaphore("cs_" + e)) for e in self.COMPUTE}
        dsem = {q: [es.enter_context(nc.semaphore(f"ds_{q}{i}")) for i in range(self.ndma)]
                for q in self.QUEUES}
        ccount = {e: 0 for e in self.COMPUTE}
        dcount = {q: [0] * self.ndma for q in self.QUEUES}
        val = [None] * len(ops)
        for o in ops:
            if o["kind"] == "d":
                q, s = o["eng"], o["slot"]
                dcount[q][s] += 16
                val[o["id"]] = (dsem[q][s], dcount[q][s])
            elif sig[o["id"]]:
                e = o["eng"]
                assert e in csem, e
                ccount[e] += 1
                val[o["id"]] = (csem[e], ccount[e])
        per = {}
        for o in ops:
            per.setdefault(o["eng"], []).append(o)
        block = es.enter_context(nc.Block())
        self.nwaits = 0

        def run(engname, eng):
            waited = {}
            for o in per.get(engname, []):
                need = {}
                for d in o["deps"]:
                    t = ops[d]
                    if not needs_wait(o, t):
                        continue
                    sem, v = val[d]
                    key = id(sem)
                    if key not in need or need[key][1] < v:
                        need[key] = (sem, v)
                for key, (sem, v) in need.items():
                    if waited.get(key, 0) >= v:
                        continue
                    waited[key] = v
                    eng.wait_ge(sem, v)
                    self.nwaits += 1
                ins = o["fn"](eng)
                if o["kind"] == "d":
                    ins.then_inc(val[o["id"]][0], 16)
                elif sig[o["id"]]:
                    ins.then_inc(val[o["id"]][0], 1)
            if engname == final_wait_eng:
                for q in self.QUEUES:
                    for s in range(self.ndma):
                        if dcount[q][s] and waited.get(id(dsem[q][s]), 0) < dcount[q][s]:
                            eng.wait_ge(dsem[q][s], dcount[q][s])

        @block.sync
        def _(e):
            run("sync", e)

        @block.scalar
        def _(e):
            run("scalar", e)

        @block.gpsimd
        def _(e):
            run("gpsimd", e)

        @block.vector
        def _(e):
            run("vector", e)

        @block.tensor
        def _(e):
            run("tensor", e)


DILS = (1, 4, 16)


def sl(start, n, step=1):
    return slice(start, start + (n - 1) * step + 1, step)


def kb_index(di, r, end):
    base = (0, 2, 10)[di]
    return base + r * 2 + end


NKB = 42

C_BADA, C_N1, C_N2, C_C, C_AG, C_SG, C_DS, C_BG, C_FG, NCOLS = 0, 48, 56, 64, 72, 76, 80, 84, 88, 96


def build(debug=False, stage=99, sub=99, tiles=None):
    nc = bass.Bass("TRN2", target_bir_lowering=False)
    dk = "ExternalOutput" if debug else "Internal"

    def din(name, shape, dt=F32):
        return nc.dram_tensor(name, list(shape), dt, kind="ExternalInput")

    xext = din("xext", [EXT, D])
    cvec = din("cvec", [8, 128])
    cosT = din("cosT", [128, EXT])
    sinT = din("sinT", [128, EXT])
    consts = din("consts", [128, 1280])
    w_ada = din("w_ada", [D, 6 * D])
    b_ada = din("b_ada", [48, 128])
    norm1_g = din("norm1_g", [8, 128])
    norm2_g = din("norm2_g", [8, 128])
    final_g = din("final_g", [8, 128])
    attn_g = din("attn_norm_g", [4, 128])
    ssm_g = din("ssm_norm_g", [4, 128])
    d_skip = din("d_skip", [4, 128])
    b_glu = din("b_glu", [4, 128])
    w_in = din("w_in", [D, 2048])
    tvalid = din("tvalid", [128, EXT // 128])
    consts2 = din("consts2", [128, 72 + 256])
    lam_re = din("lam_re", [32, 128])
    lam_im = din("lam_im", [32, 128])
    log_dt = din("log_dt", [32, 2])
    b_re = din("b_re", [32, 128, 16])
    b_im = din("b_im", [32, 128, 16])
    c_re = din("c_re", [64, 16, 64])
    c_im = din("c_im", [64, 16, 64])
    w_glu = din("w_glu", [512, 512])
    w_o = din("w_o", [D, D])
    w1 = din("w1", [D, FFN])
    w3 = din("w3", [D, FFN])
    w2 = din("w2", [FFN, D])
    attn_gT = din("attn_gT", [64, 8])
    xoth = din("xoth", [3 * SEG, D])
    flcol = din("flcol", [128, 6])
    y_out = nc.dram_tensor("y_out", [SEG, D], F32, kind="ExternalOutput")

    qT_s = nc.dram_tensor("qT_s", [512, SEG], BF16, kind=dk)
    kT_s = nc.dram_tensor("kT_s", [512, EXT], BF16, kind=dk)
    v_s = nc.dram_tensor("v_s", [EXT, 520], BF16, kind=dk)
    Us = nc.dram_tensor("Us", [4, 32, 128, 512], BF16, kind=dk)
    attn_s = nc.dram_tensor("attn_s", [8, 64, SEG], F32, kind=dk)
    y_s = nc.dram_tensor("y_s", [512, 8, 512], F32, kind=dk)
    ssmn_s = nc.dram_tensor("ssmn_s", [512, SEG], BF16, kind=dk)
    w1b = nc.dram_tensor("w1b", [NFC, 128, 8, 128], BF16, kind="Internal")
    w3b = nc.dram_tensor("w3b", [NFC, 128, 8, 128], BF16, kind="Internal")
    if debug:
        dbg_s5 = nc.dram_tensor("dbg_s5", [128, 32768], F32, kind="ExternalOutput")
        dbg_cols = nc.dram_tensor("dbg_cols", [128, 160], F32, kind="ExternalOutput")
        dbg_rows = nc.dram_tensor("dbg_rows", [128, 3 * D], F32, kind="ExternalOutput")

    with ExitStack() as es:
        S = Sched(nc, es)

        def sb(name, shape, dt=F32):
            return es.enter_context(nc.sbuf_tensor(name, list(shape), dt))

        def ps(name, shape, dt=F32):
            return es.enter_context(nc.psum_tensor(name, list(shape), dt))

        ident_f = sb("ident_f", [128, 128])
        ones_f = sb("ones_f", [128, 128])
        ident_b = sb("ident_b", [128, 128], BF16)
        perm_b = sb("perm_b", [128, 128], BF16)
        band_b = sb("band_b", [128, 2, 128], BF16)
        cols = sb("cols", [128, NCOLS])
        modc = sb("modc", [128, 48])
        sc12 = sb("sc12", [128, 16])
        silu_c = sb("silu_c", [128, 8])
        rows = sb("rows", [128, 3, D])

        S.dma("sync", lambda e: e.dma_start(out=ident_f[:], in_=consts.ap()[:, 0:128]), writes=["ident_f"])
        S.dma("sync", lambda e: e.dma_start(out=ones_f[:], in_=consts.ap()[:, 512:640]), writes=["ones_f"])
        S.dma("gpsimd", lambda e: e.dma_start(out=ident_b[:], in_=consts.ap()[:, 0:128]), writes=["ident_b"])
        S.dma("gpsimd", lambda e: e.dma_start(out=perm_b[:], in_=consts.ap()[:, 128:256]), writes=["perm_b"])
        S.dma("gpsimd", lambda e: e.dma_start(
            out=band_b[:], in_=consts.ap()[:, 256:512].rearrange("p (c q) -> p c q", c=2)), writes=["band_b"])

        with ExitStack() as p0:
            stg = p0.enter_context(nc.sbuf_tensor("stg", [NCOLS, 128], F32))
            wada_t = [p0.enter_context(nc.sbuf_tensor(f"wada{i}", [128, 6 * D], BF16)) for i in range(2)]
            wada_f = [p0.enter_context(nc.sbuf_tensor(f"wadaf{i}", [128, 6 * D], F32)) for i in range(3)]
            silu_cb = p0.enter_context(nc.sbuf_tensor("silu_cb", [128, 8], BF16))
            diag = [p0.enter_context(nc.sbuf_tensor(f"diag{i}", [128, 128], F32)) for i in range(2)]
            ps_cols = p0.enter_context(nc.psum_tensor("ps_cols", [128, NCOLS], F32))
            ps_mod = p0.enter_context(nc.psum_tensor("ps_mod", [128, 48], F32))
            ps_row = [p0.enter_context(nc.psum_tensor(f"ps_row{i}", [128, 512], F32)) for i in range(2)]
            for (src, off, n) in ((b_ada, C_BADA, 48), (norm1_g, C_N1, 8), (norm2_g, C_N2, 8), (cvec, C_C, 8),
                                  (attn_g, C_AG, 4), (ssm_g, C_SG, 4), (d_skip, C_DS, 4), (b_glu, C_BG, 4),
                                  (final_g, C_FG, 8)):
                S.dma("sync", lambda e, src=src, off=off, n=n: e.dma_start(out=stg[off:off + n, :], in_=src.ap()),
                      writes=["stg"])
            S.op("tensor", lambda e: e.transpose(out=ps_cols[:], in_=stg[:], identity=ident_f[0:NCOLS, 0:NCOLS]),
                 reads=["stg", "ident_f"], writes=["ps_cols"])
            S.op("vector", lambda e: e.tensor_copy(out=cols[:], in_=ps_cols[:]), reads=["ps_cols"], writes=["cols"])
            S.op("scalar", lambda e: e.activation(out=silu_c[:], in_=cols[:, C_C:C_C + 8], func=AF.Silu),
                 reads=["cols"], writes=["silu_c"])
            S.op("vector", lambda e: e.tensor_copy(out=silu_cb[:], in_=silu_c[:]), reads=["silu_c"], writes=["silu_cb"])
            for kc in range(8 if stage >= -1 else 0):
                b = kc % 2
                bf = kc % 3
                for hh, qn in enumerate(("sync", "gpsimd")):
                    S.dma(qn, lambda e, kc=kc, bf=bf, hh=hh: e.dma_start(
                        out=wada_f[bf][:, hh * 3072:(hh + 1) * 3072], in_=w_ada.ap()[kc * 128:(kc + 1) * 128, hh * 3072:(hh + 1) * 3072]),
                        writes=[f"wadaf{bf}_{hh}"])
                for hq in range(4):
                    cs_ = slice(hq * 1536, (hq + 1) * 1536)
                    if hq % 2 == 0:
                        S.op("scalar", lambda e, b=b, bf=bf, cs_=cs_: e.activation(out=wada_t[b][:, cs_], in_=wada_f[bf][:, cs_], func=AF.Copy),
                             reads=[f"wadaf{bf}_{hq // 2}"], writes=[f"wada{b}"])
                    else:
                        S.op("vector", lambda e, b=b, bf=bf, cs_=cs_: e.tensor_copy(out=wada_t[b][:, cs_], in_=wada_f[bf][:, cs_]),
                             reads=[f"wadaf{bf}_{hq // 2}"], writes=[f"wada{b}"])
                for j in range(48):
                    S.op("tensor", lambda e, kc=kc, b=b, j=j: e.matmul(
                        out=ps_mod[:, j:j + 1], lhsT=wada_t[b][:, j * 128:(j + 1) * 128], rhs=silu_cb[:, kc:kc + 1],
                        start=(kc == 0 and j == 0), stop=(kc == 7 and j == 47), skip_group_check=True),
                        reads=[f"wada{b}", "silu_cb"], writes=["ps_mod"])
            S.op("vector", lambda e: e.tensor_tensor(out=modc[:], in0=ps_mod[:], in1=cols[:, C_BADA:C_BADA + 48], op=ALU.add),
                 reads=["ps_mod", "cols"], writes=["modc"])
            S.op("vector", lambda e: e.scalar_tensor_tensor(out=sc12[:, 0:8], in0=modc[:, 8:16], scalar=1.0,
                                                            in1=cols[:, C_N1:C_N1 + 8], op0=ALU.add, op1=ALU.mult),
                 reads=["modc", "cols"], writes=["sc12a"])
            S.op("vector", lambda e: e.scalar_tensor_tensor(out=sc12[:, 8:16], in0=modc[:, 32:40], scalar=1.0,
                                                            in1=cols[:, C_N2:C_N2 + 8], op0=ALU.add, op1=ALU.mult),
                 reads=["modc", "cols"], writes=["sc12b"])
            k = 0
            for ri, (srct, soff, skey) in enumerate(((modc, 16, "modc"), (modc, 40, "modc"), (cols, C_FG, "cols")) if stage >= 0 else ()):
                for half in range(2):
                    pr = ps_row[(ri * 2 + half) % 2]
                    prk = f"ps_row{(ri * 2 + half) % 2}"
                    for c4 in range(4):
                        c = half * 4 + c4
                        dg = diag[k % 2]
                        dgk = f"diag{k % 2}"
                        k += 1
                        S.op("vector", lambda e, dg=dg, srct=srct, soff=soff, c=c: e.tensor_scalar(
                            out=dg[:], in0=ident_f[:], scalar1=srct[:, soff + c:soff + c + 1], scalar2=None, op0=ALU.mult),
                            reads=["ident_f", skey], writes=[dgk])
                        S.op("tensor", lambda e, dg=dg, pr=pr, c4=c4: e.matmul(
                            out=pr[:, c4 * 128:(c4 + 1) * 128], lhsT=ones_f[:], rhs=dg[:], start=True, stop=True),
                            reads=["ones_f", dgk], writes=[prk])
                    S.op("vector", lambda e, pr=pr, ri=ri, half=half: e.tensor_copy(
                        out=rows[:, ri, half * 512:(half + 1) * 512], in_=pr[:]), reads=[prk], writes=["rows"])
            if debug:
                dcol = p0.enter_context(nc.sbuf_tensor("dcol", [128, 160], F32))
                S.op("vector", lambda e: e.tensor_copy(out=dcol[:, 0:NCOLS], in_=cols[:]), reads=["cols"], writes=["dcol"])
                S.op("vector", lambda e: e.tensor_copy(out=dcol[:, 96:144], in_=modc[:]), reads=["modc"], writes=["dcol"])
                S.op("vector", lambda e: e.tensor_copy(out=dcol[:, 144:160], in_=sc12[:]), reads=["sc12a", "sc12b"], writes=["dcol"])
                S.dma("sync", lambda e: e.dma_start(out=dbg_cols.ap(), in_=dcol[:]), reads=["dcol"])
                S.dma("sync", lambda e: e.dma_start(out=dbg_rows.ap(), in_=rows[:].rearrange("p a d -> p (a d)")), reads=["rows"])


        if stage >= 5:
            for wsrc, wdst in ((w1, w1b), (w3, w3b)):
                for j in range(NFC):
                    S.dma("gpsimd", lambda e, wsrc=wsrc, wdst=wdst, j=j: e.dma_start(
                        out=wdst.ap()[j], in_=wsrc.ap()[:, j * 128:(j + 1) * 128].rearrange("(kc p) f -> p kc f", p=128)),
                        writes=["wffn_b"])

        if stage >= 1:
            S.barrier()
            with ExitStack() as pa:
                def sba(name, shape, dt=F32):
                    return pa.enter_context(nc.sbuf_tensor(name, list(shape), dt))

                def psa(name, shape, dt=F32):
                    return pa.enter_context(nc.psum_tensor(name, list(shape), dt))
                win_b = sba("win_b", [128, 8, 2048], BF16)
                S.dma("gpsimd", lambda e: e.dma_start(out=win_b[:], in_=w_in.ap().rearrange("(kc p) n -> p kc n", p=128)),
                      writes=["win_b"])
                xt = [sba(f"xt{i}", [128, 4, D]) for i in range(3)]
                cs_t = [sba(f"cs{i}", [128, 2, TA]) for i in range(3)]
                sqj = sba("sqj", [128, D], BF16)
                ss = [sba(f"ss{i}", [128, 4]) for i in range(2)]
                rstd = [sba(f"rstd{i}", [128, 4]) for i in range(2)]
                xn = [sba(f"xn{i}", [128, 4, D], BF16) for i in range(2)]
                hT = [sba(f"hT{i}", [128, 8, TA], BF16) for i in range(2)]
                qraw = [sba(f"qraw{i}", [128, TA], BF16) for i in range(2)]
                t1 = [sba(f"t1{i}", [128, TA]) for i in range(2)]
                t2 = [sba(f"t2{i}", [128, TA]) for i in range(2)]
                qk_o = [sba(f"qko{i}", [128, 8, TA], BF16) for i in range(2)]
                v_o = [sba(f"vo{i}", [128, 4, 520], BF16) for i in range(2)]
                tval = sba("tval", [128, EXT // 128])
                S.dma("sync", lambda e: e.dma_start(out=tval[:], in_=tvalid.ap()), writes=["tval"])
                ud = sba("ud", [128, 4, 8, 512], BF16)
                tp_ps = [psa(f"tp_ps{i}", [128, TA], BF16) for i in range(2)]
                pj_ps = [psa(f"pj_ps{i}", [128, 512]) for i in range(4)]
                sw_ps = [psa(f"sw_ps{i}", [128, 512]) for i in range(2)]
                pjn = 0
                swn = 0
                rn = 0
                tile_list = [("main", t) for t in (range(NTA) if tiles is None else tiles)]
                if stage >= 3 and tiles is None:
                    tile_list += [("oth", t) for t in range(3 * SEG // TA)]
                def tile_part(ti, part):
                    nonlocal pjn, swn, rn
                    tkind, t = tile_list[ti]
                    b = ti % 2
                    b3 = ti % 3
                    other = tkind == "oth"
                    own = (not other) and HALO // TA <= t < (HALO + SEG) // TA
                    to = (t % 8) if other else t - HALO // TA
                    segi = (1 + t // 8) if other else 0
                    xsrc = xoth if other else xext
                    if part == "1a":
                        S.dma("sync", lambda e, b3=b3, t=t, b=b, xsrc=xsrc: e.dma_start(
                            out=xt[b3][:], in_=xsrc.ap()[t * TA:(t + 1) * TA, :].rearrange("(st p) d -> p st d", p=128)),
                            writes=[f"xt{b3}"])
                        if not other:
                            S.dma("sync", lambda e, b3=b3, t=t, b=b: e.dma_start(out=cs_t[b3][:, 0, :], in_=cosT.ap()[:, t * TA:(t + 1) * TA]),
                                  writes=[f"cs{b3}"])
                            S.dma("sync", lambda e, b3=b3, t=t, b=b: e.dma_start(out=cs_t[b3][:, 1, :], in_=sinT.ap()[:, t * TA:(t + 1) * TA]),
                                  writes=[f"cs{b3}"])
                        for st in range(4):
                            S.op("scalar", lambda e, b3=b3, b=b, st=st: e.activation(
                                out=sqj[:], in_=xt[b3][:, st, :], func=AF.Square, accum_out=ss[b][:, st:st + 1]),
                                reads=[f"xt{b3}"], writes=["sqj", f"ss{b}"])
                        S.op("vector", lambda e, b=b: e.tensor_scalar(out=rstd[b][:], in0=ss[b][:], scalar1=1.0 / D, scalar2=EPS,
                                                                      op0=ALU.mult, op1=ALU.add),
                             reads=[f"ss{b}"], writes=[f"rstd{b}"])
                        S.op("scalar", lambda e, b=b: e.activation(out=rstd[b][:], in_=rstd[b][:], func=AF.Sqrt),
                             reads=[f"rstd{b}"], writes=[f"rstd{b}"])
                        S.op("vector", lambda e, b=b: e.reciprocal(out=rstd[b][:], in_=rstd[b][:]),
                             reads=[f"rstd{b}"], writes=[f"rstd{b}"])
                        for st in range(4):
                            S.op("vector", lambda e, b3=b3, b=b, st=st: e.tensor_scalar(
                                out=xn[b][:, st, :], in0=xt[b3][:, st, :], scalar1=rstd[b][:, st:st + 1], scalar2=None, op0=ALU.mult),
                                reads=[f"xt{b3}", f"rstd{b}"], writes=[f"xn{b}_{st}"])
                        return
                    if part == "1b":
                        for fc in range(8):
                            tb = fc % 2
                            for st in range(4):
                                S.op("tensor", lambda e, b=b, st=st, fc=fc, tb=tb: e.transpose(
                                    out=tp_ps[tb][:, st * 128:(st + 1) * 128], in_=xn[b][:, st, fc * 128:(fc + 1) * 128],
                                    identity=ident_b[:]), reads=[f"xn{b}_{st}", "ident_b"], writes=[f"tp_ps{tb}"])
                            if other and fc % 2 == 1:
                                S.op("vector", lambda e, b=b, fc=fc, tb=tb: e.tensor_scalar(
                                    out=hT[b][:, fc, :], in0=tp_ps[tb][:], scalar1=sc12[:, fc:fc + 1], scalar2=modc[:, fc:fc + 1],
                                    op0=ALU.mult, op1=ALU.add),
                                    reads=[f"tp_ps{tb}", "sc12a", "modc"], writes=[f"hT{b}_{fc}"])
                            else:
                                S.op("scalar", lambda e, b=b, fc=fc, tb=tb: e.activation(
                                    out=hT[b][:, fc, :], in_=tp_ps[tb][:], func=AF.Identity,
                                    scale=sc12[:, fc:fc + 1], bias=modc[:, fc:fc + 1]),
                                    reads=[f"tp_ps{tb}", "sc12a", "modc"], writes=[f"hT{b}_{fc}"])
                        return
                    hkeys = [f"hT{b}_{fc}" for fc in range(8)]
                    if sub < 3:
                        return
                    for which in (() if other else ((0, 1) if own else (1,))):
                        for cc in range(4):
                            pp = pj_ps[pjn % 4]; ppk = f"pj_ps{pjn % 4}"; pjn += 1
                            col0 = which * 512 + cc * 128
                            for kc in range(8):
                                S.op("tensor", lambda e, pp=pp, b=b, kc=kc, col0=col0: e.matmul(
                                    out=pp[:], lhsT=win_b[:, kc, col0:col0 + 128], rhs=hT[b][:, kc, :],
                                    start=(kc == 0), stop=(kc == 7)), reads=["win_b", hkeys[kc]], writes=[ppk])
                            r = rn % 2; rn += 1
                            S.op("scalar", lambda e, pp=pp, r=r: e.activation(out=qraw[r][:], in_=pp[:], func=AF.Copy),
                                 reads=[ppk], writes=[f"qraw{r}", ppk + "x"])
                            sp = sw_ps[swn % 2]; spk = f"sw_ps{swn % 2}"; swn += 1
                            S.op("tensor", lambda e, sp=sp, r=r: e.matmul(out=sp[:], lhsT=perm_b[:], rhs=qraw[r][:],
                                                                         start=True, stop=True),
                                 reads=["perm_b", f"qraw{r}"], writes=[spk])
                            S.op("vector", lambda e, b3=b3, pp=pp, r=r, b=b: e.tensor_tensor(
                                out=t1[r][:], in0=pp[:], in1=cs_t[b3][:, 0, :], op=ALU.mult),
                                reads=[ppk, f"cs{b3}"], writes=[f"t1{r}", ppk + "x"])
                            S.op("vector", lambda e, b3=b3, sp=sp, r=r, b=b: e.tensor_tensor(
                                out=t2[r][:], in0=sp[:], in1=cs_t[b3][:, 1, :], op=ALU.mult),
                                reads=[spk, f"cs{b3}"], writes=[f"t2{r}"])
                            S.op("vector", lambda e, r=r, b=b, which=which, cc=cc: e.tensor_tensor(
                                out=qk_o[b][:, which * 4 + cc, :], in0=t1[r][:], in1=t2[r][:], op=ALU.add),
                                reads=[f"t1{r}", f"t2{r}"], writes=[f"qko{b}"])
                    if own:
                        S.dma("gpsimd", lambda e, b=b, to=to: e.dma_start(
                            out=qT_s.ap()[:, to * TA:(to + 1) * TA].rearrange("(cc p) n -> p cc n", p=128),
                            in_=qk_o[b][:, 0:4, :]), reads=[f"qko{b}"], writes=["qT_s"])
                    if not other:
                        S.dma("gpsimd", lambda e, b=b, t=t: e.dma_start(
                            out=kT_s.ap()[:, t * TA:(t + 1) * TA].rearrange("(cc p) n -> p cc n", p=128),
                            in_=qk_o[b][:, 4:8, :]), reads=[f"qko{b}"], writes=["kT_s"])
                    if sub < 4:
                        return
                    for st in range(0 if other else 4):
                        pp = pj_ps[pjn % 4]; ppk = f"pj_ps{pjn % 4}"; pjn += 1
                        for kc in range(8):
                            S.op("tensor", lambda e, pp=pp, b=b, kc=kc, st=st: e.matmul(
                                out=pp[:], lhsT=hT[b][:, kc, st * 128:(st + 1) * 128], rhs=win_b[:, kc, 1024:1536],
                                start=(kc == 0), stop=(kc == 7)), reads=["win_b", hkeys[kc]], writes=[ppk])
                        S.op("vector", lambda e, b=b, st=st, t=t: e.tensor_copy(
                            out=v_o[b][:, st, :].rearrange("p (h e) -> p h e", e=65)[:, :, 64],
                            in_=bass.AP(tval, t * 4 + st, [[EXT // 128, 128], [0, 8]])), reads=["tval"], writes=[f"vo{b}"])
                        S.op("vector", lambda e, pp=pp, b=b, st=st, t=t: e.tensor_scalar(
                            out=v_o[b][:, st, :].rearrange("p (h e) -> p h e", e=65)[:, :, 0:64],
                            in0=pp[:].rearrange("p (h e) -> p h e", e=64), scalar1=tval[:, t * 4 + st:t * 4 + st + 1], scalar2=None,
                            op0=ALU.mult),
                             reads=[ppk, "tval"], writes=[f"vo{b}"])
                    if not other:
                        S.dma("gpsimd", lambda e, b=b, t=t: e.dma_start(
                            out=v_s.ap()[t * TA:(t + 1) * TA, :].rearrange("(st p) n -> p st n", p=128), in_=v_o[b][:]),
                            reads=[f"vo{b}"], writes=["v_s"])
                    if (own or other) and sub >= 5:
                        for cc in range(4):
                            pp = pj_ps[pjn % 4]; ppk = f"pj_ps{pjn % 4}"; pjn += 1
                            col0 = 1536 + cc * 128
                            for kc in range(8):
                                S.op("tensor", lambda e, pp=pp, b=b, kc=kc, col0=col0: e.matmul(
                                    out=pp[:], lhsT=win_b[:, kc, col0:col0 + 128], rhs=hT[b][:, kc, :],
                                    start=(kc == 0), stop=(kc == 7)), reads=["win_b", hkeys[kc]], writes=[ppk])
                            if other:
                                S.op("vector", lambda e, pp=pp, cc=cc, to=to: e.tensor_copy(
                                    out=ud[:, cc, :, to * 64:(to + 1) * 64], in_=pp[:].rearrange("p (n s) -> p s n", s=8)),
                                    reads=[ppk], writes=["ud"])
                            else:
                                S.op("scalar", lambda e, pp=pp, cc=cc, to=to: e.activation(
                                    out=ud[:, cc, :, to * 64:(to + 1) * 64], in_=pp[:].rearrange("p (n s) -> p s n", s=8),
                                    func=AF.Copy), reads=[ppk], writes=["ud"])
                    if (own or other) and to == 7 and sub >= 6:
                      for cc in range(4):
                        for gl in range(8):
                          S.dma("gpsimd", lambda e, cc=cc, gl=gl, segi=segi: e.dma_start(
                            out=Us.ap()[segi, cc * 8 + gl].rearrange("(s c) n -> c s n", c=16),
                            in_=ud[gl * 16:(gl + 1) * 16, cc, :, :]), reads=["ud"], writes=["Us"])
                ntl = len(tile_list)
                tile_part(0, "1a")
                tile_part(0, "1b")
                for ti in range(ntl):
                    if ti + 1 < ntl:
                        tile_part(ti + 1, "1a")
                    if ti >= 1:
                        tile_part(ti - 1, 2)
                    if ti + 1 < ntl:
                        tile_part(ti + 1, "1b")
                tile_part(ntl - 1, 2)

        if stage >= 2:
            S.barrier()
            with ExitStack() as pb:
                def sbb(name, shape, dt=F32):
                    return pb.enter_context(nc.sbuf_tensor(name, list(shape), dt))

                def psb(name, shape, dt=F32):
                    return pb.enter_context(nc.psum_tensor(name, list(shape), dt))
                maskT = sbb("maskT", [128, 640], BF16)
                S.dma("gpsimd", lambda e: e.dma_start(out=maskT[:], in_=consts.ap()[:, 640:1280]), writes=["maskT"])
                kT_all = sbb("kT_all", [128, 4, 4096], BF16)
                qT_all = sbb("qT_all", [128, 4, 2048], BF16)
                nchs = {1: 17, 4: 5, 16: 2}
                varr = {}
                for d in DILS:
                    for r in range(d):
                        varr[(d, r)] = sbb(f"va{d}_{r}", [128, nchs[d], 520], BF16)
                NSB = 4
                LOOK = 3
                pT = [sbb(f"pT{i}", [128, 512], BF16) for i in range(NSB)]
                pTm = [sbb(f"pTm{i}", [128, 512], BF16) for i in range(NSB)]
                accsb = [sbb(f"accsb{i}", [65, 1024]) for i in range(2)]
                acc16 = sbb("acc16", [65, 16, 128])
                rden = [sbb(f"rden{i}", [64, 512]) for i in range(2)]
                acc_ps = psb("ps_acc", [128, 1024])
                a16_ps = psb("ps_a16", [128, 512])
                s_ps = [psb(f"ps_s{i}", [128, 512]) for i in range(NSB)]
                bc_ps = psb("ps_bc", [64, 512])
                un = 0
                ev = 0
                for sbi in range(2):
                    S.dma("sync", lambda e, sbi=sbi: e.dma_start(
                        out=kT_all[:], in_=kT_s.ap()[:, 2048 * sbi:2048 * sbi + 4096].rearrange("(cc p) n -> p cc n", p=128)),
                        reads=["kT_s"], writes=["kT_all"])
                    S.dma("sync", lambda e, sbi=sbi: e.dma_start(
                        out=qT_all[:], in_=qT_s.ap()[:, 2048 * sbi:2048 * sbi + 2048].rearrange("(cc p) n -> p cc n", p=128)),
                        reads=["qT_s"], writes=["qT_all"])
                    for d in DILS:
                        nq = SEG // d // 128
                        ca0 = (nq // 2) * sbi
                        for r in range(d):
                            row0 = r + HALO - 64 * d + 128 * d * ca0
                            src = bass.AP(v_s, row0 * 520, [[d * 520, 128], [128 * d * 520, nchs[d]], [1, 520]])
                            S.dma("sync", lambda e, d=d, r=r, src=src: e.dma_start(out=varr[(d, r)][:], in_=src),
                                  reads=["v_s"], writes=[f"va{d}_{r}"])
                    for h in range(8):
                        hp, hc = (h % 2) * 64, h // 2

                        def mk_chunks(dsel, half):
                            out = []
                            for di, d in enumerate(DILS):
                                if d not in dsel:
                                    continue
                                nql = SEG // d // 128 // 2
                                if half is None:
                                    b_lo, b_hi = 0, nql
                                else:
                                    b_lo, b_hi = half * (nql // 2), (half + 1) * (nql // 2)
                                for r in range(d):
                                    for cal in range(b_lo, b_hi + 1):
                                        blocks = [bl for bl in (cal - 1, cal) if b_lo <= bl < b_hi]
                                        out.append((di, d, r, cal, blocks))
                            return out

                        def run_pipeline(chunks, pv_emit):
                            nonlocal un
                            groups, cur, used = [], [], 0
                            for ch in chunks:
                                w = 128 * len(ch[4])
                                if used + w > 512:
                                    groups.append(cur); cur, used = [], 0
                                cur.append((ch, used)); used += w
                            if cur:
                                groups.append(cur)
                            ginfo = []
                            for gi in range(len(groups) + LOOK):
                                if gi < len(groups):
                                    grp = groups[gi]
                                    sp = s_ps[un % NSB]; spk = f"ps_s{un % NSB}"
                                    pt = pT[un % NSB]; ptk = f"pT{un % NSB}"
                                    pm = pTm[un % NSB]; pmk = f"pTm{un % NSB}"
                                    un += 1
                                    ginfo.append((grp, pm, pmk))
                                    width = grp[-1][1] + 128 * len(grp[-1][0][4])
                                    for (di, d, r, cal, blocks), off in grp:
                                        nq = SEG // d // 128
                                        ca = (nq // 2) * sbi + cal
                                        kbase = r + HALO - 64 * d + 128 * d * ca - 2048 * sbi
                                        bq0 = (nq // 2) * sbi + blocks[0]
                                        qbase = r + 128 * d * bq0 - 2048 * sbi
                                        nqc = 128 * len(blocks)
                                        S.op("tensor", lambda e, sp=sp, off=off, nqc=nqc, kbase=kbase, qbase=qbase, d=d, hp=hp, hc=hc: e.matmul(
                                            out=sp[:, off:off + nqc], lhsT=kT_all[hp:hp + 64, hc, sl(kbase, 128, d)],
                                            rhs=qT_all[hp:hp + 64, hc, sl(qbase, nqc, d)], start=True, stop=True),
                                            reads=["kT_all", "qT_all"], writes=[spk])
                                    S.op("scalar", lambda e, sp=sp, pt=pt, width=width: e.activation(
                                        out=pt[:, 0:width], in_=sp[:, 0:width], func=AF.Exp, scale=0.125), reads=[spk], writes=[ptk])
                                    (_di, _d, _r, cal0, blocks0), _ = grp[0]
                                    m0 = 128 if blocks0[0] == cal0 - 1 else 0
                                    S.op("vector", lambda e, pt=pt, pm=pm, width=width, m0=m0: e.tensor_tensor(
                                        out=pm[:, 0:width], in0=pt[:, 0:width], in1=maskT[:, m0:m0 + width], op=ALU.mult),
                                        reads=[ptk, "maskT"], writes=[pmk])
                                if gi >= LOOK:
                                    grp, pm, pmk = ginfo[gi - LOOK]
                                    for ch, off in grp:
                                        pv_emit(ch, off, pm, pmk)

                        def pv16(ch, off, pm, pmk):
                            di, d, r, cal, blocks = ch
                            rr = r % 4
                            S.op("tensor", lambda e, pm=pm, off=off, rr=rr, r=r, cal=cal, h=h: e.matmul(
                                out=a16_ps[0:65, rr * 128:(rr + 1) * 128], lhsT=varr[(16, r)][:, cal, h * 65:(h + 1) * 65],
                                rhs=pm[:, off:off + 128], start=(rr == 0 and cal == 0), stop=False, skip_group_check=True),
                                reads=[pmk, f"va16_{r}"], writes=["ps_a16"])
                            if rr == 3 and cal == 1:
                                S.op("scalar", lambda e, r=r: e.activation(
                                    out=acc16[:, r - 3:r + 1, :].rearrange("p a b -> p (a b)"), in_=a16_ps[0:65, :], func=AF.Copy),
                                    reads=["ps_a16"], writes=["acc16"])
                        run_pipeline(mk_chunks((16,), None), pv16)

                        for half in range(2):
                            started = set()

                            def pv14(ch, off, pm, pmk, half=half, started=started):
                                di, d, r, cal, blocks = ch
                                lhsT = varr[(d, r)][:, cal, h * 65:(h + 1) * 65]
                                pieces = []
                                if d == 1:
                                    lb = [bl - 8 * half for bl in blocks]
                                    if len(lb) == 2 and lb[1] % 4 != 0:
                                        pieces.append((pm[:, off:off + 256], lb[0] // 4, acc_ps[0:65, lb[0] * 128:(lb[0] + 2) * 128]))
                                    else:
                                        for bi, bl in enumerate(lb):
                                            pieces.append((pm[:, off + 128 * bi:off + 128 * (bi + 1)], bl // 4,
                                                           acc_ps[0:65, bl * 128:(bl + 1) * 128]))
                                else:
                                    for bi, bl in enumerate(blocks):
                                        lbl = bl - 2 * half
                                        pieces.append((pm[:, off + 128 * bi:off + 128 * (bi + 1)], lbl,
                                                       acc_ps[0:65, sl(lbl * 512 + r, 128, 4)]))
                                for rhs, bank, o_ap in pieces:
                                    st = bank not in started
                                    started.add(bank)
                                    S.op("tensor", lambda e, lhsT=lhsT, rhs=rhs, o_ap=o_ap, st=st: e.matmul(
                                        out=o_ap, lhsT=lhsT, rhs=rhs, start=st, stop=False, skip_group_check=True),
                                        reads=[pmk, f"va{d}_{r}"], writes=["ps_acc"])
                            run_pipeline(mk_chunks((1, 4), half), pv14)
                            eb = ev % 2; ev += 1
                            S.op("scalar", lambda e, eb=eb: e.activation(out=accsb[eb][:], in_=acc_ps[0:65, :], func=AF.Copy),
                                 reads=["ps_acc"], writes=[f"accsb{eb}"])
                            S.op("vector", lambda e, eb=eb, half=half: e.tensor_tensor(
                                out=accsb[eb][:].rearrange("p (q r) -> p q r", r=16), in0=accsb[eb][:].rearrange("p (q r) -> p q r", r=16),
                                in1=acc16[:, :, 64 * half:64 * half + 64].rearrange("p r q -> p q r"), op=ALU.add),
                                reads=[f"accsb{eb}", "acc16"], writes=[f"accsb{eb}"])
                            for j in range(2):
                                rb = j % 2
                                S.op("tensor", lambda e, eb=eb, j=j: e.matmul(
                                    out=bc_ps[:], lhsT=ones_f[64:65, 0:64], rhs=accsb[eb][64:65, j * 512:(j + 1) * 512],
                                    start=True, stop=True), reads=["ones_f", f"accsb{eb}"], writes=["ps_bc"])
                                S.op("vector", lambda e, rb=rb: e.reciprocal(out=rden[rb][:], in_=bc_ps[:]),
                                     reads=["ps_bc"], writes=[f"rden{rb}"])
                                S.op("vector", lambda e, eb=eb, rb=rb, j=j: e.tensor_tensor(
                                    out=accsb[eb][0:64, j * 512:(j + 1) * 512], in0=accsb[eb][0:64, j * 512:(j + 1) * 512],
                                    in1=rden[rb][:], op=ALU.mult), reads=[f"accsb{eb}", f"rden{rb}"], writes=[f"accsb{eb}"])
                            S.dma("sync", lambda e, eb=eb, h=h, sbi=sbi, half=half: e.dma_start(
                                out=attn_s.ap()[h, :, sbi * 2048 + half * 1024:sbi * 2048 + (half + 1) * 1024], in_=accsb[eb][0:64, :]),
                                reads=[f"accsb{eb}"], writes=["attn_s"])

        if stage >= 3:
            S.barrier()
            TWO_PI = 6.283185307179586
            with ExitStack() as pc:
                def sbc(name, shape, dt=F32):
                    return pc.enter_context(nc.sbuf_tensor(name, list(shape), dt))
                kv = sbc("kv", [128, 72])
                maskfb = sbc("maskfb", [128, 2, 128])
                P8 = sbc("P8", [128, 4, 32, 9])
                Y0 = sbc("Y0", [128, 2, 32, 128])
                Xpb = sbc("Xpb", [128, 2, 32, 128], BF16)
                A64 = sbc("A64", [128, 2, 2, 32])
                CT = sbc("CT", [128, 2, 32, 16])
                Pw18 = sbc("Pw18", [128, 2, 32, 8])
                G2o = Y0[:].rearrange("p a q x -> p (a q x)").bitcast(BF16)[:, 0:3 * 64 * 2 * 32].rearrange(
                    "p (s n c q) -> p s n c q", s=3, n=64, c=2)
                S.dma("sync", lambda e: e.dma_start(out=kv[:], in_=consts2.ap()[:, 0:72]), writes=["kv"])
                S.dma("sync", lambda e: e.dma_start(out=maskfb[:], in_=consts2.ap()[:, 72:328].rearrange("p (a b) -> p a b", a=2)),
                      writes=["maskfb"])
                with ExitStack() as pc0:
                    lrli = pc0.enter_context(nc.sbuf_tensor("lrli", [128, 3, 32], F32))
                    th = pc0.enter_context(nc.sbuf_tensor("th", [128, 3, 32], F32))
                    PwR = pc0.enter_context(nc.sbuf_tensor("PwR", [128, 32, 72], F32))
                    PwI = pc0.enter_context(nc.sbuf_tensor("PwI", [128, 32, 72], F32))
                    Bb = pc0.enter_context(nc.sbuf_tensor("Bb", [128, 2, 32, 16], F32))
                    Xp = pc0.enter_context(nc.sbuf_tensor("Xp", [128, 2, 32, 128], F32))
                    lst = pc0.enter_context(nc.sbuf_tensor("lst", [32, 3, 128], F32))
                    ldt2 = pc0.enter_context(nc.sbuf_tensor("ldt2", [32, 2], F32))
                    cst = [pc0.enter_context(nc.sbuf_tensor(f"cst{i}", [128, 2, 64], F32)) for i in range(2)]
                    Braw = pc0.enter_context(nc.sbuf_tensor("Braw", [128, 2, 32, 16], F32))
                    zz = pc0.enter_context(nc.sbuf_tensor("zz", [128, 8, 32], F32))
                    tb1 = pc0.enter_context(nc.sbuf_tensor("tb1", [128, 32, 16], F32))
                    tb2 = pc0.enter_context(nc.sbuf_tensor("tb2", [128, 32, 16], F32))
                    ps_p = pc0.enter_context(nc.psum_tensor("ps_par", [128, 96], F32))
                    ps_c = [pc0.enter_context(nc.psum_tensor(f"ps_ct{i}", [128, 128], F32)) for i in range(2)]
                    S.dma("sync", lambda e: e.dma_start(out=lst[:, 0, :], in_=lam_re.ap()), writes=["lst"])
                    S.dma("sync", lambda e: e.dma_start(out=lst[:, 1, :], in_=lam_im.ap()), writes=["lst"])
                    S.dma("sync", lambda e: e.dma_start(out=ldt2[:], in_=log_dt.ap()), writes=["ldt2"])
                    S.op("vector", lambda e: e.tensor_copy(
                        out=lst[:, 2, :].rearrange("q (j p) -> q j p", j=2),
                        in_=bass.AP(ldt2, 0, [[2, 32], [1, 2], [0, 64]])), reads=["ldt2"], writes=["lst"])
                    for a in range(3):
                        S.op("tensor", lambda e, a=a: e.transpose(out=ps_p[:, a * 32:(a + 1) * 32], in_=lst[:, a, :],
                                                                 identity=ident_f[0:32, 0:32]),
                             reads=["lst", "ident_f"], writes=["ps_par"])
                    S.op("vector", lambda e: e.tensor_copy(out=lrli[:].rearrange("p a q -> p (a q)"), in_=ps_p[:]),
                         reads=["ps_par"], writes=["lrli"])
                    S.dma("sync", lambda e: e.dma_start(out=Braw[:, 0, :, :], in_=b_re.ap().rearrange("q p c -> p q c")),
                          writes=["Braw"])
                    S.dma("sync", lambda e: e.dma_start(out=Braw[:, 1, :, :], in_=b_im.ap().rearrange("q p c -> p q c")),
                          writes=["Braw"])
                    n = 0
                    for comp, csrc in enumerate((c_re, c_im)):
                        for qb in range(4):
                            cb = n % 2; n += 1
                            for ql in range(8):
                                q = qb * 8 + ql
                                S.dma("sync", lambda e, cb=cb, ql=ql, q=q, csrc=csrc: e.dma_start(
                                    out=cst[cb][ql * 16:(ql + 1) * 16, :, :],
                                    in_=csrc.ap()[2 * q:2 * q + 2].rearrange("j c p -> c j p")), writes=[f"cst{cb}"])
                            S.op("tensor", lambda e, cb=cb: e.transpose(
                                out=ps_c[cb][:], in_=cst[cb][:].rearrange("p j x -> p (j x)"), identity=ident_f[:]),
                                reads=[f"cst{cb}", "ident_f"], writes=[f"ps_ct{cb}"])
                            S.op("vector", lambda e, cb=cb, comp=comp, qb=qb: e.tensor_copy(
                                out=CT[:, comp, qb * 8:(qb + 1) * 8, :], in_=ps_c[cb][:].rearrange("p (q c) -> p q c", c=16)),
                                reads=[f"ps_ct{cb}"], writes=["CT"])
                    S.op("scalar", lambda e: e.activation(out=th[:, 2, :], in_=lrli[:, 2, :], func=AF.Exp),
                         reads=["lrli"], writes=["th"])
                    S.op("vector", lambda e: e.tensor_tensor(out=th[:, 0, :], in0=lrli[:, 0, :], in1=th[:, 2, :], op=ALU.mult),
                         reads=["lrli", "th"], writes=["th"])
                    S.op("vector", lambda e: e.tensor_tensor(out=th[:, 1, :], in0=lrli[:, 1, :], in1=th[:, 2, :], op=ALU.mult),
                         reads=["lrli", "th"], writes=["th"])
                    bt = pc0.enter_context(nc.sbuf_tensor("bt", [128, 12, 32], F32))
                    halfpi = pc0.enter_context(nc.sbuf_tensor("halfpi", [128, 1], F32))
                    cm = [pc0.enter_context(nc.sbuf_tensor(f"cm{i}", [128, 32, 32], F32)) for i in range(4)]
                    B_ = lambda i: bt[:, i, :]
                    def dvp(fn, reads, writes):
                        S.op("vector", fn, reads=reads, writes=writes)
                    dvp(lambda e: e.memset(halfpi[:], 1.5707963267948966), [], ["halfpi"])
                    S.op("scalar", lambda e: e.activation(out=B_(0), in_=th[:, 0, :], func=AF.Exp, scale=1.0 / 16), reads=["th"], writes=["bt0"])
                    S.op("scalar", lambda e: e.activation(out=B_(1), in_=th[:, 1, :], func=AF.Sin, scale=1.0 / 16), reads=["th"], writes=["bt1"])
                    S.op("scalar", lambda e: e.activation(out=B_(2), in_=th[:, 1, :], func=AF.Sin, scale=-1.0 / 16, bias=halfpi[:]),
                         reads=["th", "halfpi"], writes=["bt2"])
                    dvp(lambda e: e.tensor_tensor(out=B_(3), in0=B_(0), in1=B_(2), op=ALU.mult), ["bt0", "bt2"], ["bt3"])
                    dvp(lambda e: e.tensor_tensor(out=B_(4), in0=B_(0), in1=B_(1), op=ALU.mult), ["bt0", "bt1"], ["bt4"])
                    cur_r, cur_i = 3, 4
                    for k in range(4):
                        nr, ni = (5, 6) if cur_r == 3 else (3, 4)
                        dvp(lambda e, cr_=cur_r: e.tensor_tensor(out=B_(7), in0=B_(cr_), in1=B_(cr_), op=ALU.mult), [f"bt{cur_r}"], ["bt7"])
                        dvp(lambda e, ci_=cur_i: e.tensor_tensor(out=B_(8), in0=B_(ci_), in1=B_(ci_), op=ALU.mult), [f"bt{cur_i}"], ["bt8"])
                        dvp(lambda e, nr=nr: e.tensor_tensor(out=B_(nr), in0=B_(7), in1=B_(8), op=ALU.subtract), ["bt7", "bt8"], [f"bt{nr}"])
                        dvp(lambda e, cr_=cur_r, ci_=cur_i: e.tensor_tensor(out=B_(9), in0=B_(cr_), in1=B_(ci_), op=ALU.mult),
                            [f"bt{cur_r}", f"bt{cur_i}"], ["bt9"])
                        dvp(lambda e, ni=ni: e.tensor_scalar(out=B_(ni), in0=B_(9), scalar1=2.0, scalar2=None, op0=ALU.mult), ["bt9"], [f"bt{ni}"])
                        cur_r, cur_i = nr, ni
                    assert (cur_r, cur_i) == (3, 4)
                    dvp(lambda e: e.tensor_copy(out=PwR[:, :, 0], in_=B_(3)), ["bt3"], ["PwR"])
                    dvp(lambda e: e.tensor_copy(out=PwI[:, :, 0], in_=B_(4)), ["bt4"], ["PwI"])
                    dvp(lambda e: e.tensor_tensor(out=B_(7), in0=B_(3), in1=B_(3), op=ALU.mult), ["bt3"], ["bt7"])
                    dvp(lambda e: e.tensor_tensor(out=B_(8), in0=B_(4), in1=B_(4), op=ALU.mult), ["bt4"], ["bt8"])
                    dvp(lambda e: e.tensor_tensor(out=B_(7), in0=B_(7), in1=B_(8), op=ALU.add), ["bt7", "bt8"], ["bt7"])
                    dvp(lambda e: e.reciprocal(out=B_(7), in_=B_(7)), ["bt7"], ["bt7"])
                    dvp(lambda e: e.tensor_tensor(out=PwR[:, :, 71], in0=B_(3), in1=B_(7), op=ALU.mult), ["bt3", "bt7"], ["PwR"])
                    dvp(lambda e: e.scalar_tensor_tensor(out=PwI[:, :, 71], in0=B_(4), scalar=-1.0, in1=B_(7), op0=ALU.mult, op1=ALU.mult),
                        ["bt4", "bt7"], ["PwI"])

                    def cmul(o_sl, x_sl, y_sl, n):
                        t = [c_[:, :, 0:n] for c_ in cm]
                        xr, xi = PwR[:, :, x_sl], PwI[:, :, x_sl]
                        yr, yi = y_sl
                        dvp(lambda e: e.tensor_tensor(out=t[0], in0=xr, in1=yr, op=ALU.mult), ["PwR"], ["cm0"])
                        dvp(lambda e: e.tensor_tensor(out=t[1], in0=xi, in1=yi, op=ALU.mult), ["PwI"], ["cm1"])
                        dvp(lambda e: e.tensor_tensor(out=t[2], in0=xr, in1=yi, op=ALU.mult), ["PwR", "PwI"], ["cm2"])
                        dvp(lambda e: e.tensor_tensor(out=t[3], in0=xi, in1=yr, op=ALU.mult), ["PwR", "PwI"], ["cm3"])
                        dvp(lambda e: e.tensor_tensor(out=PwR[:, :, o_sl], in0=t[0], in1=t[1], op=ALU.subtract), ["cm0", "cm1"], ["PwR"])
                        dvp(lambda e: e.tensor_tensor(out=PwI[:, :, o_sl], in0=t[2], in1=t[3], op=ALU.add), ["cm2", "cm3"], ["PwI"])

                    def bc(kidx_, n):
                        return (bass.AP(PwR, kidx_, [[32 * 72, 128], [72, 32], [0, n]]), bass.AP(PwI, kidx_, [[32 * 72, 128], [72, 32], [0, n]]))
                    for k in range(2, 9):
                        cmul(slice(k - 1, k), slice(k - 2, k - 1), bc(0, 1), 1)
                    cmul(slice(8, 16), slice(0, 8), bc(7, 8), 8)
                    cmul(slice(16, 32), slice(0, 16), bc(15, 16), 16)
                    cmul(slice(32, 64), slice(0, 32), bc(31, 32), 32)
                    for k in range(2, 9):
                        cmul(slice(72 - k, 73 - k), slice(73 - k, 74 - k), bc(71, 1), 1)
                    S.op("vector", lambda e: e.tensor_copy(out=P8[:, 0, :, 1:9], in_=PwR[:, :, sl(7, 8, 8)]), reads=["PwR"], writes=["P8"])
                    S.op("vector", lambda e: e.tensor_copy(out=P8[:, 1, :, 1:9], in_=PwI[:, :, sl(7, 8, 8)]), reads=["PwI"], writes=["P8"])
                    S.op("vector", lambda e: e.tensor_scalar(out=P8[:, 2, :, 1:9], in0=PwR[:, :, sl(7, 8, 8)], scalar1=-1.0, scalar2=None,
                                                             op0=ALU.mult), reads=["PwR"], writes=["P8"])
                    S.op("vector", lambda e: e.tensor_scalar(out=P8[:, 3, :, 1:9], in0=PwI[:, :, sl(7, 8, 8)], scalar1=-1.0, scalar2=None,
                                                             op0=ALU.mult), reads=["PwI"], writes=["P8"])
                    lr, li = lrli[:, 0, :], lrli[:, 1, :]
                    ar, ai = PwR[:, :, 0], PwI[:, :, 0]
                    V = lambda i: zz[:, i, :]
                    def dv(fn, reads, writes):
                        S.op("vector", fn, reads=reads, writes=writes)
                    dv(lambda e: e.tensor_scalar(out=V(0), in0=ar, scalar1=-1.0, scalar2=None, op0=ALU.add), ["PwR"], ["zz"])
                    dv(lambda e: e.tensor_tensor(out=V(1), in0=lr, in1=lr, op=ALU.mult), ["lrli"], ["zz"])
                    dv(lambda e: e.tensor_tensor(out=V(2), in0=li, in1=li, op=ALU.mult), ["lrli"], ["zz"])
                    dv(lambda e: e.tensor_tensor(out=V(1), in0=V(1), in1=V(2), op=ALU.add), ["zz"], ["zz"])
                    dv(lambda e: e.reciprocal(out=V(6), in_=V(1)), ["zz"], ["zz"])
                    dv(lambda e: e.tensor_tensor(out=V(2), in0=V(0), in1=lr, op=ALU.mult), ["zz", "lrli"], ["zz"])
                    dv(lambda e: e.tensor_tensor(out=V(3), in0=ai, in1=li, op=ALU.mult), ["PwI", "lrli"], ["zz"])
                    dv(lambda e: e.tensor_tensor(out=V(2), in0=V(2), in1=V(3), op=ALU.add), ["zz"], ["zz"])
                    dv(lambda e: e.tensor_tensor(out=V(4), in0=V(2), in1=V(6), op=ALU.mult), ["zz"], ["zz"])
                    dv(lambda e: e.tensor_tensor(out=V(2), in0=ai, in1=lr, op=ALU.mult), ["PwI", "lrli"], ["zz"])
                    dv(lambda e: e.tensor_tensor(out=V(3), in0=V(0), in1=li, op=ALU.mult), ["zz", "lrli"], ["zz"])
                    dv(lambda e: e.tensor_tensor(out=V(2), in0=V(2), in1=V(3), op=ALU.subtract), ["zz"], ["zz"])
                    dv(lambda e: e.tensor_tensor(out=V(5), in0=V(2), in1=V(6), op=ALU.mult), ["zz"], ["zz"])
                    bc = lambda i: bass.AP(zz, i * 32, [[8 * 32, 128], [1, 32], [0, 16]])
                    dv(lambda e: e.tensor_tensor(out=tb1[:], in0=Braw[:, 0, :, :], in1=bc(4), op=ALU.mult), ["Braw", "zz"], ["tb1"])
                    dv(lambda e: e.tensor_tensor(out=tb2[:], in0=Braw[:, 1, :, :], in1=bc(5), op=ALU.mult), ["Braw", "zz"], ["tb2"])
                    dv(lambda e: e.tensor_tensor(out=Bb[:, 0, :, :], in0=tb1[:], in1=tb2[:], op=ALU.subtract), ["tb1", "tb2"], ["Bb"])
                    dv(lambda e: e.tensor_tensor(out=tb1[:], in0=Braw[:, 1, :, :], in1=bc(4), op=ALU.mult), ["Braw", "zz"], ["tb1"])
                    dv(lambda e: e.tensor_tensor(out=tb2[:], in0=Braw[:, 0, :, :], in1=bc(5), op=ALU.mult), ["Braw", "zz"], ["tb2"])
                    dv(lambda e: e.tensor_tensor(out=Bb[:, 1, :, :], in0=tb1[:], in1=tb2[:], op=ALU.add), ["tb1", "tb2"], ["Bb"])
                    def kidx(k):
                        return k - 1 if k >= 1 else 72 + k
                    t3 = pc0.enter_context(nc.sbuf_tensor("t3", [128, 16, 16], F32))
                    t4 = pc0.enter_context(nc.sbuf_tensor("t4", [128, 16, 16], F32))
                    for dr in range(2):
                        qs = slice(dr * 16, dr * 16 + 16)
                        for sidx in range(8):
                            tau = sidx if dr == 0 else 7 - sidx
                            for (dst, srcT, kk, sgn) in ((Xp, Bb, kidx(-tau - 1), None),):
                                pw_r = bass.AP(PwR, dr * 16 * 72 + kk, [[32 * 72, 128], [72, 16], [0, 16]])
                                pw_i = bass.AP(PwI, dr * 16 * 72 + kk, [[32 * 72, 128], [72, 16], [0, 16]])
                                o_re = dst[:, 0, qs, sidx * 16:(sidx + 1) * 16]
                                o_im = dst[:, 1, qs, sidx * 16:(sidx + 1) * 16]
                                sk = "Xp" if dst is Xp else "Y0"
                                tk = "Bb" if srcT is Bb else "CT"
                                dv(lambda e, srcT=srcT, pw_r=pw_r, qs=qs: e.tensor_tensor(out=t3[:], in0=srcT[:, 0, qs, :], in1=pw_r, op=ALU.mult), [tk, "PwR"], ["t3"])
                                dv(lambda e, srcT=srcT, pw_i=pw_i, qs=qs: e.tensor_tensor(out=t4[:], in0=srcT[:, 1, qs, :], in1=pw_i, op=ALU.mult), [tk, "PwI"], ["t4"])
                                dv(lambda e, o_re=o_re: e.tensor_tensor(out=o_re, in0=t3[:], in1=t4[:], op=ALU.subtract), ["t3", "t4"], [sk])
                                dv(lambda e, srcT=srcT, pw_i=pw_i, qs=qs: e.tensor_tensor(out=t3[:], in0=srcT[:, 0, qs, :], in1=pw_i, op=ALU.mult), [tk, "PwI"], ["t3"])
                                dv(lambda e, srcT=srcT, pw_r=pw_r, qs=qs: e.tensor_tensor(out=t4[:], in0=srcT[:, 1, qs, :], in1=pw_r, op=ALU.mult), [tk, "PwR"], ["t4"])
                                dv(lambda e, o_im=o_im: e.tensor_tensor(out=o_im, in0=t3[:], in1=t4[:], op=ALU.add), ["t3", "t4"], [sk])
                    S.op("gpsimd", lambda e: e.tensor_copy(out=Xpb[:].rearrange("p a q x -> p (a q x)"),
                                                           in_=Xp[:].rearrange("p a q x -> p (a q x)")), reads=["Xp"], writes=["Xpb"])
                    dv(lambda e: e.tensor_copy(out=Pw18[:, 0, :, :], in_=PwR[:, :, 0:8]), ["PwR"], ["Pw18"])
                    dv(lambda e: e.tensor_copy(out=Pw18[:, 1, :, :], in_=PwI[:, :, 0:8]), ["PwI"], ["Pw18"])
                    dv(lambda e: e.tensor_copy(out=A64[:, 0, 0, :], in_=P8[:, 0, :, 8]), ["P8"], ["A64"])
                    dv(lambda e: e.tensor_copy(out=A64[:, 0, 1, :], in_=P8[:, 0, :, 8]), ["P8"], ["A64"])
                    dv(lambda e: e.tensor_copy(out=A64[:, 1, 0, :], in_=P8[:, 3, :, 8]), ["P8"], ["A64"])
                    dv(lambda e: e.tensor_copy(out=A64[:, 1, 1, :], in_=P8[:, 1, :, 8]), ["P8"], ["A64"])
                    if debug:
                        S.dma("sync", lambda e: e.dma_start(out=dbg_s5.ap()[:, 0:2304], in_=PwR[:].rearrange("p q k -> p (q k)")), reads=["PwR"])
                        S.dma("sync", lambda e: e.dma_start(out=dbg_s5.ap()[:, 2304:4608], in_=PwI[:].rearrange("p q k -> p (q k)")), reads=["PwI"])
                        S.dma("sync", lambda e: e.dma_start(out=dbg_s5.ap()[:, 4608:4608 + 8192], in_=Xp[:].rearrange("p a q x -> p (a q x)")), reads=["Xp"])
                    S.barrier()
                G2 = sbc("G2", [128, 64, 2, 32])
                Hall = sbc("Hall", [128, 65, 2, 32])
                Hb = sbc("Hb", [128, 64, 2, 32], BF16)
                U_all = sbc("U_all", [128, 32, 512], BF16)
                Wtab = sbc("Wtab", [128, 64, 2, 32])
                flc = sbc("flc", [128, 6])
                S.dma("sync", lambda e: e.dma_start(out=flc[:], in_=flcol.ap()), writes=["flc"])
                wp_ = [sbc(f"wP{i}", [128, 2, 32]) for i in range(2)]
                wq_ = [sbc(f"wQ{i}", [128, 2, 32]) for i in range(2)]
                S.op("gpsimd", lambda e: e.memset(Wtab[:], 0.0), writes=["Wt_init"])
                S.op("gpsimd", lambda e: e.memset(Wtab[:, 63, 0, 0:16], 1.0), reads=["Wt_init"], writes=["Wt0"])
                S.op("gpsimd", lambda e: e.memset(Wtab[:, 0, 0, 16:32], 1.0), reads=["Wt_init"], writes=["Wt0"])
                WS = 64 * 64
                for st in range(63):
                    b2 = st % 2
                    cur = bass.AP(Wtab, (63 - st) * 64, [[WS, 128], [32, 2], [(2 * st - 63) * 64 + 16, 2], [1, 16]])
                    swp = bass.AP(Wtab, (63 - st) * 64 + 32, [[WS, 128], [-32, 2], [(2 * st - 63) * 64 + 16, 2], [1, 16]])
                    nxt = bass.AP(Wtab, (62 - st) * 64, [[WS, 128], [32, 2], [(2 * st - 61) * 64 + 16, 2], [1, 16]])
                    v4 = lambda t_: t_[:].rearrange("p c (d q) -> p c d q", d=2)
                    S.op("gpsimd", lambda e, cur=cur, b2=b2: e.tensor_tensor(out=v4(wp_[b2]), in0=cur, in1=v4(A64[:, 0, :, :]), op=ALU.mult)
                         if False else e.tensor_tensor(out=wp_[b2][:].rearrange("p c (d q) -> p c d q", d=2), in0=cur,
                                                       in1=A64[:, 0, :, :].rearrange("p c (d q) -> p c d q", d=2), op=ALU.mult),
                         reads=[f"Wt{st}", "A64"], writes=[f"wP{b2}"])
                    S.op("gpsimd", lambda e, swp=swp, b2=b2: e.tensor_tensor(
                        out=wq_[b2][:].rearrange("p c (d q) -> p c d q", d=2), in0=swp,
                        in1=A64[:, 1, :, :].rearrange("p c (d q) -> p c d q", d=2), op=ALU.mult),
                        reads=[f"Wt{st}", "A64"], writes=[f"wQ{b2}"])
                    S.op("gpsimd", lambda e, nxt=nxt, b2=b2: e.tensor_tensor(
                        out=nxt, in0=wp_[b2][:].rearrange("p c (d q) -> p c d q", d=2),
                        in1=wq_[b2][:].rearrange("p c (d q) -> p c d q", d=2), op=ALU.add),
                        reads=[f"wP{b2}", f"wQ{b2}"], writes=[f"Wt{st + 1}"])
                pcu = ExitStack()
                Uraw = [pcu.enter_context(nc.sbuf_tensor("Uraw0", [128, 8, 512], BF16))] * 2
                for gq in range(4):
                    ur = Uraw[0]; urk = "Uraw0"
                    S.dma("sync", lambda e, ur=ur, gq=gq: e.dma_start(out=ur[:], in_=Us.ap()[0, gq * 8:(gq + 1) * 8].rearrange("g p n -> p g n")),
                          reads=["Us"], writes=[urk])
                    S.op("vector", lambda e, ur=ur, gq=gq: e.tensor_copy(
                        out=U_all[:, gq * 8:(gq + 1) * 8, :].rearrange("p g (m n) -> p g m n", m=8),
                        in_=ur[:].rearrange("p g (n m) -> p g m n", m=8)), reads=[urk], writes=["U_all"])

                def rot_batch(src, sk, q, items, neg_im):
                    for ki, key_out, dstfn, (tr, ti), tkey in items:
                        cr = P8[:, 0, q, ki:ki + 1]; ci = P8[:, 1, q, ki:ki + 1]; nci = P8[:, 3, q, ki:ki + 1]
                        S.op("scalar", lambda e, tr=tr, cr=cr: e.activation(out=tr[:], in_=src[:, 0, q, :], func=AF.Copy, scale=cr),
                             reads=[sk, "P8"], writes=[tkey + "r"])
                        S.op("scalar", lambda e, ti=ti, ci=ci, nci=nci: e.activation(out=ti[:], in_=src[:, 0, q, :], func=AF.Copy,
                                                                                 scale=(nci if neg_im else ci)),
                             reads=[sk, "P8"], writes=[tkey + "i"])
                    for ki, key_out, dstfn, (tr, ti), tkey in items:
                        cr = P8[:, 0, q, ki:ki + 1]; ncr = P8[:, 2, q, ki:ki + 1]; nci = P8[:, 3, q, ki:ki + 1]
                        S.op("vector", lambda e, tr=tr, nci=nci, dstfn=dstfn: e.scalar_tensor_tensor(
                            out=dstfn(0), in0=src[:, 1, q, :], scalar=nci, in1=tr[:], op0=ALU.mult, op1=ALU.add),
                            reads=[sk, "P8", tkey + "r"], writes=[key_out])
                        S.op("vector", lambda e, ti=ti, cr=cr, ncr=ncr, dstfn=dstfn: e.scalar_tensor_tensor(
                            out=dstfn(1), in0=src[:, 1, q, :], scalar=(ncr if neg_im else cr), in1=ti[:], op0=ALU.mult, op1=ALU.add),
                            reads=[sk, "P8", tkey + "i"], writes=[key_out])

                def rot_tables(src, sk, q, ki, outs, neg_im, key_out, dstfn, tkey):
                    cr = P8[:, 0, q, ki:ki + 1]; ci = P8[:, 1, q, ki:ki + 1]
                    ncr = P8[:, 2, q, ki:ki + 1]; nci = P8[:, 3, q, ki:ki + 1]
                    tr, ti = outs
                    S.op("scalar", lambda e: e.activation(out=tr[:], in_=src[:, 0, q, :], func=AF.Copy, scale=cr),
                         reads=[sk, "P8"], writes=[tkey + "r"])
                    S.op("vector", lambda e: e.scalar_tensor_tensor(out=dstfn(0), in0=src[:, 1, q, :], scalar=nci, in1=tr[:],
                                                                    op0=ALU.mult, op1=ALU.add),
                         reads=[sk, "P8", tkey + "r"], writes=[key_out])
                    S.op("scalar", lambda e: e.activation(out=ti[:], in_=src[:, 0, q, :], func=AF.Copy,
                                                          scale=(nci if neg_im else ci)),
                         reads=[sk, "P8"], writes=[tkey + "i"])
                    S.op("vector", lambda e: e.scalar_tensor_tensor(out=dstfn(1), in0=src[:, 1, q, :],
                                                                    scalar=(ncr if neg_im else cr), in1=ti[:],
                                                                    op0=ALU.mult, op1=ALU.add),
                         reads=[sk, "P8", tkey + "i"], writes=[key_out])

                with ExitStack() as pc1:
                    XE = [pc1.enter_context(nc.sbuf_tensor(f"XE{i}", [128, 8, 2, 128], BF16)) for i in range(2)]
                    ET = [pc1.enter_context(nc.sbuf_tensor(f"ET{i}", [128, 8, 2, 128], BF16)) for i in range(2)]
                    trt = [[pc1.enter_context(nc.sbuf_tensor(f"trt{i}{j}", [128, 128], BF16)) for j in range(2)] for i in range(6)]
                    tr_ps = [pc1.enter_context(nc.psum_tensor(f"ps_tr{i}", [128, 2, 2, 128], BF16)) for i in range(2)]
                    g_ps = [pc1.enter_context(nc.psum_tensor(f"ps_g{i}", [128, 2, 4, 64], F32)) for i in range(2)]
                    Ucat = [pc1.enter_context(nc.sbuf_tensor(f"Ucat{i}", [128, 4, 2, 512], BF16)) for i in range(2)]
                    Ucr = Uraw[0][:].rearrange("p g n -> p (g n)").rearrange("p (s g n) -> p s g n", s=4, g=2)
                    tn = 0
                    trn = 0
                    for qq in range(16):
                        ub = qq % 2
                        for sg_ in range(4):
                            S.dma("sync", lambda e, sg_=sg_, qq=qq: e.dma_start(
                                out=Ucr[:, sg_, :, :], in_=Us.ap()[sg_, 2 * qq:2 * qq + 2].rearrange("g p n -> p g n")),
                                reads=["Us"], writes=["Uraw0"])
                        for sg_ in range(4):
                            if sg_ % 2 == 0:
                                S.op("vector", lambda e, ub=ub, sg_=sg_: e.tensor_copy(
                                    out=Ucat[ub][:, sg_, :, :].rearrange("p g (m n) -> p g m n", m=8),
                                    in_=Ucr[:, sg_, :, :].rearrange("p g (n m) -> p g m n", m=8)), reads=["Uraw0"], writes=[f"Ucat{ub}"])
                            else:
                                for g2_ in range(2):
                                    S.op("scalar", lambda e, ub=ub, sg_=sg_, g2_=g2_: e.activation(
                                        out=Ucat[ub][:, sg_, g2_, :].rearrange("p (m n) -> p m n", m=8),
                                        in_=Ucr[:, sg_, g2_, :].rearrange("p (n m) -> p m n", m=8), func=AF.Copy),
                                        reads=["Uraw0"], writes=[f"Ucat{ub}"])
                        for dr in range(2):
                            q = dr * 16 + qq
                            xb = q % 2 if False else dr
                            for m4 in range(0, 8, 4):
                                items = []
                                for m in range(m4, m4 + 4):
                                    mu = m if dr == 0 else 7 - m
                                    outs = trt[tn % 6]; tk_ = f"trt{tn % 6}"; tn += 1
                                    items.append((8 - mu, f"XE{xb}_{m}", (lambda comp, xb=xb, m=m: XE[xb][:, m, comp, :]), outs, tk_))
                                rot_batch(Xpb, "Xpb", q, items, False)
                            for m2 in range(0, 8, 2):
                                tp = tr_ps[trn % 2]; tpk = f"ps_tr{trn % 2}"; trn += 1
                                for mm in range(2):
                                    for comp in range(2):
                                        S.op("tensor", lambda e, tp=tp, xb=xb, m=m2 + mm, mm=mm, comp=comp: e.transpose(
                                            out=tp[:, mm, comp, :], in_=XE[xb][:, m, comp, :], identity=ident_b[:]),
                                            reads=[f"XE{xb}_{m2 + mm}", "ident_b"], writes=[tpk])
                                S.op("vector", lambda e, tp=tp, xb=xb, m2=m2: e.tensor_copy(
                                    out=ET[xb][:, m2:m2 + 2, :, :], in_=tp[:]), reads=[tpk], writes=[f"ET{xb}"])
                            gp = g_ps[dr]; gpk = f"ps_g{dr}"
                            for j2 in range(2):
                                for comp in range(2):
                                    for m in range(8):
                                        S.op("tensor", lambda e, gp=gp, xb=xb, j2=j2, comp=comp, m=m, ub=ub: e.matmul(
                                            out=gp[64 * j2:64 * j2 + 64, comp, :, :], lhsT=ET[xb][:, m, comp, 64 * j2:64 * j2 + 64],
                                            rhs=Ucat[ub][:, :, j2, m * 64:(m + 1) * 64], start=(m == 0), stop=(m == 7)),
                                            reads=[f"ET{xb}", f"Ucat{ub}"], writes=[gpk])
                            S.op("scalar", lambda e, gp=gp, q=q: e.activation(
                                out=G2[:, :, :, q].rearrange("p n c -> p c n"), in_=gp[:, :, 0, :], func=AF.Copy),
                                reads=[gpk], writes=["G2"])
                            S.op("scalar", lambda e, gp=gp, q=q: e.activation(
                                out=G2o[:, :, :, :, q].rearrange("p s n c -> p c s n"), in_=gp[:, :, 1:4, :], func=AF.Copy),
                                reads=[gpk], writes=["G2o"])
                    S.barrier()
                pcu.close()
                cacc = sbc("cacc", [128, 2, 32])
                with ExitStack() as pcc:
                    def sbx(name, shape, dt=F32):
                        return pcc.enter_context(nc.sbuf_tensor(name, list(shape), dt))
                    T1 = sbx("cT1", [128, 64, 32], BF16); T2 = sbx("cT2", [128, 64, 32], BF16)
                    Wtb = sbx("Wtb", [128, 64, 2, 32], BF16)
                    S.op("vector", lambda e: e.tensor_copy(out=Wtb[:].rearrange("p n c q -> p (n c q)"),
                                                           in_=Wtab[:].rearrange("p n c q -> p (n c q)")),
                         reads=[f"Wt{i}" for i in range(64)], writes=["Wtb"])
                    Sall = sbx("Sall", [128, 3, 2, 32])
                    Asq = [sbx(f"Asq{i}", [128, 2, 32]) for i in range(2)]
                    ctm = [sbx(f"ctm{i}", [128, 32]) for i in range(4)]
                    ctt = sbx("ctt", [128, 2, 32])
                    wt_keys = [f"Wt{i}" for i in range(64)]
                    Wc = lambda c: Wtb[:, :, c, :]
                    for i in range(3):
                        Gc = lambda c, i=i: G2o[:, i, :, c, :]
                        for comp, (wa, ga, wb_, gb_, op) in enumerate(((0, 0, 1, 1, ALU.subtract), (0, 1, 1, 0, ALU.add))):
                            S.op("vector", lambda e, wa=wa, ga=ga, Gc=Gc: e.tensor_tensor(out=T1[:], in0=Wc(wa), in1=Gc(ga), op=ALU.mult),
                                 reads=["Wtb", "G2o"], writes=["cT1"])
                            S.op("vector", lambda e, wb_=wb_, gb_=gb_, Gc=Gc: e.tensor_tensor(out=T2[:], in0=Wc(wb_), in1=Gc(gb_), op=ALU.mult),
                                 reads=["Wtb", "G2o"], writes=["cT2"])
                            S.op("vector", lambda e, op=op: e.tensor_tensor(out=T1[:], in0=T1[:], in1=T2[:], op=op),
                                 reads=["cT1", "cT2"], writes=["cT1"])
                            S.op("vector", lambda e, i=i, comp=comp: e.tensor_reduce(
                                out=Sall[:, i, comp, :], in_=T1[:].rearrange("p n q -> p q n"), axis=AX.X, op=ALU.add),
                                reads=["cT1"], writes=["Sall"])
                    S.op("vector", lambda e: e.tensor_copy(out=Asq[0][:, 0, :], in_=A64[:, 0, 0, :]), reads=["A64"], writes=["Asq0"])
                    S.op("vector", lambda e: e.tensor_copy(out=Asq[0][:, 1, :], in_=A64[:, 1, 1, :]), reads=["A64"], writes=["Asq0"])
                    for k in range(6):
                        a_, b_ = Asq[k % 2], Asq[(k + 1) % 2]
                        ak, bk = f"Asq{k % 2}", f"Asq{(k + 1) % 2}"
                        S.op("vector", lambda e, a_=a_: e.tensor_tensor(out=ctm[0][:], in0=a_[:, 0, :], in1=a_[:, 0, :], op=ALU.mult), reads=[ak], writes=["ctm0"])
                        S.op("vector", lambda e, a_=a_: e.tensor_tensor(out=ctm[1][:], in0=a_[:, 1, :], in1=a_[:, 1, :], op=ALU.mult), reads=[ak], writes=["ctm1"])
                        S.op("vector", lambda e, b_=b_: e.tensor_tensor(out=b_[:, 0, :], in0=ctm[0][:], in1=ctm[1][:], op=ALU.subtract), reads=["ctm0", "ctm1"], writes=[bk])
                        S.op("vector", lambda e, a_=a_: e.tensor_tensor(out=ctm[2][:], in0=a_[:, 0, :], in1=a_[:, 1, :], op=ALU.mult), reads=[ak], writes=["ctm2"])
                        S.op("vector", lambda e, b_=b_: e.tensor_scalar(out=b_[:, 1, :], in0=ctm[2][:], scalar1=2.0, scalar2=None, op0=ALU.mult), reads=["ctm2"], writes=[bk])
                    A4k = Asq[0]; A4kk = "Asq0"
                    S.op("vector", lambda e: e.memset(cacc[:], 0.0), writes=["cacc"])
                    for half, order, fbase in ((slice(0, 16), (0, 1, 2), 0), (slice(16, 32), (2, 1, 0), 3)):
                        for i in order:
                            S.op("vector", lambda e, half=half: e.tensor_tensor(out=ctm[0][:, half], in0=A4k[:, 0, half], in1=cacc[:, 0, half], op=ALU.mult), reads=[A4kk, "cacc"], writes=["ctm0"])
                            S.op("vector", lambda e, half=half: e.tensor_tensor(out=ctm[1][:, half], in0=A4k[:, 1, half], in1=cacc[:, 1, half], op=ALU.mult), reads=[A4kk, "cacc"], writes=["ctm1"])
                            S.op("vector", lambda e, half=half: e.tensor_tensor(out=ctm[2][:, half], in0=A4k[:, 0, half], in1=cacc[:, 1, half], op=ALU.mult), reads=[A4kk, "cacc"], writes=["ctm2"])
                            S.op("vector", lambda e, half=half: e.tensor_tensor(out=ctm[3][:, half], in0=A4k[:, 1, half], in1=cacc[:, 0, half], op=ALU.mult), reads=[A4kk, "cacc"], writes=["ctm3"])
                            S.op("vector", lambda e, half=half: e.tensor_tensor(out=ctt[:, 0, half], in0=ctm[0][:, half], in1=ctm[1][:, half], op=ALU.subtract), reads=["ctm0", "ctm1"], writes=["ctt"])
                            S.op("vector", lambda e, half=half: e.tensor_tensor(out=ctt[:, 1, half], in0=ctm[2][:, half], in1=ctm[3][:, half], op=ALU.add), reads=["ctm2", "ctm3"], writes=["ctt"])
                            S.op("vector", lambda e, half=half, i=i: e.tensor_tensor(out=ctt[:, :, half], in0=ctt[:, :, half], in1=Sall[:, i, :, half], op=ALU.add), reads=["ctt", "Sall"], writes=["ctt"])
                            S.op("vector", lambda e, half=half: e.tensor_tensor(out=ctt[:, :, half], in0=ctt[:, :, half], in1=cacc[:, :, half], op=ALU.subtract), reads=["ctt", "cacc"], writes=["ctt"])
                            S.op("vector", lambda e, half=half, i=i, fbase=fbase: e.scalar_tensor_tensor(
                                out=cacc[:, :, half], in0=ctt[:, :, half], scalar=flc[:, fbase + i:fbase + i + 1], in1=cacc[:, :, half],
                                op0=ALU.mult, op1=ALU.add), reads=["ctt", "cacc", "flc"], writes=["cacc"])
                    S.barrier()
                with ExitStack() as pcy:
                    t3y_ = pcy.enter_context(nc.sbuf_tensor("t3y", [128, 16, 16], F32))
                    t4y_ = pcy.enter_context(nc.sbuf_tensor("t4y", [128, 16, 16], F32))
                    def dvy(fn, reads, writes):
                        S.op("vector", fn, reads=reads, writes=writes)
                    for dr in range(2):
                        qs = slice(dr * 16, dr * 16 + 16)
                        for sidx in range(8):
                            tau = sidx if dr == 0 else 7 - sidx
                            pw_r = bass.AP(Pw18, (0 * 32 + dr * 16) * 8 + tau, [[2 * 32 * 8, 128], [8, 16], [0, 16]])
                            pw_i = bass.AP(Pw18, (1 * 32 + dr * 16) * 8 + tau, [[2 * 32 * 8, 128], [8, 16], [0, 16]])
                            o_re = Y0[:, 0, qs, sidx * 16:(sidx + 1) * 16]
                            o_im = Y0[:, 1, qs, sidx * 16:(sidx + 1) * 16]
                            dvy(lambda e, pw_r=pw_r, qs=qs: e.tensor_tensor(out=t3y_[:], in0=CT[:, 0, qs, :], in1=pw_r, op=ALU.mult), ["CT", "Pw18"], ["t3y"])
                            dvy(lambda e, pw_i=pw_i, qs=qs: e.tensor_tensor(out=t4y_[:], in0=CT[:, 1, qs, :], in1=pw_i, op=ALU.mult), ["CT", "Pw18"], ["t4y"])
                            dvy(lambda e, o_re=o_re: e.tensor_tensor(out=o_re, in0=t3y_[:], in1=t4y_[:], op=ALU.subtract), ["t3y", "t4y"], ["Y0"])
                            dvy(lambda e, pw_i=pw_i, qs=qs: e.tensor_tensor(out=t3y_[:], in0=CT[:, 0, qs, :], in1=pw_i, op=ALU.mult), ["CT", "Pw18"], ["t3y"])
                            dvy(lambda e, pw_r=pw_r, qs=qs: e.tensor_tensor(out=t4y_[:], in0=CT[:, 1, qs, :], in1=pw_r, op=ALU.mult), ["CT", "Pw18"], ["t4y"])
                            dvy(lambda e, o_im=o_im: e.tensor_tensor(out=o_im, in0=t3y_[:], in1=t4y_[:], op=ALU.add), ["t3y", "t4y"], ["Y0"])
                with ExitStack() as pc2:
                    hp_ = [pc2.enter_context(nc.sbuf_tensor(f"hP{i}", [128, 2, 32], F32)) for i in range(2)]
                    hq_ = [pc2.enter_context(nc.sbuf_tensor(f"hQ{i}", [128, 2, 32], F32)) for i in range(2)]
                    S.op("vector", lambda e: e.tensor_copy(out=Hall[:, 0, :, :], in_=cacc[:]), reads=["cacc"], writes=["H0"])
                    for st in range(64):
                        b2 = st % 2
                        hcur = Hall[:, st, :, :]
                        hswap = bass.AP(Hall, st * 64 + 32, [[65 * 64, 128], [-32, 2], [1, 32]])
                        gcat = bass.AP(G2, st * 64, [[64 * 64, 128], [32, 2], [(63 - 2 * st) * 64 + 16, 2], [1, 16]])
                        S.op("vector", lambda e, hcur=hcur, b2=b2: e.tensor_tensor(out=hp_[b2][:], in0=hcur, in1=A64[:, 0, :, :], op=ALU.mult),
                             reads=[f"H{st}", "A64"], writes=[f"hP{b2}"])
                        S.op("vector", lambda e, hswap=hswap, b2=b2: e.tensor_tensor(out=hq_[b2][:], in0=hswap, in1=A64[:, 1, :, :], op=ALU.mult),
                             reads=[f"H{st}", "A64"], writes=[f"hQ{b2}"])
                        S.op("vector", lambda e, b2=b2: e.tensor_tensor(out=hp_[b2][:], in0=hp_[b2][:], in1=hq_[b2][:], op=ALU.add),
                             reads=[f"hP{b2}", f"hQ{b2}"], writes=[f"hP{b2}"])
                        S.op("vector", lambda e, b2=b2, gcat=gcat, st=st: e.tensor_tensor(
                            out=Hall[:, st + 1, :, :].rearrange("p c (d q) -> p c d q", d=2), in0=hp_[b2][:].rearrange("p c (d q) -> p c d q", d=2),
                            in1=gcat, op=ALU.add), reads=[f"hP{b2}", "G2"], writes=[f"H{st + 1}"])
                    S.op("vector", lambda e: e.tensor_copy(out=Hb[:].rearrange("p n c q -> p (n c q)"),
                                                           in_=Hall[:, 0:64, :, :].rearrange("p n c q -> p (n c q)")),
                         reads=[f"H{i}" for i in range(65)], writes=["Hb"])
                    S.barrier()
                if debug:
                    S.dma("sync", lambda e: e.dma_start(out=dbg_s5.ap()[:, 20992:20992 + 4096], in_=G2[:].rearrange("p n c q -> p (n c q)")), reads=["G2"])
                    S.dma("sync", lambda e: e.dma_start(out=dbg_s5.ap()[:, 25088:25088 + 4160], in_=Hall[:].rearrange("p n c q -> p (n c q)")), reads=["Hb"])
                with ExitStack() as pc3:
                    YT = [[pc3.enter_context(nc.sbuf_tensor(f"YT{d_}{i}", [128, 8, 2, 128], BF16)) for i in range(2)] for d_ in range(2)]
                    trt = [[pc3.enter_context(nc.sbuf_tensor(f"trs{i}{j}", [128, 128], F32)) for j in range(2)] for i in range(7)]
                    Tsb = [[[pc3.enter_context(nc.sbuf_tensor(f"T{d_}{j2}{i}", [128, 8, 128], BF16)) for i in range(2)]
                            for j2 in range(2)] for d_ in range(2)]
                    ysb = [pc3.enter_context(nc.sbuf_tensor(f"ysb{i}", [128, 512], F32)) for i in range(2)]
                    t_ps = [pc3.enter_context(nc.psum_tensor(f"ps_T{i}", [128, 1024], F32)) for i in range(2)]
                    y_ps = [pc3.enter_context(nc.psum_tensor(f"ps_y{i}", [128, 512], F32)) for i in range(2)]
                    tn = 0; tpn = 0; yn = 0
                    for qq in range(16):
                        pb_ = qq % 2
                        for dr in range(2):
                            q = dr * 16 + qq
                            items = []
                            for m in range(8):
                                mu = m if dr == 0 else 7 - m
                                outs = trt[tn % 7]; tk_ = f"trs{tn % 7}"; tn += 1
                                yb = YT[dr][pb_]
                                if mu == 0:
                                    S.op("scalar", lambda e, yb=yb, m=m, q=q: e.activation(out=yb[:, m, 0, :], in_=Y0[:, 0, q, :], func=AF.Copy),
                                         reads=["Y0"], writes=[f"YT{dr}{pb_}_{m}"])
                                    S.op("vector", lambda e, yb=yb, m=m, q=q: e.tensor_scalar(out=yb[:, m, 1, :], in0=Y0[:, 1, q, :], scalar1=-1.0,
                                                                                            scalar2=None, op0=ALU.mult),
                                         reads=["Y0"], writes=[f"YT{dr}{pb_}_{m}"])
                                else:
                                    items.append((mu, f"YT{dr}{pb_}_{m}", (lambda comp, yb=yb, m=m: yb[:, m, comp, :]), outs, tk_))
                            rot_batch(Y0, "Y0", q, items, True)
                            for j2 in range(2):
                                tp = t_ps[tpn % 2]; tpk = f"ps_T{tpn % 2}"; tpn += 1
                                hs = slice(64 * j2, 64 * j2 + 64)
                                for half in range(2):
                                    for comp in range(2):
                                        S.op("tensor", lambda e, tp=tp, hs=hs, half=half, comp=comp, q=q, yb=yb: e.matmul(
                                            out=tp[:, half * 512:(half + 1) * 512].rearrange("p (m x) -> p m x", m=4),
                                            lhsT=Xpb[hs, comp, q, :], rhs=yb[hs, half * 4:(half + 1) * 4, comp, :],
                                            start=(comp == 0), stop=(comp == 1)),
                                            reads=["Xpb"] + [f"YT{dr}{pb_}_{m}" for m in range(8)], writes=[tpk])
                                tsb = Tsb[dr][j2][pb_]; tsk = f"T{dr}{j2}{pb_}"
                                m0 = 0 if dr == 0 else 7
                                S.op("scalar", lambda e, tp=tp, tsb=tsb: e.activation(
                                    out=tsb[:].rearrange("p m x -> p (m x)"), in_=tp[:], func=AF.Copy), reads=[tpk], writes=[tsk])
                                S.op("vector", lambda e, tp=tp, tsb=tsb, m0=m0, dr=dr: e.tensor_tensor(
                                    out=tsb[:, m0, :], in0=tp[:, m0 * 128:(m0 + 1) * 128], in1=maskfb[:, dr, :], op=ALU.mult),
                                    reads=[tpk, "maskfb"], writes=[tsk])
                        for j2 in range(2):
                            g = qq * 2 + j2
                            hs = slice(64 * j2, 64 * j2 + 64)
                            yp = y_ps[yn % 2]; ypk = f"ps_y{yn % 2}"; yb_ = ysb[yn % 2]; ybk = f"ysb{yn % 2}"; yn += 1
                            u1 = U_all[:, g, :]
                            first = True
                            for dr in range(2):
                                tsb = Tsb[dr][j2][pb_]; tsk = f"T{dr}{j2}{pb_}"
                                for dl in range(8):
                                    blk = dl if dr == 0 else 7 - dl
                                    if dr == 0:
                                        o_ap, r_ap = yp[:, dl * 64:512], u1[:, 0:(8 - dl) * 64]
                                    else:
                                        o_ap, r_ap = yp[:, 0:(8 - dl) * 64], u1[:, dl * 64:512]
                                    S.op("tensor", lambda e, o_ap=o_ap, r_ap=r_ap, tsb=tsb, blk=blk, first=first: e.matmul(
                                        out=o_ap, lhsT=tsb[:, blk, :], rhs=r_ap, start=first, stop=False, skip_group_check=True),
                                        reads=[tsk, "U_all"], writes=[ypk])
                                    first = False
                            for dr in range(2):
                                q = dr * 16 + qq
                                yb = YT[dr][pb_]
                                for m in range(8):
                                    for comp in range(2):
                                        if dr == 0:
                                            rhs = Hb[hs, :, comp, q]
                                        else:
                                            rhs = bass.AP(Hb, 64 * j2 * (64 * 64) + 63 * 64 + comp * 32 + q, [[64 * 64, 64], [-64, 64]])
                                        last = (dr == 1 and m == 7 and comp == 1)
                                        S.op("tensor", lambda e, yp=yp, m=m, hs=hs, yb=yb, comp=comp, rhs=rhs, last=last: e.matmul(
                                            out=yp[:, m * 64:(m + 1) * 64], lhsT=yb[hs, m, comp, :], rhs=rhs, start=False, stop=last,
                                            skip_group_check=True),
                                            reads=[f"YT{dr}{pb_}_{m}", "Hb"], writes=[ypk])
                            S.op("scalar", lambda e, yp=yp, yb_=yb_: e.activation(
                                out=yb_[:].rearrange("p (n m) -> p m n", m=8), in_=yp[:].rearrange("p (m n) -> p m n", m=8), func=AF.Copy),
                                reads=[ypk], writes=[ybk])
                            for tq in range(8):
                                S.dma("sync", lambda e, g=g, yb_=yb_, tq=tq: e.dma_start(
                                    out=y_s.ap()[16 * g:16 * g + 16, tq, :], in_=yb_[tq * 16:(tq + 1) * 16, :]),
                                    reads=[ybk], writes=["y_s"])

        if stage >= 4:
            S.barrier()
            with ExitStack() as pd:
                def sbd(name, shape, dt=F32):
                    return pd.enter_context(nc.sbuf_tensor(name, list(shape), dt))

                def psd(name, shape, dt=F32):
                    return pd.enter_context(nc.psum_tensor(name, list(shape), dt))
                udd = sbd("udD", [128, 4, 8, 512], BF16)
                yall = sbd("yall", [128, 4, 8, 512])
                wglu_b = sbd("wglu_b", [128, 4, 512], BF16)
                y2 = [sbd(f"y2{i}", [128, 8, 64]) for i in range(4)]
                g32 = [sbd(f"g32{i}", [128, 4, 512]) for i in range(2)]
                gb = [sbd(f"gb{i}", [128, 4, 512], BF16) for i in range(2)]
                sg = [sbd(f"sg{i}", [128, 512]) for i in range(4)]
                o32 = [sbd(f"o32{i}", [128, 4, 512]) for i in range(2)]
                sq = [sbd(f"sq{i}", [128, 512]) for i in range(4)]
                rs = [sbd(f"rs{i}", [128, 512]) for i in range(2)]
                sn = [sbd(f"sn{i}", [128, 4, 512], BF16) for i in range(2)]
                z_ps = [psd(f"ps_z{i}", [128, 512]) for i in range(4)]
                ss_ps = [psd(f"ps_ss{i}", [128, 512]) for i in range(2)]
                S.dma("gpsimd", lambda e: e.dma_start(out=wglu_b[:], in_=w_glu.ap().rearrange("(kc p) n -> p kc n", p=128)),
                      writes=["wglu_b"])
                for cc in range(4):
                    for gl in range(8):
                        S.dma("sync", lambda e, cc=cc, gl=gl: e.dma_start(
                            out=udd[gl * 16:(gl + 1) * 16, cc, :, :],
                            in_=Us.ap()[0, cc * 8 + gl].rearrange("(s c) n -> c s n", c=16)), reads=["Us"], writes=["udD"])
                    S.dma("sync", lambda e, cc=cc: e.dma_start(out=yall[:, cc, :, :], in_=y_s.ap()[cc * 128:(cc + 1) * 128]),
                          reads=["y_s"], writes=[f"yall{cc}"])
                yn = 0
                def d_part(tt, part):
                    nonlocal yn
                    b = tt % 2
                    nsl = slice(tt * 64, tt * 64 + 64)
                    if part == 1:
                        for cc in range(4):
                            S.op("vector", lambda e, cc=cc, nsl=nsl: e.scalar_tensor_tensor(
                                out=y2[cc][:], in0=udd[:, cc, :, nsl], scalar=cols[:, C_DS + cc:C_DS + cc + 1], in1=yall[:, cc, :, nsl],
                                op0=ALU.mult, op1=ALU.add), reads=["udD", f"yall{cc}", "cols"], writes=[f"y2{cc}"])
                        for cc in range(4):
                            S.op("scalar", lambda e, cc=cc, b=b: e.activation(
                                out=g32[b][:, cc, :].rearrange("p (n s) -> p s n", s=8), in_=y2[cc][:], func=AF.Gelu_apprx_tanh),
                                reads=[f"y2{cc}"], writes=[f"g32{b}_{cc}"])
                        for cc in range(4):
                            S.op("vector", lambda e, cc=cc, b=b: e.tensor_copy(out=gb[b][:, cc, :], in_=g32[b][:, cc, :]),
                                 reads=[f"g32{b}_{cc}"], writes=[f"gb{b}_{cc}"])
                        return
                    sp = ss_ps[b]; spk = f"ps_ss{b}"
                    for jc in range(4):
                        zp = z_ps[jc]; zpk = f"ps_z{jc}"
                        for kc in range(4):
                            S.op("tensor", lambda e, zp=zp, kc=kc, jc=jc, b=b: e.matmul(
                                out=zp[:], lhsT=wglu_b[:, kc, jc * 128:(jc + 1) * 128], rhs=gb[b][:, kc, :],
                                start=(kc == 0), stop=(kc == 3)), reads=["wglu_b", f"gb{b}_{kc}"], writes=[zpk])
                    for jc in range(4):
                        S.op("scalar", lambda e, jc=jc: e.activation(
                            out=sg[jc][:], in_=z_ps[jc][:], func=AF.Sigmoid, bias=cols[:, C_BG + jc:C_BG + jc + 1]),
                            reads=[f"ps_z{jc}", "cols"], writes=[f"sg{jc}"])
                    for jc in range(4):
                        S.op("vector", lambda e, jc=jc, b=b: e.tensor_tensor(
                            out=o32[b][:, jc, :], in0=g32[b][:, jc, :], in1=sg[jc][:], op=ALU.mult),
                            reads=[f"g32{b}_{jc}", f"sg{jc}"], writes=[f"o32{b}_{jc}"])
                    for jc in range(4):
                        S.op("scalar", lambda e, jc=jc, b=b: e.activation(out=sq[jc][:], in_=o32[b][:, jc, :], func=AF.Square),
                             reads=[f"o32{b}_{jc}"], writes=[f"sq{jc}"])
                    for jc in range(4):
                        S.op("tensor", lambda e, sp=sp, jc=jc: e.matmul(out=sp[:], lhsT=ones_f[:], rhs=sq[jc][:],
                                                                       start=(jc == 0), stop=(jc == 3)),
                             reads=["ones_f", f"sq{jc}"], writes=[spk])
                    S.op("vector", lambda e, sp=sp, b=b: e.tensor_scalar(out=rs[b][:], in0=sp[:], scalar1=1.0 / 512, scalar2=EPS,
                                                                         op0=ALU.mult, op1=ALU.add), reads=[spk], writes=[f"rs{b}"])
                    S.op("scalar", lambda e, b=b: e.activation(out=rs[b][:], in_=rs[b][:], func=AF.Sqrt),
                         reads=[f"rs{b}"], writes=[f"rs{b}"])
                    S.op("vector", lambda e, b=b: e.reciprocal(out=rs[b][:], in_=rs[b][:]), reads=[f"rs{b}"], writes=[f"rs{b}"])
                    for jc in range(4):
                        S.op("vector", lambda e, jc=jc, b=b: e.scalar_tensor_tensor(
                            out=sn[b][:, jc, :], in0=o32[b][:, jc, :], scalar=cols[:, C_SG + jc:C_SG + jc + 1], in1=rs[b][:],
                            op0=ALU.mult, op1=ALU.mult), reads=[f"o32{b}_{jc}", f"rs{b}", "cols"], writes=[f"sn{b}"])
                    S.dma("sync", lambda e, b=b, tt=tt: e.dma_start(
                        out=ssmn_s.ap()[:, tt * 512:(tt + 1) * 512].rearrange("(jc p) n -> p jc n", p=128), in_=sn[b][:]),
                        reads=[f"sn{b}"], writes=["ssmn_s"])
                for tt in range(9):
                    if tt < 8:
                        d_part(tt, 1)
                    if tt >= 1:
                        d_part(tt - 1, 2)

        if stage >= 5:
            S.barrier()
            with ExitStack() as pp_:
                def sbp(name, shape, dt=F32):
                    return pp_.enter_context(nc.sbuf_tensor(name, list(shape), dt))

                def psp(name, shape, dt=F32):
                    return pp_.enter_context(nc.psum_tensor(name, list(shape), dt))
                wo_a = sbp("wo_a", [128, 4, D], BF16)
                wo_s = sbp("wo_s", [128, 4, D], BF16)
                w2_b = sbp("w2_b", [128, NFC, D], BF16)
                agT = sbp("agT", [64, 8])
                xq = [sbp(f"xp{i}", [128, 4, D]) for i in range(2)]
                at = [sbp("at0", [128, 4, 512], BF16)] * 2
                st_ = [sbp(f"st{i}", [128, 4, 512], BF16) for i in range(2)]
                asq = [sbp(f"asq{i}", [128, 512]) for i in range(2)]
                ars = sbp("ars", [128, 512])
                junq = sbp("junkp", [128, D], BF16)
                ssq = sbp("ssp", [128, 16])
                rsq = sbp("rstdp", [128, 16])
                xnq = [sbp(f"xnp{i}", [128, 4, D], BF16) for i in range(2)]
                an_ = [x_[:, 0:2, :].rearrange("p a (b n) -> p (a b) n", n=512) for x_ in xnq]
                h2T = sbp("h2T", [128, 8, 512], BF16)
                w13 = [sbp(f"w13{i}", [128, 2, 8, 128], BF16) for i in range(3)]
                silu = [sbp("silu0", [128, 512])] * 2
                actT = sbp("actT", [128, NFC, 512], BF16)
                yo = [sbp("yo0", [128, D])] * 2
                mm_ps = [psp(f"ps_mm{i}", [128, 512]) for i in range(2)]
                tq_ps = [psp(f"tp_pp{i}", [128, 512], BF16) for i in range(2)]
                h_ps = [psp(f"ps_h{i}", [128, 512]) for i in range(4)]
                wn_ = 0
                for c in range(4):
                    wst = xq[wn_ % 2]; wsk = f"xp{wn_ % 2}"; wn_ += 1
                    S.dma("sync", lambda e, wst=wst, c=c: e.dma_start(out=wst[:, 0, :], in_=w_o.ap()[c * 128:(c + 1) * 128, :]), writes=[wsk])
                    S.op("vector", lambda e, wst=wst, c=c: e.tensor_tensor(out=wo_a[:, c, :], in0=wst[:, 0, :], in1=rows[:, 0, :], op=ALU.mult),
                         reads=[wsk, "rows"], writes=["wo_a"])
                for c in range(4):
                    wst = xq[wn_ % 2]; wsk = f"xp{wn_ % 2}"; wn_ += 1
                    S.dma("sync", lambda e, wst=wst, c=c: e.dma_start(out=wst[:, 0, :], in_=w_o.ap()[512 + c * 128:512 + (c + 1) * 128, :]), writes=[wsk])
                    S.op("vector", lambda e, wst=wst, c=c: e.tensor_tensor(out=wo_s[:, c, :], in0=wst[:, 0, :], in1=rows[:, 0, :], op=ALU.mult),
                         reads=[wsk, "rows"], writes=["wo_s"])
                for j in range(NFC):
                    wst = xq[wn_ % 2]; wsk = f"xp{wn_ % 2}"; wn_ += 1
                    S.dma("sync", lambda e, wst=wst, j=j: e.dma_start(out=wst[:, 0, :], in_=w2.ap()[j * 128:(j + 1) * 128, :]), writes=[wsk])
                    S.op("vector", lambda e, wst=wst, j=j: e.tensor_tensor(out=w2_b[:, j, :], in0=wst[:, 0, :], in1=rows[:, 1, :], op=ALU.mult),
                         reads=[wsk, "rows"], writes=["w2_b"])
                S.dma("sync", lambda e: e.dma_start(out=agT[:], in_=attn_gT.ap()), writes=["agT"])
                mmn = 0; hn = 0; wn = 0; yon = 0
                NT = 8
                def p_front(tt):
                    nonlocal mmn
                    b = tt % 2
                    tsl = slice(tt * 512, (tt + 1) * 512)
                    S.dma("sync", lambda e, tt=tt, b=b: e.dma_start(
                        out=xq[b][:], in_=xext.ap()[HALO + tt * 512:HALO + (tt + 1) * 512, :].rearrange("(st p) d -> p st d", p=128)),
                        writes=[f"xp{b}"])
                    S.dma("gpsimd", lambda e, tsl=tsl, b=b: e.dma_start(out=at[b][:], in_=attn_s.ap()[:, :, tsl].rearrange("(c t) d n -> (t d) c n", t=2)),
                          reads=["attn_s"], writes=["at0"])
                    S.dma("sync", lambda e, tsl=tsl, b=b: e.dma_start(
                        out=st_[b][:], in_=ssmn_s.ap()[:, tsl].rearrange("(c p) n -> p c n", p=128)), reads=["ssmn_s"], writes=[f"st{b}"])
                    ap_ = mm_ps[mmn % 2]; apk = f"ps_mm{mmn % 2}"; mmn += 1
                    for h in range(4):
                        S.op("scalar", lambda e, h=h, b=b: e.activation(out=asq[h % 2][:], in_=at[b][:, h, :], func=AF.Square),
                             reads=["at0"], writes=[f"asq{h % 2}"])
                        S.op("tensor", lambda e, ap_=ap_, h=h: e.matmul(out=ap_[:], lhsT=ones_f[:], rhs=asq[h % 2][:],
                                                                       start=(h == 0), stop=(h == 3)),
                             reads=["ones_f", f"asq{h % 2}"], writes=[apk])
                    S.op("vector", lambda e, ap_=ap_: e.tensor_scalar(out=ars[:], in0=ap_[:], scalar1=1.0 / 512, scalar2=EPS,
                                                                      op0=ALU.mult, op1=ALU.add), reads=[apk], writes=["ars"])
                    S.op("scalar", lambda e: e.activation(out=ars[:], in_=ars[:], func=AF.Sqrt), reads=["ars"], writes=["ars"])
                    S.op("vector", lambda e: e.reciprocal(out=ars[:], in_=ars[:]), reads=["ars"], writes=["ars"])
                    for h in range(4):
                        S.op("vector", lambda e, h=h, b=b: e.scalar_tensor_tensor(
                            out=an_[b][:, h, :], in0=at[b][:, h, :], scalar=cols[:, C_AG + h:C_AG + h + 1], in1=ars[:], op0=ALU.mult, op1=ALU.mult),
                            reads=["at0", "cols", "ars"], writes=[f"an{b}_{h}"] + [f"xnp{b}_{i}" for i in range(4)])
                    for st in range(4):
                        tk = slice(st * 128, (st + 1) * 128)
                        for half in range(2):
                            hs_ = slice(half * 512, (half + 1) * 512)
                            mp = mm_ps[mmn % 2]; mpk = f"ps_mm{mmn % 2}"; mmn += 1
                            for h in range(4):
                                S.op("tensor", lambda e, mp=mp, h=h, tk=tk, hs_=hs_: e.matmul(
                                    out=mp[:], lhsT=an_[b][:, h, tk], rhs=wo_a[:, h, hs_], start=(h == 0), stop=False),
                                    reads=[f"an{b}_{h}", "wo_a"], writes=[mpk])
                            for c in range(4):
                                S.op("tensor", lambda e, mp=mp, c=c, tk=tk, hs_=hs_, b=b: e.matmul(
                                    out=mp[:], lhsT=st_[b][:, c, tk], rhs=wo_s[:, c, hs_], start=False, stop=(c == 3)),
                                    reads=[f"st{b}", "wo_s"], writes=[mpk])
                            S.op("vector", lambda e, mp=mp, b=b, st=st, hs_=hs_: e.tensor_tensor(
                                out=xq[b][:, st, hs_], in0=mp[:], in1=xq[b][:, st, hs_], op=ALU.add),
                                reads=[mpk, f"xp{b}"], writes=[f"xp{b}"])
                    for st in range(4):
                        S.op("scalar", lambda e, b=b, st=st: e.activation(
                            out=junq[:], in_=xq[b][:, st, :], func=AF.Square, accum_out=ssq[:, 8 * b + st:8 * b + st + 1]),
                            reads=[f"xp{b}"], writes=["junkp", f"ssp{b}"])
                    S.op("vector", lambda e, b=b: e.tensor_scalar(out=rsq[:, 8 * b:8 * b + 4], in0=ssq[:, 8 * b:8 * b + 4], scalar1=1.0 / D, scalar2=EPS,
                                                             op0=ALU.mult, op1=ALU.add), reads=[f"ssp{b}"], writes=[f"rstdp{b}"])
                    S.op("scalar", lambda e, b=b: e.activation(out=rsq[:, 8 * b:8 * b + 4], in_=rsq[:, 8 * b:8 * b + 4], func=AF.Sqrt), reads=[f"rstdp{b}"], writes=[f"rstdp{b}"])
                    S.op("vector", lambda e, b=b: e.reciprocal(out=rsq[:, 8 * b:8 * b + 4], in_=rsq[:, 8 * b:8 * b + 4]), reads=[f"rstdp{b}"], writes=[f"rstdp{b}"])
                    for st in range(4):
                        S.op("scalar", lambda e, b=b, st=st: e.activation(
                            out=xnq[b][:, st, :], in_=xq[b][:, st, :], func=AF.Copy, scale=rsq[:, 8 * b + st:8 * b + st + 1]),
                            reads=[f"xp{b}", f"rstdp{b}"], writes=[f"xnp{b}_{st}"])
                def p_main(tt):
                    nonlocal mmn, hn, wn, yon
                    b = tt % 2
                    for fc in range(8):
                        tb_ = fc % 2
                        for st in range(4):
                            S.op("tensor", lambda e, st=st, fc=fc, tb_=tb_: e.transpose(
                                out=tq_ps[tb_][:, st * 128:(st + 1) * 128], in_=xnq[b][:, st, fc * 128:(fc + 1) * 128], identity=ident_b[:]),
                                reads=[f"xnp{b}_{st}", "ident_b"], writes=[f"tp_pp{tb_}"])
                        S.op("scalar", lambda e, fc=fc, tb_=tb_: e.activation(
                            out=h2T[:, fc, :], in_=tq_ps[tb_][:], func=AF.Identity, scale=sc12[:, 8 + fc:9 + fc],
                            bias=modc[:, 24 + fc:25 + fc]), reads=[f"tp_pp{tb_}", "sc12b", "modc"], writes=[f"h2T{fc}"])
                    for j in range(NFC):
                        wb = w13[wn % 3]; wbk = f"w13{wn % 3}"; wn += 1
                        S.dma("sync", lambda e, wb=wb, j=j: e.dma_start(out=wb[:, 0, :, :], in_=w1b.ap()[j]), reads=["wffn_b"], writes=[wbk])
                        S.dma("sync", lambda e, wb=wb, j=j: e.dma_start(out=wb[:, 1, :, :], in_=w3b.ap()[j]), reads=["wffn_b"], writes=[wbk])
                        h1 = h_ps[hn % 4]; h1k = f"ps_h{hn % 4}"; hn += 1
                        h3 = h_ps[hn % 4]; h3k = f"ps_h{hn % 4}"; hn += 1
                        for (hpp, hkk, wi) in ((h1, h1k, 0), (h3, h3k, 1)):
                            for kc in range(8):
                                S.op("tensor", lambda e, hpp=hpp, wb=wb, wi=wi, kc=kc: e.matmul(
                                    out=hpp[:], lhsT=wb[:, wi, kc, :], rhs=h2T[:, kc, :], start=(kc == 0), stop=(kc == 7)),
                                    reads=[wbk, f"h2T{kc}"], writes=[hkk])
                        sb_ = silu[0]; sbk = "silu0"
                        S.op("scalar", lambda e, h1=h1, sb_=sb_: e.activation(out=sb_[:], in_=h1[:], func=AF.Silu),
                             reads=[h1k], writes=[sbk])
                        S.op("vector", lambda e, h3=h3, sb_=sb_, j=j: e.tensor_tensor(out=actT[:, j, :], in0=h3[:], in1=sb_[:], op=ALU.mult),
                             reads=[h3k, sbk], writes=[f"actT{j}"])
                        if j == 6 and tt + 1 < NT:
                            p_front(tt + 1)
                    for st in range(4):
                        tk = slice(st * 128, (st + 1) * 128)
                        for half in range(2):
                            hs_ = slice(half * 512, (half + 1) * 512)
                            mp = mm_ps[mmn % 2]; mpk = f"ps_mm{mmn % 2}"; mmn += 1
                            for j in range(NFC):
                                S.op("tensor", lambda e, mp=mp, j=j, tk=tk, hs_=hs_: e.matmul(
                                    out=mp[:], lhsT=actT[:, j, tk], rhs=w2_b[:, j, hs_], start=(j == 0), stop=(j == NFC - 1)),
                                    reads=[f"actT{j}", "w2_b"], writes=[mpk])
                            S.op("vector", lambda e, mp=mp, b=b, st=st, hs_=hs_: e.tensor_tensor(
                                out=xq[b][:, st, hs_], in0=mp[:], in1=xq[b][:, st, hs_], op=ALU.add),
                                reads=[mpk, f"xp{b}"], writes=[f"xp{b}"])
                        S.op("scalar", lambda e, b=b, st=st: e.activation(
                            out=junq[:], in_=xq[b][:, st, :], func=AF.Square, accum_out=ssq[:, 4 + st:5 + st]),
                            reads=[f"xp{b}"], writes=["junkp", f"ss3_{st}"])
                        S.op("vector", lambda e, st=st: e.tensor_scalar(out=rsq[:, 4 + st:5 + st], in0=ssq[:, 4 + st:5 + st], scalar1=1.0 / D,
                                                                        scalar2=EPS, op0=ALU.mult, op1=ALU.add),
                             reads=[f"ss3_{st}"], writes=[f"rs3_{st}"])
                        S.op("scalar", lambda e, st=st: e.activation(out=rsq[:, 4 + st:5 + st], in_=rsq[:, 4 + st:5 + st], func=AF.Sqrt),
                             reads=[f"rs3_{st}"], writes=[f"rs3_{st}"])
                        S.op("vector", lambda e, st=st: e.reciprocal(out=rsq[:, 4 + st:5 + st], in_=rsq[:, 4 + st:5 + st]),
                             reads=[f"rs3_{st}"], writes=[f"rs3_{st}"])
                        yb = yo[0]; ybk = "yo0"
                        S.op("vector", lambda e, b=b, st=st, yb=yb: e.scalar_tensor_tensor(
                            out=yb[:], in0=xq[b][:, st, :], scalar=rsq[:, 4 + st:5 + st], in1=rows[:, 2, :], op0=ALU.mult, op1=ALU.mult),
                            reads=[f"xp{b}", f"rs3_{st}", "rows"], writes=[ybk])
                        S.dma("gpsimd", lambda e, yb=yb, tt=tt, st=st: e.dma_start(
                            out=y_out.ap()[tt * 512 + st * 128: tt * 512 + (st + 1) * 128, :], in_=yb[:]), reads=[ybk], writes=["y_out"])

                p_front(0)
                for tt in range(NT):
                    p_main(tt)
        S.emit()
        nops = len(S.ops)
    return nc, nops


def _rope_tables(pos):
    inv = 1.0 / (10000.0 ** (np.arange(0, 64, 2, dtype=np.float64) / 64.0))
    ang = pos.astype(np.float64)[:, None] * inv[None, :]
    c, s = np.cos(ang).astype(np.float32), np.sin(ang).astype(np.float32)
    cos2 = np.concatenate([c, c], 1)
    sin2 = np.concatenate([-s, s], 1)
    cosT = np.ascontiguousarray(np.concatenate([cos2, cos2], 1).T)
    sinT = np.ascontiguousarray(np.concatenate([sin2, sin2], 1).T)
    return cosT, sinT


def _consts():
    c = np.zeros((128, 1280), np.float32)
    c[:, 0:128] = np.eye(128, dtype=np.float32)
    perm = np.zeros((128, 128), np.float32)
    for m in range(128):
        j = m % 64
        k = m - j + (j + 32) % 64
        perm[k, m] = 1.0
    c[:, 128:256] = perm
    kk = np.arange(128)[:, None]
    p = np.arange(128)[None, :]
    c[:, 256:384] = np.where(kk >= p, 0.0, -30000.0)
    c[:, 384:512] = np.where(kk <= p, 0.0, -30000.0)
    c[:, 512:640] = 1.0
    for i in range(5):
        c[:, 640 + 128 * i:768 + 128 * i] = (kk >= p) if i % 2 == 0 else (kk <= p)
    return c


def _consts2():
    c = np.zeros((128, 72 + 256), np.float32)
    c[:, 0:64] = np.arange(1, 65, dtype=np.float32)[None, :]
    c[:, 64:72] = np.arange(-8, 0, dtype=np.float32)[None, :]
    sidx = (np.arange(128) // 16)[:, None]
    tidx = (np.arange(128) // 16)[None, :]
    c[:, 72:200] = (tidx >= sidx)
    c[:, 200:328] = (tidx <= sidx)
    return c


def make_in_maps(inp):
    f = lambda a: np.ascontiguousarray(np.asarray(a, dtype=np.float32))
    xp = f(inp["x_prompt"])[0]
    xs = f(inp["x_sample"])
    cp = f(inp["c_prompt"])
    csm = f(inp["c_sample"])
    shared = {
        "consts": _consts(),
        "w_ada": f(inp["w_ada"])[0],
        "b_ada": f(inp["b_ada"]).reshape(48, 128),
        "norm1_g": f(inp["norm1_g"]).reshape(8, 128),
        "norm2_g": f(inp["norm2_g"]).reshape(8, 128),
        "final_g": f(inp["final_g"]).reshape(8, 128),
        "attn_norm_g": f(inp["attn_norm_g"]).reshape(4, 128),
        "ssm_norm_g": f(inp["ssm_norm_g"]).reshape(4, 128),
        "d_skip": f(inp["d_skip"]).reshape(4, 128),
        "b_glu": f(inp["b_glu"]).reshape(4, 128),
        "w_in": f(inp["w_in"])[0],
        "consts2": _consts2(),
        "lam_re": f(inp["lam_re"]).reshape(32, 128),
        "lam_im": f(inp["lam_im"]).reshape(32, 128),
        "log_dt": f(inp["log_dt"]).reshape(32, 2),
        "b_re": f(inp["b_re"]).reshape(32, 128, 16),
        "b_im": f(inp["b_im"]).reshape(32, 128, 16),
        "c_re": f(inp["c_re"]).reshape(64, 16, 64),
        "c_im": f(inp["c_im"]).reshape(64, 16, 64),
        "w_glu": f(inp["w_glu"])[0],
        "w_o": f(inp["w_o"])[0],
        "w1": f(inp["w1"])[0],
        "w3": f(inp["w3"])[0],
        "w2": f(inp["w2"])[0],
        "attn_gT": np.ascontiguousarray(f(inp["attn_norm_g"]).reshape(8, 64).T),
    }
    maps = []
    for core in range(8):
        if core < 4:
            seq, start, L, c = xs[core], 0, SEG, csm[core]
        else:
            seq, start, L, c = xp, (core - 4) * SEG, 4 * SEG, cp[0]
        xe = np.zeros((EXT, D), np.float32)
        lo, hi = start - HALO, start + SEG + HALO
        slo, shi = max(lo, 0), min(hi, L)
        xe[slo - lo:shi - lo] = seq[slo:shi]
        pos = np.arange(lo, hi)
        cT, sT = _rope_tables(pos)
        valid = ((pos >= 0) & (pos < L))
        fl = np.zeros((128, 6), np.float32)
        if core < 4:
            xo = np.zeros((3 * SEG, D), np.float32)
        else:
            r = core - 4
            others = [j for j in range(4) if j != r]
            xo = np.concatenate([seq[j * SEG:(j + 1) * SEG] for j in others], 0)
            for i, j in enumerate(others):
                fl[:, i] = 1.0 if j < r else 0.0
                fl[:, 3 + i] = 1.0 if j > r else 0.0
        m = dict(shared)
        m["xoth"] = np.ascontiguousarray(xo)
        m["flcol"] = fl
        m["tvalid"] = np.ascontiguousarray(valid.astype(np.float32).reshape(EXT // 128, 128).T)
        m.update({"xext": xe, "cvec": np.ascontiguousarray(c.reshape(8, 128)), "cosT": cT, "sinT": sT})
        maps.append(m)
    return maps


_CACHE = {}


def kernel(**inputs):
    maps = make_in_maps(inputs)
    if "nc" not in _CACHE:
        _CACHE["nc"] = build(debug=False)[0]
    nc = _CACHE["nc"]
    res = run_bass_kernel_spmd(nc, maps, core_ids=list(range(8)))
    outs = [np.asarray(r["y_out"], dtype=np.float32) for r in res.results]
    y_sample = np.stack(outs[0:4], 0)
    y_prompt = np.concatenate(outs[4:8], 0)[None]
    return (y_prompt, y_sample)
```

```python
import os
import numpy as np
from contextlib import ExitStack
import concourse.bass as bass
import concourse.mybir as mybir
from concourse.bass_utils import run_bass_kernel_spmd

F32 = mybir.dt.float32
BF16 = mybir.dt.bfloat16
I32 = mybir.dt.int32
ALU = mybir.AluOpType
AF = mybir.ActivationFunctionType
AX = mybir.AxisListType

D = 1024
SEG = 4096
HALO = 1024
EXT = SEG + 2 * HALO
TA = 512
NTA = EXT // TA
EPS = 1e-6
FFN = 2816
NFC = FFN // 128


class Sched:
    COMPUTE = ("tensor", "vector", "scalar", "gpsimd")
    QUEUES = ("sync", "gpsimd")

    def __init__(self, nc, es, ndma=12):
        self.nc = nc
        self.es = es
        self.ops = []
        self.last_w = {}
        self.readers = {}
        self.ndma = ndma
        self.dma_pool = {q: [None] * ndma for q in self.QUEUES}
        self.dma_rr = {q: 0 for q in self.QUEUES}

    def _deps(self, reads, writes):
        d = set()
        for k in reads:
            if k in self.last_w:
                d.add(self.last_w[k])
        for k in writes:
            if k in self.last_w:
                d.add(self.last_w[k])
            for r in self.readers.get(k, ()):
                d.add(r)
        return d

    def _commit(self, oid, reads, writes):
        for k in reads:
            self.readers.setdefault(k, []).append(oid)
        for k in writes:
            self.last_w[k] = oid
            self.readers[k] = []

    @staticmethod
    def _excl(reads, writes):
        px = [k for k in reads if k.startswith(("ps", "pj", "tp", "sw"))]
        return (reads, list(writes) + px) if px else (reads, writes)

    def op(self, eng, fn, reads=(), writes=()):
        reads, writes = self._excl(reads, writes)
        oid = len(self.ops)
        deps = self._deps(reads, writes)
        self.ops.append(dict(eng=eng, fn=fn, deps=deps, kind="c", id=oid))
        self._commit(oid, reads, writes)
        return oid

    def dma(self, q, fn, reads=(), writes=()):
        oid = len(self.ops)
        deps = self._deps(reads, writes)
        slot = self.dma_rr[q]
        self.dma_rr[q] = (slot + 1) % self.ndma
        prev = self.dma_pool[q][slot]
        if prev is not None:
            deps.add(prev)
        self.dma_pool[q][slot] = oid
        self.ops.append(dict(eng=q, fn=fn, deps=deps, kind="d", id=oid, slot=slot))
        self._commit(oid, reads, writes)
        return oid

    def barrier(self):
        last = {}
        for o in self.ops:
            if o["kind"] == "c":
                if o["eng"] != "sync":
                    last[("c", o["eng"])] = o["id"]
            else:
                last[("d", o["eng"], o["slot"])] = o["id"]
        deps = set(last.values())
        for eng in ("tensor", "vector", "scalar", "gpsimd", "sync"):
            oid = len(self.ops)
            self.ops.append(dict(eng=eng, fn=(lambda e: e.nop()), deps=set(deps), kind="c", id=oid))

    def emit(self, final_wait_eng="sync"):
        nc, es = self.nc, self.es
        ops = self.ops

        def needs_wait(o, t):
            if t["kind"] == "d":
                return True
            if t["eng"] == o["eng"] and o["kind"] == "c" and o["eng"] == "tensor":
                return False
            return True

        sig = [False] * len(ops)
        for o in ops:
            for d in o["deps"]:
                t = ops[d]
                if t["kind"] == "c" and needs_wait(o, t):
                    sig[d] = True
        csem = {e: es.enter_context(nc.semaphore("cs_" + e)) for e in self.COMPUTE}
        dsem = {q: [es.enter_context(nc.semaphore(f"ds_{q}{i}")) for i in range(self.ndma)]
                for q in self.QUEUES}
        ccount = {e: 0 for e in self.COMPUTE}
        dcount = {q: [0] * self.ndma for q in self.QUEUES}
        val = [None] * len(ops)
        for o in ops:
            if o["kind"] == "d":
                q, s = o["eng"], o["slot"]
                dcount[q][s] += 16
                val[o["id"]] = (dsem[q][s], dcount[q][s])
            elif sig[o["id"]]:
                e = o["eng"]
                assert e in csem, e
                ccount[e] += 1
                val[o["id"]] = (csem[e], ccount[e])
        per = {}
        for o in ops:
            per.setdefault(o["eng"], []).append(o)
        block = es.enter_context(nc.Block())
        self.nwaits = 0

        def run(engname, eng):
            waited = {}
            for o in per.get(engname, []):
                need = {}
                for d in o["deps"]:
                    t = ops[d]
                    if not needs_wait(o, t):
                        continue
                    sem, v = val[d]
                    key = id(sem)
                    if key not in need or need[key][1] < v:
                        need[key] = (sem, v)
                for key, (sem, v) in need.items():
                    if waited.get(key, 0) >= v:
                        continue
                    waited[key] = v
                    eng.wait_ge(sem, v)
                    self.nwaits += 1
                ins = o["fn"](eng)
                if o["kind"] == "d":
                    ins.then_inc(val[o["id"]][0], 16)
                elif sig[o["id"]]:
                    ins.then_inc(val[o["id"]][0], 1)
            if engname == final_wait_eng:
                for q in self.QUEUES:
                    for s in range(self.ndma):
                        if dcount[q][s] and waited.get(id(dsem[q][s]), 0) < dcount[q][s]:
                            eng.wait_ge(dsem[q][s], dcount[q][s])

        @block.sync
        def _(e):
            run("sync", e)

        @block.scalar
        def _(e):
            run("scalar", e)

        @block.gpsimd
        def _(e):
            run("gpsimd", e)

        @block.vector
        def _(e):
            run("vector", e)

        @block.tensor
        def _(e):
            run("tensor", e)


DILS = (1, 4, 16)


def sl(start, n, step=1):
    return slice(start, start + (n - 1) * step + 1, step)


def kb_index(di, r, end):
    base = (0, 2, 10)[di]
    return base + r * 2 + end


NKB = 42

C_BADA, C_N1, C_N2, C_C, C_AG, C_SG, C_DS, C_BG, C_FG, NCOLS = 0, 48, 56, 64, 72, 76, 80, 84, 88, 96


def build(debug=False, stage=99, sub=99, tiles=None):
    nc = bass.Bass("TRN2", target_bir_lowering=False)
    dk = "ExternalOutput" if debug else "Internal"

    def din(name, shape, dt=F32):
        return nc.dram_tensor(name, list(shape), dt, kind="ExternalInput")

    xext = din("xext", [EXT, D])
    cvec = din("cvec", [8, 128])
    cosT = din("cosT", [128, EXT])
    sinT = din("sinT", [128, EXT])
    consts = din("consts", [128, 1280])
    w_ada = din("w_ada", [D, 6 * D])
    b_ada = din("b_ada", [48, 128])
    norm1_g = din("norm1_g", [8, 128])
    norm2_g = din("norm2_g", [8, 128])
    final_g = din("final_g", [8, 128])
    attn_g = din("attn_norm_g", [4, 128])
    ssm_g = din("ssm_norm_g", [4, 128])
    d_skip = din("d_skip", [4, 128])
    b_glu = din("b_glu", [4, 128])
    w_in = din("w_in", [D, 2048])
    tvalid = din("tvalid", [128, EXT // 128])
    consts2 = din("consts2", [128, 72 + 256])
    lam_re = din("lam_re", [32, 128])
    lam_im = din("lam_im", [32, 128])
    log_dt = din("log_dt", [32, 2])
    b_re = din("b_re", [32, 128, 16])
    b_im = din("b_im", [32, 128, 16])
    c_re = din("c_re", [64, 16, 64])
    c_im = din("c_im", [64, 16, 64])
    w_glu = din("w_glu", [512, 512])
    w_o = din("w_o", [D, D])
    w1 = din("w1", [D, FFN])
    w3 = din("w3", [D, FFN])
    w2 = din("w2", [FFN, D])
    attn_gT = din("attn_gT", [64, 8])
    xoth = din("xoth", [3 * SEG, D])
    flcol = din("flcol", [128, 6])
    y_out = nc.dram_tensor("y_out", [SEG, D], F32, kind="ExternalOutput")

    qT_s = nc.dram_tensor("qT_s", [512, SEG], BF16, kind=dk)
    kT_s = nc.dram_tensor("kT_s", [512, EXT], BF16, kind=dk)
    v_s = nc.dram_tensor("v_s", [EXT, 520], BF16, kind=dk)
    Us = nc.dram_tensor("Us", [4, 32, 128, 512], BF16, kind=dk)
    attn_s = nc.dram_tensor("attn_s", [8, 64, SEG], F32, kind=dk)
    y_s = nc.dram_tensor("y_s", [512, 8, 512], F32, kind=dk)
    ssmn_s = nc.dram_tensor("ssmn_s", [512, SEG], BF16, kind=dk)
    w1b = nc.dram_tensor("w1b", [NFC, 128, 8, 128], BF16, kind="Internal")
    w3b = nc.dram_tensor("w3b", [NFC, 128, 8, 128], BF16, kind="Internal")
    if debug:
        dbg_s5 = nc.dram_tensor("dbg_s5", [128, 32768], F32, kind="ExternalOutput")
        dbg_cols = nc.dram_tensor("dbg_cols", [128, 160], F32, kind="ExternalOutput")
        dbg_rows = nc.dram_tensor("dbg_rows", [128, 3 * D], F32, kind="ExternalOutput")

    with ExitStack() as es:
        S = Sched(nc, es)

        def sb(name, shape, dt=F32):
            return es.enter_context(nc.sbuf_tensor(name, list(shape), dt))

        def ps(name, shape, dt=F32):
            return es.enter_context(nc.psum_tensor(name, list(shape), dt))

        ident_f = sb("ident_f", [128, 128])
        ones_f = sb("ones_f", [128, 128])
        ident_b = sb("ident_b", [128, 128], BF16)
        perm_b = sb("perm_b", [128, 128], BF16)
        band_b = sb("band_b", [128, 2, 128], BF16)
        cols = sb("cols", [128, NCOLS])
        modc = sb("modc", [128, 48])
        sc12 = sb("sc12", [128, 16])
        silu_c = sb("silu_c", [128, 8])
        rows = sb("rows", [128, 3, D])

        S.dma("sync", lambda e: e.dma_start(out=ident_f[:], in_=consts.ap()[:, 0:128]), writes=["ident_f"])
        S.dma("sync", lambda e: e.dma_start(out=ones_f[:], in_=consts.ap()[:, 512:640]), writes=["ones_f"])
        S.dma("gpsimd", lambda e: e.dma_start(out=ident_b[:], in_=consts.ap()[:, 0:128]), writes=["ident_b"])
        S.dma("gpsimd", lambda e: e.dma_start(out=perm_b[:], in_=consts.ap()[:, 128:256]), writes=["perm_b"])
        S.dma("gpsimd", lambda e: e.dma_start(
            out=band_b[:], in_=consts.ap()[:, 256:512].rearrange("p (c q) -> p c q", c=2)), writes=["band_b"])

        with ExitStack() as p0:
            stg = p0.enter_context(nc.sbuf_tensor("stg", [NCOLS, 128], F32))
            wada_t = [p0.enter_context(nc.sbuf_tensor(f"wada{i}", [128, 6 * D], BF16)) for i in range(2)]
            wada_f = [p0.enter_context(nc.sbuf_tensor(f"wadaf{i}", [128, 6 * D], F32)) for i in range(3)]
            silu_cb = p0.enter_context(nc.sbuf_tensor("silu_cb", [128, 8], BF16))
            diag = [p0.enter_context(nc.sbuf_tensor(f"diag{i}", [128, 128], F32)) for i in range(2)]
            ps_cols = p0.enter_context(nc.psum_tensor("ps_cols", [128, NCOLS], F32))
            ps_mod = p0.enter_context(nc.psum_tensor("ps_mod", [128, 48], F32))
            ps_row = [p0.enter_context(nc.psum_tensor(f"ps_row{i}", [128, 512], F32)) for i in range(2)]
            for (src, off, n) in ((b_ada, C_BADA, 48), (norm1_g, C_N1, 8), (norm2_g, C_N2, 8), (cvec, C_C, 8),
                                  (attn_g, C_AG, 4), (ssm_g, C_SG, 4), (d_skip, C_DS, 4), (b_glu, C_BG, 4),
                                  (final_g, C_FG, 8)):
                S.dma("sync", lambda e, src=src, off=off, n=n: e.dma_start(out=stg[off:off + n, :], in_=src.ap()),
                      writes=["stg"])
            S.op("tensor", lambda e: e.transpose(out=ps_cols[:], in_=stg[:], identity=ident_f[0:NCOLS, 0:NCOLS]),
                 reads=["stg", "ident_f"], writes=["ps_cols"])
            S.op("vector", lambda e: e.tensor_copy(out=cols[:], in_=ps_cols[:]), reads=["ps_cols"], writes=["cols"])
            S.op("scalar", lambda e: e.activation(out=silu_c[:], in_=cols[:, C_C:C_C + 8], func=AF.Silu),
                 reads=["cols"], writes=["silu_c"])
            S.op("vector", lambda e: e.tensor_copy(out=silu_cb[:], in_=silu_c[:]), reads=["silu_c"], writes=["silu_cb"])
            for kc in range(8 if stage >= -1 else 0):
                b = kc % 2
                bf = kc % 3
                S.dma("sync", lambda e, kc=kc, bf=bf: e.dma_start(out=wada_f[bf][:], in_=w_ada.ap()[kc * 128:(kc + 1) * 128, :]),
                      writes=[f"wadaf{bf}"])
                for hq in range(4):
                    cs_ = slice(hq * 1536, (hq + 1) * 1536)
                    if hq % 2 == 0:
                        S.op("scalar", lambda e, b=b, bf=bf, cs_=cs_: e.activation(out=wada_t[b][:, cs_], in_=wada_f[bf][:, cs_], func=AF.Copy),
                             reads=[f"wadaf{bf}"], writes=[f"wada{b}"])
                    else:
                        S.op("vector", lambda e, b=b, bf=bf, cs_=cs_: e.tensor_copy(out=wada_t[b][:, cs_], in_=wada_f[bf][:, cs_]),
                             reads=[f"wadaf{bf}"], writes=[f"wada{b}"])
                for j in range(48):
                    S.op("tensor", lambda e, kc=kc, b=b, j=j: e.matmul(
                        out=ps_mod[:, j:j + 1], lhsT=wada_t[b][:, j * 128:(j + 1) * 128], rhs=silu_cb[:, kc:kc + 1],
                        start=(kc == 0 and j == 0), stop=(kc == 7 and j == 47), skip_group_check=True),
                        reads=[f"wada{b}", "silu_cb"], writes=["ps_mod"])
            S.op("vector", lambda e: e.tensor_tensor(out=modc[:], in0=ps_mod[:], in1=cols[:, C_BADA:C_BADA + 48], op=ALU.add),
                 reads=["ps_mod", "cols"], writes=["modc"])
            S.op("vector", lambda e: e.scalar_tensor_tensor(out=sc12[:, 0:8], in0=modc[:, 8:16], scalar=1.0,
                                                            in1=cols[:, C_N1:C_N1 + 8], op0=ALU.add, op1=ALU.mult),
                 reads=["modc", "cols"], writes=["sc12a"])
            S.op("vector", lambda e: e.scalar_tensor_tensor(out=sc12[:, 8:16], in0=modc[:, 32:40], scalar=1.0,
                                                            in1=cols[:, C_N2:C_N2 + 8], op0=ALU.add, op1=ALU.mult),
                 reads=["modc", "cols"], writes=["sc12b"])
            k = 0
            for ri, (srct, soff, skey) in enumerate(((modc, 16, "modc"), (modc, 40, "modc"), (cols, C_FG, "cols")) if stage >= 0 else ()):
                for half in range(2):
                    pr = ps_row[(ri * 2 + half) % 2]
                    prk = f"ps_row{(ri * 2 + half) % 2}"
                    for c4 in range(4):
                        c = half * 4 + c4
                        dg = diag[k % 2]
                        dgk = f"diag{k % 2}"
                        k += 1
                        S.op("vector", lambda e, dg=dg, srct=srct, soff=soff, c=c: e.tensor_scalar(
                            out=dg[:], in0=ident_f[:], scalar1=srct[:, soff + c:soff + c + 1], scalar2=None, op0=ALU.mult),
                            reads=["ident_f", skey], writes=[dgk])
                        S.op("tensor", lambda e, dg=dg, pr=pr, c4=c4: e.matmul(
                            out=pr[:, c4 * 128:(c4 + 1) * 128], lhsT=ones_f[:], rhs=dg[:], start=True, stop=True),
                            reads=["ones_f", dgk], writes=[prk])
                    S.op("vector", lambda e, pr=pr, ri=ri, half=half: e.tensor_copy(
                        out=rows[:, ri, half * 512:(half + 1) * 512], in_=pr[:]), reads=[prk], writes=["rows"])
            if debug:
                dcol = p0.enter_context(nc.sbuf_tensor("dcol", [128, 160], F32))
                S.op("vector", lambda e: e.tensor_copy(out=dcol[:, 0:NCOLS], in_=cols[:]), reads=["cols"], writes=["dcol"])
                S.op("vector", lambda e: e.tensor_copy(out=dcol[:, 96:144], in_=modc[:]), reads=["modc"], writes=["dcol"])
                S.op("vector", lambda e: e.tensor_copy(out=dcol[:, 144:160], in_=sc12[:]), reads=["sc12a", "sc12b"], writes=["dcol"])
                S.dma("sync", lambda e: e.dma_start(out=dbg_cols.ap(), in_=dcol[:]), reads=["dcol"])
                S.dma("sync", lambda e: e.dma_start(out=dbg_rows.ap(), in_=rows[:].rearrange("p a d -> p (a d)")), reads=["rows"])


        if stage >= 5:
            for wsrc, wdst in ((w1, w1b), (w3, w3b)):
                for j in range(NFC):
                    S.dma("gpsimd", lambda e, wsrc=wsrc, wdst=wdst, j=j: e.dma_start(
                        out=wdst.ap()[j], in_=wsrc.ap()[:, j * 128:(j + 1) * 128].rearrange("(kc p) f -> p kc f", p=128)),
                        writes=["wffn_b"])

        if stage >= 1:
            S.barrier()
            with ExitStack() as pa:
                def sba(name, shape, dt=F32):
                    return pa.enter_context(nc.sbuf_tensor(name, list(shape), dt))

                def psa(name, shape, dt=F32):
                    return pa.enter_context(nc.psum_tensor(name, list(shape), dt))
                win_b = sba("win_b", [128, 8, 2048], BF16)
                S.dma("gpsimd", lambda e: e.dma_start(out=win_b[:], in_=w_in.ap().rearrange("(kc p) n -> p kc n", p=128)),
                      writes=["win_b"])
                xt = [sba(f"xt{i}", [128, 4, D]) for i in range(3)]
                cs_t = [sba(f"cs{i}", [128, 2, TA]) for i in range(3)]
                sqj = sba("sqj", [128, D], BF16)
                ss = [sba(f"ss{i}", [128, 4]) for i in range(2)]
                rstd = [sba(f"rstd{i}", [128, 4]) for i in range(2)]
                xn = [sba(f"xn{i}", [128, 4, D], BF16) for i in range(2)]
                hT = [sba(f"hT{i}", [128, 8, TA], BF16) for i in range(2)]
                qraw = [sba(f"qraw{i}", [128, TA], BF16) for i in range(2)]
                t1 = [sba(f"t1{i}", [128, TA]) for i in range(2)]
                t2 = [sba(f"t2{i}", [128, TA]) for i in range(2)]
                qk_o = [sba(f"qko{i}", [128, 8, TA], BF16) for i in range(2)]
                v_o = [sba(f"vo{i}", [128, 4, 520], BF16) for i in range(2)]
                tval = sba("tval", [128, EXT // 128])
                S.dma("sync", lambda e: e.dma_start(out=tval[:], in_=tvalid.ap()), writes=["tval"])
                ud = sba("ud", [128, 4, 8, 512], BF16)
                tp_ps = [psa(f"tp_ps{i}", [128, TA], BF16) for i in range(2)]
                pj_ps = [psa(f"pj_ps{i}", [128, 512]) for i in range(4)]
                sw_ps = [psa(f"sw_ps{i}", [128, 512]) for i in range(2)]
                pjn = 0
                swn = 0
                rn = 0
                tile_list = [("main", t) for t in (range(NTA) if tiles is None else tiles)]
                if stage >= 3 and tiles is None:
                    tile_list += [("oth", t) for t in range(3 * SEG // TA)]
                def tile_part(ti, part):
                    nonlocal pjn, swn, rn
                    tkind, t = tile_list[ti]
                    b = ti % 2
                    b3 = ti % 3
                    other = tkind == "oth"
                    own = (not other) and HALO // TA <= t < (HALO + SEG) // TA
                    to = (t % 8) if other else t - HALO // TA
                    segi = (1 + t // 8) if other else 0
                    xsrc = xoth if other else xext
                    if part == "1a":
                        S.dma("sync", lambda e, b3=b3, t=t, b=b, xsrc=xsrc: e.dma_start(
                            out=xt[b3][:], in_=xsrc.ap()[t * TA:(t + 1) * TA, :].rearrange("(st p) d -> p st d", p=128)),
                            writes=[f"xt{b3}"])
                        if not other:
                            S.dma("sync", lambda e, b3=b3, t=t, b=b: e.dma_start(out=cs_t[b3][:, 0, :], in_=cosT.ap()[:, t * TA:(t + 1) * TA]),
                                  writes=[f"cs{b3}"])
                            S.dma("sync", lambda e, b3=b3, t=t, b=b: e.dma_start(out=cs_t[b3][:, 1, :], in_=sinT.ap()[:, t * TA:(t + 1) * TA]),
                                  writes=[f"cs{b3}"])
                        for st in range(4):
                            S.op("scalar", lambda e, b3=b3, b=b, st=st: e.activation(
                                out=sqj[:], in_=xt[b3][:, st, :], func=AF.Square, accum_out=ss[b][:, st:st + 1]),
                                reads=[f"xt{b3}"], writes=["sqj", f"ss{b}"])
                        S.op("vector", lambda e, b=b: e.tensor_scalar(out=rstd[b][:], in0=ss[b][:], scalar1=1.0 / D, scalar2=EPS,
                                                                      op0=ALU.mult, op1=ALU.add),
                             reads=[f"ss{b}"], writes=[f"rstd{b}"])
                        S.op("scalar", lambda e, b=b: e.activation(out=rstd[b][:], in_=rstd[b][:], func=AF.Sqrt),
                             reads=[f"rstd{b}"], writes=[f"rstd{b}"])
                        S.op("vector", lambda e, b=b: e.reciprocal(out=rstd[b][:], in_=rstd[b][:]),
                             reads=[f"rstd{b}"], writes=[f"rstd{b}"])
                        for st in range(4):
                            S.op("vector", lambda e, b3=b3, b=b, st=st: e.tensor_scalar(
                                out=xn[b][:, st, :], in0=xt[b3][:, st, :], scalar1=rstd[b][:, st:st + 1], scalar2=None, op0=ALU.mult),
                                reads=[f"xt{b3}", f"rstd{b}"], writes=[f"xn{b}_{st}"])
                        return
                    if part == "1b":
                        for fc in range(8):
                            tb = fc % 2
                            for st in range(4):
                                S.op("tensor", lambda e, b=b, st=st, fc=fc, tb=tb: e.transpose(
                                    out=tp_ps[tb][:, st * 128:(st + 1) * 128], in_=xn[b][:, st, fc * 128:(fc + 1) * 128],
                                    identity=ident_b[:]), reads=[f"xn{b}_{st}", "ident_b"], writes=[f"tp_ps{tb}"])
                            if other and fc % 2 == 1:
                                S.op("vector", lambda e, b=b, fc=fc, tb=tb: e.tensor_scalar(
                                    out=hT[b][:, fc, :], in0=tp_ps[tb][:], scalar1=sc12[:, fc:fc + 1], scalar2=modc[:, fc:fc + 1],
                                    op0=ALU.mult, op1=ALU.add),
                                    reads=[f"tp_ps{tb}", "sc12a", "modc"], writes=[f"hT{b}_{fc}"])
                            else:
                                S.op("scalar", lambda e, b=b, fc=fc, tb=tb: e.activation(
                                    out=hT[b][:, fc, :], in_=tp_ps[tb][:], func=AF.Identity,
                                    scale=sc12[:, fc:fc + 1], bias=modc[:, fc:fc + 1]),
                                    reads=[f"tp_ps{tb}", "sc12a", "modc"], writes=[f"hT{b}_{fc}"])
                        return
                    hkeys = [f"hT{b}_{fc}" for fc in range(8)]
                    if sub < 3:
                        return
                    for which in (() if other else ((0, 1) if own else (1,))):
                        for cc in range(4):
                            pp = pj_ps[pjn % 4]; ppk = f"pj_ps{pjn % 4}"; pjn += 1
                            col0 = which * 512 + cc * 128
                            for kc in range(8):
                                S.op("tensor", lambda e, pp=pp, b=b, kc=kc, col0=col0: e.matmul(
                                    out=pp[:], lhsT=win_b[:, kc, col0:col0 + 128], rhs=hT[b][:, kc, :],
                                    start=(kc == 0), stop=(kc == 7)), reads=["win_b", hkeys[kc]], writes=[ppk])
                            r = rn % 2; rn += 1
                            S.op("scalar", lambda e, pp=pp, r=r: e.activation(out=qraw[r][:], in_=pp[:], func=AF.Copy),
                                 reads=[ppk], writes=[f"qraw{r}", ppk + "x"])
                            sp = sw_ps[swn % 2]; spk = f"sw_ps{swn % 2}"; swn += 1
                            S.op("tensor", lambda e, sp=sp, r=r: e.matmul(out=sp[:], lhsT=perm_b[:], rhs=qraw[r][:],
                                                                         start=True, stop=True),
                                 reads=["perm_b", f"qraw{r}"], writes=[spk])
                            S.op("vector", lambda e, b3=b3, pp=pp, r=r, b=b: e.tensor_tensor(
                                out=t1[r][:], in0=pp[:], in1=cs_t[b3][:, 0, :], op=ALU.mult),
                                reads=[ppk, f"cs{b3}"], writes=[f"t1{r}", ppk + "x"])
                            S.op("vector", lambda e, b3=b3, sp=sp, r=r, b=b: e.tensor_tensor(
                                out=t2[r][:], in0=sp[:], in1=cs_t[b3][:, 1, :], op=ALU.mult),
                                reads=[spk, f"cs{b3}"], writes=[f"t2{r}"])
                            S.op("vector", lambda e, r=r, b=b, which=which, cc=cc: e.tensor_tensor(
                                out=qk_o[b][:, which * 4 + cc, :], in0=t1[r][:], in1=t2[r][:], op=ALU.add),
                                reads=[f"t1{r}", f"t2{r}"], writes=[f"qko{b}"])
                    if own:
                        S.dma("gpsimd", lambda e, b=b, to=to: e.dma_start(
                            out=qT_s.ap()[:, to * TA:(to + 1) * TA].rearrange("(cc p) n -> p cc n", p=128),
                            in_=qk_o[b][:, 0:4, :]), reads=[f"qko{b}"], writes=["qT_s"])
                    if not other:
                        S.dma("gpsimd", lambda e, b=b, t=t: e.dma_start(
                            out=kT_s.ap()[:, t * TA:(t + 1) * TA].rearrange("(cc p) n -> p cc n", p=128),
                            in_=qk_o[b][:, 4:8, :]), reads=[f"qko{b}"], writes=["kT_s"])
                    if sub < 4:
                        return
                    for st in range(0 if other else 4):
                        pp = pj_ps[pjn % 4]; ppk = f"pj_ps{pjn % 4}"; pjn += 1
                        for kc in range(8):
                            S.op("tensor", lambda e, pp=pp, b=b, kc=kc, st=st: e.matmul(
                                out=pp[:], lhsT=hT[b][:, kc, st * 128:(st + 1) * 128], rhs=win_b[:, kc, 1024:1536],
                                start=(kc == 0), stop=(kc == 7)), reads=["win_b", hkeys[kc]], writes=[ppk])
                        S.op("vector", lambda e, b=b, st=st, t=t: e.tensor_copy(
                            out=v_o[b][:, st, :].rearrange("p (h e) -> p h e", e=65)[:, :, 64],
                            in_=bass.AP(tval, t * 4 + st, [[EXT // 128, 128], [0, 8]])), reads=["tval"], writes=[f"vo{b}"])
                        S.op("vector", lambda e, pp=pp, b=b, st=st, t=t: e.tensor_scalar(
                            out=v_o[b][:, st, :].rearrange("p (h e) -> p h e", e=65)[:, :, 0:64],
                            in0=pp[:].rearrange("p (h e) -> p h e", e=64), scalar1=tval[:, t * 4 + st:t * 4 + st + 1], scalar2=None,
                            op0=ALU.mult),
                             reads=[ppk, "tval"], writes=[f"vo{b}"])
                    if not other:
                        S.dma("gpsimd", lambda e, b=b, t=t: e.dma_start(
                            out=v_s.ap()[t * TA:(t + 1) * TA, :].rearrange("(st p) n -> p st n", p=128), in_=v_o[b][:]),
                            reads=[f"vo{b}"], writes=["v_s"])
                    if (own or other) and sub >= 5:
                        for cc in range(4):
                            pp = pj_ps[pjn % 4]; ppk = f"pj_ps{pjn % 4}"; pjn += 1
                            col0 = 1536 + cc * 128
                            for kc in range(8):
                                S.op("tensor", lambda e, pp=pp, b=b, kc=kc, col0=col0: e.matmul(
                                    out=pp[:], lhsT=win_b[:, kc, col0:col0 + 128], rhs=hT[b][:, kc, :],
                                    start=(kc == 0), stop=(kc == 7)), reads=["win_b", hkeys[kc]], writes=[ppk])
                            if other:
                                S.op("vector", lambda e, pp=pp, cc=cc, to=to: e.tensor_copy(
                                    out=ud[:, cc, :, to * 64:(to + 1) * 64], in_=pp[:].rearrange("p (n s) -> p s n", s=8)),
                                    reads=[ppk], writes=["ud"])
                            else:
                                S.op("scalar", lambda e, pp=pp, cc=cc, to=to: e.activation(
                                    out=ud[:, cc, :, to * 64:(to + 1) * 64], in_=pp[:].rearrange("p (n s) -> p s n", s=8),
                                    func=AF.Copy), reads=[ppk], writes=["ud"])
                    if (own or other) and to == 7 and sub >= 6:
                      for cc in range(4):
                        for gl in range(8):
                          S.dma("gpsimd", lambda e, cc=cc, gl=gl, segi=segi: e.dma_start(
                            out=Us.ap()[segi, cc * 8 + gl].rearrange("(s c) n -> c s n", c=16),
                            in_=ud[gl * 16:(gl + 1) * 16, cc, :, :]), reads=["ud"], writes=["Us"])
                ntl = len(tile_list)
                tile_part(0, "1a")
                tile_part(0, "1b")
                for ti in range(ntl):
                    if ti + 1 < ntl:
                        tile_part(ti + 1, "1a")
                    if ti >= 1:
                        tile_part(ti - 1, 2)
                    if ti + 1 < ntl:
                        tile_part(ti + 1, "1b")
                tile_part(ntl - 1, 2)

        if stage >= 2:
            S.barrier()
            with ExitStack() as pb:
                def sbb(name, shape, dt=F32):
                    return pb.enter_context(nc.sbuf_tensor(name, list(shape), dt))

                def psb(name, shape, dt=F32):
                    return pb.enter_context(nc.psum_tensor(name, list(shape), dt))
                maskT = sbb("maskT", [128, 640], BF16)
                S.dma("gpsimd", lambda e: e.dma_start(out=maskT[:], in_=consts.ap()[:, 640:1280]), writes=["maskT"])
                kT_all = sbb("kT_all", [128, 4, 4096], BF16)
                qT_all = sbb("qT_all", [128, 4, 2048], BF16)
                nchs = {1: 17, 4: 5, 16: 2}
                varr = {}
                for d in DILS:
                    for r in range(d):
                        varr[(d, r)] = sbb(f"va{d}_{r}", [128, nchs[d], 520], BF16)
                NSB = 4
                LOOK = 3
                pT = [sbb(f"pT{i}", [128, 512], BF16) for i in range(NSB)]
                pTm = [sbb(f"pTm{i}", [128, 512], BF16) for i in range(NSB)]
                accsb = [sbb(f"accsb{i}", [65, 1024]) for i in range(2)]
                acc16 = sbb("acc16", [65, 16, 128])
                rden = [sbb(f"rden{i}", [64, 512]) for i in range(2)]
                acc_ps = psb("ps_acc", [128, 1024])
                a16_ps = psb("ps_a16", [128, 512])
                s_ps = [psb(f"ps_s{i}", [128, 512]) for i in range(NSB)]
                bc_ps = psb("ps_bc", [64, 512])
                un = 0
                ev = 0
                for sbi in range(2):
                    S.dma("sync", lambda e, sbi=sbi: e.dma_start(
                        out=kT_all[:], in_=kT_s.ap()[:, 2048 * sbi:2048 * sbi + 4096].rearrange("(cc p) n -> p cc n", p=128)),
                        reads=["kT_s"], writes=["kT_all"])
                    S.dma("sync", lambda e, sbi=sbi: e.dma_start(
                        out=qT_all[:], in_=qT_s.ap()[:, 2048 * sbi:2048 * sbi + 2048].rearrange("(cc p) n -> p cc n", p=128)),
                        reads=["qT_s"], writes=["qT_all"])
                    for d in DILS:
                        nq = SEG // d // 128
                        ca0 = (nq // 2) * sbi
                        for r in range(d):
                            row0 = r + HALO - 64 * d + 128 * d * ca0
                            src = bass.AP(v_s, row0 * 520, [[d * 520, 128], [128 * d * 520, nchs[d]], [1, 520]])
                            S.dma("sync", lambda e, d=d, r=r, src=src: e.dma_start(out=varr[(d, r)][:], in_=src),
                                  reads=["v_s"], writes=[f"va{d}_{r}"])
                    for h in range(8):
                        hp, hc = (h % 2) * 64, h // 2

                        def mk_chunks(dsel, half):
                            out = []
                            for di, d in enumerate(DILS):
                                if d not in dsel:
                                    continue
                                nql = SEG // d // 128 // 2
                                if half is None:
                                    b_lo, b_hi = 0, nql
                                else:
                                    b_lo, b_hi = half * (nql // 2), (half + 1) * (nql // 2)
                                for r in range(d):
                                    for cal in range(b_lo, b_hi + 1):
                                        blocks = [bl for bl in (cal - 1, cal) if b_lo <= bl < b_hi]
                                        out.append((di, d, r, cal, blocks))
                            return out

                        def run_pipeline(chunks, pv_emit):
                            nonlocal un
                            groups, cur, used = [], [], 0
                            for ch in chunks:
                                w = 128 * len(ch[4])
                                if used + w > 512:
                                    groups.append(cur); cur, used = [], 0
                                cur.append((ch, used)); used += w
                            if cur:
                                groups.append(cur)
                            ginfo = []
                            for gi in range(len(groups) + LOOK):
                                if gi < len(groups):
                                    grp = groups[gi]
                                    sp = s_ps[un % NSB]; spk = f"ps_s{un % NSB}"
                                    pt = pT[un % NSB]; ptk = f"pT{un % NSB}"
                                    pm = pTm[un % NSB]; pmk = f"pTm{un % NSB}"
                                    un += 1
                                    ginfo.append((grp, pm, pmk))
                                    width = grp[-1][1] + 128 * len(grp[-1][0][4])
                                    for (di, d, r, cal, blocks), off in grp:
                                        nq = SEG // d // 128
                                        ca = (nq // 2) * sbi + cal
                                        kbase = r + HALO - 64 * d + 128 * d * ca - 2048 * sbi
                                        bq0 = (nq // 2) * sbi + blocks[0]
                                        qbase = r + 128 * d * bq0 - 2048 * sbi
                                        nqc = 128 * len(blocks)
                                        S.op("tensor", lambda e, sp=sp, off=off, nqc=nqc, kbase=kbase, qbase=qbase, d=d, hp=hp, hc=hc: e.matmul(
                                            out=sp[:, off:off + nqc], lhsT=kT_all[hp:hp + 64, hc, sl(kbase, 128, d)],
                                            rhs=qT_all[hp:hp + 64, hc, sl(qbase, nqc, d)], start=True, stop=True),
                                            reads=["kT_all", "qT_all"], writes=[spk])
                                    S.op("scalar", lambda e, sp=sp, pt=pt, width=width: e.activation(
                                        out=pt[:, 0:width], in_=sp[:, 0:width], func=AF.Exp, scale=0.125), reads=[spk], writes=[ptk])
                                    (_di, _d, _r, cal0, blocks0), _ = grp[0]
                                    m0 = 128 if blocks0[0] == cal0 - 1 else 0
                                    S.op("vector", lambda e, pt=pt, pm=pm, width=width, m0=m0: e.tensor_tensor(
                                        out=pm[:, 0:width], in0=pt[:, 0:width], in1=maskT[:, m0:m0 + width], op=ALU.mult),
                                        reads=[ptk, "maskT"], writes=[pmk])
                                if gi >= LOOK:
                                    grp, pm, pmk = ginfo[gi - LOOK]
                                    for ch, off in grp:
                                        pv_emit(ch, off, pm, pmk)

                        def pv16(ch, off, pm, pmk):
                            di, d, r, cal, blocks = ch
                            rr = r % 4
                            S.op("tensor", lambda e, pm=pm, off=off, rr=rr, r=r, cal=cal, h=h: e.matmul(
                                out=a16_ps[0:65, rr * 128:(rr + 1) * 128], lhsT=varr[(16, r)][:, cal, h * 65:(h + 1) * 65],
                                rhs=pm[:, off:off + 128], start=(rr == 0 and cal == 0), stop=False, skip_group_check=True),
                                reads=[pmk, f"va16_{r}"], writes=["ps_a16"])
                            if rr == 3 and cal == 1:
                                S.op("scalar", lambda e, r=r: e.activation(
                                    out=acc16[:, r - 3:r + 1, :].rearrange("p a b -> p (a b)"), in_=a16_ps[0:65, :], func=AF.Copy),
                                    reads=["ps_a16"], writes=["acc16"])
                        run_pipeline(mk_chunks((16,), None), pv16)

                        for half in range(2):
                            started = set()

                            def pv14(ch, off, pm, pmk, half=half, started=started):
                                di, d, r, cal, blocks = ch
                                lhsT = varr[(d, r)][:, cal, h * 65:(h + 1) * 65]
                                pieces = []
                                if d == 1:
                                    lb = [bl - 8 * half for bl in blocks]
                                    if len(lb) == 2 and lb[1] % 4 != 0:
                                        pieces.append((pm[:, off:off + 256], lb[0] // 4, acc_ps[0:65, lb[0] * 128:(lb[0] + 2) * 128]))
                                    else:
                                        for bi, bl in enumerate(lb):
                                            pieces.append((pm[:, off + 128 * bi:off + 128 * (bi + 1)], bl // 4,
                                                           acc_ps[0:65, bl * 128:(bl + 1) * 128]))
                                else:
                                    for bi, bl in enumerate(blocks):
                                        lbl = bl - 2 * half
                                        pieces.append((pm[:, off + 128 * bi:off + 128 * (bi + 1)], lbl,
                                                       acc_ps[0:65, sl(lbl * 512 + r, 128, 4)]))
                                for rhs, bank, o_ap in pieces:
                                    st = bank not in started
                                    started.add(bank)
                                    S.op("tensor", lambda e, lhsT=lhsT, rhs=rhs, o_ap=o_ap, st=st: e.matmul(
                                        out=o_ap, lhsT=lhsT, rhs=rhs, start=st, stop=False, skip_group_check=True),
                                        reads=[pmk, f"va{d}_{r}"], writes=["ps_acc"])
                            run_pipeline(mk_chunks((1, 4), half), pv14)
                            eb = ev % 2; ev += 1
                            S.op("scalar", lambda e, eb=eb: e.activation(out=accsb[eb][:], in_=acc_ps[0:65, :], func=AF.Copy),
                                 reads=["ps_acc"], writes=[f"accsb{eb}"])
                            S.op("vector", lambda e, eb=eb, half=half: e.tensor_tensor(
                                out=accsb[eb][:].rearrange("p (q r) -> p q r", r=16), in0=accsb[eb][:].rearrange("p (q r) -> p q r", r=16),
                                in1=acc16[:, :, 64 * half:64 * half + 64].rearrange("p r q -> p q r"), op=ALU.add),
                                reads=[f"accsb{eb}", "acc16"], writes=[f"accsb{eb}"])
                            for j in range(2):
                                rb = j % 2
                                S.op("tensor", lambda e, eb=eb, j=j: e.matmul(
                                    out=bc_ps[:], lhsT=ones_f[64:65, 0:64], rhs=accsb[eb][64:65, j * 512:(j + 1) * 512],
                                    start=True, stop=True), reads=["ones_f", f"accsb{eb}"], writes=["ps_bc"])
                                S.op("vector", lambda e, rb=rb: e.reciprocal(out=rden[rb][:], in_=bc_ps[:]),
                                     reads=["ps_bc"], writes=[f"rden{rb}"])
                                S.op("vector", lambda e, eb=eb, rb=rb, j=j: e.tensor_tensor(
                                    out=accsb[eb][0:64, j * 512:(j + 1) * 512], in0=accsb[eb][0:64, j * 512:(j + 1) * 512],
                                    in1=rden[rb][:], op=ALU.mult), reads=[f"accsb{eb}", f"rden{rb}"], writes=[f"accsb{eb}"])
                            S.dma("sync", lambda e, eb=eb, h=h, sbi=sbi, half=half: e.dma_start(
                                out=attn_s.ap()[h, :, sbi * 2048 + half * 1024:sbi * 2048 + (half + 1) * 1024], in_=accsb[eb][0:64, :]),
                                reads=[f"accsb{eb}"], writes=["attn_s"])

        if stage >= 3:
            S.barrier()
            TWO_PI = 6.283185307179586
            with ExitStack() as pc:
                def sbc(name, shape, dt=F32):
                    return pc.enter_context(nc.sbuf_tensor(name, list(shape), dt))
                kv = sbc("kv", [128, 72])
                maskfb = sbc("maskfb", [128, 2, 128])
                P8 = sbc("P8", [128, 4, 32, 9])
                Y0 = sbc("Y0", [128, 2, 32, 128])
                Xpb = sbc("Xpb", [128, 2, 32, 128], BF16)
                A64 = sbc("A64", [128, 2, 2, 32])
                CT = sbc("CT", [128, 2, 32, 16])
                Pw18 = sbc("Pw18", [128, 2, 32, 8])
                G2o = Y0[:].rearrange("p a q x -> p (a q x)").bitcast(BF16)[:, 0:3 * 64 * 2 * 32].rearrange(
                    "p (s n c q) -> p s n c q", s=3, n=64, c=2)
                S.dma("sync", lambda e: e.dma_start(out=kv[:], in_=consts2.ap()[:, 0:72]), writes=["kv"])
                S.dma("sync", lambda e: e.dma_start(out=maskfb[:], in_=consts2.ap()[:, 72:328].rearrange("p (a b) -> p a b", a=2)),
                      writes=["maskfb"])
                with ExitStack() as pc0:
                    lrli = pc0.enter_context(nc.sbuf_tensor("lrli", [128, 3, 32], F32))
                    th = pc0.enter_context(nc.sbuf_tensor("th", [128, 3, 32], F32))
                    PwR = pc0.enter_context(nc.sbuf_tensor("PwR", [128, 32, 72], F32))
                    PwI = pc0.enter_context(nc.sbuf_tensor("PwI", [128, 32, 72], F32))
                    Bb = pc0.enter_context(nc.sbuf_tensor("Bb", [128, 2, 32, 16], F32))
                    Xp = pc0.enter_context(nc.sbuf_tensor("Xp", [128, 2, 32, 128], F32))
                    lst = pc0.enter_context(nc.sbuf_tensor("lst", [32, 3, 128], F32))
                    ldt2 = pc0.enter_context(nc.sbuf_tensor("ldt2", [32, 2], F32))
                    cst = [pc0.enter_context(nc.sbuf_tensor(f"cst{i}", [128, 2, 64], F32)) for i in range(2)]
                    Braw = pc0.enter_context(nc.sbuf_tensor("Braw", [128, 2, 32, 16], F32))
                    zz = pc0.enter_context(nc.sbuf_tensor("zz", [128, 8, 32], F32))
                    tb1 = pc0.enter_context(nc.sbuf_tensor("tb1", [128, 32, 16], F32))
                    tb2 = pc0.enter_context(nc.sbuf_tensor("tb2", [128, 32, 16], F32))
                    ps_p = pc0.enter_context(nc.psum_tensor("ps_par", [128, 96], F32))
                    ps_c = [pc0.enter_context(nc.psum_tensor(f"ps_ct{i}", [128, 128], F32)) for i in range(2)]
                    S.dma("sync", lambda e: e.dma_start(out=lst[:, 0, :], in_=lam_re.ap()), writes=["lst"])
                    S.dma("sync", lambda e: e.dma_start(out=lst[:, 1, :], in_=lam_im.ap()), writes=["lst"])
                    S.dma("sync", lambda e: e.dma_start(out=ldt2[:], in_=log_dt.ap()), writes=["ldt2"])
                    S.op("vector", lambda e: e.tensor_copy(
                        out=lst[:, 2, :].rearrange("q (j p) -> q j p", j=2),
                        in_=bass.AP(ldt2, 0, [[2, 32], [1, 2], [0, 64]])), reads=["ldt2"], writes=["lst"])
                    for a in range(3):
                        S.op("tensor", lambda e, a=a: e.transpose(out=ps_p[:, a * 32:(a + 1) * 32], in_=lst[:, a, :],
                                                                 identity=ident_f[0:32, 0:32]),
                             reads=["lst", "ident_f"], writes=["ps_par"])
                    S.op("vector", lambda e: e.tensor_copy(out=lrli[:].rearrange("p a q -> p (a q)"), in_=ps_p[:]),
                         reads=["ps_par"], writes=["lrli"])
                    S.dma("sync", lambda e: e.dma_start(out=Braw[:, 0, :, :], in_=b_re.ap().rearrange("q p c -> p q c")),
                          writes=["Braw"])
                    S.dma("sync", lambda e: e.dma_start(out=Braw[:, 1, :, :], in_=b_im.ap().rearrange("q p c -> p q c")),
                          writes=["Braw"])
                    n = 0
                    for comp, csrc in enumerate((c_re, c_im)):
                        for qb in range(4):
                            cb = n % 2; n += 1
                            for ql in range(8):
                                q = qb * 8 + ql
                                S.dma("sync", lambda e, cb=cb, ql=ql, q=q, csrc=csrc: e.dma_start(
                                    out=cst[cb][ql * 16:(ql + 1) * 16, :, :],
                                    in_=csrc.ap()[2 * q:2 * q + 2].rearrange("j c p -> c j p")), writes=[f"cst{cb}"])
                            S.op("tensor", lambda e, cb=cb: e.transpose(
                                out=ps_c[cb][:], in_=cst[cb][:].rearrange("p j x -> p (j x)"), identity=ident_f[:]),
                                reads=[f"cst{cb}", "ident_f"], writes=[f"ps_ct{cb}"])
                            S.op("vector", lambda e, cb=cb, comp=comp, qb=qb: e.tensor_copy(
                                out=CT[:, comp, qb * 8:(qb + 1) * 8, :], in_=ps_c[cb][:].rearrange("p (q c) -> p q c", c=16)),
                                reads=[f"ps_ct{cb}"], writes=["CT"])
                    S.op("scalar", lambda e: e.activation(out=th[:, 2, :], in_=lrli[:, 2, :], func=AF.Exp),
                         reads=["lrli"], writes=["th"])
                    S.op("vector", lambda e: e.tensor_tensor(out=th[:, 0, :], in0=lrli[:, 0, :], in1=th[:, 2, :], op=ALU.mult),
                         reads=["lrli", "th"], writes=["th"])
                    S.op("vector", lambda e: e.tensor_tensor(out=th[:, 1, :], in0=lrli[:, 1, :], in1=th[:, 2, :], op=ALU.mult),
                         reads=["lrli", "th"], writes=["th"])
                    bt = pc0.enter_context(nc.sbuf_tensor("bt", [128, 12, 32], F32))
                    halfpi = pc0.enter_context(nc.sbuf_tensor("halfpi", [128, 1], F32))
                    cm = [pc0.enter_context(nc.sbuf_tensor(f"cm{i}", [128, 32, 32], F32)) for i in range(4)]
                    B_ = lambda i: bt[:, i, :]
                    def dvp(fn, reads, writes):
                        S.op("vector", fn, reads=reads, writes=writes)
                    dvp(lambda e: e.memset(halfpi[:], 1.5707963267948966), [], ["halfpi"])
                    S.op("scalar", lambda e: e.activation(out=B_(0), in_=th[:, 0, :], func=AF.Exp, scale=1.0 / 16), reads=["th"], writes=["bt0"])
                    S.op("scalar", lambda e: e.activation(out=B_(1), in_=th[:, 1, :], func=AF.Sin, scale=1.0 / 16), reads=["th"], writes=["bt1"])
                    S.op("scalar", lambda e: e.activation(out=B_(2), in_=th[:, 1, :], func=AF.Sin, scale=-1.0 / 16, bias=halfpi[:]),
                         reads=["th", "halfpi"], writes=["bt2"])
                    dvp(lambda e: e.tensor_tensor(out=B_(3), in0=B_(0), in1=B_(2), op=ALU.mult), ["bt0", "bt2"], ["bt3"])
                    dvp(lambda e: e.tensor_tensor(out=B_(4), in0=B_(0), in1=B_(1), op=ALU.mult), ["bt0", "bt1"], ["bt4"])
                    cur_r, cur_i = 3, 4
                    for k in range(4):
                        nr, ni = (5, 6) if cur_r == 3 else (3, 4)
                        dvp(lambda e, cr_=cur_r: e.tensor_tensor(out=B_(7), in0=B_(cr_), in1=B_(cr_), op=ALU.mult), [f"bt{cur_r}"], ["bt7"])
                        dvp(lambda e, ci_=cur_i: e.tensor_tensor(out=B_(8), in0=B_(ci_), in1=B_(ci_), op=ALU.mult), [f"bt{cur_i}"], ["bt8"])
                        dvp(lambda e, nr=nr: e.tensor_tensor(out=B_(nr), in0=B_(7), in1=B_(8), op=ALU.subtract), ["bt7", "bt8"], [f"bt{nr}"])
                        dvp(lambda e, cr_=cur_r, ci_=cur_i: e.tensor_tensor(out=B_(9), in0=B_(cr_), in1=B_(ci_), op=ALU.mult),
                            [f"bt{cur_r}", f"bt{cur_i}"], ["bt9"])
                        dvp(lambda e, ni=ni: e.tensor_scalar(out=B_(ni), in0=B_(9), scalar1=2.0, scalar2=None, op0=ALU.mult), ["bt9"], [f"bt{ni}"])
                        cur_r, cur_i = nr, ni
                    assert (cur_r, cur_i) == (3, 4)
                    dvp(lambda e: e.tensor_copy(out=PwR[:, :, 0], in_=B_(3)), ["bt3"], ["PwR"])
                    dvp(lambda e: e.tensor_copy(out=PwI[:, :, 0], in_=B_(4)), ["bt4"], ["PwI"])
                    dvp(lambda e: e.tensor_tensor(out=B_(7), in0=B_(3), in1=B_(3), op=ALU.mult), ["bt3"], ["bt7"])
                    dvp(lambda e: e.tensor_tensor(out=B_(8), in0=B_(4), in1=B_(4), op=ALU.mult), ["bt4"], ["bt8"])
                    dvp(lambda e: e.tensor_tensor(out=B_(7), in0=B_(7), in1=B_(8), op=ALU.add), ["bt7", "bt8"], ["bt7"])
                    dvp(lambda e: e.reciprocal(out=B_(7), in_=B_(7)), ["bt7"], ["bt7"])
                    dvp(lambda e: e.tensor_tensor(out=PwR[:, :, 71], in0=B_(3), in1=B_(7), op=ALU.mult), ["bt3", "bt7"], ["PwR"])
                    dvp(lambda e: e.scalar_tensor_tensor(out=PwI[:, :, 71], in0=B_(4), scalar=-1.0, in1=B_(7), op0=ALU.mult, op1=ALU.mult),
                        ["bt4", "bt7"], ["PwI"])

                    def cmul(o_sl, x_sl, y_sl, n):
                        t = [c_[:, :, 0:n] for c_ in cm]
                        xr, xi = PwR[:, :, x_sl], PwI[:, :, x_sl]
                        yr, yi = y_sl
                        dvp(lambda e: e.tensor_tensor(out=t[0], in0=xr, in1=yr, op=ALU.mult), ["PwR"], ["cm0"])
                        dvp(lambda e: e.tensor_tensor(out=t[1], in0=xi, in1=yi, op=ALU.mult), ["PwI"], ["cm1"])
                        dvp(lambda e: e.tensor_tensor(out=t[2], in0=xr, in1=yi, op=ALU.mult), ["PwR", "PwI"], ["cm2"])
                        dvp(lambda e: e.tensor_tensor(out=t[3], in0=xi, in1=yr, op=ALU.mult), ["PwR", "PwI"], ["cm3"])
                        dvp(lambda e: e.tensor_tensor(out=PwR[:, :, o_sl], in0=t[0], in1=t[1], op=ALU.subtract), ["cm0", "cm1"], ["PwR"])
                        dvp(lambda e: e.tensor_tensor(out=PwI[:, :, o_sl], in0=t[2], in1=t[3], op=ALU.add), ["cm2", "cm3"], ["PwI"])

                    def bc(kidx_, n):
                        return (bass.AP(PwR, kidx_, [[32 * 72, 128], [72, 32], [0, n]]), bass.AP(PwI, kidx_, [[32 * 72, 128], [72, 32], [0, n]]))
                    for k in range(2, 9):
                        cmul(slice(k - 1, k), slice(k - 2, k - 1), bc(0, 1), 1)
                    cmul(slice(8, 16), slice(0, 8), bc(7, 8), 8)
                    cmul(slice(16, 32), slice(0, 16), bc(15, 16), 16)
                    cmul(slice(32, 64), slice(0, 32), bc(31, 32), 32)
                    for k in range(2, 9):
                        cmul(slice(72 - k, 73 - k), slice(73 - k, 74 - k), bc(71, 1), 1)
                    S.op("vector", lambda e: e.tensor_copy(out=P8[:, 0, :, 1:9], in_=PwR[:, :, sl(7, 8, 8)]), reads=["PwR"], writes=["P8"])
                    S.op("vector", lambda e: e.tensor_copy(out=P8[:, 1, :, 1:9], in_=PwI[:, :, sl(7, 8, 8)]), reads=["PwI"], writes=["P8"])
                    S.op("vector", lambda e: e.tensor_scalar(out=P8[:, 2, :, 1:9], in0=PwR[:, :, sl(7, 8, 8)], scalar1=-1.0, scalar2=None,
                                                             op0=ALU.mult), reads=["PwR"], writes=["P8"])
                    S.op("vector", lambda e: e.tensor_scalar(out=P8[:, 3, :, 1:9], in0=PwI[:, :, sl(7, 8, 8)], scalar1=-1.0, scalar2=None,
                                                             op0=ALU.mult), reads=["PwI"], writes=["P8"])
                    lr, li = lrli[:, 0, :], lrli[:, 1, :]
                    ar, ai = PwR[:, :, 0], PwI[:, :, 0]
                    V = lambda i: zz[:, i, :]
                    def dv(fn, reads, writes):
                        S.op("vector", fn, reads=reads, writes=writes)
                    dv(lambda e: e.tensor_scalar(out=V(0), in0=ar, scalar1=-1.0, scalar2=None, op0=ALU.add), ["PwR"], ["zz"])
                    dv(lambda e: e.tensor_tensor(out=V(1), in0=lr, in1=lr, op=ALU.mult), ["lrli"], ["zz"])
                    dv(lambda e: e.tensor_tensor(out=V(2), in0=li, in1=li, op=ALU.mult), ["lrli"], ["zz"])
                    dv(lambda e: e.tensor_tensor(out=V(1), in0=V(1), in1=V(2), op=ALU.add), ["zz"], ["zz"])
                    dv(lambda e: e.reciprocal(out=V(6), in_=V(1)), ["zz"], ["zz"])
                    dv(lambda e: e.tensor_tensor(out=V(2), in0=V(0), in1=lr, op=ALU.mult), ["zz", "lrli"], ["zz"])
                    dv(lambda e: e.tensor_tensor(out=V(3), in0=ai, in1=li, op=ALU.mult), ["PwI", "lrli"], ["zz"])
                    dv(lambda e: e.tensor_tensor(out=V(2), in0=V(2), in1=V(3), op=ALU.add), ["zz"], ["zz"])
                    dv(lambda e: e.tensor_tensor(out=V(4), in0=V(2), in1=V(6), op=ALU.mult), ["zz"], ["zz"])
                    dv(lambda e: e.tensor_tensor(out=V(2), in0=ai, in1=lr, op=ALU.mult), ["PwI", "lrli"], ["zz"])
                    dv(lambda e: e.tensor_tensor(out=V(3), in0=V(0), in1=li, op=ALU.mult), ["zz", "lrli"], ["zz"])
                    dv(lambda e: e.tensor_tensor(out=V(2), in0=V(2), in1=V(3), op=ALU.subtract), ["zz"], ["zz"])
                    dv(lambda e: e.tensor_tensor(out=V(5), in0=V(2), in1=V(6), op=ALU.mult), ["zz"], ["zz"])
                    bc = lambda i: bass.AP(zz, i * 32, [[8 * 32, 128], [1, 32], [0, 16]])
                    dv(lambda e: e.tensor_tensor(out=tb1[:], in0=Braw[:, 0, :, :], in1=bc(4), op=ALU.mult), ["Braw", "zz"], ["tb1"])
                    dv(lambda e: e.tensor_tensor(out=tb2[:], in0=Braw[:, 1, :, :], in1=bc(5), op=ALU.mult), ["Braw", "zz"], ["tb2"])
                    dv(lambda e: e.tensor_tensor(out=Bb[:, 0, :, :], in0=tb1[:], in1=tb2[:], op=ALU.subtract), ["tb1", "tb2"], ["Bb"])
                    dv(lambda e: e.tensor_tensor(out=tb1[:], in0=Braw[:, 1, :, :], in1=bc(4), op=ALU.mult), ["Braw", "zz"], ["tb1"])
                    dv(lambda e: e.tensor_tensor(out=tb2[:], in0=Braw[:, 0, :, :], in1=bc(5), op=ALU.mult), ["Braw", "zz"], ["tb2"])
                    dv(lambda e: e.tensor_tensor(out=Bb[:, 1, :, :], in0=tb1[:], in1=tb2[:], op=ALU.add), ["tb1", "tb2"], ["Bb"])
                    def kidx(k):
                        return k - 1 if k >= 1 else 72 + k
                    t3 = pc0.enter_context(nc.sbuf_tensor("t3", [128, 16, 16], F32))
                    t4 = pc0.enter_context(nc.sbuf_tensor("t4", [128, 16, 16], F32))
                    for dr in range(2):
                        qs = slice(dr * 16, dr * 16 + 16)
                        for sidx in range(8):
                            tau = sidx if dr == 0 else 7 - sidx
                            for (dst, srcT, kk, sgn) in ((Xp, Bb, kidx(-tau - 1), None),):
                                pw_r = bass.AP(PwR, dr * 16 * 72 + kk, [[32 * 72, 128], [72, 16], [0, 16]])
                                pw_i = bass.AP(PwI, dr * 16 * 72 + kk, [[32 * 72, 128], [72, 16], [0, 16]])
                                o_re = dst[:, 0, qs, sidx * 16:(sidx + 1) * 16]
                                o_im = dst[:, 1, qs, sidx * 16:(sidx + 1) * 16]
                                sk = "Xp" if dst is Xp else "Y0"
                                tk = "Bb" if srcT is Bb else "CT"
                                dv(lambda e, srcT=srcT, pw_r=pw_r, qs=qs: e.tensor_tensor(out=t3[:], in0=srcT[:, 0, qs, :], in1=pw_r, op=ALU.mult), [tk, "PwR"], ["t3"])
                                dv(lambda e, srcT=srcT, pw_i=pw_i, qs=qs: e.tensor_tensor(out=t4[:], in0=srcT[:, 1, qs, :], in1=pw_i, op=ALU.mult), [tk, "PwI"], ["t4"])
                                dv(lambda e, o_re=o_re: e.tensor_tensor(out=o_re, in0=t3[:], in1=t4[:], op=ALU.subtract), ["t3", "t4"], [sk])
                                dv(lambda e, srcT=srcT, pw_i=pw_i, qs=qs: e.tensor_tensor(out=t3[:], in0=srcT[:, 0, qs, :], in1=pw_i, op=ALU.mult), [tk, "PwI"], ["t3"])
                                dv(lambda e, srcT=srcT, pw_r=pw_r, qs=qs: e.tensor_tensor(out=t4[:], in0=srcT[:, 1, qs, :], in1=pw_r, op=ALU.mult), [tk, "PwR"], ["t4"])
                                dv(lambda e, o_im=o_im: e.tensor_tensor(out=o_im, in0=t3[:], in1=t4[:], op=ALU.add), ["t3", "t4"], [sk])
                    S.op("gpsimd", lambda e: e.tensor_copy(out=Xpb[:].rearrange("p a q x -> p (a q x)"),
                                                           in_=Xp[:].rearrange("p a q x -> p (a q x)")), reads=["Xp"], writes=["Xpb"])
                    dv(lambda e: e.tensor_copy(out=Pw18[:, 0, :, :], in_=PwR[:, :, 0:8]), ["PwR"], ["Pw18"])
                    dv(lambda e: e.tensor_copy(out=Pw18[:, 1, :, :], in_=PwI[:, :, 0:8]), ["PwI"], ["Pw18"])
                    dv(lambda e: e.tensor_copy(out=A64[:, 0, 0, :], in_=P8[:, 0, :, 8]), ["P8"], ["A64"])
                    dv(lambda e: e.tensor_copy(out=A64[:, 0, 1, :], in_=P8[:, 0, :, 8]), ["P8"], ["A64"])
                    dv(lambda e: e.tensor_copy(out=A64[:, 1, 0, :], in_=P8[:, 3, :, 8]), ["P8"], ["A64"])
                    dv(lambda e: e.tensor_copy(out=A64[:, 1, 1, :], in_=P8[:, 1, :, 8]), ["P8"], ["A64"])
                    if debug:
                        S.dma("sync", lambda e: e.dma_start(out=dbg_s5.ap()[:, 0:2304], in_=PwR[:].rearrange("p q k -> p (q k)")), reads=["PwR"])
                        S.dma("sync", lambda e: e.dma_start(out=dbg_s5.ap()[:, 2304:4608], in_=PwI[:].rearrange("p q k -> p (q k)")), reads=["PwI"])
                        S.dma("sync", lambda e: e.dma_start(out=dbg_s5.ap()[:, 4608:4608 + 8192], in_=Xp[:].rearrange("p a q x -> p (a q x)")), reads=["Xp"])
                    S.barrier()
                G2 = sbc("G2", [128, 64, 2, 32])
                Hall = sbc("Hall", [128, 65, 2, 32])
                Hb = sbc("Hb", [128, 64, 2, 32], BF16)
                U_all = sbc("U_all", [128, 32, 512], BF16)
                Wtab = sbc("Wtab", [128, 64, 2, 32])
                flc = sbc("flc", [128, 6])
                S.dma("sync", lambda e: e.dma_start(out=flc[:], in_=flcol.ap()), writes=["flc"])
                wp_ = [sbc(f"wP{i}", [128, 2, 32]) for i in range(2)]
                wq_ = [sbc(f"wQ{i}", [128, 2, 32]) for i in range(2)]
                S.op("gpsimd", lambda e: e.memset(Wtab[:], 0.0), writes=["Wt_init"])
                S.op("gpsimd", lambda e: e.memset(Wtab[:, 63, 0, 0:16], 1.0), reads=["Wt_init"], writes=["Wt0"])
                S.op("gpsimd", lambda e: e.memset(Wtab[:, 0, 0, 16:32], 1.0), reads=["Wt_init"], writes=["Wt0"])
                WS = 64 * 64
                for st in range(63):
                    b2 = st % 2
                    cur = bass.AP(Wtab, (63 - st) * 64, [[WS, 128], [32, 2], [(2 * st - 63) * 64 + 16, 2], [1, 16]])
                    swp = bass.AP(Wtab, (63 - st) * 64 + 32, [[WS, 128], [-32, 2], [(2 * st - 63) * 64 + 16, 2], [1, 16]])
                    nxt = bass.AP(Wtab, (62 - st) * 64, [[WS, 128], [32, 2], [(2 * st - 61) * 64 + 16, 2], [1, 16]])
                    v4 = lambda t_: t_[:].rearrange("p c (d q) -> p c d q", d=2)
                    S.op("gpsimd", lambda e, cur=cur, b2=b2: e.tensor_tensor(out=v4(wp_[b2]), in0=cur, in1=v4(A64[:, 0, :, :]), op=ALU.mult)
                         if False else e.tensor_tensor(out=wp_[b2][:].rearrange("p c (d q) -> p c d q", d=2), in0=cur,
                                                       in1=A64[:, 0, :, :].rearrange("p c (d q) -> p c d q", d=2), op=ALU.mult),
                         reads=[f"Wt{st}", "A64"], writes=[f"wP{b2}"])
                    S.op("gpsimd", lambda e, swp=swp, b2=b2: e.tensor_tensor(
                        out=wq_[b2][:].rearrange("p c (d q) -> p c d q", d=2), in0=swp,
                        in1=A64[:, 1, :, :].rearrange("p c (d q) -> p c d q", d=2), op=ALU.mult),
                        reads=[f"Wt{st}", "A64"], writes=[f"wQ{b2}"])
                    S.op("gpsimd", lambda e, nxt=nxt, b2=b2: e.tensor_tensor(
                        out=nxt, in0=wp_[b2][:].rearrange("p c (d q) -> p c d q", d=2),
                        in1=wq_[b2][:].rearrange("p c (d q) -> p c d q", d=2), op=ALU.add),
                        reads=[f"wP{b2}", f"wQ{b2}"], writes=[f"Wt{st + 1}"])
                pcu = ExitStack()
                Uraw = [pcu.enter_context(nc.sbuf_tensor("Uraw0", [128, 8, 512], BF16))] * 2
                for gq in range(4):
                    ur = Uraw[0]; urk = "Uraw0"
                    S.dma("sync", lambda e, ur=ur, gq=gq: e.dma_start(out=ur[:], in_=Us.ap()[0, gq * 8:(gq + 1) * 8].rearrange("g p n -> p g n")),
                          reads=["Us"], writes=[urk])
                    S.op("vector", lambda e, ur=ur, gq=gq: e.tensor_copy(
                        out=U_all[:, gq * 8:(gq + 1) * 8, :].rearrange("p g (m n) -> p g m n", m=8),
                        in_=ur[:].rearrange("p g (n m) -> p g m n", m=8)), reads=[urk], writes=["U_all"])

                def rot_batch(src, sk, q, items, neg_im):
                    for ki, key_out, dstfn, (tr, ti), tkey in items:
                        cr = P8[:, 0, q, ki:ki + 1]; ci = P8[:, 1, q, ki:ki + 1]; nci = P8[:, 3, q, ki:ki + 1]
                        S.op("scalar", lambda e, tr=tr, cr=cr: e.activation(out=tr[:], in_=src[:, 0, q, :], func=AF.Copy, scale=cr),
                             reads=[sk, "P8"], writes=[tkey + "r"])
                        S.op("scalar", lambda e, ti=ti, ci=ci, nci=nci: e.activation(out=ti[:], in_=src[:, 0, q, :], func=AF.Copy,
                                                                                 scale=(nci if neg_im else ci)),
                             reads=[sk, "P8"], writes=[tkey + "i"])
                    for ki, key_out, dstfn, (tr, ti), tkey in items:
                        cr = P8[:, 0, q, ki:ki + 1]; ncr = P8[:, 2, q, ki:ki + 1]; nci = P8[:, 3, q, ki:ki + 1]
                        S.op("vector", lambda e, tr=tr, nci=nci, dstfn=dstfn: e.scalar_tensor_tensor(
                            out=dstfn(0), in0=src[:, 1, q, :], scalar=nci, in1=tr[:], op0=ALU.mult, op1=ALU.add),
                            reads=[sk, "P8", tkey + "r"], writes=[key_out])
                        S.op("vector", lambda e, ti=ti, cr=cr, ncr=ncr, dstfn=dstfn: e.scalar_tensor_tensor(
                            out=dstfn(1), in0=src[:, 1, q, :], scalar=(ncr if neg_im else cr), in1=ti[:], op0=ALU.mult, op1=ALU.add),
                            reads=[sk, "P8", tkey + "i"], writes=[key_out])

                def rot_tables(src, sk, q, ki, outs, neg_im, key_out, dstfn, tkey):
                    cr = P8[:, 0, q, ki:ki + 1]; ci = P8[:, 1, q, ki:ki + 1]
                    ncr = P8[:, 2, q, ki:ki + 1]; nci = P8[:, 3, q, ki:ki + 1]
                    tr, ti = outs
                    S.op("scalar", lambda e: e.activation(out=tr[:], in_=src[:, 0, q, :], func=AF.Copy, scale=cr),
                         reads=[sk, "P8"], writes=[tkey + "r"])
                    S.op("vector", lambda e: e.scalar_tensor_tensor(out=dstfn(0), in0=src[:, 1, q, :], scalar=nci, in1=tr[:],
                                                                    op0=ALU.mult, op1=ALU.add),
                         reads=[sk, "P8", tkey + "r"], writes=[key_out])
                    S.op("scalar", lambda e: e.activation(out=ti[:], in_=src[:, 0, q, :], func=AF.Copy,
                                                          scale=(nci if neg_im else ci)),
                         reads=[sk, "P8"], writes=[tkey + "i"])
                    S.op("vector", lambda e: e.scalar_tensor_tensor(out=dstfn(1), in0=src[:, 1, q, :],
                                                                    scalar=(ncr if neg_im else cr), in1=ti[:],
                                                                    op0=ALU.mult, op1=ALU.add),
                         reads=[sk, "P8", tkey + "i"], writes=[key_out])

                with ExitStack() as pc1:
                    XE = [pc1.enter_context(nc.sbuf_tensor(f"XE{i}", [128, 8, 2, 128], BF16)) for i in range(2)]
                    ET = [pc1.enter_context(nc.sbuf_tensor(f"ET{i}", [128, 8, 2, 128], BF16)) for i in range(2)]
                    trt = [[pc1.enter_context(nc.sbuf_tensor(f"trt{i}{j}", [128, 128], BF16)) for j in range(2)] for i in range(6)]
                    tr_ps = [pc1.enter_context(nc.psum_tensor(f"ps_tr{i}", [128, 2, 2, 128], BF16)) for i in range(2)]
                    g_ps = [pc1.enter_context(nc.psum_tensor(f"ps_g{i}", [128, 2, 4, 64], F32)) for i in range(2)]
                    Ucat = [pc1.enter_context(nc.sbuf_tensor(f"Ucat{i}", [128, 4, 2, 512], BF16)) for i in range(2)]
                    Ucr = Uraw[0][:].rearrange("p g n -> p (g n)").rearrange("p (s g n) -> p s g n", s=4, g=2)
                    tn = 0
                    trn = 0
                    for qq in range(16):
                        ub = qq % 2
                        for sg_ in range(4):
                            S.dma("sync", lambda e, sg_=sg_, qq=qq: e.dma_start(
                                out=Ucr[:, sg_, :, :], in_=Us.ap()[sg_, 2 * qq:2 * qq + 2].rearrange("g p n -> p g n")),
                                reads=["Us"], writes=["Uraw0"])
                        for sg_ in range(4):
                            if sg_ % 2 == 0:
                                S.op("vector", lambda e, ub=ub, sg_=sg_: e.tensor_copy(
                                    out=Ucat[ub][:, sg_, :, :].rearrange("p g (m n) -> p g m n", m=8),
                                    in_=Ucr[:, sg_, :, :].rearrange("p g (n m) -> p g m n", m=8)), reads=["Uraw0"], writes=[f"Ucat{ub}"])
                            else:
                                for g2_ in range(2):
                                    S.op("scalar", lambda e, ub=ub, sg_=sg_, g2_=g2_: e.activation(
                                        out=Ucat[ub][:, sg_, g2_, :].rearrange("p (m n) -> p m n", m=8),
                                        in_=Ucr[:, sg_, g2_, :].rearrange("p (n m) -> p m n", m=8), func=AF.Copy),
                                        reads=["Uraw0"], writes=[f"Ucat{ub}"])
                        for dr in range(2):
                            q = dr * 16 + qq
                            xb = q % 2 if False else dr
                            for m4 in range(0, 8, 4):
                                items = []
                                for m in range(m4, m4 + 4):
                                    mu = m if dr == 0 else 7 - m
                                    outs = trt[tn % 6]; tk_ = f"trt{tn % 6}"; tn += 1
                                    items.append((8 - mu, f"XE{xb}_{m}", (lambda comp, xb=xb, m=m: XE[xb][:, m, comp, :]), outs, tk_))
                                rot_batch(Xpb, "Xpb", q, items, False)
                            for m2 in range(0, 8, 2):
                                tp = tr_ps[trn % 2]; tpk = f"ps_tr{trn % 2}"; trn += 1
                                for mm in range(2):
                                    for comp in range(2):
                                        S.op("tensor", lambda e, tp=tp, xb=xb, m=m2 + mm, mm=mm, comp=comp: e.transpose(
                                            out=tp[:, mm, comp, :], in_=XE[xb][:, m, comp, :], identity=ident_b[:]),
                                            reads=[f"XE{xb}_{m2 + mm}", "ident_b"], writes=[tpk])
                                S.op("vector", lambda e, tp=tp, xb=xb, m2=m2: e.tensor_copy(
                                    out=ET[xb][:, m2:m2 + 2, :, :], in_=tp[:]), reads=[tpk], writes=[f"ET{xb}"])
                            gp = g_ps[dr]; gpk = f"ps_g{dr}"
                            for j2 in range(2):
                                for comp in range(2):
                                    for m in range(8):
                                        S.op("tensor", lambda e, gp=gp, xb=xb, j2=j2, comp=comp, m=m, ub=ub: e.matmul(
                                            out=gp[64 * j2:64 * j2 + 64, comp, :, :], lhsT=ET[xb][:, m, comp, 64 * j2:64 * j2 + 64],
                                            rhs=Ucat[ub][:, :, j2, m * 64:(m + 1) * 64], start=(m == 0), stop=(m == 7)),
                                            reads=[f"ET{xb}", f"Ucat{ub}"], writes=[gpk])
                            S.op("scalar", lambda e, gp=gp, q=q: e.activation(
                                out=G2[:, :, :, q].rearrange("p n c -> p c n"), in_=gp[:, :, 0, :], func=AF.Copy),
                                reads=[gpk], writes=["G2"])
                            S.op("scalar", lambda e, gp=gp, q=q: e.activation(
                                out=G2o[:, :, :, :, q].rearrange("p s n c -> p c s n"), in_=gp[:, :, 1:4, :], func=AF.Copy),
                                reads=[gpk], writes=["G2o"])
                    S.barrier()
                pcu.close()
                cacc = sbc("cacc", [128, 2, 32])
                with ExitStack() as pcc:
                    def sbx(name, shape, dt=F32):
                        return pcc.enter_context(nc.sbuf_tensor(name, list(shape), dt))
                    T1 = sbx("cT1", [128, 64, 32], BF16); T2 = sbx("cT2", [128, 64, 32], BF16)
                    Wtb = sbx("Wtb", [128, 64, 2, 32], BF16)
                    S.op("vector", lambda e: e.tensor_copy(out=Wtb[:].rearrange("p n c q -> p (n c q)"),
                                                           in_=Wtab[:].rearrange("p n c q -> p (n c q)")),
                         reads=[f"Wt{i}" for i in range(64)], writes=["Wtb"])
                    Sall = sbx("Sall", [128, 3, 2, 32])
                    Asq = [sbx(f"Asq{i}", [128, 2, 32]) for i in range(2)]
                    ctm = [sbx(f"ctm{i}", [128, 32]) for i in range(4)]
                    ctt = sbx("ctt", [128, 2, 32])
                    wt_keys = [f"Wt{i}" for i in range(64)]
                    Wc = lambda c: Wtb[:, :, c, :]
                    for i in range(3):
                        Gc = lambda c, i=i: G2o[:, i, :, c, :]
                        for comp, (wa, ga, wb_, gb_, op) in enumerate(((0, 0, 1, 1, ALU.subtract), (0, 1, 1, 0, ALU.add))):
                            S.op("vector", lambda e, wa=wa, ga=ga, Gc=Gc: e.tensor_tensor(out=T1[:], in0=Wc(wa), in1=Gc(ga), op=ALU.mult),
                                 reads=["Wtb", "G2o"], writes=["cT1"])
                            S.op("vector", lambda e, wb_=wb_, gb_=gb_, Gc=Gc: e.tensor_tensor(out=T2[:], in0=Wc(wb_), in1=Gc(gb_), op=ALU.mult),
                                 reads=["Wtb", "G2o"], writes=["cT2"])
                            S.op("vector", lambda e, op=op: e.tensor_tensor(out=T1[:], in0=T1[:], in1=T2[:], op=op),
                                 reads=["cT1", "cT2"], writes=["cT1"])
                            S.op("vector", lambda e, i=i, comp=comp: e.tensor_reduce(
                                out=Sall[:, i, comp, :], in_=T1[:].rearrange("p n q -> p q n"), axis=AX.X, op=ALU.add),
                                reads=["cT1"], writes=["Sall"])
                    S.op("vector", lambda e: e.tensor_copy(out=Asq[0][:, 0, :], in_=A64[:, 0, 0, :]), reads=["A64"], writes=["Asq0"])
                    S.op("vector", lambda e: e.tensor_copy(out=Asq[0][:, 1, :], in_=A64[:, 1, 1, :]), reads=["A64"], writes=["Asq0"])
                    for k in range(6):
                        a_, b_ = Asq[k % 2], Asq[(k + 1) % 2]
                        ak, bk = f"Asq{k % 2}", f"Asq{(k + 1) % 2}"
                        S.op("vector", lambda e, a_=a_: e.tensor_tensor(out=ctm[0][:], in0=a_[:, 0, :], in1=a_[:, 0, :], op=ALU.mult), reads=[ak], writes=["ctm0"])
                        S.op("vector", lambda e, a_=a_: e.tensor_tensor(out=ctm[1][:], in0=a_[:, 1, :], in1=a_[:, 1, :], op=ALU.mult), reads=[ak], writes=["ctm1"])
                        S.op("vector", lambda e, b_=b_: e.tensor_tensor(out=b_[:, 0, :], in0=ctm[0][:], in1=ctm[1][:], op=ALU.subtract), reads=["ctm0", "ctm1"], writes=[bk])
                        S.op("vector", lambda e, a_=a_: e.tensor_tensor(out=ctm[2][:], in0=a_[:, 0, :], in1=a_[:, 1, :], op=ALU.mult), reads=[ak], writes=["ctm2"])
                        S.op("vector", lambda e, b_=b_: e.tensor_scalar(out=b_[:, 1, :], in0=ctm[2][:], scalar1=2.0, scalar2=None, op0=ALU.mult), reads=["ctm2"], writes=[bk])
                    A4k = Asq[0]; A4kk = "Asq0"
                    S.op("vector", lambda e: e.memset(cacc[:], 0.0), writes=["cacc"])
                    for half, order, fbase in ((slice(0, 16), (0, 1, 2), 0), (slice(16, 32), (2, 1, 0), 3)):
                        for i in order:
                            S.op("vector", lambda e, half=half: e.tensor_tensor(out=ctm[0][:, half], in0=A4k[:, 0, half], in1=cacc[:, 0, half], op=ALU.mult), reads=[A4kk, "cacc"], writes=["ctm0"])
                            S.op("vector", lambda e, half=half: e.tensor_tensor(out=ctm[1][:, half], in0=A4k[:, 1, half], in1=cacc[:, 1, half], op=ALU.mult), reads=[A4kk, "cacc"], writes=["ctm1"])
                            S.op("vector", lambda e, half=half: e.tensor_tensor(out=ctm[2][:, half], in0=A4k[:, 0, half], in1=cacc[:, 1, half], op=ALU.mult), reads=[A4kk, "cacc"], writes=["ctm2"])
                            S.op("vector", lambda e, half=half: e.tensor_tensor(out=ctm[3][:, half], in0=A4k[:, 1, half], in1=cacc[:, 0, half], op=ALU.mult), reads=[A4kk, "cacc"], writes=["ctm3"])
                            S.op("vector", lambda e, half=half: e.tensor_tensor(out=ctt[:, 0, half], in0=ctm[0][:, half], in1=ctm[1][:, half], op=ALU.subtract), reads=["ctm0", "ctm1"], writes=["ctt"])
                            S.op("vector", lambda e, half=half: e.tensor_tensor(out=ctt[:, 1, half], in0=ctm[2][:, half], in1=ctm[3][:, half], op=ALU.add), reads=["ctm2", "ctm3"], writes=["ctt"])
                            S.op("vector", lambda e, half=half, i=i: e.tensor_tensor(out=ctt[:, :, half], in0=ctt[:, :, half], in1=Sall[:, i, :, half], op=ALU.add), reads=["ctt", "Sall"], writes=["ctt"])
                            S.op("vector", lambda e, half=half: e.tensor_tensor(out=ctt[:, :, half], in0=ctt[:, :, half], in1=cacc[:, :, half], op=ALU.subtract), reads=["ctt", "cacc"], writes=["ctt"])
                            S.op("vector", lambda e, half=half, i=i, fbase=fbase: e.scalar_tensor_tensor(
                                out=cacc[:, :, half], in0=ctt[:, :, half], scalar=flc[:, fbase + i:fbase + i + 1], in1=cacc[:, :, half],
                                op0=ALU.mult, op1=ALU.add), reads=["ctt", "cacc", "flc"], writes=["cacc"])
                    S.barrier()
                with ExitStack() as pcy:
                    t3y_ = pcy.enter_context(nc.sbuf_tensor("t3y", [128, 16, 16], F32))
                    t4y_ = pcy.enter_context(nc.sbuf_tensor("t4y", [128, 16, 16], F32))
                    def dvy(fn, reads, writes):
                        S.op("vector", fn, reads=reads, writes=writes)
                    for dr in range(2):
                        qs = slice(dr * 16, dr * 16 + 16)
                        for sidx in range(8):
                            tau = sidx if dr == 0 else 7 - sidx
                            pw_r = bass.AP(Pw18, (0 * 32 + dr * 16) * 8 + tau, [[2 * 32 * 8, 128], [8, 16], [0, 16]])
                            pw_i = bass.AP(Pw18, (1 * 32 + dr * 16) * 8 + tau, [[2 * 32 * 8, 128], [8, 16], [0, 16]])
                            o_re = Y0[:, 0, qs, sidx * 16:(sidx + 1) * 16]
                            o_im = Y0[:, 1, qs, sidx * 16:(sidx + 1) * 16]
                            dvy(lambda e, pw_r=pw_r, qs=qs: e.tensor_tensor(out=t3y_[:], in0=CT[:, 0, qs, :], in1=pw_r, op=ALU.mult), ["CT", "Pw18"], ["t3y"])
                            dvy(lambda e, pw_i=pw_i, qs=qs: e.tensor_tensor(out=t4y_[:], in0=CT[:, 1, qs, :], in1=pw_i, op=ALU.mult), ["CT", "Pw18"], ["t4y"])
                            dvy(lambda e, o_re=o_re: e.tensor_tensor(out=o_re, in0=t3y_[:], in1=t4y_[:], op=ALU.subtract), ["t3y", "t4y"], ["Y0"])
                            dvy(lambda e, pw_i=pw_i, qs=qs: e.tensor_tensor(out=t3y_[:], in0=CT[:, 0, qs, :], in1=pw_i, op=ALU.mult), ["CT", "Pw18"], ["t3y"])
                            dvy(lambda e, pw_r=pw_r, qs=qs: e.tensor_tensor(out=t4y_[:], in0=CT[:, 1, qs, :], in1=pw_r, op=ALU.mult), ["CT", "Pw18"], ["t4y"])
                            dvy(lambda e, o_im=o_im: e.tensor_tensor(out=o_im, in0=t3y_[:], in1=t4y_[:], op=ALU.add), ["t3y", "t4y"], ["Y0"])
                with ExitStack() as pc2:
                    hp_ = [pc2.enter_context(nc.sbuf_tensor(f"hP{i}", [128, 2, 32], F32)) for i in range(2)]
                    hq_ = [pc2.enter_context(nc.sbuf_tensor(f"hQ{i}", [128, 2, 32], F32)) for i in range(2)]
                    S.op("vector", lambda e: e.tensor_copy(out=Hall[:, 0, :, :], in_=cacc[:]), reads=["cacc"], writes=["H0"])
                    for st in range(64):
                        b2 = st % 2
                        hcur = Hall[:, st, :, :]
                        hswap = bass.AP(Hall, st * 64 + 32, [[65 * 64, 128], [-32, 2], [1, 32]])
                        gcat = bass.AP(G2, st * 64, [[64 * 64, 128], [32, 2], [(63 - 2 * st) * 64 + 16, 2], [1, 16]])
                        S.op("vector", lambda e, hcur=hcur, b2=b2: e.tensor_tensor(out=hp_[b2][:], in0=hcur, in1=A64[:, 0, :, :], op=ALU.mult),
                             reads=[f"H{st}", "A64"], writes=[f"hP{b2}"])
                        S.op("vector", lambda e, hswap=hswap, b2=b2: e.tensor_tensor(out=hq_[b2][:], in0=hswap, in1=A64[:, 1, :, :], op=ALU.mult),
                             reads=[f"H{st}", "A64"], writes=[f"hQ{b2}"])
                        S.op("vector", lambda e, b2=b2: e.tensor_tensor(out=hp_[b2][:], in0=hp_[b2][:], in1=hq_[b2][:], op=ALU.add),
                             reads=[f"hP{b2}", f"hQ{b2}"], writes=[f"hP{b2}"])
                        S.op("vector", lambda e, b2=b2, gcat=gcat, st=st: e.tensor_tensor(
                            out=Hall[:, st + 1, :, :].rearrange("p c (d q) -> p c d q", d=2), in0=hp_[b2][:].rearrange("p c (d q) -> p c d q", d=2),
                            in1=gcat, op=ALU.add), reads=[f"hP{b2}", "G2"], writes=[f"H{st + 1}"])
                    S.op("vector", lambda e: e.tensor_copy(out=Hb[:].rearrange("p n c q -> p (n c q)"),
                                                           in_=Hall[:, 0:64, :, :].rearrange("p n c q -> p (n c q)")),
                         reads=[f"H{i}" for i in range(65)], writes=["Hb"])
                    S.barrier()
                if debug:
                    S.dma("sync", lambda e: e.dma_start(out=dbg_s5.ap()[:, 20992:20992 + 4096], in_=G2[:].rearrange("p n c q -> p (n c q)")), reads=["G2"])
                    S.dma("sync", lambda e: e.dma_start(out=dbg_s5.ap()[:, 25088:25088 + 4160], in_=Hall[:].rearrange("p n c q -> p (n c q)")), reads=["Hb"])
                with ExitStack() as pc3:
                    YT = [[pc3.enter_context(nc.sbuf_tensor(f"YT{d_}{i}", [128, 8, 2, 128], BF16)) for i in range(2)] for d_ in range(2)]
                    trt = [[pc3.enter_context(nc.sbuf_tensor(f"trs{i}{j}", [128, 128], F32)) for j in range(2)] for i in range(7)]
                    Tsb = [[[pc3.enter_context(nc.sbuf_tensor(f"T{d_}{j2}{i}", [128, 8, 128], BF16)) for i in range(2)]
                            for j2 in range(2)] for d_ in range(2)]
                    ysb = [pc3.enter_context(nc.sbuf_tensor(f"ysb{i}", [128, 512], F32)) for i in range(2)]
                    t_ps = [pc3.enter_context(nc.psum_tensor(f"ps_T{i}", [128, 1024], F32)) for i in range(2)]
                    y_ps = [pc3.enter_context(nc.psum_tensor(f"ps_y{i}", [128, 512], F32)) for i in range(2)]
                    tn = 0; tpn = 0; yn = 0
                    for qq in range(16):
                        pb_ = qq % 2
                        for dr in range(2):
                            q = dr * 16 + qq
                            items = []
                            for m in range(8):
                                mu = m if dr == 0 else 7 - m
                                outs = trt[tn % 7]; tk_ = f"trs{tn % 7}"; tn += 1
                                yb = YT[dr][pb_]
                                if mu == 0:
                                    S.op("scalar", lambda e, yb=yb, m=m, q=q: e.activation(out=yb[:, m, 0, :], in_=Y0[:, 0, q, :], func=AF.Copy),
                                         reads=["Y0"], writes=[f"YT{dr}{pb_}_{m}"])
                                    S.op("vector", lambda e, yb=yb, m=m, q=q: e.tensor_scalar(out=yb[:, m, 1, :], in0=Y0[:, 1, q, :], scalar1=-1.0,
                                                                                            scalar2=None, op0=ALU.mult),
                                         reads=["Y0"], writes=[f"YT{dr}{pb_}_{m}"])
                                else:
                                    items.append((mu, f"YT{dr}{pb_}_{m}", (lambda comp, yb=yb, m=m: yb[:, m, comp, :]), outs, tk_))
                            rot_batch(Y0, "Y0", q, items, True)
                            for j2 in range(2):
                                tp = t_ps[tpn % 2]; tpk = f"ps_T{tpn % 2}"; tpn += 1
                                hs = slice(64 * j2, 64 * j2 + 64)
                                for half in range(2):
                                    for comp in range(2):
                                        S.op("tensor", lambda e, tp=tp, hs=hs, half=half, comp=comp, q=q, yb=yb: e.matmul(
                                            out=tp[:, half * 512:(half + 1) * 512].rearrange("p (m x) -> p m x", m=4),
                                            lhsT=Xpb[hs, comp, q, :], rhs=yb[hs, half * 4:(half + 1) * 4, comp, :],
                                            start=(comp == 0), stop=(comp == 1)),
                                            reads=["Xpb"] + [f"YT{dr}{pb_}_{m}" for m in range(8)], writes=[tpk])
                                tsb = Tsb[dr][j2][pb_]; tsk = f"T{dr}{j2}{pb_}"
                                m0 = 0 if dr == 0 else 7
                                S.op("scalar", lambda e, tp=tp, tsb=tsb: e.activation(
                                    out=tsb[:].rearrange("p m x -> p (m x)"), in_=tp[:], func=AF.Copy), reads=[tpk], writes=[tsk])
                                S.op("vector", lambda e, tp=tp, tsb=tsb, m0=m0, dr=dr: e.tensor_tensor(
                                    out=tsb[:, m0, :], in0=tp[:, m0 * 128:(m0 + 1) * 128], in1=maskfb[:, dr, :], op=ALU.mult),
                                    reads=[tpk, "maskfb"], writes=[tsk])
                        for j2 in range(2):
                            g = qq * 2 + j2
                            hs = slice(64 * j2, 64 * j2 + 64)
                            yp = y_ps[yn % 2]; ypk = f"ps_y{yn % 2}"; yb_ = ysb[yn % 2]; ybk = f"ysb{yn % 2}"; yn += 1
                            u1 = U_all[:, g, :]
                            first = True
                            for dr in range(2):
                                tsb = Tsb[dr][j2][pb_]; tsk = f"T{dr}{j2}{pb_}"
                                for dl in range(8):
                                    blk = dl if dr == 0 else 7 - dl
                                    if dr == 0:
                                        o_ap, r_ap = yp[:, dl * 64:512], u1[:, 0:(8 - dl) * 64]
                                    else:
                                        o_ap, r_ap = yp[:, 0:(8 - dl) * 64], u1[:, dl * 64:512]
                                    S.op("tensor", lambda e, o_ap=o_ap, r_ap=r_ap, tsb=tsb, blk=blk, first=first: e.matmul(
                                        out=o_ap, lhsT=tsb[:, blk, :], rhs=r_ap, start=first, stop=False, skip_group_check=True),
                                        reads=[tsk, "U_all"], writes=[ypk])
                                    first = False
                            for dr in range(2):
                                q = dr * 16 + qq
                                yb = YT[dr][pb_]
                                for m in range(8):
                                    for comp in range(2):
                                        if dr == 0:
                                            rhs = Hb[hs, :, comp, q]
                                        else:
                                            rhs = bass.AP(Hb, 64 * j2 * (64 * 64) + 63 * 64 + comp * 32 + q, [[64 * 64, 64], [-64, 64]])
                                        last = (dr == 1 and m == 7 and comp == 1)
                                        S.op("tensor", lambda e, yp=yp, m=m, hs=hs, yb=yb, comp=comp, rhs=rhs, last=last: e.matmul(
                                            out=yp[:, m * 64:(m + 1) * 64], lhsT=yb[hs, m, comp, :], rhs=rhs, start=False, stop=last,
                                            skip_group_check=True),
                                            reads=[f"YT{dr}{pb_}_{m}", "Hb"], writes=[ypk])
                            S.op("scalar", lambda e, yp=yp, yb_=yb_: e.activation(
                                out=yb_[:].rearrange("p (n m) -> p m n", m=8), in_=yp[:].rearrange("p (m n) -> p m n", m=8), func=AF.Copy),
                                reads=[ypk], writes=[ybk])
                            for tq in range(8):
                                S.dma("sync", lambda e, g=g, yb_=yb_, tq=tq: e.dma_start(
                                    out=y_s.ap()[16 * g:16 * g + 16, tq, :], in_=yb_[tq * 16:(tq + 1) * 16, :]),
                                    reads=[ybk], writes=["y_s"])

        if stage >= 4:
            S.barrier()
            with ExitStack() as pd:
                def sbd(name, shape, dt=F32):
                    return pd.enter_context(nc.sbuf_tensor(name, list(shape), dt))

                def psd(name, shape, dt=F32):
                    return pd.enter_context(nc.psum_tensor(name, list(shape), dt))
                udd = sbd("udD", [128, 4, 8, 512], BF16)
                yall = sbd("yall", [128, 4, 8, 512])
                wglu_b = sbd("wglu_b", [128, 4, 512], BF16)
                y2 = [sbd(f"y2{i}", [128, 8, 64]) for i in range(4)]
                g32 = [sbd(f"g32{i}", [128, 4, 512]) for i in range(2)]
                gb = [sbd(f"gb{i}", [128, 4, 512], BF16) for i in range(2)]
                sg = [sbd(f"sg{i}", [128, 512]) for i in range(4)]
                o32 = [sbd(f"o32{i}", [128, 4, 512]) for i in range(2)]
                sq = [sbd(f"sq{i}", [128, 512]) for i in range(4)]
                rs = [sbd(f"rs{i}", [128, 512]) for i in range(2)]
                sn = [sbd(f"sn{i}", [128, 4, 512], BF16) for i in range(2)]
                z_ps = [psd(f"ps_z{i}", [128, 512]) for i in range(4)]
                ss_ps = [psd(f"ps_ss{i}", [128, 512]) for i in range(2)]
                S.dma("gpsimd", lambda e: e.dma_start(out=wglu_b[:], in_=w_glu.ap().rearrange("(kc p) n -> p kc n", p=128)),
                      writes=["wglu_b"])
                for cc in range(4):
                    for gl in range(8):
                        S.dma("sync", lambda e, cc=cc, gl=gl: e.dma_start(
                            out=udd[gl * 16:(gl + 1) * 16, cc, :, :],
                            in_=Us.ap()[0, cc * 8 + gl].rearrange("(s c) n -> c s n", c=16)), reads=["Us"], writes=["udD"])
                    S.dma("sync", lambda e, cc=cc: e.dma_start(out=yall[:, cc, :, :], in_=y_s.ap()[cc * 128:(cc + 1) * 128]),
                          reads=["y_s"], writes=[f"yall{cc}"])
                yn = 0
                def d_part(tt, part):
                    nonlocal yn
                    b = tt % 2
                    nsl = slice(tt * 64, tt * 64 + 64)
                    if part == 1:
                        for cc in range(4):
                            S.op("vector", lambda e, cc=cc, nsl=nsl: e.scalar_tensor_tensor(
                                out=y2[cc][:], in0=udd[:, cc, :, nsl], scalar=cols[:, C_DS + cc:C_DS + cc + 1], in1=yall[:, cc, :, nsl],
                                op0=ALU.mult, op1=ALU.add), reads=["udD", f"yall{cc}", "cols"], writes=[f"y2{cc}"])
                        for cc in range(4):
                            S.op("scalar", lambda e, cc=cc, b=b: e.activation(
                                out=g32[b][:, cc, :].rearrange("p (n s) -> p s n", s=8), in_=y2[cc][:], func=AF.Gelu_apprx_tanh),
                                reads=[f"y2{cc}"], writes=[f"g32{b}_{cc}"])
                        for cc in range(4):
                            S.op("vector", lambda e, cc=cc, b=b: e.tensor_copy(out=gb[b][:, cc, :], in_=g32[b][:, cc, :]),
                                 reads=[f"g32{b}_{cc}"], writes=[f"gb{b}_{cc}"])
                        return
                    sp = ss_ps[b]; spk = f"ps_ss{b}"
                    for jc in range(4):
                        zp = z_ps[jc]; zpk = f"ps_z{jc}"
                        for kc in range(4):
                            S.op("tensor", lambda e, zp=zp, kc=kc, jc=jc, b=b: e.matmul(
                                out=zp[:], lhsT=wglu_b[:, kc, jc * 128:(jc + 1) * 128], rhs=gb[b][:, kc, :],
                                start=(kc == 0), stop=(kc == 3)), reads=["wglu_b", f"gb{b}_{kc}"], writes=[zpk])
                    for jc in range(4):
                        S.op("scalar", lambda e, jc=jc: e.activation(
                            out=sg[jc][:], in_=z_ps[jc][:], func=AF.Sigmoid, bias=cols[:, C_BG + jc:C_BG + jc + 1]),
                            reads=[f"ps_z{jc}", "cols"], writes=[f"sg{jc}"])
                    for jc in range(4):
                        S.op("vector", lambda e, jc=jc, b=b: e.tensor_tensor(
                            out=o32[b][:, jc, :], in0=g32[b][:, jc, :], in1=sg[jc][:], op=ALU.mult),
                            reads=[f"g32{b}_{jc}", f"sg{jc}"], writes=[f"o32{b}_{jc}"])
                    for jc in range(4):
                        S.op("scalar", lambda e, jc=jc, b=b: e.activation(out=sq[jc][:], in_=o32[b][:, jc, :], func=AF.Square),
                             reads=[f"o32{b}_{jc}"], writes=[f"sq{jc}"])
                    for jc in range(4):
                        S.op("tensor", lambda e, sp=sp, jc=jc: e.matmul(out=sp[:], lhsT=ones_f[:], rhs=sq[jc][:],
                                                                       start=(jc == 0), stop=(jc == 3)),
                             reads=["ones_f", f"sq{jc}"], writes=[spk])
                    S.op("vector", lambda e, sp=sp, b=b: e.tensor_scalar(out=rs[b][:], in0=sp[:], scalar1=1.0 / 512, scalar2=EPS,
                                                                         op0=ALU.mult, op1=ALU.add), reads=[spk], writes=[f"rs{b}"])
                    S.op("scalar", lambda e, b=b: e.activation(out=rs[b][:], in_=rs[b][:], func=AF.Sqrt),
                         reads=[f"rs{b}"], writes=[f"rs{b}"])
                    S.op("vector", lambda e, b=b: e.reciprocal(out=rs[b][:], in_=rs[b][:]), reads=[f"rs{b}"], writes=[f"rs{b}"])
                    for jc in range(4):
                        S.op("vector", lambda e, jc=jc, b=b: e.scalar_tensor_tensor(
                            out=sn[b][:, jc, :], in0=o32[b][:, jc, :], scalar=cols[:, C_SG + jc:C_SG + jc + 1], in1=rs[b][:],
                            op0=ALU.mult, op1=ALU.mult), reads=[f"o32{b}_{jc}", f"rs{b}", "cols"], writes=[f"sn{b}"])
                    S.dma("sync", lambda e, b=b, tt=tt: e.dma_start(
                        out=ssmn_s.ap()[:, tt * 512:(tt + 1) * 512].rearrange("(jc p) n -> p jc n", p=128), in_=sn[b][:]),
                        reads=[f"sn{b}"], writes=["ssmn_s"])
                for tt in range(9):
                    if tt < 8:
                        d_part(tt, 1)
                    if tt >= 1:
                        d_part(tt - 1, 2)

        if stage >= 5:
            S.barrier()
            with ExitStack() as pp_:
                def sbp(name, shape, dt=F32):
                    return pp_.enter_context(nc.sbuf_tensor(name, list(shape), dt))

                def psp(name, shape, dt=F32):
                    return pp_.enter_context(nc.psum_tensor(name, list(shape), dt))
                wo_a = sbp("wo_a", [128, 4, D], BF16)
                wo_s = sbp("wo_s", [128, 4, D], BF16)
                w2_b = sbp("w2_b", [128, NFC, D], BF16)
                agT = sbp("agT", [64, 8])
                xq = [sbp(f"xp{i}", [128, 4, D]) for i in range(2)]
                at = [sbp("at0", [128, 4, 512], BF16)] * 2
                st_ = [sbp(f"st{i}", [128, 4, 512], BF16) for i in range(2)]
                asq = [sbp(f"asq{i}", [128, 512]) for i in range(2)]
                ars = sbp("ars", [128, 512])
                junq = sbp("junkp", [128, D], BF16)
                ssq = sbp("ssp", [128, 16])
                rsq = sbp("rstdp", [128, 16])
                xnq = [sbp(f"xnp{i}", [128, 4, D], BF16) for i in range(2)]
                an_ = [x_[:, 0:2, :].rearrange("p a (b n) -> p (a b) n", n=512) for x_ in xnq]
                h2T = sbp("h2T", [128, 8, 512], BF16)
                w13 = [sbp(f"w13{i}", [128, 2, 8, 128], BF16) for i in range(3)]
                silu = [sbp("silu0", [128, 512])] * 2
                actT = sbp("actT", [128, NFC, 512], BF16)
                yo = [sbp("yo0", [128, D])] * 2
                mm_ps = [psp(f"ps_mm{i}", [128, 512]) for i in range(2)]
                tq_ps = [psp(f"tp_pp{i}", [128, 512], BF16) for i in range(2)]
                h_ps = [psp(f"ps_h{i}", [128, 512]) for i in range(4)]
                wn_ = 0
                for c in range(4):
                    wst = xq[wn_ % 2]; wsk = f"xp{wn_ % 2}"; wn_ += 1
                    S.dma("sync", lambda e, wst=wst, c=c: e.dma_start(out=wst[:, 0, :], in_=w_o.ap()[c * 128:(c + 1) * 128, :]), writes=[wsk])
                    S.op("vector", lambda e, wst=wst, c=c: e.tensor_tensor(out=wo_a[:, c, :], in0=wst[:, 0, :], in1=rows[:, 0, :], op=ALU.mult),
                         reads=[wsk, "rows"], writes=["wo_a"])
                for c in range(4):
                    wst = xq[wn_ % 2]; wsk = f"xp{wn_ % 2}"; wn_ += 1
                    S.dma("sync", lambda e, wst=wst, c=c: e.dma_start(out=wst[:, 0, :], in_=w_o.ap()[512 + c * 128:512 + (c + 1) * 128, :]), writes=[wsk])
                    S.op("vector", lambda e, wst=wst, c=c: e.tensor_tensor(out=wo_s[:, c, :], in0=wst[:, 0, :], in1=rows[:, 0, :], op=ALU.mult),
                         reads=[wsk, "rows"], writes=["wo_s"])
                for j in range(NFC):
                    wst = xq[wn_ % 2]; wsk = f"xp{wn_ % 2}"; wn_ += 1
                    S.dma("sync", lambda e, wst=wst, j=j: e.dma_start(out=wst[:, 0, :], in_=w2.ap()[j * 128:(j + 1) * 128, :]), writes=[wsk])
                    S.op("vector", lambda e, wst=wst, j=j: e.tensor_tensor(out=w2_b[:, j, :], in0=wst[:, 0, :], in1=rows[:, 1, :], op=ALU.mult),
                         reads=[wsk, "rows"], writes=["w2_b"])
                S.dma("sync", lambda e: e.dma_start(out=agT[:], in_=attn_gT.ap()), writes=["agT"])
                mmn = 0; hn = 0; wn = 0; yon = 0
                NT = 8
                def p_front(tt, part):
                    nonlocal mmn
                    b = tt % 2
                    tsl = slice(tt * 512, (tt + 1) * 512)
                    if part == 1:
                        S.dma("sync", lambda e, tt=tt, b=b: e.dma_start(
                            out=xq[b][:], in_=xext.ap()[HALO + tt * 512:HALO + (tt + 1) * 512, :].rearrange("(st p) d -> p st d", p=128)),
                            writes=[f"xp{b}"])
                        S.dma("gpsimd", lambda e, tsl=tsl, b=b: e.dma_start(out=at[b][:], in_=attn_s.ap()[:, :, tsl].rearrange("(c t) d n -> (t d) c n", t=2)),
                              reads=["attn_s"], writes=["at0"])
                        S.dma("sync", lambda e, tsl=tsl, b=b: e.dma_start(
                            out=st_[b][:], in_=ssmn_s.ap()[:, tsl].rearrange("(c p) n -> p c n", p=128)), reads=["ssmn_s"], writes=[f"st{b}"])
                        ap_ = mm_ps[mmn % 2]; apk = f"ps_mm{mmn % 2}"; mmn += 1
                        for h in range(4):
                            S.op("scalar", lambda e, h=h, b=b: e.activation(out=asq[h % 2][:], in_=at[b][:, h, :], func=AF.Square),
                                 reads=["at0"], writes=[f"asq{h % 2}"])
                            S.op("tensor", lambda e, ap_=ap_, h=h: e.matmul(out=ap_[:], lhsT=ones_f[:], rhs=asq[h % 2][:],
                                                                           start=(h == 0), stop=(h == 3)),
                                 reads=["ones_f", f"asq{h % 2}"], writes=[apk])
                        S.op("vector", lambda e, ap_=ap_: e.tensor_scalar(out=ars[:], in0=ap_[:], scalar1=1.0 / 512, scalar2=EPS,
                                                                          op0=ALU.mult, op1=ALU.add), reads=[apk], writes=["ars"])
                        S.op("scalar", lambda e: e.activation(out=ars[:], in_=ars[:], func=AF.Sqrt), reads=["ars"], writes=["ars"])
                        S.op("vector", lambda e: e.reciprocal(out=ars[:], in_=ars[:]), reads=["ars"], writes=["ars"])
                        for h in range(4):
                            S.op("vector", lambda e, h=h, b=b: e.scalar_tensor_tensor(
                                out=an_[b][:, h, :], in0=at[b][:, h, :], scalar=cols[:, C_AG + h:C_AG + h + 1], in1=ars[:], op0=ALU.mult, op1=ALU.mult),
                                reads=["at0", "cols", "ars"], writes=[f"an{b}_{h}"] + [f"xnp{b}_{i}" for i in range(4)])
                        return
                    for st in range(4):
                        tk = slice(st * 128, (st + 1) * 128)
                        for half in range(2):
                            hs_ = slice(half * 512, (half + 1) * 512)
                            mp = mm_ps[mmn % 2]; mpk = f"ps_mm{mmn % 2}"; mmn += 1
                            for h in range(4):
                                S.op("tensor", lambda e, mp=mp, h=h, tk=tk, hs_=hs_: e.matmul(
                                    out=mp[:], lhsT=an_[b][:, h, tk], rhs=wo_a[:, h, hs_], start=(h == 0), stop=False),
                                    reads=[f"an{b}_{h}", "wo_a"], writes=[mpk])
                            for c in range(4):
                                S.op("tensor", lambda e, mp=mp, c=c, tk=tk, hs_=hs_, b=b: e.matmul(
                                    out=mp[:], lhsT=st_[b][:, c, tk], rhs=wo_s[:, c, hs_], start=False, stop=(c == 3)),
                                    reads=[f"st{b}", "wo_s"], writes=[mpk])
                            S.op("vector", lambda e, mp=mp, b=b, st=st, hs_=hs_: e.tensor_tensor(
                                out=xq[b][:, st, hs_], in0=mp[:], in1=xq[b][:, st, hs_], op=ALU.add),
                                reads=[mpk, f"xp{b}"], writes=[f"xp{b}"])
                    for st in range(4):
                        S.op("scalar", lambda e, b=b, st=st: e.activation(
                            out=junq[:], in_=xq[b][:, st, :], func=AF.Square, accum_out=ssq[:, 8 * b + st:8 * b + st + 1]),
                            reads=[f"xp{b}"], writes=["junkp", f"ssp{b}"])
                    S.op("vector", lambda e, b=b: e.tensor_scalar(out=rsq[:, 8 * b:8 * b + 4], in0=ssq[:, 8 * b:8 * b + 4], scalar1=1.0 / D, scalar2=EPS,
                                                             op0=ALU.mult, op1=ALU.add), reads=[f"ssp{b}"], writes=[f"rstdp{b}"])
                    S.op("scalar", lambda e, b=b: e.activation(out=rsq[:, 8 * b:8 * b + 4], in_=rsq[:, 8 * b:8 * b + 4], func=AF.Sqrt), reads=[f"rstdp{b}"], writes=[f"rstdp{b}"])
                    S.op("vector", lambda e, b=b: e.reciprocal(out=rsq[:, 8 * b:8 * b + 4], in_=rsq[:, 8 * b:8 * b + 4]), reads=[f"rstdp{b}"], writes=[f"rstdp{b}"])
                    for st in range(4):
                        S.op("scalar", lambda e, b=b, st=st: e.activation(
                            out=xnq[b][:, st, :], in_=xq[b][:, st, :], func=AF.Copy, scale=rsq[:, 8 * b + st:8 * b + st + 1]),
                            reads=[f"xp{b}", f"rstdp{b}"], writes=[f"xnp{b}_{st}"])
                def p_main(tt):
                    nonlocal mmn, hn, wn, yon
                    b = tt % 2
                    for fc in range(8):
                        tb_ = fc % 2
                        for st in range(4):
                            S.op("tensor", lambda e, st=st, fc=fc, tb_=tb_: e.transpose(
                                out=tq_ps[tb_][:, st * 128:(st + 1) * 128], in_=xnq[b][:, st, fc * 128:(fc + 1) * 128], identity=ident_b[:]),
                                reads=[f"xnp{b}_{st}", "ident_b"], writes=[f"tp_pp{tb_}"])
                        S.op("scalar", lambda e, fc=fc, tb_=tb_: e.activation(
                            out=h2T[:, fc, :], in_=tq_ps[tb_][:], func=AF.Identity, scale=sc12[:, 8 + fc:9 + fc],
                            bias=modc[:, 24 + fc:25 + fc]), reads=[f"tp_pp{tb_}", "sc12b", "modc"], writes=[f"h2T{fc}"])
                    for j in range(NFC):
                        wb = w13[wn % 3]; wbk = f"w13{wn % 3}"; wn += 1
                        S.dma("sync", lambda e, wb=wb, j=j: e.dma_start(out=wb[:, 0, :, :], in_=w1b.ap()[j]), reads=["wffn_b"], writes=[wbk])
                        S.dma("sync", lambda e, wb=wb, j=j: e.dma_start(out=wb[:, 1, :, :], in_=w3b.ap()[j]), reads=["wffn_b"], writes=[wbk])
                        h1 = h_ps[hn % 4]; h1k = f"ps_h{hn % 4}"; hn += 1
                        h3 = h_ps[hn % 4]; h3k = f"ps_h{hn % 4}"; hn += 1
                        for (hpp, hkk, wi) in ((h1, h1k, 0), (h3, h3k, 1)):
                            for kc in range(8):
                                S.op("tensor", lambda e, hpp=hpp, wb=wb, wi=wi, kc=kc: e.matmul(
                                    out=hpp[:], lhsT=wb[:, wi, kc, :], rhs=h2T[:, kc, :], start=(kc == 0), stop=(kc == 7)),
                                    reads=[wbk, f"h2T{kc}"], writes=[hkk])
                        sb_ = silu[0]; sbk = "silu0"
                        S.op("scalar", lambda e, h1=h1, sb_=sb_: e.activation(out=sb_[:], in_=h1[:], func=AF.Silu),
                             reads=[h1k], writes=[sbk])
                        S.op("vector", lambda e, h3=h3, sb_=sb_, j=j: e.tensor_tensor(out=actT[:, j, :], in0=h3[:], in1=sb_[:], op=ALU.mult),
                             reads=[h3k, sbk], writes=[f"actT{j}"])
                        if j == 3 and tt + 1 < NT:
                            p_front(tt + 1, 1)
                        if j == 10 and tt + 1 < NT:
                            p_front(tt + 1, 2)
                    for st in range(4):
                        tk = slice(st * 128, (st + 1) * 128)
                        for half in range(2):
                            hs_ = slice(half * 512, (half + 1) * 512)
                            mp = mm_ps[mmn % 2]; mpk = f"ps_mm{mmn % 2}"; mmn += 1
                            for j in range(NFC):
                                S.op("tensor", lambda e, mp=mp, j=j, tk=tk, hs_=hs_: e.matmul(
                                    out=mp[:], lhsT=actT[:, j, tk], rhs=w2_b[:, j, hs_], start=(j == 0), stop=(j == NFC - 1)),
                                    reads=[f"actT{j}", "w2_b"], writes=[mpk])
                            S.op("vector", lambda e, mp=mp, b=b, st=st, hs_=hs_: e.tensor_tensor(
                                out=xq[b][:, st, hs_], in0=mp[:], in1=xq[b][:, st, hs_], op=ALU.add),
                                reads=[mpk, f"xp{b}"], writes=[f"xp{b}"])
                        S.op("scalar", lambda e, b=b, st=st: e.activation(
                            out=junq[:], in_=xq[b][:, st, :], func=AF.Square, accum_out=ssq[:, 4 + st:5 + st]),
                            reads=[f"xp{b}"], writes=["junkp", f"ss3_{st}"])
                        S.op("vector", lambda e, st=st: e.tensor_scalar(out=rsq[:, 4 + st:5 + st], in0=ssq[:, 4 + st:5 + st], scalar1=1.0 / D,
                                                                        scalar2=EPS, op0=ALU.mult, op1=ALU.add),
                             reads=[f"ss3_{st}"], writes=[f"rs3_{st}"])
                        S.op("scalar", lambda e, st=st: e.activation(out=rsq[:, 4 + st:5 + st], in_=rsq[:, 4 + st:5 + st], func=AF.Sqrt),
                             reads=[f"rs3_{st}"], writes=[f"rs3_{st}"])
                        S.op("vector", lambda e, st=st: e.reciprocal(out=rsq[:, 4 + st:5 + st], in_=rsq[:, 4 + st:5 + st]),
                             reads=[f"rs3_{st}"], writes=[f"rs3_{st}"])
                        yb = yo[0]; ybk = "yo0"
                        S.op("vector", lambda e, b=b, st=st, yb=yb: e.scalar_tensor_tensor(
                            out=yb[:], in0=xq[b][:, st, :], scalar=rsq[:, 4 + st:5 + st], in1=rows[:, 2, :], op0=ALU.mult, op1=ALU.mult),
                            reads=[f"xp{b}", f"rs3_{st}", "rows"], writes=[ybk])
                        S.dma("gpsimd", lambda e, yb=yb, tt=tt, st=st: e.dma_start(
                            out=y_out.ap()[tt * 512 + st * 128: tt * 512 + (st + 1) * 128, :], in_=yb[:]), reads=[ybk], writes=["y_out"])

                p_front(0, 1)
                p_front(0, 2)
                for tt in range(NT):
                    p_main(tt)
        S.emit()
        nops = len(S.ops)
    return nc, nops


def _rope_tables(pos):
    inv = 1.0 / (10000.0 ** (np.arange(0, 64, 2, dtype=np.float64) / 64.0))
    ang = pos.astype(np.float64)[:, None] * inv[None, :]
    c, s = np.cos(ang).astype(np.float32), np.sin(ang).astype(np.float32)
    cos2 = np.concatenate([c, c], 1)
    sin2 = np.concatenate([-s, s], 1)
    cosT = np.ascontiguousarray(np.concatenate([cos2, cos2], 1).T)
    sinT = np.ascontiguousarray(np.concatenate([sin2, sin2], 1).T)
    return cosT, sinT


def _consts():
    c = np.zeros((128, 1280), np.float32)
    c[:, 0:128] = np.eye(128, dtype=np.float32)
    perm = np.zeros((128, 128), np.float32)
    for m in range(128):
        j = m % 64
        k = m - j + (j + 32) % 64
        perm[k, m] = 1.0
    c[:, 128:256] = perm
    kk = np.arange(128)[:, None]
    p = np.arange(128)[None, :]
    c[:, 256:384] = np.where(kk >= p, 0.0, -30000.0)
    c[:, 384:512] = np.where(kk <= p, 0.0, -30000.0)
    c[:, 512:640] = 1.0
    for i in range(5):
        c[:, 640 + 128 * i:768 + 128 * i] = (kk >= p) if i % 2 == 0 else (kk <= p)
    return c


def _consts2():
    c = np.zeros((128, 72 + 256), np.float32)
    c[:, 0:64] = np.arange(1, 65, dtype=np.float32)[None, :]
    c[:, 64:72] = np.arange(-8, 0, dtype=np.float32)[None, :]
    sidx = (np.arange(128) // 16)[:, None]
    tidx = (np.arange(128) // 16)[None, :]
    c[:, 72:200] = (tidx >= sidx)
    c[:, 200:328] = (tidx <= sidx)
    return c


def make_in_maps(inp):
    f = lambda a: np.ascontiguousarray(np.asarray(a, dtype=np.float32))
    xp = f(inp["x_prompt"])[0]
    xs = f(inp["x_sample"])
    cp = f(inp["c_prompt"])
    csm = f(inp["c_sample"])
    shared = {
        "consts": _consts(),
        "w_ada": f(inp["w_ada"])[0],
        "b_ada": f(inp["b_ada"]).reshape(48, 128),
        "norm1_g": f(inp["norm1_g"]).reshape(8, 128),
        "norm2_g": f(inp["norm2_g"]).reshape(8, 128),
        "final_g": f(inp["final_g"]).reshape(8, 128),
        "attn_norm_g": f(inp["attn_norm_g"]).reshape(4, 128),
        "ssm_norm_g": f(inp["ssm_norm_g"]).reshape(4, 128),
        "d_skip": f(inp["d_skip"]).reshape(4, 128),
        "b_glu": f(inp["b_glu"]).reshape(4, 128),
        "w_in": f(inp["w_in"])[0],
        "consts2": _consts2(),
        "lam_re": f(inp["lam_re"]).reshape(32, 128),
        "lam_im": f(inp["lam_im"]).reshape(32, 128),
        "log_dt": f(inp["log_dt"]).reshape(32, 2),
        "b_re": f(inp["b_re"]).reshape(32, 128, 16),
        "b_im": f(inp["b_im"]).reshape(32, 128, 16),
        "c_re": f(inp["c_re"]).reshape(64, 16, 64),
        "c_im": f(inp["c_im"]).reshape(64, 16, 64),
        "w_glu": f(inp["w_glu"])[0],
        "w_o": f(inp["w_o"])[0],
        "w1": f(inp["w1"])[0],
        "w3": f(inp["w3"])[0],
        "w2": f(inp["w2"])[0],
        "attn_gT": np.ascontiguousarray(f(inp["attn_norm_g"]).reshape(8, 64).T),
    }
    maps = []
    for core in range(8):
        if core < 4:
            seq, start, L, c = xs[core], 0, SEG, csm[core]
        else:
            seq, start, L, c = xp, (core - 4) * SEG, 4 * SEG, cp[0]
        xe = np.zeros((EXT, D), np.float32)
        lo, hi = start - HALO, start + SEG + HALO
        slo, shi = max(lo, 0), min(hi, L)
        xe[slo - lo:shi - lo] = seq[slo:shi]
        pos = np.arange(lo, hi)
        cT, sT = _rope_tables(pos)
        valid = ((pos >= 0) & (pos < L))
        fl = np.zeros((128, 6), np.float32)
        if core < 4:
            xo = np.zeros((3 * SEG, D), np.float32)
        else:
            r = core - 4
            others = [j for j in range(4) if j != r]
            xo = np.concatenate([seq[j * SEG:(j + 1) * SEG] for j in others], 0)
            for i, j in enumerate(others):
                fl[:, i] = 1.0 if j < r else 0.0
                fl[:, 3 + i] = 1.0 if j > r else 0.0
        m = dict(shared)
        m["xoth"] = np.ascontiguousarray(xo)
        m["flcol"] = fl
        m["tvalid"] = np.ascontiguousarray(valid.astype(np.float32).reshape(EXT // 128, 128).T)
        m.update({"xext": xe, "cvec": np.ascontiguousarray(c.reshape(8, 128)), "cosT": cT, "sinT": sT})
        maps.append(m)
    return maps


_CACHE = {}


def kernel(**inputs):
    maps = make_in_maps(inputs)
    if "nc" not in _CACHE:
        _CACHE["nc"] = build(debug=False)[0]
    nc = _CACHE["nc"]
    res = run_bass_kernel_spmd(nc, maps, core_ids=list(range(8)))
    outs = [np.asarray(r["y_out"], dtype=np.float32) for r in res.results]
    y_sample = np.stack(outs[0:4], 0)
    y_prompt = np.concatenate(outs[4:8], 0)[None]
    return (y_prompt, y_sample)
```
